# Optimizing a Trainium2 kernel written in Bass

```python
import math
import jax, jax.numpy as jnp
from jax import lax
import numpy as np

D_MODEL = 1024
BATCH = 8
SEQ = 4096
DEPTH = 1

D_FF = 2816
RMS_EPS = 1e-6
N_DIR = 2
D_HYENA = 512
HYENA_ORDER = 2
HYENA_BANDS = 16
HYENA_EMB = 2 * HYENA_BANDS + 1
HYENA_FFN = 64
HYENA_MIN_DECAY = abs(math.log(1e-2)) / 1.5
HYENA_MAX_DECAY = abs(math.log(1e-2)) / 0.3
FILTER_EPS = 1e-6
D_RWKV = 512
RWKV_HEAD = 64
RWKV_HEADS = D_RWKV // RWKV_HEAD
DECAY_LORA = 64
AAA_LORA = 64
GATE_LORA = 160
GN_EPS = 64e-5
N_HY_IN = 3 * D_HYENA
N_RW_IN = 3 * D_RWKV + N_DIR * (DECAY_LORA + AAA_LORA) + GATE_LORA
N_GATE = 2 * D_MODEL
N_IN = N_HY_IN + N_RW_IN + N_GATE

kernel_name = "hybrid_hyena_rwkv7_macaron_encoder"


def _rmsnorm(x, g):
    xf = x.astype(jnp.float32)
    y = xf * lax.rsqrt(jnp.mean(xf * xf, axis=-1, keepdims=True) + RMS_EPS)
    return (y * g.astype(jnp.float32)).astype(x.dtype)


def _swiglu(h, w_gu, w_down):
    gate, up = jnp.split(h @ w_gu, 2, axis=-1)
    return (jax.nn.silu(gate) * up) @ w_down


def _centred_conv3(u, w, b):
    up = jnp.pad(u, ((0, 0), (1, 1), (0, 0)))
    return w[0] * up[:, :-2] + w[1] * up[:, 1:-1] + w[2] * up[:, 2:] + b


def _centred_token_shift(u, mu):
    prev = jnp.pad(u, ((0, 0), (1, 0), (0, 0)))[:, :-1]
    nxt = jnp.pad(u, ((0, 0), (0, 1), (0, 0)))[:, 1:]
    return u + mu[0] * (prev - u) + mu[1] * (nxt - u)


def _hyena_filters(L, w1, b1, w2, b2, w3, b3, freq, decay):
    f32 = jnp.float32
    pos = jnp.arange(L, dtype=f32)
    t = (pos / max(L - 1, 1))[:, None]
    ang = (2.0 * math.pi / L) * pos[:, None]
    bands = jnp.linspace(1e-4, HYENA_BANDS - 1, HYENA_BANDS, dtype=f32)[None, :]
    feats = jnp.concatenate([t, jnp.cos(bands * ang), -jnp.sin(bands * ang)], axis=-1)
    fr = freq.astype(f32)
    h = jnp.sin(fr * (feats @ w1.astype(f32) + b1.astype(f32)))
    h = jnp.sin(fr * (h @ w2.astype(f32) + b2.astype(f32)))
    h = h @ w3.astype(f32) + b3.astype(f32)
    h = h.reshape(L, N_DIR, HYENA_ORDER, D_HYENA)
    h = h * jnp.exp(-t[:, :, None, None] * jnp.abs(decay.astype(f32))[None])
    fwd, bwd = h[:, 0], h[:, 1]
    zero = jnp.zeros((1, HYENA_ORDER, D_HYENA), f32)
    k = jnp.concatenate([fwd, zero, jnp.flip(bwd[1:], axis=0)], axis=0)
    k = k * lax.rsqrt(jnp.sum(k * k, axis=0, keepdims=True) + FILTER_EPS)
    return k


def _long_conv(z, k_f, d, L):
    zf = jnp.fft.rfft(z, n=2 * L, axis=1)
    y = jnp.fft.irfft(zf * k_f[None], n=2 * L, axis=1)[:, :L]
    return y + d * z


def _hyena_branch(u, conv_w, conv_b, w1, b1, w2, b2, w3, b3, freq, decay, bias_d, w_o):
    out_dtype = u.dtype
    L = u.shape[1]
    uc = _centred_conv3(u, conv_w, conv_b).astype(jnp.float32)
    x1, x2, v = jnp.split(uc, 3, axis=-1)
    k = _hyena_filters(L, w1, b1, w2, b2, w3, b3, freq, decay)
    k_f = jnp.fft.rfft(k, axis=0)
    d = bias_d.astype(jnp.float32)
    z = x1 * _long_conv(v, k_f[:, 0], d[0], L)
    z = x2 * _long_conv(z, k_f[:, 1], d[1], L)
    return z.astype(out_dtype) @ w_o


def _time_major_pair(fwd, bwd):
    return jnp.moveaxis(jnp.stack([fwd, jnp.flip(bwd, axis=1)], axis=0), 2, 0)


def _rwkv7_step(S, inp):
    r, w, k, v, kk, a = inp
    sa = jnp.einsum('dbhij,dbhj->dbhi', S, -kk)
    S = S * w[..., None, :] + sa[..., None] * (kk * a)[..., None, :] + v[..., None] * k[..., None, :]
    y = jnp.einsum('dbhij,dbhj->dbhi', S, r)
    return S, y


def _rwkv7_branch(u, mu, w0, w2, a0, a2, g2, k_k, k_a, r_k, ln_w, ln_b, w_o):
    out_dtype = u.dtype
    B, L, _ = u.shape
    H, N = RWKV_HEADS, RWKV_HEAD
    f32 = jnp.float32
    u = _centred_token_shift(u, mu).astype(f32)
    splits = [D_RWKV, 2 * D_RWKV, 3 * D_RWKV, 3 * D_RWKV + N_DIR * DECAY_LORA,
              3 * D_RWKV + N_DIR * (DECAY_LORA + AAA_LORA)]
    r, k, v, wd, ad, gd = jnp.split(u, splits, axis=-1)
    wd = wd.reshape(B, L, N_DIR, DECAY_LORA)
    ad = ad.reshape(B, L, N_DIR, AAA_LORA)
    w = -jax.nn.softplus(-(w0 + jnp.einsum('bldr,drc->bldc', jnp.tanh(wd), w2))) - 0.5
    decay = jnp.exp(-jnp.exp(w))
    a = jax.nn.sigmoid(a0 + jnp.einsum('bldr,drc->bldc', ad, a2))
    g = jax.nn.sigmoid(gd) @ g2
    kk = (k * k_k).reshape(B, L, H, N)
    kk = kk / jnp.maximum(jnp.sqrt(jnp.sum(kk * kk, axis=-1, keepdims=True)), 1e-12)
    k_dir = k[:, :, None, :] * (1.0 + (a - 1.0) * k_a)

    def heads(t):
        return t.reshape(t.shape[:-1] + (H, N))

    r_h, v_h = heads(r), heads(v)
    decay_h, a_h, k_h = heads(decay), heads(a), heads(k_dir)
    seq = (_time_major_pair(r_h, r_h),
           _time_major_pair(decay_h[:, :, 0], decay_h[:, :, 1]),
           _time_major_pair(k_h[:, :, 0], k_h[:, :, 1]),
           _time_major_pair(v_h, v_h),
           _time_major_pair(kk, kk),
           _time_major_pair(a_h[:, :, 0], a_h[:, :, 1]))
    S0 = jnp.zeros((N_DIR, B, H, N, N), f32)
    _, ys = lax.scan(_rwkv7_step, S0, seq)
    y = jnp.moveaxis(ys[:, 0] + jnp.flip(ys[:, 1], axis=0), 0, 1)
    mean = jnp.mean(y, axis=-1, keepdims=True)
    var = jnp.mean(jnp.square(y - mean), axis=-1, keepdims=True)
    y = ((y - mean) * lax.rsqrt(var + GN_EPS)).reshape(B, L, D_RWKV) * ln_w + ln_b
    bonus = jnp.sum(r_h[:, :, None] * k_h * r_k, axis=-1, keepdims=True)
    y = y + (jnp.sum(bonus, axis=2) * v_h).reshape(B, L, D_RWKV)
    y = y * g
    return y.astype(out_dtype) @ w_o


def setup_inputs(seed: int = 0) -> dict:
    key = jax.random.key(seed)
    ks = jax.random.split(key, 35)
    f32 = jnp.float32

    def nrm(i, shape, scale):
        return scale * jax.random.normal(ks[i], shape, f32)

    hy_decay_base = jnp.linspace(HYENA_MIN_DECAY, HYENA_MAX_DECAY, D_HYENA, dtype=f32)
    rw_w0_base = jnp.linspace(-6.0, -1.0, D_RWKV, dtype=f32)
    return {
        "x": nrm(0, (BATCH, SEQ, D_MODEL), 1.0),
        "ffn1_norm": 1.0 + nrm(1, (D_MODEL,), 0.05),
        "ffn1_w_gu": nrm(2, (D_MODEL, 2 * D_FF), D_MODEL ** -0.5),
        "ffn1_w_down": nrm(3, (D_FF, D_MODEL), D_FF ** -0.5),
        "mix_norm": 1.0 + nrm(4, (D_MODEL,), 0.05),
        "w_in": nrm(5, (D_MODEL, N_IN), D_MODEL ** -0.5),
        "hy_conv_w": nrm(6, (3, N_HY_IN), 3 ** -0.5),
        "hy_conv_b": nrm(7, (N_HY_IN,), 0.02),
        "hy_ffn_w1": nrm(8, (HYENA_EMB, HYENA_FFN), HYENA_EMB ** -0.5),
        "hy_ffn_b1": nrm(9, (HYENA_FFN,), 0.1),
        "hy_ffn_w2": nrm(10, (HYENA_FFN, HYENA_FFN), HYENA_FFN ** -0.5),
        "hy_ffn_b2": nrm(11, (HYENA_FFN,), 0.1),
        "hy_ffn_w3": nrm(12, (HYENA_FFN, N_DIR * HYENA_ORDER * D_HYENA), HYENA_FFN ** -0.5),
        "hy_ffn_b3": nrm(13, (N_DIR * HYENA_ORDER * D_HYENA,), 0.1),
        "hy_sin_freq": 1.0 + nrm(14, (HYENA_FFN,), 0.1),
        "hy_decay": hy_decay_base + nrm(15, (N_DIR, HYENA_ORDER, D_HYENA), 0.1),
        "hy_bias_d": nrm(16, (HYENA_ORDER, D_HYENA), 0.5),
        "hy_out": nrm(17, (D_HYENA, D_MODEL), D_HYENA ** -0.5),
        "rw_mu": jax.random.uniform(ks[18], (2, N_RW_IN), f32, 0.0, 0.5),
        "rw_w0": rw_w0_base + nrm(19, (N_DIR, D_RWKV), 0.1),
        "rw_w2": nrm(20, (N_DIR, DECAY_LORA, D_RWKV), 0.1),
        "rw_a0": nrm(21, (N_DIR, D_RWKV), 0.3),
        "rw_a2": nrm(22, (N_DIR, AAA_LORA, D_RWKV), 0.1),
        "rw_g2": nrm(23, (GATE_LORA, D_RWKV), GATE_LORA ** -0.5),
        "rw_k_k": 0.85 + nrm(24, (D_RWKV,), 0.05),
        "rw_k_a": 1.0 + nrm(25, (D_RWKV,), 0.05),
        "rw_r_k": nrm(26, (RWKV_HEADS, RWKV_HEAD), 0.1),
        "rw_ln_w": 1.0 + nrm(27, (D_RWKV,), 0.05),
        "rw_ln_b": nrm(28, (D_RWKV,), 0.02),
        "rw_out": nrm(29, (D_RWKV, D_MODEL), D_RWKV ** -0.5),
        "w_out": nrm(30, (D_MODEL, D_MODEL), D_MODEL ** -0.5),
        "ffn2_norm": 1.0 + nrm(31, (D_MODEL,), 0.05),
        "ffn2_w_gu": nrm(32, (D_MODEL, 2 * D_FF), D_MODEL ** -0.5),
        "ffn2_w_down": nrm(33, (D_FF, D_MODEL), D_FF ** -0.5),
        "final_norm": 1.0 + nrm(34, (D_MODEL,), 0.05),
    }


def reference(x, ffn1_norm, ffn1_w_gu, ffn1_w_down, mix_norm, w_in, hy_conv_w, hy_conv_b,
              hy_ffn_w1, hy_ffn_b1, hy_ffn_w2, hy_ffn_b2, hy_ffn_w3, hy_ffn_b3, hy_sin_freq,
              hy_decay, hy_bias_d, hy_out, rw_mu, rw_w0, rw_w2, rw_a0, rw_a2, rw_g2, rw_k_k,
              rw_k_a, rw_r_k, rw_ln_w, rw_ln_b, rw_out, w_out, ffn2_norm, ffn2_w_gu,
              ffn2_w_down, final_norm):
    split_at = [N_HY_IN, N_HY_IN + N_RW_IN, N_HY_IN + N_RW_IN + D_MODEL]
    for _ in range(DEPTH):
        x = x + 0.5 * _swiglu(_rmsnorm(x, ffn1_norm), ffn1_w_gu, ffn1_w_down)
        h = _rmsnorm(x, mix_norm)
        u_hy, u_rw, g_hy, g_rw = jnp.split(h @ w_in, split_at, axis=-1)
        y_hy = _hyena_branch(u_hy, hy_conv_w, hy_conv_b, hy_ffn_w1, hy_ffn_b1, hy_ffn_w2,
                             hy_ffn_b2, hy_ffn_w3, hy_ffn_b3, hy_sin_freq, hy_decay,
                             hy_bias_d, hy_out)
        y_rw = _rwkv7_branch(u_rw, rw_mu, rw_w0, rw_w2, rw_a0, rw_a2, rw_g2, rw_k_k, rw_k_a,
                             rw_r_k, rw_ln_w, rw_ln_b, rw_out)
        merged = jax.nn.sigmoid(g_hy) * y_hy + jax.nn.sigmoid(g_rw) * y_rw
        x = x + merged @ w_out
        x = x + 0.5 * _swiglu(_rmsnorm(x, ffn2_norm), ffn2_w_gu, ffn2_w_down)
    return _rmsnorm(x, final_norm)
```

```python
import contextlib
import math
import numpy as np
import concourse.bass as bass
import concourse.mybir as mybir
from concourse.bass_utils import run_bass_kernel_spmd

F32 = mybir.dt.float32
BF16 = mybir.dt.bfloat16
AF = mybir.ActivationFunctionType
ALU = mybir.AluOpType
AX = mybir.AxisListType

D = 1024
L = 4096
DFF = 2816
NHY = 1536
NRW = 1952
NG = 2048
NIN = NHY + NRW + NG
RMS_EPS = 1e-6
GN_EPS = 64e-5

SEM_EPOCH = 30000
N_DMA_SLOTS = 12


class Buf:
    __slots__ = ("lw", "rd")

    def __init__(self):
        self.lw = None
        self.rd = []


class Op:
    __slots__ = ("eng", "fn", "deps", "dma", "needed", "tok", "slot", "noinst")


class Sched:
    ENGS = ("pe", "act", "dve", "pool", "sp")

    def __init__(self, nc):
        self.nc = nc
        self.streams = {e: [] for e in self.ENGS}
        self.dma_count = {e: 0 for e in self.ENGS}
        self.dma_slot_last = {}

    def op(self, eng, fn, reads=(), writes=(), dma=False, extra=(), noinst=False):
        o = Op()
        o.noinst = noinst
        o.eng = eng
        o.fn = fn
        o.dma = dma
        o.needed = False
        deps = {}
        for b in reads:
            if b.lw is not None:
                deps[id(b.lw)] = b.lw
        for b in writes:
            if b.lw is not None:
                deps[id(b.lw)] = b.lw
            for r in b.rd:
                deps[id(r)] = r
        for d in extra:
            deps[id(d)] = d
        if dma:
            k = self.dma_count[eng]
            self.dma_count[eng] += 1
            o.slot = (eng, k % N_DMA_SLOTS, k // N_DMA_SLOTS)
            prev = self.dma_slot_last.get((eng, k % N_DMA_SLOTS))
            if prev is not None:
                deps[id(prev)] = prev
            self.dma_slot_last[(eng, k % N_DMA_SLOTS)] = o
        dl = []
        for d in deps.values():
            if d is o:
                continue
            if (not d.dma) and d.eng == eng and eng == "pe" and not o.dma:
                continue
            assert not d.noinst
            d.needed = True
            dl.append(d)
        o.deps = dl
        for b in reads:
            b.rd.append(o)
        for b in writes:
            b.lw = o
            b.rd = []
        self.streams[eng].append(o)
        return o

    def barrier(self):
        lasts = list(self.dma_slot_last.values())
        for s in self.streams.values():
            for o in reversed(s):
                if not o.noinst:
                    lasts.append(o)
                    break
        for e in self.ENGS:
            self.op(e, lambda eh: None, extra=lasts, noinst=True)

    def emit(self):
        nc = self.nc
        with contextlib.ExitStack() as st:
            esem = {}
            for e in self.ENGS:
                n_sig = sum(1 for o in self.streams[e] if o.needed and not o.dma)
                n_ep = max(1, (n_sig + SEM_EPOCH - 1) // SEM_EPOCH)
                esem[e] = [st.enter_context(nc.semaphore(f"s_{e}_{i}")) for i in range(n_ep)]
            dsem = {}
            for e in self.ENGS:
                if self.dma_count[e] > 0:
                    dsem[e] = [st.enter_context(nc.semaphore(f"d_{e}_{i}")) for i in range(N_DMA_SLOTS)]
            for e in self.ENGS:
                c = 0
                for o in self.streams[e]:
                    if o.dma:
                        _, s, r = o.slot
                        o.tok = (dsem[e][s], 16 * (r + 1), ("d", e, s))
                    elif o.needed:
                        ep = c // SEM_EPOCH
                        o.tok = (esem[e][ep], (c % SEM_EPOCH) + 1, ("e", e, ep))
                        c += 1
                    else:
                        o.tok = None
            block = st.enter_context(nc.Block())
            hmap = {"pe": block.tensor, "act": block.scalar, "dve": block.vector,
                    "pool": block.gpsimd, "sp": block.sync}
            for e in self.ENGS:
                stream = self.streams[e]
                if not stream:
                    continue

                def section(eh, stream=stream):
                    waited = {}
                    for o in stream:
                        for d in o.deps:
                            sem, val, key = d.tok
                            if waited.get(key, 0) >= val:
                                continue
                            eh.wait_ge(sem, val)
                            waited[key] = val
                        ins = o.fn(eh)
                        if ins is None:
                            continue
                        if o.dma:
                            ins.then_inc(o.tok[0], 16)
                        elif o.needed:
                            ins.then_inc(o.tok[0], 1)

                hmap[e](section)


class Tile:
    def __init__(self, ap, n=1):
        self.ap = ap
        self.b = [Buf() for _ in range(n)]


class KB:
    def __init__(self, nc, st):
        self.nc = nc
        self.S = Sched(nc)
        self.arena = st.enter_context(nc.sbuf_tensor("arena", [128, 49152], F32))
        self.ps = st.enter_context(nc.psum_tensor("psum", [128, 8, 512], F32))
        self.psb = [Buf() for _ in range(8)]
        self.ps_i = 0
        self.reserved = set()
        self.base = 0
        self.off = 0
        self.rr = 0

    def f32(self, n, nb=1):
        assert self.off + n <= 49152, ("sbuf overflow", self.off, n)
        ap = self.arena[:, self.off:self.off + n]
        self.off += n
        return Tile(ap, nb)

    def bf16(self, n, nb=1):
        w = (n + 1) // 2
        assert self.off + w <= 49152, ("sbuf overflow", self.off, w)
        ap = self.arena[:, self.off:self.off + w].bitcast(BF16)[:, 0:n]
        self.off += w
        return Tile(ap, nb)

    def persist(self):
        self.base = self.off

    def phase_end(self):
        self.S.barrier()
        self.off = self.base

    def bank(self):
        while self.ps_i in self.reserved:
            self.ps_i = (self.ps_i + 1) % 8
        i = self.ps_i
        self.ps_i = (self.ps_i + 1) % 8
        return i

    def op(self, eng, fn, reads=(), writes=(), dma=False):
        return self.S.op(eng, fn, reads=reads, writes=writes, dma=dma)

    def ew_eng(self):
        self.rr += 1
        return "dve" if (self.rr % 3) else "pool"


def dma(kb, eng, out, in_, reads=(), writes=(), slow=False):
    if slow:
        return kb.op(eng, lambda e: e.dma_start(out=out, in_=in_, allow_slow_non_contiguous=True),
                     reads=reads, writes=writes, dma=True)
    return kb.op(eng, lambda e: e.dma_start(out=out, in_=in_), reads=reads, writes=writes, dma=True)


def convert_weights(kb, pairs):
    CW = 2816
    stg = [kb.f32(CW) for _ in range(3)]
    stb = [kb.bf16(CW) for _ in range(3)]
    i = 0
    engs = ("dve", "act", "pool")
    for src, dst, R, C in pairs:
        for r0 in range(0, R, 128):
            rr = min(128, R - r0)
            for c0 in range(0, C, CW):
                cc = min(CW, C - c0)
                a, b = stg[i % 3], stb[i % 3]
                dma(kb, "sp", a.ap[0:rr, 0:cc], src[r0:r0 + rr, c0:c0 + cc], writes=a.b)
                eng = engs[i % 3]
                if eng == "act":
                    kb.op("act", lambda e, a=a, b=b, rr=rr, cc=cc: e.copy(out=b.ap[0:rr, 0:cc], in_=a.ap[0:rr, 0:cc]),
                          reads=a.b, writes=b.b)
                else:
                    kb.op(eng, lambda e, a=a, b=b, rr=rr, cc=cc: e.tensor_copy(out=b.ap[0:rr, 0:cc], in_=a.ap[0:rr, 0:cc]),
                          reads=a.b, writes=b.b)
                dma(kb, "pool" if i % 2 else "act", dst[r0:r0 + rr, c0:c0 + cc], b.ap[0:rr, 0:cc], reads=b.b)
                i += 1


def rmsnorm_T(kb, C, xt, g, out_ap_fn, sq, rstd):
    x3 = xt.ap
    sq3 = sq.ap.rearrange("p (a b) -> p a b", b=512)
    bk = kb.bank()
    for kc in range(8):
        kb.op("act", lambda e, kc=kc: e.activation(out=sq3[:, kc, :], in_=x3[:, kc, :], func=AF.Square),
              reads=xt.b, writes=[sq.b[kc]])
    for kc in range(8):
        kb.op("pe", lambda e, kc=kc: e.matmul(kb.ps[:, bk, :], lhsT=C["ones"].ap, rhs=sq3[:, kc, :],
                                              start=(kc == 0), stop=(kc == 7)),
              reads=[sq.b[kc]] + C["ones"].b, writes=[kb.psb[bk]])
    kb.op("act", lambda e: e.activation(out=rstd.ap, in_=kb.ps[:, bk, :], func=AF.Sqrt, scale=1.0 / D, bias=RMS_EPS),
          reads=[kb.psb[bk]], writes=rstd.b)
    kb.op("dve", lambda e: e.reciprocal(out=rstd.ap, in_=rstd.ap), reads=rstd.b, writes=rstd.b)
    outs = []
    for kc in range(8):
        o = out_ap_fn(kc)
        outs.append(o)
    return outs


def ffn_phase(kb, C, xin, xout, gname, wgu, wdn):
    TQ = 1024
    xn = kb.bf16(8 * TQ, 8)
    G = kb.bf16(22 * TQ, 22)
    xn3 = xn.ap.rearrange("p (a b) -> p a b", b=TQ)
    G3 = G.ap.rearrange("p (a b) -> p a b", b=TQ)
    xts = [kb.f32(8 * 512) for _ in range(2)]
    sq = kb.f32(8 * 512, 8)
    rstd = kb.f32(512)
    wg = [kb.bf16(8 * 256) for _ in range(2)]
    wu = [kb.bf16(8 * 256) for _ in range(2)]
    wd = [kb.bf16(22 * 512) for _ in range(2)]
    sg = [kb.f32(512) for _ in range(2)]
    xr = [kb.f32(512) for _ in range(2)]
    xo = [kb.f32(512) for _ in range(2)]
    g = C[gname]
    xin3 = xin.rearrange("(kc p) t -> p kc t", p=128)
    wgu3 = wgu.rearrange("(kc p) f -> p kc f", p=128)
    wdn3 = wdn.rearrange("(fc p) d -> p fc d", p=128)
    cnt = 0
    for q in range(L // TQ):
        for t in range(2):
            tok = q * TQ + t * 512
            xt = xts[t]
            x3 = xt.ap.rearrange("p (a b) -> p a b", b=512)
            xt3 = Tile(x3)
            xt3.b = xt.b
            dma(kb, "sp", x3, xin3[:, :, tok:tok + 512], writes=xt.b)
            rmsnorm_T(kb, C, xt3, g, lambda kc: None, sq, rstd)
            for kc in range(8):
                eng = "dve"
                kb.op(eng, lambda e, kc=kc, x3=x3, t=t: e.scalar_tensor_tensor(
                    out=xn3[:, kc, t * 512:(t + 1) * 512], in0=x3[:, kc, :], scalar=g.ap[:, kc:kc + 1],
                    in1=rstd.ap, op0=ALU.mult, op1=ALU.mult),
                    reads=xt.b + rstd.b + g.b, writes=[xn.b[kc]])
        for s in range(11):
            a, b = wg[s % 2], wu[s % 2]
            a3 = a.ap.rearrange("p (a b) -> p a b", b=256)
            b3 = b.ap.rearrange("p (a b) -> p a b", b=256)
            dma(kb, "sp", a3, wgu3[:, :, s * 256:(s + 1) * 256], writes=a.b)
            dma(kb, "sp", b3, wgu3[:, :, DFF + s * 256:DFF + (s + 1) * 256], writes=b.b)
            for fcl in range(2):
                fc = s * 2 + fcl
                for t in range(2):
                    bg = kb.bank()
                    bu = kb.bank()
                    for kc in range(8):
                        kb.op("pe", lambda e, kc=kc, a3=a3, fcl=fcl, t=t, bg=bg: e.matmul(
                            kb.ps[:, bg, :], lhsT=a3[:, kc, fcl * 128:(fcl + 1) * 128],
                            rhs=xn3[:, kc, t * 512:(t + 1) * 512], start=(kc == 0), stop=(kc == 7)),
                            reads=a.b + [xn.b[kc]], writes=[kb.psb[bg]])
                    for kc in range(8):
                        kb.op("pe", lambda e, kc=kc, b3=b3, fcl=fcl, t=t, bu=bu: e.matmul(
                            kb.ps[:, bu, :], lhsT=b3[:, kc, fcl * 128:(fcl + 1) * 128],
                            rhs=xn3[:, kc, t * 512:(t + 1) * 512], start=(kc == 0), stop=(kc == 7)),
                            reads=b.b + [xn.b[kc]], writes=[kb.psb[bu]])
                    sgt = sg[cnt % 2]
                    cnt += 1
                    kb.op("act", lambda e, sgt=sgt, bg=bg: e.activation(out=sgt.ap, in_=kb.ps[:, bg, :], func=AF.Silu),
                          reads=[kb.psb[bg]], writes=sgt.b)
                    kb.op("dve", lambda e, sgt=sgt, bu=bu, fc=fc, t=t: e.tensor_tensor(
                        out=G3[:, fc, t * 512:(t + 1) * 512], in0=kb.ps[:, bu, :], in1=sgt.ap, op=ALU.mult),
                        reads=[kb.psb[bu]] + sgt.b, writes=[G.b[fc]])
        for ds in range(2):
            w = wd[ds % 2]
            w3 = w.ap.rearrange("p (a b) -> p a b", b=512)
            dma(kb, "sp", w3, wdn3[:, :, ds * 512:(ds + 1) * 512], writes=w.b)
            for dcl in range(4):
                dc = ds * 4 + dcl
                for t in range(2):
                    tok = q * TQ + t * 512
                    bo = kb.bank()
                    xrt, xot = xr[cnt % 2], xo[cnt % 2]
                    cnt += 1
                    dma(kb, "sp", xrt.ap, xin[dc * 128:(dc + 1) * 128, tok:tok + 512], writes=xrt.b)
                    for fc in range(22):
                        kb.op("pe", lambda e, fc=fc, w3=w3, dcl=dcl, t=t, bo=bo: e.matmul(
                            kb.ps[:, bo, :], lhsT=w3[:, fc, dcl * 128:(dcl + 1) * 128],
                            rhs=G3[:, fc, t * 512:(t + 1) * 512], start=(fc == 0), stop=(fc == 21)),
                            reads=w.b + [G.b[fc]], writes=[kb.psb[bo]])
                    kb.op("dve", lambda e, xrt=xrt, xot=xot, bo=bo: e.scalar_tensor_tensor(
                        out=xot.ap, in0=kb.ps[:, bo, :], scalar=0.5, in1=xrt.ap, op0=ALU.mult, op1=ALU.add),
                        reads=[kb.psb[bo]] + xrt.b, writes=xot.b)
                    dma(kb, "pool", xout[dc * 128:(dc + 1) * 128, tok:tok + 512], xot.ap, reads=xot.b)


def final_norm_phase(kb, C, xin, out, gname):
    g = C[gname]
    xts = [kb.f32(8 * 512) for _ in range(2)]
    ots = [kb.f32(8 * 512) for _ in range(2)]
    sq = kb.f32(8 * 512, 8)
    rstd = kb.f32(512)
    xin3 = xin.rearrange("(kc p) t -> p kc t", p=128)
    out3 = out.rearrange("(kc p) t -> p kc t", p=128)
    for t in range(L // 512):
        xt, ot = xts[t % 2], ots[t % 2]
        x3 = xt.ap.rearrange("p (a b) -> p a b", b=512)
        o3 = ot.ap.rearrange("p (a b) -> p a b", b=512)
        xt3 = Tile(x3)
        xt3.b = xt.b
        dma(kb, "sp", x3, xin3[:, :, t * 512:(t + 1) * 512], writes=xt.b)
        rmsnorm_T(kb, C, xt3, g, lambda kc: None, sq, rstd)
        for kc in range(8):
            eng = "dve"
            kb.op(eng, lambda e, kc=kc, x3=x3, o3=o3: e.scalar_tensor_tensor(
                out=o3[:, kc, :], in0=x3[:, kc, :], scalar=g.ap[:, kc:kc + 1], in1=rstd.ap,
                op0=ALU.mult, op1=ALU.mult), reads=xt.b + rstd.b + g.b, writes=ot.b)
        dma(kb, "pool", out3[:, :, t * 512:(t + 1) * 512], o3, reads=ot.b)


def inproj_phase(kb, C, x1T, win, uhy, urwT, gT):
    TQ = 1024
    g = C["mix_norm"]
    xn = kb.bf16(8 * TQ, 8)
    xn3 = xn.ap.rearrange("p (a b) -> p a b", b=TQ)
    xts = [kb.f32(8 * 512) for _ in range(2)]
    sq = kb.f32(8 * 512, 8)
    rstd = kb.f32(512)
    why = kb.bf16(8 * NHY)
    why3 = why.ap.rearrange("p (a b) -> p a b", b=NHY)
    slabs = [kb.bf16(8 * 512) for _ in range(2)]
    stg = [kb.f32(512) for _ in range(4)]
    zero = kb.f32(NHY)
    x1T3 = x1T.rearrange("(kc p) t -> p kc t", p=128)
    win3 = win.rearrange("(kc p) f -> p kc f", p=128)
    kb.op("pool", lambda e: e.memset(zero.ap, 0.0), writes=zero.b)
    dma(kb, "sp", uhy[0:1, :], zero.ap[0:1, :], reads=zero.b)
    dma(kb, "sp", uhy[L + 1:L + 2, :], zero.ap[0:1, :], reads=zero.b)
    for r0 in range(0, NRW, 128):
        rr = min(128, NRW - r0)
        dma(kb, "sp", urwT[r0:r0 + rr, 0:1], zero.ap[0:rr, 0:1], reads=zero.b, slow=True)
        dma(kb, "sp", urwT[r0:r0 + rr, L + 1:L + 2], zero.ap[0:rr, 0:1], reads=zero.b, slow=True)
    dma(kb, "sp", why3, win3[:, :, 0:NHY], writes=why.b)
    cnt = 0
    for q in range(L // TQ):
        for t in range(2):
            tok = q * TQ + t * 512
            xt = xts[t]
            x3 = xt.ap.rearrange("p (a b) -> p a b", b=512)
            xt3 = Tile(x3)
            xt3.b = xt.b
            dma(kb, "sp", x3, x1T3[:, :, tok:tok + 512], writes=xt.b)
            rmsnorm_T(kb, C, xt3, g, lambda kc: None, sq, rstd)
            for kc in range(8):
                kb.op("dve", lambda e, kc=kc, x3=x3, t=t: e.scalar_tensor_tensor(
                    out=xn3[:, kc, t * 512:(t + 1) * 512], in0=x3[:, kc, :], scalar=g.ap[:, kc:kc + 1],
                    in1=rstd.ap, op0=ALU.mult, op1=ALU.mult),
                    reads=xt.b + rstd.b + g.b, writes=[xn.b[kc]])
        for (dst, col0, ncols, gate, coff) in ((urwT, NHY, NRW, False, 1), (gT, NHY + NRW, NG, True, 0)):
            for s0 in range(0, ncols, 512):
                cw = min(512, ncols - s0)
                sl = slabs[cnt % 2]
                sl3 = sl.ap.rearrange("p (a b) -> p a b", b=512)
                dma(kb, "sp", sl3[:, :, 0:cw], win3[:, :, col0 + s0:col0 + s0 + cw], writes=sl.b)
                for c0 in range(0, cw, 128):
                    m = min(128, cw - c0)
                    for t in range(2):
                        tok = q * TQ + t * 512
                        bk = kb.bank()
                        for kc in range(8):
                            kb.op("pe", lambda e, kc=kc, sl3=sl3, c0=c0, m=m, t=t, bk=bk: e.matmul(
                                kb.ps[0:m, bk, :], lhsT=sl3[:, kc, c0:c0 + m],
                                rhs=xn3[:, kc, t * 512:(t + 1) * 512], start=(kc == 0), stop=(kc == 7)),
                                reads=sl.b + [xn.b[kc]], writes=[kb.psb[bk]])
                        sg_ = stg[cnt % 4]
                        cnt += 1
                        if gate:
                            kb.op("act", lambda e, sg_=sg_, bk=bk, m=m: e.activation(
                                out=sg_.ap[0:m, :], in_=kb.ps[0:m, bk, :], func=AF.Sigmoid),
                                reads=[kb.psb[bk]], writes=sg_.b)
                        else:
                            kb.op("dve", lambda e, sg_=sg_, bk=bk, m=m: e.tensor_copy(
                                out=sg_.ap[0:m, :], in_=kb.ps[0:m, bk, :]),
                                reads=[kb.psb[bk]], writes=sg_.b)
                        r0 = s0 + c0
                        dma(kb, "pool", dst[r0:r0 + m, coff + tok:coff + tok + 512], sg_.ap[0:m, :], reads=sg_.b)
        for tb in range(TQ // 128):
            tok = q * TQ + tb * 128
            for cs in range(3):
                bk = kb.bank()
                for kc in range(8):
                    kb.op("pe", lambda e, kc=kc, tb=tb, cs=cs, bk=bk: e.matmul(
                        kb.ps[:, bk, :], lhsT=xn3[:, kc, tb * 128:(tb + 1) * 128],
                        rhs=why3[:, kc, cs * 512:(cs + 1) * 512], start=(kc == 0), stop=(kc == 7)),
                        reads=why.b + [xn.b[kc]], writes=[kb.psb[bk]])
                sg_ = stg[cnt % 4]
                cnt += 1
                kb.op("act", lambda e, sg_=sg_, bk=bk: e.copy(out=sg_.ap, in_=kb.ps[:, bk, :]),
                      reads=[kb.psb[bk]], writes=sg_.b)
                dma(kb, "pool", uhy[1 + tok:1 + tok + 128, cs * 512:(cs + 1) * 512], sg_.ap, reads=sg_.b)


def _bl(*tiles):
    out = []
    for t in tiles:
        out.extend(t.b if isinstance(t, Tile) else [t])
    return out


def _tt(kb, eng, out, a, b, op, R, W):
    return kb.op(eng, lambda e: e.tensor_tensor(out=out, in0=a, in1=b, op=op), reads=_bl(*R), writes=_bl(*W))


def _ts(kb, eng, out, a, s1, s2, op0, op1, R, W):
    if op1 is None:
        return kb.op(eng, lambda e: e.tensor_scalar(out=out, in0=a, scalar1=s1, scalar2=None, op0=op0),
                     reads=_bl(*R), writes=_bl(*W))
    return kb.op(eng, lambda e: e.tensor_scalar(out=out, in0=a, scalar1=s1, scalar2=s2, op0=op0, op1=op1),
                 reads=_bl(*R), writes=_bl(*W))


def _stt(kb, out, a, s, b, op0, op1, R, W):
    return kb.op("dve", lambda e: e.scalar_tensor_tensor(out=out, in0=a, scalar=s, in1=b, op0=op0, op1=op1),
                 reads=_bl(*R), writes=_bl(*W))


def _act(kb, out, a, func, R, W, scale=1.0, bias=0.0):
    return kb.op("act", lambda e: e.activation(out=out, in_=a, func=func, scale=scale, bias=bias),
                 reads=_bl(*R), writes=_bl(*W))


def _cp(kb, eng, out, a, R, W):
    if eng == "act":
        return kb.op("act", lambda e: e.copy(out=out, in_=a), reads=_bl(*R), writes=_bl(*W))
    return kb.op(eng, lambda e: e.tensor_copy(out=out, in_=a), reads=_bl(*R), writes=_bl(*W))


def _mm(kb, out, lhsT, rhs, R, W, start=True, stop=True):
    return kb.op("pe", lambda e: e.matmul(out, lhsT=lhsT, rhs=rhs, start=start, stop=stop),
                 reads=_bl(*R), writes=_bl(*W))


def _tr(kb, out, in_, ident, R, W):
    return kb.op("pe", lambda e: e.transpose(out, in_, ident.ap), reads=_bl(*R) + ident.b, writes=_bl(*W))


KAPPA = math.exp(-0.5)
SC = 256
NCH = SC // 64


RW_DBG = {}


class _NS:
    pass


def rwkv_phase(kb, C, P, urwT, yrwT, RD):
    import itertools
    dbg = RW_DBG
    ident, bd64 = C["ident"], C["bd64"]
    W = SC
    WH = W + 2
    NI = NCH * 2
    yT, bT = RD["yT"], RD["bT"]
    yT_b = [[[Buf() for _ in range(L // W)] for _ in range(4)] for _ in range(2)]
    bT_b = [[[Buf() for _ in range(L // W)] for _ in range(4)] for _ in range(2)]
    w2t = kb.f32(1024)
    a2t = kb.f32(1024)
    g2a = kb.f32(512)
    g2b = kb.f32(512)
    mNB = [kb.f32(512), kb.f32(512)]
    mAAB = [kb.f32(512), kb.f32(512)]
    I8 = kb.f32(512)
    vecs = kb.f32(20)
    w0t = kb.f32(8)
    a0t = kb.f32(8)
    mu_rkv = kb.f32(24)
    mu_wa = kb.f32(8)
    mu_g = kb.f32(4)
    dma(kb, "sp", w2t.ap[0:64, :], P["rw_w2t"], writes=w2t.b)
    dma(kb, "sp", a2t.ap[0:64, :], P["rw_a2t"], writes=a2t.b)
    dma(kb, "sp", g2a.ap, P["rw_g2"][0:128, :], writes=g2a.b)
    dma(kb, "sp", g2b.ap[0:32, :], P["rw_g2"][128:160, :], writes=g2b.b)
    dma(kb, "sp", mNB[0].ap[0:64, :], P["mNBf"], writes=mNB[0].b)
    dma(kb, "sp", mNB[1].ap[0:64, :], P["mNBb"], writes=mNB[1].b)
    dma(kb, "sp", mAAB[0].ap[0:64, :], P["mAABf"], writes=mAAB[0].b)
    dma(kb, "sp", mAAB[1].ap[0:64, :], P["mAABb"], writes=mAAB[1].b)
    dma(kb, "sp", I8.ap[0:64, :], P["I8"], writes=I8.b)
    rmask = kb.f32(SC)
    dma(kb, "sp", rmask.ap, P["rmask"], writes=rmask.b)
    dma(kb, "sp", vecs.ap, P["rw_vecs"], writes=vecs.b)
    for fc in range(4):
        dma(kb, "sp", w0t.ap[:, fc * 2:fc * 2 + 2], P["rw_w0T"][fc * 128:(fc + 1) * 128, :], writes=w0t.b)
        dma(kb, "sp", a0t.ap[:, fc * 2:fc * 2 + 2], P["rw_a0T"][fc * 128:(fc + 1) * 128, :], writes=a0t.b)
        for kind in range(3):
            o = (kind * 4 + fc) * 2
            r0 = kind * 512 + fc * 128
            dma(kb, "sp", mu_rkv.ap[:, o:o + 2], P["rw_muT"][r0:r0 + 128, :], writes=mu_rkv.b)
    for i in range(4):
        r0 = 1536 + i * 64
        dma(kb, "sp", mu_wa.ap[0:64, i * 2:i * 2 + 2], P["rw_muT"][r0:r0 + 64, :], writes=mu_wa.b)
    dma(kb, "sp", mu_g.ap[:, 0:2], P["rw_muT"][1792:1920, :], writes=mu_g.b)
    dma(kb, "sp", mu_g.ap[0:32, 2:4], P["rw_muT"][1920:1952, :], writes=mu_g.b)

    def vec(i, fc):
        return vecs.ap[:, i * 4 + fc:i * 4 + fc + 1]

    def v3(t, b=64):
        return t.ap.rearrange("p (a b) -> p a b", b=b)

    def z4(t):
        return t.ap.rearrange("p (c h t) -> p c h t", h=2, t=64)

    N3 = lambda t: t.ap.rearrange("p (i t) -> p i t", t=64)

    def alloc_stream():
        T = _NS()
        T.ld = [kb.f32(WH + 6) for _ in range(5)]
        T.tp = [kb.f32(W) for _ in range(23)]
        T.ar = kb.f32(2 * W)
        T.tok = [kb.f32(NCH * 128) for _ in range(4)]
        T.NBt = kb.f32(NCH * 2 * 128)
        T.KBt = kb.f32(NCH * 2 * 128)
        T.AAB = kb.bf16(NI * 64)
        T.N0 = kb.bf16(NI * 64)
        T.Nk = [kb.bf16(NI * 64) for _ in range(2)]
        T.Ak = [kb.bf16(NI * 64) for _ in range(2)]
        T.Pk = [kb.bf16(NI * 64) for _ in range(2)]
        T.Pf = kb.f32(NI * 64)
        T.AKV = kb.f32(NI * 64)
        T.W2 = kb.f32(NI * 64)
        T.Hs = [kb.f32(128), kb.f32(128)]
        T.Usb = kb.f32(128)
        T.gTt = kb.f32(NCH)
        T.ysc = kb.f32(W)
        T.bsc = kb.f32(W)
        T.pad = [kb.f32(NI * 64) for _ in range(5)]
        for zt in T.pad:
            kb.op("pool", lambda e, zt=zt: e.memset(zt.ap, 0.0), writes=zt.b)
        return T

    TS = [alloc_stream(), alloc_stream()]

    def shift(T, u, out, mu0, mu1, np_=128):
        t1, t2 = T.tp[5], T.tp[6]
        mus = [mu_rkv, mu_wa, mu_g]
        _tt(kb, "pool", t1.ap[0:np_, :], u.ap[0:np_, 0:W], u.ap[0:np_, 1:W + 1], ALU.subtract, [u], [t1])
        _stt(kb, out.ap[0:np_, :], t1.ap[0:np_, :], mu0, u.ap[0:np_, 1:W + 1], ALU.mult, ALU.add, [t1, u] + mus, [out])
        _tt(kb, "pool", t2.ap[0:np_, :], u.ap[0:np_, 2:W + 2], u.ap[0:np_, 1:W + 1], ALU.subtract, [u], [t2])
        _stt(kb, out.ap[0:np_, :], t2.ap[0:np_, :], mu1, out.ap[0:np_, :], ALU.mult, ALU.add, [t2, out] + mus, [out])

    def sc_gen(fc, d, T):
        hcur = 0
        kb.op("pool", lambda e: e.memset(T.Hs[0].ap, 0.0), writes=T.Hs[0].b)
        sc_list = list(range(L // W)) if d == 0 else list(range(L // W - 1, -1, -1))
        sc_list = sc_list[:dbg.get('nsc', len(sc_list))]
        (r, k, v, wdx, adx, t1, t2, sg, lr, kraw, rn, kk, kd, bv, cA, cB, ginc, gexc, ginv, gts,
         bt, bh, kh) = T.tp
        tw, tmp = t1, t2
        ar, tok, NBt, KBt, AAB, Nk, Ak, Pk, AKV, W2 = T.ar, T.tok, T.NBt, T.KBt, T.AAB, T.Nk, T.Ak, T.Pk, T.AKV, T.W2
        N0, Pf = T.N0, T.Pf
        btz, ktz, atz, rz, W1Tz = T.pad
        Hs, Usb, gTt, ysc, bsc = T.Hs, T.Usb, T.gTt, T.ysc, T.bsc
        ar4 = ar.ap.rearrange("p (c q t) -> p c q t", q=2, t=64)
        NB4 = NBt.ap.rearrange("p (i q t) -> p i q t", q=2, t=64)
        KB4 = KBt.ap.rearrange("p (i q t) -> p i q t", q=2, t=64)
        tok3 = [t.ap.rearrange("p (c f) -> p c f", f=128) for t in tok]
        for sci, sc in enumerate(sc_list):
            t0 = sc * W
            (ur, uk, uv, uw, ua) = T.ld
            dma(kb, "sp", uw.ap[0:64, 0:WH], urwT[1536 + d * 64:1536 + (d + 1) * 64, t0:t0 + WH], writes=uw.b)
            dma(kb, "sp", ua.ap[0:64, 0:WH], urwT[1664 + d * 64:1664 + (d + 1) * 64, t0:t0 + WH], writes=ua.b)
            dma(kb, "sp", uk.ap[:, 0:WH], urwT[512 + fc * 128:512 + (fc + 1) * 128, t0:t0 + WH], writes=uk.b)
            dma(kb, "sp", ur.ap[:, 0:WH], urwT[fc * 128:(fc + 1) * 128, t0:t0 + WH], writes=ur.b)
            dma(kb, "sp", uv.ap[:, 0:WH], urwT[1024 + fc * 128:1024 + (fc + 1) * 128, t0:t0 + WH], writes=uv.b)

            def mu3(kind):
                o = (kind * 4 + fc) * 2
                return mu_rkv.ap[:, o:o + 1], mu_rkv.ap[:, o + 1:o + 2]

            shift(T, uw, wdx, mu_wa.ap[0:64, d * 2:d * 2 + 1], mu_wa.ap[0:64, d * 2 + 1:d * 2 + 2], 64)
            yield
            shift(T, ua, adx, mu_wa.ap[0:64, 4 + d * 2:5 + d * 2], mu_wa.ap[0:64, 5 + d * 2:6 + d * 2], 64)
            yield
            shift(T, uk, k, *mu3(1))
            yield
            shift(T, ur, r, *mu3(0))
            yield
            shift(T, uv, v, *mu3(2))
            yield
            _act(kb, tw.ap[0:64, :], wdx.ap[0:64, :], AF.Tanh, [wdx], [tw])
            b1 = kb.bank()
            _mm(kb, kb.ps[:, b1, 0:W], w2t.ap[0:64, d * 512 + fc * 128:d * 512 + (fc + 1) * 128], tw.ap[0:64, :],
                [w2t, tw], [kb.psb[b1]])
            _act(kb, sg.ap, kb.ps[:, b1, 0:W], AF.Sigmoid, [kb.psb[b1], w0t], [sg],
                 bias=w0t.ap[:, fc * 2 + d:fc * 2 + d + 1])
            b2 = kb.bank()
            _mm(kb, kb.ps[:, b2, 0:W], a2t.ap[0:64, d * 512 + fc * 128:d * 512 + (fc + 1) * 128], adx.ap[0:64, :],
                [a2t, adx], [kb.psb[b2]])
            _act(kb, lr.ap, kb.ps[:, b2, 0:W], AF.Sigmoid, [kb.psb[b2], a0t], [lr],
                 bias=a0t.ap[:, fc * 2 + d:fc * 2 + d + 1])
            yield
            _act(kb, kraw.ap, k.ap, AF.Square, [k, vecs], [kraw], scale=vec(0, fc))
            b3 = kb.bank()
            _mm(kb, kb.ps[:, b3, 0:W], bd64.ap, kraw.ap, [bd64, kraw], [kb.psb[b3]])
            _act(kb, rn.ap, kb.ps[:, b3, 0:W], AF.Sqrt, [kb.psb[b3]], [rn])
            _ts(kb, "dve", rn.ap, rn.ap, 1e-12, None, ALU.max, None, [rn], [rn])
            kb.op("dve", lambda e: e.reciprocal(out=rn.ap, in_=rn.ap), reads=rn.b, writes=rn.b)
            _stt(kb, kk.ap, k.ap, vec(0, fc), rn.ap, ALU.mult, ALU.mult, [k, vecs, rn], [kk])
            yield
            _ts(kb, "dve", tmp.ap, lr.ap, -1.0, vec(1, fc), ALU.add, ALU.mult, [lr, vecs], [tmp])
            _stt(kb, kd.ap, tmp.ap, 1.0, k.ap, ALU.add, ALU.mult, [tmp, k], [kd])
            _tt(kb, "pool", bv.ap, kk.ap, lr.ap, ALU.mult, [kk, lr], [bv])
            _stt(kb, bsc.ap, r.ap, vec(2, fc), kd.ap, ALU.mult, ALU.mult, [r, kd, vecs], [bsc])
            dma(kb, "pool", bT[d, fc * 128:(fc + 1) * 128, t0:t0 + W], bsc.ap, reads=bsc.b, writes=[bT_b[d][fc][sc]])
            if d == 0:
                kb.op("dve", lambda e: e.tensor_tensor_scan(out=cB.ap, data0=rmask.ap, data1=sg.ap, initial=0.0,
                                                           op0=ALU.mult, op1=ALU.add),
                      reads=_bl(rmask, sg), writes=cB.b)
            else:
                kb.op("dve", lambda e: e.tensor_tensor_scan(out=cA.ap, data0=rmask.ap, data1=sg.ap, initial=0.0,
                                                           op0=ALU.mult, op1=ALU.add),
                      reads=_bl(rmask, sg), writes=cA.b)
                pre3 = v3(cA)
                _tt(kb, "dve", v3(cB), pre3, pre3[:, :, 63:64].to_broadcast([128, NCH, 64]), ALU.subtract, [cA], [cB])
                _tt(kb, "dve", cB.ap, sg.ap, cB.ap, ALU.subtract, [sg, cB], [cB])
            yield
            cs = cB
            cs3 = v3(cs)
            ti = 63 if d == 0 else 0
            totb = cs3[:, :, ti:ti + 1].to_broadcast([128, NCH, 64])
            _act(kb, ginc.ap, cs.ap, AF.Exp, [cs], [ginc], scale=-KAPPA)
            _act(kb, ginv.ap, cs.ap, AF.Exp, [cs], [ginv], scale=KAPPA)
            _tt(kb, "pool", tmp.ap, cs.ap, sg.ap, ALU.subtract, [cs, sg], [tmp])
            _act(kb, gexc.ap, tmp.ap, AF.Exp, [tmp], [gexc], scale=-KAPPA)
            _tt(kb, "dve", v3(cA), cs3, totb, ALU.subtract, [cs], [cA])
            _act(kb, gts.ap, cA.ap, AF.Exp, [cA], [gts], scale=KAPPA)
            _act(kb, gTt.ap, cs3[:, :, ti], AF.Exp, [cs], [gTt], scale=-KAPPA)
            yield
            _stt(kb, ar4[:, :, 0, :], v3(kk), -1.0, v3(gexc), ALU.mult, ALU.mult, [kk, gexc], [ar])
            _tt(kb, "pool", ar4[:, :, 1, :], v3(r), v3(ginc), ALU.mult, [r, ginc], [ar])
            _tt(kb, "pool", bt.ap, bv.ap, ginv.ap, ALU.mult, [bv, ginv], [bt])
            yield
            for h2 in range(2):
                ps_ = slice(h2 * 64, (h2 + 1) * 64)
                _stt(kb, z4(atz)[ps_, :, h2, :], v3(kk)[ps_], -1.0, v3(gexc)[ps_], ALU.mult, ALU.mult, [kk, gexc], [atz])
                _tt(kb, "pool", z4(rz)[ps_, :, h2, :], v3(r)[ps_], v3(ginc)[ps_], ALU.mult, [r, ginc], [rz])
                _tt(kb, "pool", z4(btz)[ps_, :, h2, :], v3(bv)[ps_], v3(ginv)[ps_], ALU.mult, [bv, ginv], [btz])
                _tt(kb, "dve", z4(ktz)[ps_, :, h2, :], v3(kd)[ps_], v3(ginv)[ps_], ALU.mult, [kd, ginv], [ktz])
            yield
            _tt(kb, "pool", bh.ap, bv.ap, gts.ap, ALU.mult, [bv, gts], [bh])
            _tt(kb, "dve", kh.ap, kd.ap, gts.ap, ALU.mult, [kd, gts], [kh])
            yield
            for qi, (srct, fn) in enumerate(((ar, lambda c: ar4[:, c, 0, :]), (bh, lambda c: bh.ap[:, c * 64:(c + 1) * 64]),
                                            (kh, lambda c: kh.ap[:, c * 64:(c + 1) * 64]),
                                            (v, lambda c: v.ap[:, c * 64:(c + 1) * 64]))):
                bk = kb.bank()
                for c in range(NCH):
                    _tr(kb, kb.ps[0:64, bk, c * 128:(c + 1) * 128], fn(c), ident, [srct], [kb.psb[bk]])
                _cp(kb, "act" if qi % 2 else "dve", tok[qi].ap[0:64, :], kb.ps[0:64, bk, :], [kb.psb[bk]], [tok[qi]])
                if qi % 2 == 1:
                    yield
            yield
            for (lt, dstt) in ((btz, NBt), (ktz, KBt)):
                for half in range(2):
                    bk = kb.bank()
                    for ii in range(4):
                        i = half * 4 + ii
                        c, h2 = i // 2, i % 2
                        _mm(kb, kb.ps[0:64, bk, ii * 128:(ii + 1) * 128], z4(lt)[:, c, h2, :],
                            ar.ap[:, c * 128:(c + 1) * 128], [lt, ar], [kb.psb[bk]])
                    _tt(kb, "dve", dstt.ap[0:64, half * 512:(half + 1) * 512], kb.ps[0:64, bk, :],
                        mNB[d].ap[0:64, :], ALU.mult, [kb.psb[bk], mNB[d]], [dstt])
                yield
            bk = kb.bank()
            for i in range(NI):
                c, h2 = i // 2, i % 2
                _mm(kb, kb.ps[0:64, bk, i * 64:(i + 1) * 64], z4(atz)[:, c, h2, :],
                    bt.ap[:, c * 64:(c + 1) * 64], [atz, bt], [kb.psb[bk]])
            _tt(kb, "dve", AAB.ap[0:64, :], kb.ps[0:64, bk, :], mAAB[d].ap[0:64, :], ALU.mult,
                [kb.psb[bk], mAAB[d]], [AAB])
            _tt(kb, "pool", N3(Pk[0])[0:64], NB4[0:64, :, 0, :], N3(I8)[0:64], ALU.add, [NBt, I8], [Pk[0]])
            _cp(kb, "pool", N3(N0)[0:64], NB4[0:64, :, 0, :], [NBt], [N0])
            yield
            curN = lambda i: N3(N0)[0:64, i, :]
            curNt = N0
            curA = AAB
            pc = 0
            for lev in range(5):
                Nn, An = Nk[lev % 2], Ak[lev % 2]
                bA = kb.bank()
                for i in range(NI):
                    _mm(kb, kb.ps[0:64, bA, i * 64:(i + 1) * 64], curN(i), N3(curA)[0:64, i, :], [curNt, curA], [kb.psb[bA]])
                if lev < 4:
                    bN = kb.bank()
                    for i in range(NI):
                        _mm(kb, kb.ps[0:64, bN, i * 64:(i + 1) * 64], N3(curA)[0:64, i, :], curN(i), [curNt, curA], [kb.psb[bN]])
                _cp(kb, "act", An.ap[0:64, :], kb.ps[0:64, bA, :], [kb.psb[bA]], [An])
                if lev < 4:
                    _cp(kb, "dve", Nn.ap[0:64, :], kb.ps[0:64, bN, :], [kb.psb[bN]], [Nn])
                yield
                bP = kb.bank()
                for i in range(NI):
                    _mm(kb, kb.ps[0:64, bP, i * 64:(i + 1) * 64], N3(An)[0:64, i, :], N3(Pk[pc])[0:64, i, :],
                        [An, Pk[pc]], [kb.psb[bP]])
                _tt(kb, "dve", Pk[1 - pc].ap[0:64, :], kb.ps[0:64, bP, :], Pk[pc].ap[0:64, :], ALU.add,
                    [kb.psb[bP], Pk[pc]], [Pk[1 - pc]])
                pc = 1 - pc
                curA = An
                curNt = Nn
                curN = (lambda Nn: (lambda i: N3(Nn)[0:64, i, :]))(Nn)
                yield
            _cp(kb, "dve", Pf.ap[0:64, :], Pk[pc].ap[0:64, :], [Pk[pc]], [Pf])
            Pm = Pf
            bk = kb.bank()
            for i in range(NI):
                c, h2 = i // 2, i % 2
                _mm(kb, kb.ps[0:64, bk, i * 64:(i + 1) * 64], KB4[0:64, i, 0, :],
                    tok3[3][0:64, c, h2 * 64:(h2 + 1) * 64], [KBt, tok[3]], [kb.psb[bk]])
            _cp(kb, "act", AKV.ap[0:64, :], kb.ps[0:64, bk, :], [kb.psb[bk]], [AKV])
            bk2 = kb.bank()
            for i in range(NI):
                c, h2 = i // 2, i % 2
                _mm(kb, kb.ps[:, bk2, i * 64:(i + 1) * 64], tok3[0][0:64, c, :], N3(Pm)[0:64, i, :],
                    [tok[0], Pm], [kb.psb[bk2]])
            ps4 = kb.ps[:, bk2, :].rearrange("p (c h t) -> p c h t", h=2, t=64)
            _cp(kb, "dve", z4(W1Tz)[0:64, :, 0, :], ps4[0:64, :, 0, :], [kb.psb[bk2]], [W1Tz])
            _cp(kb, "dve", z4(W1Tz)[64:128, :, 1, :], ps4[64:128, :, 1, :], [kb.psb[bk2]], [W1Tz])
            yield
            bk = kb.bank()
            for i in range(NI):
                _mm(kb, kb.ps[0:64, bk, i * 64:(i + 1) * 64], N3(Pm)[0:64, i, :], N3(AKV)[0:64, i, :],
                    [Pm, AKV], [kb.psb[bk]])
            _cp(kb, "act", W2.ap[0:64, :], kb.ps[0:64, bk, :], [kb.psb[bk]], [W2])
            W1T3 = N3(W1Tz)
            yield
            corder = list(range(NCH)) if d == 0 else list(range(NCH - 1, -1, -1))
            for c in corder:
                H, Hn = Hs[hcur], Hs[1 - hcur]
                bU = kb.bank()
                for h2 in range(2):
                    _mm(kb, kb.ps[0:64, bU, h2 * 64:(h2 + 1) * 64], W1T3[:, c * 2 + h2, :],
                        H.ap[:, h2 * 64:(h2 + 1) * 64], [W1Tz, H], [kb.psb[bU]])
                _tt(kb, "dve", Usb.ap[0:64, :], kb.ps[0:64, bU, 0:128], W2.ap[0:64, c * 128:(c + 1) * 128], ALU.add,
                    [kb.psb[bU], W2], [Usb])
                yield
                bH = kb.bank()
                _mm(kb, kb.ps[:, bH, 0:128], tok3[2][0:64, c, :], tok3[3][0:64, c, :], [tok[2], tok[3]],
                    [kb.psb[bH]], start=True, stop=False)
                _mm(kb, kb.ps[:, bH, 0:128], tok3[1][0:64, c, :], Usb.ap[0:64, :], [tok[1], Usb],
                    [kb.psb[bH]], start=False, stop=True)
                bY = kb.bank()
                psY3 = kb.ps[:, bY, 0:128].rearrange("p (h t) -> p h t", t=64)
                _mm(kb, psY3, H.ap, z4(rz)[:, c, :, :], [H, rz], [kb.psb[bY]], start=True, stop=False)
                _mm(kb, psY3, Usb.ap[0:64, :], NB4[0:64, c * 2:c * 2 + 2, 1, :],
                    [Usb, NBt], [kb.psb[bY]], start=False, stop=False)
                _mm(kb, psY3, tok3[3][0:64, c, :], KB4[0:64, c * 2:c * 2 + 2, 1, :],
                    [tok[3], KBt], [kb.psb[bY]], start=False, stop=True)
                _stt(kb, Hn.ap, H.ap, gTt.ap[:, c:c + 1], kb.ps[:, bH, 0:128], ALU.mult, ALU.add,
                     [H, gTt, kb.psb[bH]], [Hn])
                _cp(kb, "act", ysc.ap[0:64, c * 64:(c + 1) * 64], kb.ps[0:64, bY, 0:64], [kb.psb[bY]], [ysc])
                _cp(kb, "act", ysc.ap[64:128, c * 64:(c + 1) * 64], kb.ps[64:128, bY, 64:128], [kb.psb[bY]], [ysc])
                hcur = 1 - hcur
                yield
            dma(kb, "pool", yT[d, fc * 128:(fc + 1) * 128, t0:t0 + W], ysc.ap, reads=ysc.b, writes=[yT_b[d][fc][sc]])

    TP = _NS()
    TP.ld = [kb.f32(WH + 6) for _ in range(3)]
    TP.tp = [kb.f32(W) for _ in range(15)]

    def shift_p(u, out, mu0, mu1, np_=128):
        t1, t2 = TP.tp[13], TP.tp[14]
        mus = [mu_rkv, mu_wa, mu_g]
        _tt(kb, "pool", t1.ap[0:np_, :], u.ap[0:np_, 0:W], u.ap[0:np_, 1:W + 1], ALU.subtract, [u], [t1])
        _stt(kb, out.ap[0:np_, :], t1.ap[0:np_, :], mu0, u.ap[0:np_, 1:W + 1], ALU.mult, ALU.add, [t1, u] + mus, [out])
        _tt(kb, "pool", t2.ap[0:np_, :], u.ap[0:np_, 2:W + 2], u.ap[0:np_, 1:W + 1], ALU.subtract, [u], [t2])
        _stt(kb, out.ap[0:np_, :], t2.ap[0:np_, :], mu1, out.ap[0:np_, :], ALU.mult, ALU.add, [t2, out] + mus, [out])

    def post_gen(fc):
        for ti_ in range(L // W if dbg.get('post', True) else 0):
            t0 = ti_ * W
            (uv, ug0, ug1) = TP.ld
            (y, cen, sq_, rs, yn, vv, g0, g1, bvv, ob_, y1, bo0, bo1) = TP.tp[0:13]
            dma(kb, "sp", uv.ap[:, 0:WH], urwT[1024 + fc * 128:1024 + (fc + 1) * 128, t0:t0 + WH], writes=uv.b)
            dma(kb, "sp", ug0.ap[:, 0:WH], urwT[1792:1920, t0:t0 + WH], writes=ug0.b)
            dma(kb, "sp", ug1.ap[0:32, 0:WH], urwT[1920:1952, t0:t0 + WH], writes=ug1.b)
            fsl = slice(fc * 128, (fc + 1) * 128)
            dma(kb, "sp", y.ap, yT[0, fsl, t0:t0 + W], reads=[yT_b[0][fc][ti_]], writes=y.b)
            dma(kb, "sp", y1.ap, yT[1, fsl, t0:t0 + W], reads=[yT_b[1][fc][ti_]], writes=y1.b)
            dma(kb, "sp", bo0.ap, bT[0, fsl, t0:t0 + W], reads=[bT_b[0][fc][ti_]], writes=bo0.b)
            dma(kb, "sp", bo1.ap, bT[1, fsl, t0:t0 + W], reads=[bT_b[1][fc][ti_]], writes=bo1.b)
            o = (2 * 4 + fc) * 2
            shift_p(uv, vv, mu_rkv.ap[:, o:o + 1], mu_rkv.ap[:, o + 1:o + 2])
            shift_p(ug0, g0, mu_g.ap[:, 0:1], mu_g.ap[:, 1:2])
            yield
            shift_p(ug1, g1, mu_g.ap[0:32, 2:3], mu_g.ap[0:32, 3:4], 32)
            _act(kb, g0.ap, g0.ap, AF.Sigmoid, [g0], [g0])
            _act(kb, g1.ap[0:32, :], g1.ap[0:32, :], AF.Sigmoid, [g1], [g1])
            _tt(kb, "pool", y.ap, y.ap, y1.ap, ALU.add, [y, y1], [y])
            _tt(kb, "pool", bo0.ap, bo0.ap, bo1.ap, ALU.add, [bo0, bo1], [bo0])
            yield
            bM = kb.bank()
            _mm(kb, kb.ps[:, bM, 0:W], bd64.ap, y.ap, [bd64, y], [kb.psb[bM]])
            _stt(kb, cen.ap, kb.ps[:, bM, 0:W], -1.0 / 64, y.ap, ALU.mult, ALU.add, [kb.psb[bM], y], [cen])
            _tt(kb, "pool", sq_.ap, cen.ap, cen.ap, ALU.mult, [cen], [sq_])
            yield
            bV = kb.bank()
            _mm(kb, kb.ps[:, bV, 0:W], bd64.ap, sq_.ap, [bd64, sq_], [kb.psb[bV]])
            _act(kb, rs.ap, kb.ps[:, bV, 0:W], AF.Sqrt, [kb.psb[bV]], [rs], scale=1.0 / 64, bias=GN_EPS)
            kb.op("dve", lambda e, rs=rs: e.reciprocal(out=rs.ap, in_=rs.ap), reads=rs.b, writes=rs.b)
            _tt(kb, "pool", yn.ap, cen.ap, rs.ap, ALU.mult, [cen, rs], [yn])
            _ts(kb, "dve", yn.ap, yn.ap, vec(3, fc), vec(4, fc), ALU.mult, ALU.add, [yn, vecs], [yn])
            yield
            bB = kb.bank()
            _mm(kb, kb.ps[:, bB, 0:W], bd64.ap, bo0.ap, [bd64, bo0], [kb.psb[bB]])
            _tt(kb, "dve", bvv.ap, kb.ps[:, bB, 0:W], vv.ap, ALU.mult, [kb.psb[bB], vv], [bvv])
            _tt(kb, "pool", yn.ap, yn.ap, bvv.ap, ALU.add, [yn, bvv], [yn])
            bG = kb.bank()
            _mm(kb, kb.ps[:, bG, 0:W], g2a.ap[:, fc * 128:(fc + 1) * 128], g0.ap, [g2a, g0], [kb.psb[bG]],
                start=True, stop=False)
            _mm(kb, kb.ps[:, bG, 0:W], g2b.ap[0:32, fc * 128:(fc + 1) * 128], g1.ap[0:32, :], [g2b, g1], [kb.psb[bG]],
                start=False, stop=True)
            ob = ob_.ap.bitcast(BF16)[:, 0:W]
            _tt(kb, "dve", ob, kb.ps[:, bG, 0:W], yn.ap, ALU.mult, [kb.psb[bG], yn], [ob_])
            dma(kb, "pool", yrwT[fc * 128:(fc + 1) * 128, t0:t0 + W], ob, reads=ob_.b)
            yield

    nfc = dbg.get('fcs', 4)
    for fc in range(nfc + 1):
        gens = []
        if fc < nfc:
            gens += [sc_gen(fc, d, TS[d]) for d in range(dbg.get('dirs', 2))]
        if fc > 0:
            gens.append(post_gen(fc - 1))
        for _ in itertools.zip_longest(*gens):
            pass


def merge_phase(kb, C, zhyT, yrwT, gT, x1T, x2T, hyo, rwo, wo):
    wh = kb.bf16(4 * D)
    wr = kb.bf16(4 * D)
    wo_ = kb.bf16(8 * D)
    wh3 = wh.ap.rearrange("p (k d) -> p k d", d=D)
    wr3 = wr.ap.rearrange("p (k d) -> p k d", d=D)
    wo3 = wo_.ap.rearrange("p (k d) -> p k d", d=D)
    dma(kb, "sp", wh3, hyo.rearrange("(k p) d -> p k d", p=128), writes=wh.b)
    dma(kb, "sp", wr3, rwo.rearrange("(k p) d -> p k d", p=128), writes=wr.b)
    dma(kb, "sp", wo3, wo.rearrange("(k p) d -> p k d", p=128), writes=wo_.b)
    zt = [kb.bf16(4 * 512) for _ in range(2)]
    yt = [kb.bf16(4 * 512) for _ in range(2)]
    mrg = [kb.bf16(8 * 512, 8) for _ in range(2)]
    gh = [kb.f32(512) for _ in range(2)]
    gr = [kb.f32(512) for _ in range(2)]
    m1 = [kb.f32(512) for _ in range(2)]
    m2 = [kb.f32(512) for _ in range(2)]
    xr = [kb.f32(512) for _ in range(2)]
    xo = [kb.f32(512) for _ in range(2)]
    zh3 = zhyT.rearrange("(k p) t -> p k t", p=128)
    yr3 = yrwT.rearrange("(k p) t -> p k t", p=128)
    cnt = 0
    for t in range(L // 512):
        ts_ = slice(t * 512, (t + 1) * 512)
        z_, y_, mg = zt[t % 2], yt[t % 2], mrg[t % 2]
        z3 = z_.ap.rearrange("p (k t) -> p k t", t=512)
        y3 = y_.ap.rearrange("p (k t) -> p k t", t=512)
        mg3 = mg.ap.rearrange("p (k t) -> p k t", t=512)
        dma(kb, "sp", z3, zh3[:, :, ts_], writes=z_.b)
        dma(kb, "sp", y3, yr3[:, :, ts_], writes=y_.b)
        for dc in range(8):
            i2 = cnt % 2
            cnt += 1
            dsl = slice(dc * 128, (dc + 1) * 128)
            dma(kb, "sp", gh[i2].ap, gT[dc * 128:(dc + 1) * 128, ts_], writes=gh[i2].b)
            dma(kb, "sp", gr[i2].ap, gT[D + dc * 128:D + (dc + 1) * 128, ts_], writes=gr[i2].b)
            bh, br = kb.bank(), kb.bank()
            for kc in range(4):
                _mm(kb, kb.ps[:, bh, :], wh3[:, kc, dsl], z3[:, kc, :], [wh, z_], [kb.psb[bh]], start=(kc == 0), stop=(kc == 3))
            for kc in range(4):
                _mm(kb, kb.ps[:, br, :], wr3[:, kc, dsl], y3[:, kc, :], [wr, y_], [kb.psb[br]], start=(kc == 0), stop=(kc == 3))
            _tt(kb, "dve", m1[i2].ap, kb.ps[:, bh, :], gh[i2].ap, ALU.mult, [kb.psb[bh], gh[i2]], [m1[i2]])
            _tt(kb, "dve", m2[i2].ap, kb.ps[:, br, :], gr[i2].ap, ALU.mult, [kb.psb[br], gr[i2]], [m2[i2]])
            _tt(kb, "pool", mg3[:, dc, :], m1[i2].ap, m2[i2].ap, ALU.add, [m1[i2], m2[i2]], [mg.b[dc]])
        for dc in range(8):
            i2 = cnt % 2
            cnt += 1
            dsl = slice(dc * 128, (dc + 1) * 128)
            dma(kb, "sp", xr[i2].ap, x1T[dc * 128:(dc + 1) * 128, ts_], writes=xr[i2].b)
            bo = kb.bank()
            for kc in range(8):
                _mm(kb, kb.ps[:, bo, :], wo3[:, kc, dsl], mg3[:, kc, :], [wo_, mg.b[kc]], [kb.psb[bo]],
                    start=(kc == 0), stop=(kc == 7))
            _tt(kb, "dve", xo[i2].ap, kb.ps[:, bo, :], xr[i2].ap, ALU.add, [kb.psb[bo], xr[i2]], [xo[i2]])
            dma(kb, "pool", x2T[dc * 128:(dc + 1) * 128, ts_], xo[i2].ap, reads=xo[i2].b)


def _host_consts():
    c = {}
    c["ones"] = np.ones((128, 128), np.float32)
    c["ident"] = np.eye(128, dtype=np.float32)
    bd = np.zeros((128, 128), np.float32)
    bd[0:64, 0:64] = 1.0
    bd[64:128, 64:128] = 1.0
    c["bd64"] = bd
    s = np.arange(64)[:, None]
    t = np.arange(64)[None, :]
    lt, le, gt, ge = (s < t), (s <= t), (s > t), (s >= t)

    def nb(m0, m1):
        m = np.zeros((64, 4, 2, 64), np.float32)
        m[:, :, 0, :] = m0[:, None, :]
        m[:, :, 1, :] = m1[:, None, :]
        return m.reshape(64, 512)

    c["mNBf"] = nb(lt, le)
    c["mNBb"] = nb(gt, ge)
    c["mAABf"] = np.broadcast_to(gt[:, None, :], (64, 8, 64)).astype(np.float32).reshape(64, 512).copy()
    c["mAABb"] = np.broadcast_to(lt[:, None, :], (64, 8, 64)).astype(np.float32).reshape(64, 512).copy()
    c["I8"] = np.broadcast_to(np.eye(64, dtype=np.float32)[:, None, :], (64, 8, 64)).reshape(64, 512).copy()
    rm = np.ones((128, SC), np.float32)
    rm[:, 0::64] = 0.0
    c["rmask"] = rm
    return c


CONST_SHAPES = {"ones": [128, 128], "ident": [128, 128], "bd64": [128, 128], "mNBf": [64, 512], "mNBb": [64, 512],
                "mAABf": [64, 512], "mAABb": [64, 512], "I8": [64, 512], "rmask": [128, 256]}

PARAM_SHAPES = {
    "norms": [128, 32],
    "rw_muT": [NRW, 2], "rw_w0T": [512, 2], "rw_a0T": [512, 2], "rw_w2t": [64, 1024], "rw_a2t": [64, 1024],
    "rw_g2": [160, 512], "rw_vecs": [128, 20],
}

WEIGHTS = {"ffn1_w_gu": [D, 2 * DFF], "ffn1_w_down": [DFF, D], "ffn2_w_gu": [D, 2 * DFF], "ffn2_w_down": [DFF, D],
           "w_in": [D, NIN], "hy_out": [512, D], "rw_out": [512, D], "w_out": [D, D]}


def _host_params(inputs):
    f = lambda k: np.asarray(inputs[k], np.float32)
    p = {}
    p["norms"] = np.ascontiguousarray(np.concatenate(
        [f(n).reshape(8, 128).T for n in ("ffn1_norm", "mix_norm", "ffn2_norm", "final_norm")], axis=1))
    p["rw_muT"] = np.ascontiguousarray(f("rw_mu").T)
    p["rw_w0T"] = np.ascontiguousarray(f("rw_w0").T)
    p["rw_a0T"] = np.ascontiguousarray(f("rw_a0").T)
    p["rw_w2t"] = np.ascontiguousarray(f("rw_w2").transpose(1, 0, 2).reshape(64, 1024))
    p["rw_a2t"] = np.ascontiguousarray(f("rw_a2").transpose(1, 0, 2).reshape(64, 1024))
    p["rw_g2"] = np.ascontiguousarray(f("rw_g2"))
    p["rw_vecs"] = np.ascontiguousarray(np.concatenate(
        [f(n).reshape(4, 128).T for n in ("rw_k_k", "rw_k_a", "rw_r_k", "rw_ln_w", "rw_ln_b")], axis=1))
    return p


def build_program(stage="full"):
    nc = bass.Bass("TRN2", target_bir_lowering=False)
    I = {}

    def inp(name, shape, dt=F32):
        I[name] = nc.dram_tensor(name, list(shape), dt, kind="ExternalInput").ap()
        return I[name]

    def scr(name, shape, dt=F32, ext=None):
        if ext == "in":
            return inp(name, shape, dt)
        if ext == "out":
            return nc.dram_tensor(name, list(shape), dt, kind="ExternalOutput").ap()
        return nc.dram_tensor(name, list(shape), dt).ap()

    full = stage == "full"
    front = stage in ("full", "front")
    do_rw = stage in ("full", "rwkv")
    do_hy = stage in ("full", "hyena")
    for n, shp in CONST_SHAPES.items():
        inp(n, shp)
    inp("norms", PARAM_SHAPES["norms"])
    if do_rw:
        for n, shp in PARAM_SHAPES.items():
            if n != "norms":
                inp(n, shp)
    if do_hy:
        for n, shp in list(HY_CONST_SHAPES.items()) + list(HY_PARAM_SHAPES.items()):
            inp(n, shp)
    if front:
        inp("xT", [D, L])
        for n, shp in WEIGHTS.items():
            if full or n in ("ffn1_w_gu", "ffn1_w_down", "w_in"):
                inp(n, shp)
    back = stage == "back"
    if back:
        for n in ("hy_out", "rw_out", "w_out", "ffn2_w_gu", "ffn2_w_down"):
            inp(n, WEIGHTS[n])
    uhy = scr("uhy", [L + 2, NHY], ext={"front": "out", "hyena": "in"}.get(stage))
    urwT = scr("urwT", [NRW, L + 2], ext={"front": "out", "rwkv": "in"}.get(stage))
    gT = scr("gT", [NG, L], ext={"front": "out", "back": "in"}.get(stage))
    yrwT = scr("yrwT", [512, L], BF16, ext={"rwkv": "out", "back": "in"}.get(stage))
    zhyT = scr("zhyT", [512, L], BF16, ext={"hyena": "out", "back": "in"}.get(stage))
    outT = None
    if full or stage == "back":
        outT = nc.dram_tensor("outT", [D, L], F32, kind="ExternalOutput").ap()
    wb = {}
    for n, shp in WEIGHTS.items():
        if n in I:
            wb[n] = scr(n + "_b", shp, BF16)
    x1T = scr("x1T", [D, L], ext={"front": "out", "back": "in"}.get(stage))
    x2T = scr("x2T", [D, L])
    x3T = scr("x3T", [D, L])
    HD = {}
    if do_hy:
        HD["ucb"] = scr("hy_ucb", [3, 4, 64, 64 * CG])
        HD["Abuf"] = scr("hy_Abuf", [2, 2, 128, 64, CG])
        HD["Bbuf"] = scr("hy_Bbuf", [2, 2, 64, 128, CG])
        HD["Kf"] = scr("hy_Kf", [2, 4, 2, 64, 128 * CG])
        HD["ucb_b"] = [[Buf() for _ in range(4)] for _ in range(3)]
        HD["Kf_b"] = [[[Buf() for _ in range(32)] for _ in range(4)] for _ in range(2)]
        if stage == "hyena":
            HD["kf_dbg"] = scr("kf_dbg", [2, 4, 128, 64 * CG], ext="out")
            HD["z1_dbg"] = scr("z1_dbg", [4, 64, 64 * CG], ext="out")

    with contextlib.ExitStack() as st:
        kb = KB(nc, st)
        C = {}
        for n in ("ones", "ident", "bd64"):
            C[n] = kb.f32(128)
            dma(kb, "sp", C[n].ap, I[n], writes=C[n].b)
        nrm = kb.f32(32)
        for i, n in enumerate(("ffn1_norm", "mix_norm", "ffn2_norm", "final_norm")):
            t = Tile(nrm.ap[:, i * 8:(i + 1) * 8])
            t.b = nrm.b
            C[n] = t
        dma(kb, "sp", nrm.ap, I["norms"], writes=nrm.b)
        kb.persist()

        if front:
            convert_weights(kb, [(I[n], wb[n], WEIGHTS[n][0], WEIGHTS[n][1]) for n in wb])
            kb.phase_end()
            ffn_phase(kb, C, I["xT"], x1T, "ffn1_norm", wb["ffn1_w_gu"], wb["ffn1_w_down"])
            kb.phase_end()
            inproj_phase(kb, C, x1T, wb["w_in"], uhy, urwT, gT)
            kb.phase_end()
        if do_rw:
            rwkv_phase(kb, C, I, urwT, yrwT, {"yT": scr("rw_yT", [2, 512, L]), "bT": scr("rw_bT", [2, 512, L])})
            kb.phase_end()
        if do_hy:
            hyena_phase(kb, C, I, uhy, zhyT, HD)
            kb.phase_end()
        if back:
            convert_weights(kb, [(I[n], wb[n], WEIGHTS[n][0], WEIGHTS[n][1]) for n in wb])
            kb.phase_end()
        if full or back:
            merge_phase(kb, C, zhyT, yrwT, gT, x1T, x2T, wb["hy_out"], wb["rw_out"], wb["w_out"])
            kb.phase_end()
            ffn_phase(kb, C, x2T, x3T, "ffn2_norm", wb["ffn2_w_gu"], wb["ffn2_w_down"])
            kb.phase_end()
            final_norm_phase(kb, C, x3T, outT, "final_norm")
            kb.phase_end()
        kb.S.emit()
    return nc, list(I.keys())


_NC_CACHE = {}


def _get_program(stage):
    if stage not in _NC_CACHE:
        _NC_CACHE[stage] = build_program(stage)
    return _NC_CACHE[stage]


def host_shared(inputs, names):
    shared = dict(_host_consts())
    shared.update(_host_params(inputs))
    shared.update(_hy_consts())
    shared.update(_hy_host_params(inputs))
    for n in WEIGHTS:
        shared[n] = np.ascontiguousarray(inputs[n], np.float32)
    return {k: v for k, v in shared.items() if k in names}


def kernel(**inputs):
    nc, names = _get_program("full")
    shared = host_shared(inputs, names)
    x = np.asarray(inputs["x"], np.float32)
    in_maps = []
    for b in range(8):
        m = dict(shared)
        m["xT"] = np.ascontiguousarray(x[b].T)
        in_maps.append(m)
    res = run_bass_kernel_spmd(nc, in_maps, core_ids=list(range(8)))
    out = np.stack([np.ascontiguousarray(r["outT"].T) for r in res.results], axis=0)
    return out.astype(np.float32)


NFFT = 8192
CG = 128
HY_DBG = {}


def _hy_consts():
    c = {}
    n1 = np.arange(128, dtype=np.float64)[:, None, None]
    n2 = np.arange(64, dtype=np.float64)[None, :, None]
    k1 = np.arange(128, dtype=np.float64)[None, None, :]
    ang = 2 * np.pi * (n1 * k1 / 128.0 + n2 * k1 / NFFT)
    G = np.stack([np.cos(ang), -np.sin(ang)], axis=2)
    c["hy_G"] = G.reshape(128, 64 * 2 * 128).astype(np.float32)
    a2 = 2 * np.pi * np.arange(64)[:, None] * np.arange(64)[None, :] / 64.0
    c["hy_F2"] = np.concatenate([np.cos(a2), np.sin(a2), -np.sin(a2)], axis=1).astype(np.float32)
    k2 = np.arange(64, dtype=np.float64)[:, None, None]
    k1b = np.arange(128, dtype=np.float64)[None, :, None]
    nl = np.arange(64, dtype=np.float64)[None, None, :]
    angp = 2 * np.pi * (k2 * nl / 64.0 + k1b * nl / NFFT)
    Gp = np.stack([np.cos(angp), np.sin(angp), -np.sin(angp)], axis=2)
    c["hy_Gp"] = Gp.reshape(64, 128 * 3 * 64).astype(np.float32)
    a3 = 2 * np.pi * np.arange(128)[:, None] * np.arange(64)[None, :] / 128.0
    wk = np.full((128, 1), 2.0)
    wk[0] = 1.0
    wk[64] = 1.0
    wk[65:] = 0.0
    c["hy_I2"] = (wk * np.concatenate([np.cos(a3), -np.sin(a3)], axis=1) / NFFT).astype(np.float32)
    n = np.arange(NFFT)
    j = np.where(n <= L, n, NFFT - n).astype(np.float64)
    j[L] = 0
    t = j / (L - 1)
    angf = (2.0 * math.pi / L) * j
    bands = np.linspace(1e-4, 15, 16)
    feats = np.concatenate([t[None, :], np.cos(bands[:, None] * angf[None, :]), -np.sin(bands[:, None] * angf[None, :])], axis=0)
    c["hy_feats"] = feats.astype(np.float32)
    c["hy_negt"] = (-t).reshape(128, 64).astype(np.float32)
    return c


HY_CONST_SHAPES = {"hy_G": [128, 64 * 2 * 128], "hy_F2": [64, 192], "hy_Gp": [64, 128 * 3 * 64], "hy_I2": [128, 128],
                   "hy_feats": [33, NFFT], "hy_negt": [128, 64]}
HY_PARAM_SHAPES = {"hy_conv_w": [3, NHY], "hy_conv_b": [1, NHY], "hy_w1": [33, 64], "hy_w2": [64, 64],
                   "hy_w3a": [65, 2048], "hy_b1f": [64, 3], "hy_decay": [4, 512], "hy_bias_d": [2, 512]}


def _hy_host_params(inputs):
    f = lambda k: np.asarray(inputs[k], np.float32)
    p = {}
    p["hy_conv_w"] = np.ascontiguousarray(f("hy_conv_w"))
    p["hy_conv_b"] = np.ascontiguousarray(f("hy_conv_b").reshape(1, NHY))
    p["hy_w1"] = np.ascontiguousarray(f("hy_ffn_w1"))
    p["hy_w2"] = np.ascontiguousarray(f("hy_ffn_w2"))
    p["hy_w3a"] = np.ascontiguousarray(np.concatenate([f("hy_ffn_w3"), f("hy_ffn_b3").reshape(1, 2048)], axis=0))
    p["hy_b1f"] = np.ascontiguousarray(np.stack([f("hy_ffn_b1"), f("hy_ffn_b2"), f("hy_sin_freq")], axis=1))
    p["hy_decay"] = np.ascontiguousarray(f("hy_decay").reshape(4, 512))
    p["hy_bias_d"] = np.ascontiguousarray(f("hy_bias_d"))
    return p


def _sin_act(kb, out_ap, out_tile, pre, scr, np_):
    TWO_PI = 2.0 * math.pi
    MAGIC = 12582912.0
    _ts(kb, "dve", scr.ap[0:np_, :], pre.ap[0:np_, :], 1.0 / TWO_PI, MAGIC, ALU.mult, ALU.add, [pre], [scr])
    _ts(kb, "dve", scr.ap[0:np_, :], scr.ap[0:np_, :], MAGIC, None, ALU.subtract, None, [scr], [scr])
    _stt(kb, scr.ap[0:np_, :], scr.ap[0:np_, :], -TWO_PI, pre.ap[0:np_, :], ALU.mult, ALU.add, [scr, pre], [scr])
    _ts(kb, "dve", scr.ap[0:np_, :], scr.ap[0:np_, :], 3.141592, -3.141592, ALU.min, ALU.max, [scr], [scr])
    _act(kb, out_ap, scr.ap[0:np_, :], AF.Sin, [scr], [out_tile])


def hyena_phase(kb, C, P, uhy, zhyT, D_):
    dbg = HY_DBG
    ident = C["ident"]
    ucb, Abuf_all, Bbuf_all, Kf = D_["ucb"], D_["Abuf"], D_["Bbuf"], D_["Kf"]
    NG4 = dbg.get("groups", 4)
    cw = [kb.f32(NHY) for _ in range(3)]
    cb = kb.f32(NHY)
    for i in range(3):
        dma(kb, "sp", cw[i].ap[0:64, :], P["hy_conv_w"][i:i + 1, :].partition_broadcast(64), writes=cw[i].b)
        dma(kb, "sp", cw[i].ap[64:128, 0:NHY - CG], P["hy_conv_w"][i:i + 1, CG:NHY].partition_broadcast(64), writes=cw[i].b)
    dma(kb, "sp", cb.ap[0:64, :], P["hy_conv_b"].partition_broadcast(64), writes=cb.b)
    dma(kb, "sp", cb.ap[64:128, 0:NHY - CG], P["hy_conv_b"][:, CG:NHY].partition_broadcast(64), writes=cb.b)
    uin = [kb.f32(66 * CG) for _ in range(2)]
    uo = [kb.f32(64 * CG) for _ in range(2)]
    tmpc = kb.f32(64 * CG)
    uhy_b = uhy[0:L, :].rearrange("(p n) c -> p n c", n=64)
    uhy_h = uhy[2:L + 2, :].rearrange("(p n) c -> p n c", n=64)
    it = 0
    for kind in range(3):
        for gp in range(0, NG4, 2):
            npair = min(2, NG4 - gp)
            NP = 64 * npair
            c0 = kind * 512 + gp * CG
            ui, uo_ = uin[it % 2], uo[it % 2]
            it += 1
            u3 = ui.ap.rearrange("p (n c) -> p n c", c=CG)
            o3 = uo_.ap.rearrange("p (n c) -> p n c", c=CG)
            t3 = tmpc.ap.rearrange("p (n c) -> p n c", c=CG)
            for h in range(npair):
                ch = c0 + h * CG
                dma(kb, "sp", u3[h * 64:(h + 1) * 64, 0:64, :], uhy_b[:, :, ch:ch + CG], writes=ui.b)
                dma(kb, "act", u3[h * 64:(h + 1) * 64, 64:66, :], uhy_h[:, 62:64, ch:ch + CG], writes=ui.b)

            def bc(t):
                return t.ap[0:NP, c0:c0 + CG].unsqueeze(1).to_broadcast([NP, 64, CG])

            _tt(kb, "dve", o3[0:NP], u3[0:NP, 0:64, :], bc(cw[0]), ALU.mult, [ui, cw[0]], [uo_])
            _tt(kb, "pool", t3[0:NP], u3[0:NP, 1:65, :], bc(cw[1]), ALU.mult, [ui, cw[1]], [tmpc])
            _tt(kb, "dve", o3[0:NP], o3[0:NP], t3[0:NP], ALU.add, [uo_, tmpc], [uo_])
            _tt(kb, "pool", t3[0:NP], u3[0:NP, 2:66, :], bc(cw[2]), ALU.mult, [ui, cw[2]], [tmpc])
            _tt(kb, "dve", o3[0:NP], o3[0:NP], t3[0:NP], ALU.add, [uo_, tmpc], [uo_])
            _tt(kb, "dve", o3[0:NP], o3[0:NP], bc(cb), ALU.add, [uo_, cb], [uo_])
            for h in range(npair):
                dma(kb, "pool", ucb[kind, gp + h], uo_.ap[h * 64:(h + 1) * 64, :], reads=uo_.b,
                    writes=[D_["ucb_b"][kind][gp + h]])
    kb.phase_end()

    F2t = kb.f32(192)
    I2t = kb.f32(128)
    dma(kb, "sp", F2t.ap[0:64, :], P["hy_F2"], writes=F2t.b)
    dma(kb, "sp", I2t.ap, P["hy_I2"], writes=I2t.b)
    kb.persist()
    NK1 = 65
    KBATCH = [(k0, min(4, NK1 - k0)) for k0 in range(0, NK1, 4)]
    A_bs = [[Buf() for _ in range(16)] for _ in range(2)]
    B_bs = [[Buf() for _ in range(17)] for _ in range(2)]
    Gc = [kb.f32(1024) for _ in range(2)]
    Ast = [[kb.f32(512) for _ in range(2)] for _ in range(2)]
    Ain = [[kb.f32(512) for _ in range(2)] for _ in range(2)]
    cnt = {"g": 0, "a": 0, "i": 0}

    def fft_s1(src3, srcT, K, aset=0):
        Abuf, A_b = Abuf_all[aset], A_bs[aset]
        for n2b in range(16):
            gt = Gc[cnt["g"] % 2]
            cnt["g"] += 1
            dma(kb, "sp", gt.ap[0:K, :], P["hy_G"][0:K, n2b * 1024:(n2b + 1) * 1024], writes=gt.b)
            g4 = gt.ap.rearrange("p (j r k) -> p j r k", r=2, k=128)
            banks = [kb.bank(), kb.bank()]
            for j in range(4):
                for ri in range(2):
                    _mm(kb, kb.ps[0:NK1, banks[ri], j * 128:(j + 1) * 128], g4[0:K, j, ri, 0:NK1], src3[0:K, n2b * 4 + j, :],
                        [gt, srcT], [kb.psb[banks[ri]]])
            for ri in range(2):
                st_ = Ast[ri][cnt["a"] % 2]
                _cp(kb, "act" if ri == 0 else "dve", st_.ap[0:NK1, :], kb.ps[0:NK1, banks[ri], :], [kb.psb[banks[ri]]], [st_])
                dma(kb, "pool", Abuf[ri, 0:NK1, n2b * 4:(n2b + 1) * 4, :], st_.ap[0:NK1, :].rearrange("p (n c) -> p n c", c=CG),
                    reads=st_.b, writes=[A_b[n2b]])
            cnt["a"] += 1

    def fft_s2(k0, nk, aset=0):
        Abuf, A_b = Abuf_all[aset], A_bs[aset]
        tin = []
        nn = nk * CG
        for ri in range(2):
            t_ = Ain[ri][cnt["i"] % 2]
            dma(kb, "sp", t_.ap[0:64, 0:nn].rearrange("p (k c) -> p k c", c=CG),
                Abuf[ri, k0:k0 + nk, :, :].rearrange("k n c -> n k c"), reads=A_b, writes=t_.b)
            tin.append(t_)
        cnt["i"] += 1
        bre, bim = kb.bank(), kb.bank()
        c2, s2, ns2 = F2t.ap[0:64, 0:64], F2t.ap[0:64, 64:128], F2t.ap[0:64, 128:192]
        _mm(kb, kb.ps[0:64, bre, 0:nn], c2, tin[0].ap[0:64, 0:nn], [F2t, tin[0]], [kb.psb[bre]], start=True, stop=False)
        _mm(kb, kb.ps[0:64, bre, 0:nn], s2, tin[1].ap[0:64, 0:nn], [F2t, tin[1]], [kb.psb[bre]], start=False, stop=True)
        _mm(kb, kb.ps[0:64, bim, 0:nn], c2, tin[1].ap[0:64, 0:nn], [F2t, tin[1]], [kb.psb[bim]], start=True, stop=False)
        _mm(kb, kb.ps[0:64, bim, 0:nn], ns2, tin[0].ap[0:64, 0:nn], [F2t, tin[0]], [kb.psb[bim]], start=False, stop=True)
        return bre, bim

    mark = kb.off
    w1t = kb.f32(64)
    w2t = kb.f32(64)
    b1f = kb.f32(3)
    w3a = kb.f32(2048)
    negt = kb.f32(64)
    h2f = kb.f32(NFFT)
    h2b = kb.f32(NFFT)
    dma(kb, "sp", w1t.ap[0:33, :], P["hy_w1"], writes=w1t.b)
    dma(kb, "sp", w2t.ap[0:64, :], P["hy_w2"], writes=w2t.b)
    dma(kb, "sp", b1f.ap[0:64, :], P["hy_b1f"], writes=b1f.b)
    dma(kb, "sp", w3a.ap[0:65, :], P["hy_w3a"], writes=w3a.b)
    dma(kb, "sp", negt.ap, P["hy_negt"], writes=negt.b)
    kb.op("pool", lambda e: e.memset(h2f.ap, 0.0), writes=h2f.b)
    kb.op("pool", lambda e: e.memset(h2b.ap, 0.0), writes=h2b.b)
    kb.op("pool", lambda e: e.memset(h2f.ap[64:65, 0:L], 1.0), reads=h2f.b, writes=h2f.b)
    kb.op("pool", lambda e: e.memset(h2b.ap[64:65, L + 1:NFFT], 1.0), reads=h2b.b, writes=h2b.b)
    fch = [kb.f32(512) for _ in range(2)]
    pre = kb.f32(512)
    scr = kb.f32(512)
    h1c = kb.f32(512)
    for pc in range(16):
        ft = fch[pc % 2]
        dma(kb, "sp", ft.ap[0:33, :], P["hy_feats"][:, pc * 512:(pc + 1) * 512], writes=ft.b)
        b1 = kb.bank()
        _mm(kb, kb.ps[0:64, b1, :], w1t.ap[0:33, :], ft.ap[0:33, :], [w1t, ft], [kb.psb[b1]])
        _ts(kb, "dve", pre.ap[0:64, :], kb.ps[0:64, b1, :], b1f.ap[0:64, 0:1], b1f.ap[0:64, 2:3], ALU.add, ALU.mult,
            [kb.psb[b1], b1f], [pre])
        _sin_act(kb, h1c.ap[0:64, :], h1c, pre, scr, 64)
        b2 = kb.bank()
        _mm(kb, kb.ps[0:64, b2, :], w2t.ap[0:64, :], h1c.ap[0:64, :], [w2t, h1c], [kb.psb[b2]])
        _ts(kb, "dve", pre.ap[0:64, :], kb.ps[0:64, b2, :], b1f.ap[0:64, 1:2], b1f.ap[0:64, 2:3], ALU.add, ALU.mult,
            [kb.psb[b2], b1f], [pre])
        hdst = h2f if pc < 8 else h2b
        TWO_PI = 2.0 * math.pi
        MAGIC = 12582912.0
        _ts(kb, "dve", scr.ap[0:64, :], pre.ap[0:64, :], 1.0 / TWO_PI, MAGIC, ALU.mult, ALU.add, [pre], [scr])
        _ts(kb, "dve", scr.ap[0:64, :], scr.ap[0:64, :], MAGIC, None, ALU.subtract, None, [scr], [scr])
        _stt(kb, scr.ap[0:64, :], scr.ap[0:64, :], -TWO_PI, pre.ap[0:64, :], ALU.mult, ALU.add, [scr, pre], [scr])
        _ts(kb, "dve", scr.ap[0:64, :], scr.ap[0:64, :], 3.141592, -3.141592, ALU.min, ALU.max, [scr], [scr])
        _act(kb, hdst.ap[0:64, pc * 512:(pc + 1) * 512], scr.ap[0:64, :], AF.Sin, [scr], [hdst])
    kb.op("pool", lambda e: e.memset(h2b.ap[0:64, L:L + 1], 0.0), reads=h2b.b, writes=h2b.b)
    kfs = [kb.f32(64 * CG), kb.f32(64 * CG)]
    absd = kb.f32(CG)
    et = [kb.f32(512) for _ in range(2)]
    sqc = [kb.f32(512) for _ in range(2)]
    rnf = kb.f32(CG)
    kst = [[kb.f32(512) for _ in range(2)] for _ in range(2)]
    h2f3 = h2f.ap.rearrange("p (a b) -> p b a", b=64)
    h2b3 = h2b.ap.rearrange("p (a b) -> p b a", b=64)

    def gen_filter(o, g, kf):
        kf3 = kf.ap.rearrange("p (n c) -> p n c", c=CG)
        for dr in range(2):
            dma(kb, "sp", absd.ap[dr * 64:(dr + 1) * 64, :],
                P["hy_decay"][dr * 2 + o:dr * 2 + o + 1, g * CG:(g + 1) * CG].partition_broadcast(64), writes=absd.b)
        _stt(kb, absd.ap, absd.ap, -1.0, absd.ap, ALU.mult, ALU.max, [absd], [absd])
        bss = kb.bank()
        kb.reserved.add(bss)
        for n2b in range(16):
            bk = kb.bank()
            e_ = et[n2b % 2]
            sq_ = sqc[n2b % 2]
            for j in range(4):
                n2 = n2b * 4 + j
                cf = o * 512 + g * CG
                _mm(kb, kb.ps[:, bk, j * 128:(j + 1) * 128], h2f3[0:65, n2, :], w3a.ap[0:65, cf:cf + CG],
                    [h2f, w3a], [kb.psb[bk]], start=True, stop=False)
                _mm(kb, kb.ps[:, bk, j * 128:(j + 1) * 128], h2b3[0:65, n2, :], w3a.ap[0:65, 1024 + cf:1024 + cf + CG],
                    [h2b, w3a], [kb.psb[bk]], start=False, stop=True)
                _act(kb, e_.ap[:, j * 128:(j + 1) * 128], absd.ap, AF.Exp, [absd, negt], [e_], scale=negt.ap[:, n2:n2 + 1])
            _tt(kb, "dve", kf.ap[:, n2b * 512:(n2b + 1) * 512], kb.ps[:, bk, :], e_.ap, ALU.mult, [kb.psb[bk], e_], [kf])
            _tt(kb, "pool", sq_.ap, kf.ap[:, n2b * 512:(n2b + 1) * 512], kf.ap[:, n2b * 512:(n2b + 1) * 512], ALU.mult,
                [kf], [sq_])
            for j in range(4):
                _mm(kb, kb.ps[:, bss, 0:CG], C["ones"].ap, sq_.ap[:, j * 128:(j + 1) * 128], [C["ones"], sq_],
                    [kb.psb[bss]], start=(n2b == 0 and j == 0), stop=(n2b == 15 and j == 3))
        kb.reserved.discard(bss)
        _act(kb, rnf.ap, kb.ps[:, bss, 0:CG], AF.Sqrt, [kb.psb[bss]], [rnf], bias=1e-6)
        kb.op("dve", lambda e: e.reciprocal(out=rnf.ap, in_=rnf.ap), reads=rnf.b, writes=rnf.b)
        _tt(kb, "dve", kf3, kf3, rnf.ap.unsqueeze(1).to_broadcast([128, 64, CG]), ALU.mult, [kf, rnf], [kf])
        if "kf_dbg" in D_:
            dma(kb, "pool", D_["kf_dbg"][o, g], kf.ap, reads=kf.b)

    def fft_filter(o, g, kf, aset):
        kf3 = kf.ap.rearrange("p (n c) -> p n c", c=CG)
        fft_s1(kf3, kf, 128, aset)
        for k1b, (k0, nk) in enumerate(KBATCH):
            nn = nk * CG
            bre, bim = fft_s2(k0, nk, aset)
            for ri, bk_ in enumerate((bre, bim)):
                st_ = kst[ri][k1b % 2]
                _cp(kb, "act" if ri == 0 else "dve", st_.ap[0:64, 0:nn], kb.ps[0:64, bk_, 0:nn], [kb.psb[bk_]], [st_])
                dma(kb, "pool", Kf[o, g, ri, :, k0 * CG:k0 * CG + nn], st_.ap[0:64, 0:nn], reads=st_.b,
                    writes=[D_["Kf_b"][o][g][k1b]])

    items = [(o, g) for o in range(2) for g in range(NG4)]
    gen_filter(items[0][0], items[0][1], kfs[0])
    for i, (o, g) in enumerate(items):
        if i + 1 < len(items):
            gen_filter(items[i + 1][0], items[i + 1][1], kfs[(i + 1) % 2])
        fft_filter(o, g, kfs[i % 2], i % 2)
    kb.S.barrier()
    kb.off = mark

    import itertools

    def alloc_c():
        T = _NS()
        T.zbuf = kb.f32(64 * CG)
        T.Kt = [kb.f32(512) for _ in range(2)]
        T.Gp = kb.f32(768)
        T.xr, T.xi = kb.f32(512), kb.f32(512)
        T.tt = [kb.f32(512) for _ in range(4)]
        T.Yt = [kb.f32(512) for _ in range(2)]
        T.Bst = [kb.f32(512) for _ in range(2)]
        T.Bin = [kb.f32(512) for _ in range(2)]
        T.xg, T.sk, T.te, T.zo, T.dbc = [kb.f32(512) for _ in range(5)]
        T.zT = kb.bf16(L)
        return T

    def stage_c(g, T, aset):
        Bbuf, B_b = Bbuf_all[aset], B_bs[aset]
        zbuf = T.zbuf
        z3 = zbuf.ap.rearrange("p (n c) -> p n c", c=CG)
        zT3 = T.zT.ap.rearrange("c (p n) -> c p n", n=64)
        dma(kb, "sp", zbuf.ap[0:64, :], ucb[2, g], reads=[D_["ucb_b"][2][g]], writes=zbuf.b)
        for o in range(dbg.get("orders", 2)):
            for rep in range(4):
                dma(kb, "sp", T.dbc.ap[0:64, rep * CG:(rep + 1) * CG],
                    P["hy_bias_d"][o:o + 1, g * CG:(g + 1) * CG].partition_broadcast(64), writes=T.dbc.b)
            fft_s1(z3, zbuf, 64, aset)
            yield
            for k1b, (k0, nk) in enumerate(KBATCH):
                nn = nk * CG
                kt = T.Kt
                for ri in range(2):
                    dma(kb, "sp", kt[ri].ap[0:64, 0:nn], Kf[o, g, ri, :, k0 * CG:k0 * CG + nn],
                        reads=[D_["Kf_b"][o][g][k1b]], writes=kt[ri].b)
                gp = T.Gp
                dma(kb, "sp", gp.ap[0:64, 0:nk * 192], P["hy_Gp"][:, k0 * 192:(k0 + nk) * 192], writes=gp.b)
                gp4 = gp.ap.rearrange("p (j q n) -> p j q n", q=3, n=64)
                bre, bim = fft_s2(k0, nk, aset)
                xr_, xi_ = T.xr, T.xi
                _cp(kb, "act", xr_.ap[0:64, 0:nn], kb.ps[0:64, bre, 0:nn], [kb.psb[bre]], [xr_])
                _cp(kb, "act", xi_.ap[0:64, 0:nn], kb.ps[0:64, bim, 0:nn], [kb.psb[bim]], [xi_])
                yre, yim = T.Yt
                ta, tb_, tc_, td = T.tt
                _tt(kb, "dve", ta.ap[0:64, 0:nn], xr_.ap[0:64, 0:nn], kt[0].ap[0:64, 0:nn], ALU.mult, [xr_, kt[0]], [ta])
                _tt(kb, "dve", tb_.ap[0:64, 0:nn], xi_.ap[0:64, 0:nn], kt[1].ap[0:64, 0:nn], ALU.mult, [xi_, kt[1]], [tb_])
                _tt(kb, "dve", yre.ap[0:64, 0:nn], ta.ap[0:64, 0:nn], tb_.ap[0:64, 0:nn], ALU.subtract, [ta, tb_], [yre])
                _tt(kb, "pool", tc_.ap[0:64, 0:nn], xr_.ap[0:64, 0:nn], kt[1].ap[0:64, 0:nn], ALU.mult, [xr_, kt[1]], [tc_])
                _tt(kb, "pool", td.ap[0:64, 0:nn], xi_.ap[0:64, 0:nn], kt[0].ap[0:64, 0:nn], ALU.mult, [xi_, kt[0]], [td])
                _tt(kb, "pool", yim.ap[0:64, 0:nn], tc_.ap[0:64, 0:nn], td.ap[0:64, 0:nn], ALU.add, [tc_, td], [yim])
                yield
                cre, cim = kb.bank(), kb.bank()
                for j in range(nk):
                    cs_ = slice(j * 128, (j + 1) * 128)
                    _mm(kb, kb.ps[0:64, cre, cs_], gp4[0:64, j, 0, :], yre.ap[0:64, cs_], [gp, yre], [kb.psb[cre]],
                        start=True, stop=False)
                    _mm(kb, kb.ps[0:64, cre, cs_], gp4[0:64, j, 2, :], yim.ap[0:64, cs_], [gp, yim], [kb.psb[cre]],
                        start=False, stop=True)
                    _mm(kb, kb.ps[0:64, cim, cs_], gp4[0:64, j, 1, :], yre.ap[0:64, cs_], [gp, yre], [kb.psb[cim]],
                        start=True, stop=False)
                    _mm(kb, kb.ps[0:64, cim, cs_], gp4[0:64, j, 0, :], yim.ap[0:64, cs_], [gp, yim], [kb.psb[cim]],
                        start=False, stop=True)
                for ri, bk_ in enumerate((cre, cim)):
                    st_ = T.Bst[ri]
                    _cp(kb, "act" if ri == 0 else "dve", st_.ap[0:64, 0:nn], kb.ps[0:64, bk_, 0:nn], [kb.psb[bk_]], [st_])
                    dma(kb, "pool", Bbuf[ri, :, k0:k0 + nk, :], st_.ap[0:64, 0:nn].rearrange("p (k c) -> p k c", c=CG),
                        reads=st_.b, writes=[B_b[k1b]])
                yield
            for nb in range(16):
                bin_ = T.Bin
                for ri in range(2):
                    dma(kb, "sp", bin_[ri].ap[0:NK1, :].rearrange("p (n c) -> p n c", c=CG),
                        Bbuf[ri, nb * 4:(nb + 1) * 4, 0:NK1, :].rearrange("n k c -> k n c"), reads=B_b, writes=bin_[ri].b)
                xg_ = T.xg
                dma(kb, "sp", xg_.ap[0:64, :], ucb[o, g, :, nb * 512:(nb + 1) * 512], reads=[D_["ucb_b"][o][g]], writes=xg_.b)
                by = kb.bank()
                _mm(kb, kb.ps[0:64, by, :], I2t.ap[0:NK1, 0:64], bin_[0].ap[0:NK1, :], [I2t, bin_[0]], [kb.psb[by]], start=True, stop=False)
                _mm(kb, kb.ps[0:64, by, :], I2t.ap[0:NK1, 64:128], bin_[1].ap[0:NK1, :], [I2t, bin_[1]], [kb.psb[by]], start=False, stop=True)
                te = T.te
                if o == 0:
                    sk_ = T.sk
                    dma(kb, "sp", sk_.ap[0:64, :], ucb[2, g, :, nb * 512:(nb + 1) * 512], reads=[D_["ucb_b"][2][g]], writes=sk_.b)
                    _tt(kb, "pool", te.ap[0:64, :], sk_.ap[0:64, :], T.dbc.ap[0:64, :], ALU.mult, [sk_, T.dbc], [te])
                else:
                    _tt(kb, "pool", te.ap[0:64, :], zbuf.ap[0:64, nb * 512:(nb + 1) * 512], T.dbc.ap[0:64, :], ALU.mult,
                        [zbuf, T.dbc], [te])
                _tt(kb, "dve", te.ap[0:64, :], kb.ps[0:64, by, :], te.ap[0:64, :], ALU.add, [kb.psb[by], te], [te])
                if o == 0:
                    _tt(kb, "pool", zbuf.ap[0:64, nb * 512:(nb + 1) * 512], xg_.ap[0:64, :], te.ap[0:64, :], ALU.mult,
                        [xg_, te], [zbuf])
                else:
                    zo_ = T.zo
                    _tt(kb, "pool", zo_.ap[0:64, :], xg_.ap[0:64, :], te.ap[0:64, :], ALU.mult, [xg_, te], [zo_])
                    bt_ = kb.bank()
                    for j in range(4):
                        _tr(kb, kb.ps[:, bt_, j * 64:(j + 1) * 64], zo_.ap[0:64, j * 128:(j + 1) * 128],
                            _IdentView(ident), [zo_], [kb.psb[bt_]])
                    _cp(kb, "act", zT3[:, :, nb * 4:(nb + 1) * 4],
                        kb.ps[:, bt_, 0:256].rearrange("c (j p) -> c p j", p=64), [kb.psb[bt_]], [T.zT])
                yield
            if o == 0 and "z1_dbg" in D_:
                dma(kb, "pool", D_["z1_dbg"][g], zbuf.ap[0:64, :], reads=zbuf.b)
            if o == 1:
                dma(kb, "pool", zhyT[g * CG:(g + 1) * CG, :], T.zT.ap, reads=T.zT.b)

    TC = [alloc_c(), alloc_c()]
    for g0 in range(0, NG4, 2):
        gens = [stage_c(g0 + i, TC[i], i) for i in range(min(2, NG4 - g0))]
        for _ in itertools.zip_longest(*gens):
            pass


class _IdentView:
    def __init__(self, ident):
        self.ap = ident.ap[0:64, 0:64]
        self.b = ident.b
```

```python
import contextlib
import math
import numpy as np
import concourse.bass as bass
import concourse.mybir as mybir
from concourse.bass_utils import run_bass_kernel_spmd

F32 = mybir.dt.float32
BF16 = mybir.dt.bfloat16
AF = mybir.ActivationFunctionType
ALU = mybir.AluOpType
AX = mybir.AxisListType

D = 1024
L = 4096
DFF = 2816
NHY = 1536
NRW = 1952
NG = 2048
NIN = NHY + NRW + NG
RMS_EPS = 1e-6
GN_EPS = 64e-5

SEM_EPOCH = 30000
N_DMA_SLOTS = 12


class Buf:
    __slots__ = ("lw", "rd")

    def __init__(self):
        self.lw = None
        self.rd = []


class Op:
    __slots__ = ("eng", "fn", "deps", "dma", "needed", "tok", "slot", "noinst")


class Sched:
    ENGS = ("pe", "act", "dve", "pool", "sp")

    def __init__(self, nc):
        self.nc = nc
        self.streams = {e: [] for e in self.ENGS}
        self.dma_count = {e: 0 for e in self.ENGS}
        self.dma_slot_last = {}

    def op(self, eng, fn, reads=(), writes=(), dma=False, extra=(), noinst=False):
        o = Op()
        o.noinst = noinst
        o.eng = eng
        o.fn = fn
        o.dma = dma
        o.needed = False
        deps = {}
        for b in reads:
            if b.lw is not None:
                deps[id(b.lw)] = b.lw
        for b in writes:
            if b.lw is not None:
                deps[id(b.lw)] = b.lw
            for r in b.rd:
                deps[id(r)] = r
        for d in extra:
            deps[id(d)] = d
        if dma:
            k = self.dma_count[eng]
            self.dma_count[eng] += 1
            o.slot = (eng, k % N_DMA_SLOTS, k // N_DMA_SLOTS)
            prev = self.dma_slot_last.get((eng, k % N_DMA_SLOTS))
            if prev is not None:
                deps[id(prev)] = prev
            self.dma_slot_last[(eng, k % N_DMA_SLOTS)] = o
        dl = []
        for d in deps.values():
            if d is o:
                continue
            if (not d.dma) and d.eng == eng and eng == "pe" and not o.dma:
                continue
            assert not d.noinst
            d.needed = True
            dl.append(d)
        o.deps = dl
        for b in reads:
            b.rd.append(o)
        for b in writes:
            b.lw = o
            b.rd = []
        self.streams[eng].append(o)
        return o

    def barrier(self):
        lasts = list(self.dma_slot_last.values())
        for s in self.streams.values():
            for o in reversed(s):
                if not o.noinst:
                    lasts.append(o)
                    break
        for e in self.ENGS:
            self.op(e, lambda eh: None, extra=lasts, noinst=True)

    def emit(self):
        nc = self.nc
        with contextlib.ExitStack() as st:
            esem = {}
            for e in self.ENGS:
                n_sig = sum(1 for o in self.streams[e] if o.needed and not o.dma)
                n_ep = max(1, (n_sig + SEM_EPOCH - 1) // SEM_EPOCH)
                esem[e] = [st.enter_context(nc.semaphore(f"s_{e}_{i}")) for i in range(n_ep)]
            dsem = {}
            for e in self.ENGS:
                if self.dma_count[e] > 0:
                    dsem[e] = [st.enter_context(nc.semaphore(f"d_{e}_{i}")) for i in range(N_DMA_SLOTS)]
            for e in self.ENGS:
                c = 0
                for o in self.streams[e]:
                    if o.dma:
                        _, s, r = o.slot
                        o.tok = (dsem[e][s], 16 * (r + 1), ("d", e, s))
                    elif o.needed:
                        ep = c // SEM_EPOCH
                        o.tok = (esem[e][ep], (c % SEM_EPOCH) + 1, ("e", e, ep))
                        c += 1
                    else:
                        o.tok = None
            block = st.enter_context(nc.Block())
            hmap = {"pe": block.tensor, "act": block.scalar, "dve": block.vector,
                    "pool": block.gpsimd, "sp": block.sync}
            for e in self.ENGS:
                stream = self.streams[e]
                if not stream:
                    continue

                def section(eh, stream=stream):
                    waited = {}
                    for o in stream:
                        for d in o.deps:
                            sem, val, key = d.tok
                            if waited.get(key, 0) >= val:
                                continue
                            eh.wait_ge(sem, val)
                            waited[key] = val
                        ins = o.fn(eh)
                        if ins is None:
                            continue
                        if o.dma:
                            ins.then_inc(o.tok[0], 16)
                        elif o.needed:
                            ins.then_inc(o.tok[0], 1)

                hmap[e](section)


class Tile:
    def __init__(self, ap, n=1):
        self.ap = ap
        self.b = [Buf() for _ in range(n)]


class KB:
    def __init__(self, nc, st):
        self.nc = nc
        self.S = Sched(nc)
        self.arena = st.enter_context(nc.sbuf_tensor("arena", [128, 49152], F32))
        self.ps = st.enter_context(nc.psum_tensor("psum", [128, 8, 512], F32))
        self.psb = [Buf() for _ in range(8)]
        self.ps_i = 0
        self.reserved = set()
        self.base = 0
        self.off = 0
        self.rr = 0

    def f32(self, n, nb=1):
        assert self.off + n <= 49152, ("sbuf overflow", self.off, n)
        ap = self.arena[:, self.off:self.off + n]
        self.off += n
        return Tile(ap, nb)

    def bf16(self, n, nb=1):
        w = (n + 1) // 2
        assert self.off + w <= 49152, ("sbuf overflow", self.off, w)
        ap = self.arena[:, self.off:self.off + w].bitcast(BF16)[:, 0:n]
        self.off += w
        return Tile(ap, nb)

    def persist(self):
        self.base = self.off

    def phase_end(self):
        self.S.barrier()
        self.off = self.base

    def bank(self):
        while self.ps_i in self.reserved:
            self.ps_i = (self.ps_i + 1) % 8
        i = self.ps_i
        self.ps_i = (self.ps_i + 1) % 8
        return i

    def op(self, eng, fn, reads=(), writes=(), dma=False):
        return self.S.op(eng, fn, reads=reads, writes=writes, dma=dma)

    def ew_eng(self):
        self.rr += 1
        return "dve" if (self.rr % 3) else "pool"


def dma(kb, eng, out, in_, reads=(), writes=(), slow=False):
    if slow:
        return kb.op(eng, lambda e: e.dma_start(out=out, in_=in_, allow_slow_non_contiguous=True),
                     reads=reads, writes=writes, dma=True)
    return kb.op(eng, lambda e: e.dma_start(out=out, in_=in_), reads=reads, writes=writes, dma=True)


def convert_weights(kb, pairs):
    CW = 2816
    stg = [kb.f32(CW) for _ in range(3)]
    stb = [kb.bf16(CW) for _ in range(3)]
    i = 0
    engs = ("dve", "act", "dve", "act", "pool")
    for src, dst, R, C in pairs:
        for r0 in range(0, R, 128):
            rr = min(128, R - r0)
            for c0 in range(0, C, CW):
                cc = min(CW, C - c0)
                a, b = stg[i % 3], stb[i % 3]
                dma(kb, "sp", a.ap[0:rr, 0:cc], src[r0:r0 + rr, c0:c0 + cc], writes=a.b)
                eng = engs[i % 5]
                if eng == "act":
                    kb.op("act", lambda e, a=a, b=b, rr=rr, cc=cc: e.copy(out=b.ap[0:rr, 0:cc], in_=a.ap[0:rr, 0:cc]),
                          reads=a.b, writes=b.b)
                else:
                    kb.op(eng, lambda e, a=a, b=b, rr=rr, cc=cc: e.tensor_copy(out=b.ap[0:rr, 0:cc], in_=a.ap[0:rr, 0:cc]),
                          reads=a.b, writes=b.b)
                dma(kb, "pool" if i % 2 else "act", dst[r0:r0 + rr, c0:c0 + cc], b.ap[0:rr, 0:cc], reads=b.b)
                i += 1


def rmsnorm_T(kb, C, xt, g, out_ap_fn, sq, rstd):
    x3 = xt.ap
    sq3 = sq.ap.rearrange("p (a b) -> p a b", b=512)
    bk = kb.bank()
    for kc in range(8):
        kb.op("act", lambda e, kc=kc: e.activation(out=sq3[:, kc, :], in_=x3[:, kc, :], func=AF.Square),
              reads=xt.b, writes=[sq.b[kc]])
    for kc in range(8):
        kb.op("pe", lambda e, kc=kc: e.matmul(kb.ps[:, bk, :], lhsT=C["ones"].ap, rhs=sq3[:, kc, :],
                                              start=(kc == 0), stop=(kc == 7)),
              reads=[sq.b[kc]] + C["ones"].b, writes=[kb.psb[bk]])
    kb.op("act", lambda e: e.activation(out=rstd.ap, in_=kb.ps[:, bk, :], func=AF.Sqrt, scale=1.0 / D, bias=RMS_EPS),
          reads=[kb.psb[bk]], writes=rstd.b)
    kb.op("dve", lambda e: e.reciprocal(out=rstd.ap, in_=rstd.ap), reads=rstd.b, writes=rstd.b)
    outs = []
    for kc in range(8):
        o = out_ap_fn(kc)
        outs.append(o)
    return outs


def ffn_phase(kb, C, xin, xout, gname, wgu, wdn):
    TQ = 1024
    xn = kb.bf16(8 * TQ, 8)
    G = kb.bf16(22 * TQ, 22)
    xn3 = xn.ap.rearrange("p (a b) -> p a b", b=TQ)
    G3 = G.ap.rearrange("p (a b) -> p a b", b=TQ)
    xts = [kb.f32(8 * 512) for _ in range(2)]
    sq = kb.f32(8 * 512, 8)
    rstd = kb.f32(512)
    wg = [kb.bf16(8 * 256) for _ in range(2)]
    wu = [kb.bf16(8 * 256) for _ in range(2)]
    wd = [kb.bf16(22 * 512) for _ in range(2)]
    sg = [kb.f32(512) for _ in range(2)]
    xr = [kb.f32(512) for _ in range(2)]
    xo = [kb.f32(512) for _ in range(2)]
    g = C[gname]
    xin3 = xin.rearrange("(kc p) t -> p kc t", p=128)
    wgu3 = wgu.rearrange("(kc p) f -> p kc f", p=128)
    wdn3 = wdn.rearrange("(fc p) d -> p fc d", p=128)
    cnt = 0
    for q in range(L // TQ):
        for t in range(2):
            tok = q * TQ + t * 512
            xt = xts[t]
            x3 = xt.ap.rearrange("p (a b) -> p a b", b=512)
            xt3 = Tile(x3)
            xt3.b = xt.b
            dma(kb, "sp", x3, xin3[:, :, tok:tok + 512], writes=xt.b)
            rmsnorm_T(kb, C, xt3, g, lambda kc: None, sq, rstd)
            for kc in range(8):
                eng = "dve"
                kb.op(eng, lambda e, kc=kc, x3=x3, t=t: e.scalar_tensor_tensor(
                    out=xn3[:, kc, t * 512:(t + 1) * 512], in0=x3[:, kc, :], scalar=g.ap[:, kc:kc + 1],
                    in1=rstd.ap, op0=ALU.mult, op1=ALU.mult),
                    reads=xt.b + rstd.b + g.b, writes=[xn.b[kc]])
        for s in range(11):
            a, b = wg[s % 2], wu[s % 2]
            a3 = a.ap.rearrange("p (a b) -> p a b", b=256)
            b3 = b.ap.rearrange("p (a b) -> p a b", b=256)
            dma(kb, "sp", a3, wgu3[:, :, s * 256:(s + 1) * 256], writes=a.b)
            dma(kb, "sp", b3, wgu3[:, :, DFF + s * 256:DFF + (s + 1) * 256], writes=b.b)
            for fcl in range(2):
                fc = s * 2 + fcl
                for t in range(2):
                    bg = kb.bank()
                    bu = kb.bank()
                    for kc in range(8):
                        kb.op("pe", lambda e, kc=kc, a3=a3, fcl=fcl, t=t, bg=bg: e.matmul(
                            kb.ps[:, bg, :], lhsT=a3[:, kc, fcl * 128:(fcl + 1) * 128],
                            rhs=xn3[:, kc, t * 512:(t + 1) * 512], start=(kc == 0), stop=(kc == 7)),
                            reads=a.b + [xn.b[kc]], writes=[kb.psb[bg]])
                    for kc in range(8):
                        kb.op("pe", lambda e, kc=kc, b3=b3, fcl=fcl, t=t, bu=bu: e.matmul(
                            kb.ps[:, bu, :], lhsT=b3[:, kc, fcl * 128:(fcl + 1) * 128],
                            rhs=xn3[:, kc, t * 512:(t + 1) * 512], start=(kc == 0), stop=(kc == 7)),
                            reads=b.b + [xn.b[kc]], writes=[kb.psb[bu]])
                    sgt = sg[cnt % 2]
                    cnt += 1
                    kb.op("act", lambda e, sgt=sgt, bg=bg: e.activation(out=sgt.ap, in_=kb.ps[:, bg, :], func=AF.Silu),
                          reads=[kb.psb[bg]], writes=sgt.b)
                    kb.op("dve", lambda e, sgt=sgt, bu=bu, fc=fc, t=t: e.tensor_tensor(
                        out=G3[:, fc, t * 512:(t + 1) * 512], in0=kb.ps[:, bu, :], in1=sgt.ap, op=ALU.mult),
                        reads=[kb.psb[bu]] + sgt.b, writes=[G.b[fc]])
        for ds in range(2):
            w = wd[ds % 2]
            w3 = w.ap.rearrange("p (a b) -> p a b", b=512)
            dma(kb, "sp", w3, wdn3[:, :, ds * 512:(ds + 1) * 512], writes=w.b)
            for dcl in range(4):
                dc = ds * 4 + dcl
                for t in range(2):
                    tok = q * TQ + t * 512
                    bo = kb.bank()
                    xrt, xot = xr[cnt % 2], xo[cnt % 2]
                    cnt += 1
                    dma(kb, "sp", xrt.ap, xin[dc * 128:(dc + 1) * 128, tok:tok + 512], writes=xrt.b)
                    for fc in range(22):
                        kb.op("pe", lambda e, fc=fc, w3=w3, dcl=dcl, t=t, bo=bo: e.matmul(
                            kb.ps[:, bo, :], lhsT=w3[:, fc, dcl * 128:(dcl + 1) * 128],
                            rhs=G3[:, fc, t * 512:(t + 1) * 512], start=(fc == 0), stop=(fc == 21)),
                            reads=w.b + [G.b[fc]], writes=[kb.psb[bo]])
                    kb.op("dve", lambda e, xrt=xrt, xot=xot, bo=bo: e.scalar_tensor_tensor(
                        out=xot.ap, in0=kb.ps[:, bo, :], scalar=0.5, in1=xrt.ap, op0=ALU.mult, op1=ALU.add),
                        reads=[kb.psb[bo]] + xrt.b, writes=xot.b)
                    dma(kb, "pool", xout[dc * 128:(dc + 1) * 128, tok:tok + 512], xot.ap, reads=xot.b)


def final_norm_phase(kb, C, xin, out, gname):
    g = C[gname]
    xts = [kb.f32(8 * 512) for _ in range(2)]
    ots = [kb.f32(8 * 512) for _ in range(2)]
    sq = kb.f32(8 * 512, 8)
    rstd = kb.f32(512)
    xin3 = xin.rearrange("(kc p) t -> p kc t", p=128)
    out3 = out.rearrange("(kc p) t -> p kc t", p=128)
    for t in range(L // 512):
        xt, ot = xts[t % 2], ots[t % 2]
        x3 = xt.ap.rearrange("p (a b) -> p a b", b=512)
        o3 = ot.ap.rearrange("p (a b) -> p a b", b=512)
        xt3 = Tile(x3)
        xt3.b = xt.b
        dma(kb, "sp", x3, xin3[:, :, t * 512:(t + 1) * 512], writes=xt.b)
        rmsnorm_T(kb, C, xt3, g, lambda kc: None, sq, rstd)
        for kc in range(8):
            eng = "dve"
            kb.op(eng, lambda e, kc=kc, x3=x3, o3=o3: e.scalar_tensor_tensor(
                out=o3[:, kc, :], in0=x3[:, kc, :], scalar=g.ap[:, kc:kc + 1], in1=rstd.ap,
                op0=ALU.mult, op1=ALU.mult), reads=xt.b + rstd.b + g.b, writes=ot.b)
        dma(kb, "pool", out3[:, :, t * 512:(t + 1) * 512], o3, reads=ot.b)


def inproj_phase(kb, C, x1T, win, uhy, urwT, gT):
    TQ = 1024
    g = C["mix_norm"]
    xn = kb.bf16(8 * TQ, 8)
    xn3 = xn.ap.rearrange("p (a b) -> p a b", b=TQ)
    xts = [kb.f32(8 * 512) for _ in range(2)]
    sq = kb.f32(8 * 512, 8)
    rstd = kb.f32(512)
    why = kb.bf16(8 * NHY)
    why3 = why.ap.rearrange("p (a b) -> p a b", b=NHY)
    slabs = [kb.bf16(8 * 512) for _ in range(2)]
    stg = [kb.f32(512) for _ in range(4)]
    zero = kb.f32(NHY)
    x1T3 = x1T.rearrange("(kc p) t -> p kc t", p=128)
    win3 = win.rearrange("(kc p) f -> p kc f", p=128)
    kb.op("pool", lambda e: e.memset(zero.ap, 0.0), writes=zero.b)
    dma(kb, "sp", uhy[0:1, :], zero.ap[0:1, :], reads=zero.b)
    dma(kb, "sp", uhy[L + 1:L + 2, :], zero.ap[0:1, :], reads=zero.b)
    for r0 in range(0, NRW, 128):
        rr = min(128, NRW - r0)
        dma(kb, "sp", urwT[r0:r0 + rr, 0:1], zero.ap[0:rr, 0:1], reads=zero.b, slow=True)
        dma(kb, "sp", urwT[r0:r0 + rr, L + 1:L + 2], zero.ap[0:rr, 0:1], reads=zero.b, slow=True)
    dma(kb, "sp", why3, win3[:, :, 0:NHY], writes=why.b)
    cnt = 0
    for q in range(L // TQ):
        for t in range(2):
            tok = q * TQ + t * 512
            xt = xts[t]
            x3 = xt.ap.rearrange("p (a b) -> p a b", b=512)
            xt3 = Tile(x3)
            xt3.b = xt.b
            dma(kb, "sp", x3, x1T3[:, :, tok:tok + 512], writes=xt.b)
            rmsnorm_T(kb, C, xt3, g, lambda kc: None, sq, rstd)
            for kc in range(8):
                kb.op("dve", lambda e, kc=kc, x3=x3, t=t: e.scalar_tensor_tensor(
                    out=xn3[:, kc, t * 512:(t + 1) * 512], in0=x3[:, kc, :], scalar=g.ap[:, kc:kc + 1],
                    in1=rstd.ap, op0=ALU.mult, op1=ALU.mult),
                    reads=xt.b + rstd.b + g.b, writes=[xn.b[kc]])
        for (dst, col0, ncols, gate, coff) in ((urwT, NHY, NRW, False, 1), (gT, NHY + NRW, NG, True, 0)):
            for s0 in range(0, ncols, 512):
                cw = min(512, ncols - s0)
                sl = slabs[cnt % 2]
                sl3 = sl.ap.rearrange("p (a b) -> p a b", b=512)
                dma(kb, "sp", sl3[:, :, 0:cw], win3[:, :, col0 + s0:col0 + s0 + cw], writes=sl.b)
                for c0 in range(0, cw, 128):
                    m = min(128, cw - c0)
                    for t in range(2):
                        tok = q * TQ + t * 512
                        bk = kb.bank()
                        for kc in range(8):
                            kb.op("pe", lambda e, kc=kc, sl3=sl3, c0=c0, m=m, t=t, bk=bk: e.matmul(
                                kb.ps[0:m, bk, :], lhsT=sl3[:, kc, c0:c0 + m],
                                rhs=xn3[:, kc, t * 512:(t + 1) * 512], start=(kc == 0), stop=(kc == 7)),
                                reads=sl.b + [xn.b[kc]], writes=[kb.psb[bk]])
                        sg_ = stg[cnt % 4]
                        cnt += 1
                        if gate:
                            kb.op("act", lambda e, sg_=sg_, bk=bk, m=m: e.activation(
                                out=sg_.ap[0:m, :], in_=kb.ps[0:m, bk, :], func=AF.Sigmoid),
                                reads=[kb.psb[bk]], writes=sg_.b)
                        else:
                            kb.op("dve", lambda e, sg_=sg_, bk=bk, m=m: e.tensor_copy(
                                out=sg_.ap[0:m, :], in_=kb.ps[0:m, bk, :]),
                                reads=[kb.psb[bk]], writes=sg_.b)
                        r0 = s0 + c0
                        dma(kb, "pool", dst[r0:r0 + m, coff + tok:coff + tok + 512], sg_.ap[0:m, :], reads=sg_.b)
        for tb in range(TQ // 128):
            tok = q * TQ + tb * 128
            for cs in range(3):
                bk = kb.bank()
                for kc in range(8):
                    kb.op("pe", lambda e, kc=kc, tb=tb, cs=cs, bk=bk: e.matmul(
                        kb.ps[:, bk, :], lhsT=xn3[:, kc, tb * 128:(tb + 1) * 128],
                        rhs=why3[:, kc, cs * 512:(cs + 1) * 512], start=(kc == 0), stop=(kc == 7)),
                        reads=why.b + [xn.b[kc]], writes=[kb.psb[bk]])
                sg_ = stg[cnt % 4]
                cnt += 1
                kb.op("act", lambda e, sg_=sg_, bk=bk: e.copy(out=sg_.ap, in_=kb.ps[:, bk, :]),
                      reads=[kb.psb[bk]], writes=sg_.b)
                dma(kb, "pool", uhy[1 + tok:1 + tok + 128, cs * 512:(cs + 1) * 512], sg_.ap, reads=sg_.b)


def _bl(*tiles):
    out = []
    for t in tiles:
        out.extend(t.b if isinstance(t, Tile) else [t])
    return out


def _tt(kb, eng, out, a, b, op, R, W):
    return kb.op(eng, lambda e: e.tensor_tensor(out=out, in0=a, in1=b, op=op), reads=_bl(*R), writes=_bl(*W))


def _ts(kb, eng, out, a, s1, s2, op0, op1, R, W):
    if op1 is None:
        return kb.op(eng, lambda e: e.tensor_scalar(out=out, in0=a, scalar1=s1, scalar2=None, op0=op0),
                     reads=_bl(*R), writes=_bl(*W))
    return kb.op(eng, lambda e: e.tensor_scalar(out=out, in0=a, scalar1=s1, scalar2=s2, op0=op0, op1=op1),
                 reads=_bl(*R), writes=_bl(*W))


def _stt(kb, out, a, s, b, op0, op1, R, W):
    return kb.op("dve", lambda e: e.scalar_tensor_tensor(out=out, in0=a, scalar=s, in1=b, op0=op0, op1=op1),
                 reads=_bl(*R), writes=_bl(*W))


def _act(kb, out, a, func, R, W, scale=1.0, bias=0.0):
    return kb.op("act", lambda e: e.activation(out=out, in_=a, func=func, scale=scale, bias=bias),
                 reads=_bl(*R), writes=_bl(*W))


def _cp(kb, eng, out, a, R, W):
    if eng == "act":
        return kb.op("act", lambda e: e.copy(out=out, in_=a), reads=_bl(*R), writes=_bl(*W))
    return kb.op(eng, lambda e: e.tensor_copy(out=out, in_=a), reads=_bl(*R), writes=_bl(*W))


def _mm(kb, out, lhsT, rhs, R, W, start=True, stop=True):
    return kb.op("pe", lambda e: e.matmul(out, lhsT=lhsT, rhs=rhs, start=start, stop=stop),
                 reads=_bl(*R), writes=_bl(*W))


def _tr(kb, out, in_, ident, R, W):
    return kb.op("pe", lambda e: e.transpose(out, in_, ident.ap), reads=_bl(*R) + ident.b, writes=_bl(*W))


KAPPA = math.exp(-0.5)
SC = 256
NCH = SC // 64


RW_DBG = {}


class _NS:
    pass


def rwkv_phase(kb, C, P, urwT, yrwT, RD):
    import itertools
    dbg = RW_DBG
    ident, bd64 = C["ident"], C["bd64"]
    W = SC
    WH = W + 2
    NI = NCH * 2
    yT, bT = RD["yT"], RD["bT"]
    yT_b = [[[Buf() for _ in range(L // W)] for _ in range(4)] for _ in range(2)]
    bT_b = [[[Buf() for _ in range(L // W)] for _ in range(4)] for _ in range(2)]
    w2t = kb.f32(1024)
    a2t = kb.f32(1024)
    g2a = kb.f32(512)
    g2b = kb.f32(512)
    mNB = [kb.f32(512), kb.f32(512)]
    mAAB = [kb.f32(512), kb.f32(512)]
    I8 = kb.f32(512)
    vecs = kb.f32(20)
    w0t = kb.f32(8)
    a0t = kb.f32(8)
    mu_rkv = kb.f32(24)
    mu_wa = kb.f32(8)
    mu_g = kb.f32(4)
    dma(kb, "sp", w2t.ap[0:64, :], P["rw_w2t"], writes=w2t.b)
    dma(kb, "sp", a2t.ap[0:64, :], P["rw_a2t"], writes=a2t.b)
    dma(kb, "sp", g2a.ap, P["rw_g2"][0:128, :], writes=g2a.b)
    dma(kb, "sp", g2b.ap[0:32, :], P["rw_g2"][128:160, :], writes=g2b.b)
    dma(kb, "sp", mNB[0].ap[0:64, :], P["mNBf"], writes=mNB[0].b)
    dma(kb, "sp", mNB[1].ap[0:64, :], P["mNBb"], writes=mNB[1].b)
    dma(kb, "sp", mAAB[0].ap[0:64, :], P["mAABf"], writes=mAAB[0].b)
    dma(kb, "sp", mAAB[1].ap[0:64, :], P["mAABb"], writes=mAAB[1].b)
    dma(kb, "sp", I8.ap[0:64, :], P["I8"], writes=I8.b)
    rmask = kb.f32(SC)
    dma(kb, "sp", rmask.ap, P["rmask"], writes=rmask.b)
    dma(kb, "sp", vecs.ap, P["rw_vecs"], writes=vecs.b)
    for fc in range(4):
        dma(kb, "sp", w0t.ap[:, fc * 2:fc * 2 + 2], P["rw_w0T"][fc * 128:(fc + 1) * 128, :], writes=w0t.b)
        dma(kb, "sp", a0t.ap[:, fc * 2:fc * 2 + 2], P["rw_a0T"][fc * 128:(fc + 1) * 128, :], writes=a0t.b)
        for kind in range(3):
            o = (kind * 4 + fc) * 2
            r0 = kind * 512 + fc * 128
            dma(kb, "sp", mu_rkv.ap[:, o:o + 2], P["rw_muT"][r0:r0 + 128, :], writes=mu_rkv.b)
    for i in range(4):
        r0 = 1536 + i * 64
        dma(kb, "sp", mu_wa.ap[0:64, i * 2:i * 2 + 2], P["rw_muT"][r0:r0 + 64, :], writes=mu_wa.b)
    dma(kb, "sp", mu_g.ap[:, 0:2], P["rw_muT"][1792:1920, :], writes=mu_g.b)
    dma(kb, "sp", mu_g.ap[0:32, 2:4], P["rw_muT"][1920:1952, :], writes=mu_g.b)

    def vec(i, fc):
        return vecs.ap[:, i * 4 + fc:i * 4 + fc + 1]

    def v3(t, b=64):
        return t.ap.rearrange("p (a b) -> p a b", b=b)

    def z4(t):
        return t.ap.rearrange("p (c h t) -> p c h t", h=2, t=64)

    N3 = lambda t: t.ap.rearrange("p (i t) -> p i t", t=64)

    def alloc_stream():
        T = _NS()
        T.ld = [kb.f32(WH + 6) for _ in range(5)]
        T.tp = [kb.f32(W) for _ in range(23)]
        T.ar = kb.f32(2 * W)
        T.tok = [kb.f32(NCH * 128) for _ in range(4)]
        T.NBt = kb.f32(NCH * 2 * 128)
        T.KBt = kb.f32(NCH * 2 * 128)
        T.AAB = kb.bf16(NI * 64)
        T.N0 = kb.bf16(NI * 64)
        T.Nk = [kb.bf16(NI * 64) for _ in range(2)]
        T.Ak = [kb.bf16(NI * 64) for _ in range(2)]
        T.Pk = [kb.bf16(NI * 64) for _ in range(2)]
        T.Pf = kb.f32(NI * 64)
        T.AKV = kb.f32(NI * 64)
        T.W2 = kb.f32(NI * 64)
        T.Hs = [kb.f32(128), kb.f32(128)]
        T.Usb = kb.f32(128)
        T.gTt = kb.f32(NCH)
        T.ysc = kb.f32(W)
        T.bsc = kb.f32(W)
        T.pad = [kb.f32(NI * 64) for _ in range(5)]
        for zt in T.pad:
            kb.op("pool", lambda e, zt=zt: e.memset(zt.ap, 0.0), writes=zt.b)
        return T

    TS = [alloc_stream(), alloc_stream()]

    def shift(T, u, out, mu0, mu1, np_=128):
        t1, t2 = T.tp[5], T.tp[6]
        mus = [mu_rkv, mu_wa, mu_g]
        _tt(kb, "pool", t1.ap[0:np_, :], u.ap[0:np_, 0:W], u.ap[0:np_, 1:W + 1], ALU.subtract, [u], [t1])
        _stt(kb, out.ap[0:np_, :], t1.ap[0:np_, :], mu0, u.ap[0:np_, 1:W + 1], ALU.mult, ALU.add, [t1, u] + mus, [out])
        _tt(kb, "pool", t2.ap[0:np_, :], u.ap[0:np_, 2:W + 2], u.ap[0:np_, 1:W + 1], ALU.subtract, [u], [t2])
        _stt(kb, out.ap[0:np_, :], t2.ap[0:np_, :], mu1, out.ap[0:np_, :], ALU.mult, ALU.add, [t2, out] + mus, [out])

    def sc_gen(fc, d, T):
        hcur = 0
        kb.op("pool", lambda e: e.memset(T.Hs[0].ap, 0.0), writes=T.Hs[0].b)
        sc_list = list(range(L // W)) if d == 0 else list(range(L // W - 1, -1, -1))
        sc_list = sc_list[:dbg.get('nsc', len(sc_list))]
        (r, k, v, wdx, adx, t1, t2, sg, lr, kraw, rn, kk, kd, bv, cA, cB, ginc, gexc, ginv, gts,
         bt, bh, kh) = T.tp
        tw, tmp = t1, t2
        ar, tok, NBt, KBt, AAB, Nk, Ak, Pk, AKV, W2 = T.ar, T.tok, T.NBt, T.KBt, T.AAB, T.Nk, T.Ak, T.Pk, T.AKV, T.W2
        N0, Pf = T.N0, T.Pf
        btz, ktz, atz, rz, W1Tz = T.pad
        Hs, Usb, gTt, ysc, bsc = T.Hs, T.Usb, T.gTt, T.ysc, T.bsc
        ar4 = ar.ap.rearrange("p (c q t) -> p c q t", q=2, t=64)
        NB4 = NBt.ap.rearrange("p (i q t) -> p i q t", q=2, t=64)
        KB4 = KBt.ap.rearrange("p (i q t) -> p i q t", q=2, t=64)
        tok3 = [t.ap.rearrange("p (c f) -> p c f", f=128) for t in tok]
        for sci, sc in enumerate(sc_list):
            t0 = sc * W
            (ur, uk, uv, uw, ua) = T.ld
            dma(kb, "sp", uw.ap[0:64, 0:WH], urwT[1536 + d * 64:1536 + (d + 1) * 64, t0:t0 + WH], writes=uw.b)
            dma(kb, "sp", ua.ap[0:64, 0:WH], urwT[1664 + d * 64:1664 + (d + 1) * 64, t0:t0 + WH], writes=ua.b)
            dma(kb, "sp", uk.ap[:, 0:WH], urwT[512 + fc * 128:512 + (fc + 1) * 128, t0:t0 + WH], writes=uk.b)
            dma(kb, "sp", ur.ap[:, 0:WH], urwT[fc * 128:(fc + 1) * 128, t0:t0 + WH], writes=ur.b)
            dma(kb, "sp", uv.ap[:, 0:WH], urwT[1024 + fc * 128:1024 + (fc + 1) * 128, t0:t0 + WH], writes=uv.b)

            def mu3(kind):
                o = (kind * 4 + fc) * 2
                return mu_rkv.ap[:, o:o + 1], mu_rkv.ap[:, o + 1:o + 2]

            shift(T, uw, wdx, mu_wa.ap[0:64, d * 2:d * 2 + 1], mu_wa.ap[0:64, d * 2 + 1:d * 2 + 2], 64)
            yield
            shift(T, ua, adx, mu_wa.ap[0:64, 4 + d * 2:5 + d * 2], mu_wa.ap[0:64, 5 + d * 2:6 + d * 2], 64)
            yield
            shift(T, uk, k, *mu3(1))
            yield
            shift(T, ur, r, *mu3(0))
            yield
            shift(T, uv, v, *mu3(2))
            yield
            _act(kb, tw.ap[0:64, :], wdx.ap[0:64, :], AF.Tanh, [wdx], [tw])
            b1 = kb.bank()
            _mm(kb, kb.ps[:, b1, 0:W], w2t.ap[0:64, d * 512 + fc * 128:d * 512 + (fc + 1) * 128], tw.ap[0:64, :],
                [w2t, tw], [kb.psb[b1]])
            _act(kb, sg.ap, kb.ps[:, b1, 0:W], AF.Sigmoid, [kb.psb[b1], w0t], [sg],
                 bias=w0t.ap[:, fc * 2 + d:fc * 2 + d + 1])
            b2 = kb.bank()
            _mm(kb, kb.ps[:, b2, 0:W], a2t.ap[0:64, d * 512 + fc * 128:d * 512 + (fc + 1) * 128], adx.ap[0:64, :],
                [a2t, adx], [kb.psb[b2]])
            _act(kb, lr.ap, kb.ps[:, b2, 0:W], AF.Sigmoid, [kb.psb[b2], a0t], [lr],
                 bias=a0t.ap[:, fc * 2 + d:fc * 2 + d + 1])
            yield
            _act(kb, kraw.ap, k.ap, AF.Square, [k, vecs], [kraw], scale=vec(0, fc))
            b3 = kb.bank()
            _mm(kb, kb.ps[:, b3, 0:W], bd64.ap, kraw.ap, [bd64, kraw], [kb.psb[b3]])
            _act(kb, rn.ap, kb.ps[:, b3, 0:W], AF.Sqrt, [kb.psb[b3]], [rn])
            _ts(kb, "dve", rn.ap, rn.ap, 1e-12, None, ALU.max, None, [rn], [rn])
            kb.op("dve", lambda e: e.reciprocal(out=rn.ap, in_=rn.ap), reads=rn.b, writes=rn.b)
            _stt(kb, kk.ap, k.ap, vec(0, fc), rn.ap, ALU.mult, ALU.mult, [k, vecs, rn], [kk])
            yield
            _ts(kb, "dve", tmp.ap, lr.ap, -1.0, vec(1, fc), ALU.add, ALU.mult, [lr, vecs], [tmp])
            _stt(kb, kd.ap, tmp.ap, 1.0, k.ap, ALU.add, ALU.mult, [tmp, k], [kd])
            _tt(kb, "pool", bv.ap, kk.ap, lr.ap, ALU.mult, [kk, lr], [bv])
            _stt(kb, bsc.ap, r.ap, vec(2, fc), kd.ap, ALU.mult, ALU.mult, [r, kd, vecs], [bsc])
            dma(kb, "pool", bT[d, fc * 128:(fc + 1) * 128, t0:t0 + W], bsc.ap, reads=bsc.b, writes=[bT_b[d][fc][sc]])
            if d == 0:
                kb.op("dve", lambda e: e.tensor_tensor_scan(out=cB.ap, data0=rmask.ap, data1=sg.ap, initial=0.0,
                                                           op0=ALU.mult, op1=ALU.add),
                      reads=_bl(rmask, sg), writes=cB.b)
            else:
                kb.op("dve", lambda e: e.tensor_tensor_scan(out=cA.ap, data0=rmask.ap, data1=sg.ap, initial=0.0,
                                                           op0=ALU.mult, op1=ALU.add),
                      reads=_bl(rmask, sg), writes=cA.b)
                pre3 = v3(cA)
                _tt(kb, "dve", v3(cB), pre3, pre3[:, :, 63:64].to_broadcast([128, NCH, 64]), ALU.subtract, [cA], [cB])
                _tt(kb, "dve", cB.ap, sg.ap, cB.ap, ALU.subtract, [sg, cB], [cB])
            yield
            cs = cB
            cs3 = v3(cs)
            ti = 63 if d == 0 else 0
            totb = cs3[:, :, ti:ti + 1].to_broadcast([128, NCH, 64])
            _act(kb, ginc.ap, cs.ap, AF.Exp, [cs], [ginc], scale=-KAPPA)
            _act(kb, ginv.ap, cs.ap, AF.Exp, [cs], [ginv], scale=KAPPA)
            _tt(kb, "pool", tmp.ap, cs.ap, sg.ap, ALU.subtract, [cs, sg], [tmp])
            _act(kb, gexc.ap, tmp.ap, AF.Exp, [tmp], [gexc], scale=-KAPPA)
            _tt(kb, "dve", v3(cA), cs3, totb, ALU.subtract, [cs], [cA])
            _act(kb, gts.ap, cA.ap, AF.Exp, [cA], [gts], scale=KAPPA)
            _act(kb, gTt.ap, cs3[:, :, ti], AF.Exp, [cs], [gTt], scale=-KAPPA)
            yield
            _stt(kb, ar4[:, :, 0, :], v3(kk), -1.0, v3(gexc), ALU.mult, ALU.mult, [kk, gexc], [ar])
            _tt(kb, "pool", ar4[:, :, 1, :], v3(r), v3(ginc), ALU.mult, [r, ginc], [ar])
            _tt(kb, "pool", bt.ap, bv.ap, ginv.ap, ALU.mult, [bv, ginv], [bt])
            yield
            for h2 in range(2):
                ps_ = slice(h2 * 64, (h2 + 1) * 64)
                _stt(kb, z4(atz)[ps_, :, h2, :], v3(kk)[ps_], -1.0, v3(gexc)[ps_], ALU.mult, ALU.mult, [kk, gexc], [atz])
                _tt(kb, "pool", z4(rz)[ps_, :, h2, :], v3(r)[ps_], v3(ginc)[ps_], ALU.mult, [r, ginc], [rz])
                _tt(kb, "pool", z4(btz)[ps_, :, h2, :], v3(bv)[ps_], v3(ginv)[ps_], ALU.mult, [bv, ginv], [btz])
                _tt(kb, "dve", z4(ktz)[ps_, :, h2, :], v3(kd)[ps_], v3(ginv)[ps_], ALU.mult, [kd, ginv], [ktz])
            yield
            _tt(kb, "pool", bh.ap, bv.ap, gts.ap, ALU.mult, [bv, gts], [bh])
            _tt(kb, "dve", kh.ap, kd.ap, gts.ap, ALU.mult, [kd, gts], [kh])
            yield
            for qi, (srct, fn) in enumerate(((ar, lambda c: ar4[:, c, 0, :]), (bh, lambda c: bh.ap[:, c * 64:(c + 1) * 64]),
                                            (kh, lambda c: kh.ap[:, c * 64:(c + 1) * 64]),
                                            (v, lambda c: v.ap[:, c * 64:(c + 1) * 64]))):
                bk = kb.bank()
                for c in range(NCH):
                    _tr(kb, kb.ps[0:64, bk, c * 128:(c + 1) * 128], fn(c), ident, [srct], [kb.psb[bk]])
                _cp(kb, "act" if qi % 2 else "dve", tok[qi].ap[0:64, :], kb.ps[0:64, bk, :], [kb.psb[bk]], [tok[qi]])
                if qi % 2 == 1:
                    yield
            yield
            for (lt, dstt) in ((btz, NBt), (ktz, KBt)):
                for half in range(2):
                    bk = kb.bank()
                    for ii in range(4):
                        i = half * 4 + ii
                        c, h2 = i // 2, i % 2
                        _mm(kb, kb.ps[0:64, bk, ii * 128:(ii + 1) * 128], z4(lt)[:, c, h2, :],
                            ar.ap[:, c * 128:(c + 1) * 128], [lt, ar], [kb.psb[bk]])
                    _tt(kb, "dve", dstt.ap[0:64, half * 512:(half + 1) * 512], kb.ps[0:64, bk, :],
                        mNB[d].ap[0:64, :], ALU.mult, [kb.psb[bk], mNB[d]], [dstt])
                yield
            bk = kb.bank()
            for i in range(NI):
                c, h2 = i // 2, i % 2
                _mm(kb, kb.ps[0:64, bk, i * 64:(i + 1) * 64], z4(atz)[:, c, h2, :],
                    bt.ap[:, c * 64:(c + 1) * 64], [atz, bt], [kb.psb[bk]])
            _tt(kb, "dve", AAB.ap[0:64, :], kb.ps[0:64, bk, :], mAAB[d].ap[0:64, :], ALU.mult,
                [kb.psb[bk], mAAB[d]], [AAB])
            _tt(kb, "pool", N3(Pk[0])[0:64], NB4[0:64, :, 0, :], N3(I8)[0:64], ALU.add, [NBt, I8], [Pk[0]])
            _cp(kb, "pool", N3(N0)[0:64], NB4[0:64, :, 0, :], [NBt], [N0])
            yield
            curN = lambda i: N3(N0)[0:64, i, :]
            curNt = N0
            curA = AAB
            pc = 0
            for lev in range(5):
                Nn, An = Nk[lev % 2], Ak[lev % 2]
                bA = kb.bank()
                for i in range(NI):
                    _mm(kb, kb.ps[0:64, bA, i * 64:(i + 1) * 64], curN(i), N3(curA)[0:64, i, :], [curNt, curA], [kb.psb[bA]])
                if lev < 4:
                    bN = kb.bank()
                    for i in range(NI):
                        _mm(kb, kb.ps[0:64, bN, i * 64:(i + 1) * 64], N3(curA)[0:64, i, :], curN(i), [curNt, curA], [kb.psb[bN]])
                _cp(kb, "act", An.ap[0:64, :], kb.ps[0:64, bA, :], [kb.psb[bA]], [An])
                if lev < 4:
                    _cp(kb, "dve", Nn.ap[0:64, :], kb.ps[0:64, bN, :], [kb.psb[bN]], [Nn])
                yield
                bP = kb.bank()
                for i in range(NI):
                    _mm(kb, kb.ps[0:64, bP, i * 64:(i + 1) * 64], N3(An)[0:64, i, :], N3(Pk[pc])[0:64, i, :],
                        [An, Pk[pc]], [kb.psb[bP]])
                _tt(kb, "dve", Pk[1 - pc].ap[0:64, :], kb.ps[0:64, bP, :], Pk[pc].ap[0:64, :], ALU.add,
                    [kb.psb[bP], Pk[pc]], [Pk[1 - pc]])
                pc = 1 - pc
                curA = An
                curNt = Nn
                curN = (lambda Nn: (lambda i: N3(Nn)[0:64, i, :]))(Nn)
                yield
            _cp(kb, "dve", Pf.ap[0:64, :], Pk[pc].ap[0:64, :], [Pk[pc]], [Pf])
            Pm = Pf
            bk = kb.bank()
            for i in range(NI):
                c, h2 = i // 2, i % 2
                _mm(kb, kb.ps[0:64, bk, i * 64:(i + 1) * 64], KB4[0:64, i, 0, :],
                    tok3[3][0:64, c, h2 * 64:(h2 + 1) * 64], [KBt, tok[3]], [kb.psb[bk]])
            _cp(kb, "act", AKV.ap[0:64, :], kb.ps[0:64, bk, :], [kb.psb[bk]], [AKV])
            bk2 = kb.bank()
            for i in range(NI):
                c, h2 = i // 2, i % 2
                _mm(kb, kb.ps[:, bk2, i * 64:(i + 1) * 64], tok3[0][0:64, c, :], N3(Pm)[0:64, i, :],
                    [tok[0], Pm], [kb.psb[bk2]])
            ps4 = kb.ps[:, bk2, :].rearrange("p (c h t) -> p c h t", h=2, t=64)
            _cp(kb, "dve", z4(W1Tz)[0:64, :, 0, :], ps4[0:64, :, 0, :], [kb.psb[bk2]], [W1Tz])
            _cp(kb, "dve", z4(W1Tz)[64:128, :, 1, :], ps4[64:128, :, 1, :], [kb.psb[bk2]], [W1Tz])
            yield
            bk = kb.bank()
            for i in range(NI):
                _mm(kb, kb.ps[0:64, bk, i * 64:(i + 1) * 64], N3(Pm)[0:64, i, :], N3(AKV)[0:64, i, :],
                    [Pm, AKV], [kb.psb[bk]])
            _cp(kb, "act", W2.ap[0:64, :], kb.ps[0:64, bk, :], [kb.psb[bk]], [W2])
            W1T3 = N3(W1Tz)
            yield
            corder = list(range(NCH)) if d == 0 else list(range(NCH - 1, -1, -1))
            for c in corder:
                H, Hn = Hs[hcur], Hs[1 - hcur]
                bU = kb.bank()
                for h2 in range(2):
                    _mm(kb, kb.ps[0:64, bU, h2 * 64:(h2 + 1) * 64], W1T3[:, c * 2 + h2, :],
                        H.ap[:, h2 * 64:(h2 + 1) * 64], [W1Tz, H], [kb.psb[bU]])
                _tt(kb, "dve", Usb.ap[0:64, :], kb.ps[0:64, bU, 0:128], W2.ap[0:64, c * 128:(c + 1) * 128], ALU.add,
                    [kb.psb[bU], W2], [Usb])
                yield
                bH = kb.bank()
                _mm(kb, kb.ps[:, bH, 0:128], tok3[2][0:64, c, :], tok3[3][0:64, c, :], [tok[2], tok[3]],
                    [kb.psb[bH]], start=True, stop=False)
                _mm(kb, kb.ps[:, bH, 0:128], tok3[1][0:64, c, :], Usb.ap[0:64, :], [tok[1], Usb],
                    [kb.psb[bH]], start=False, stop=True)
                bY = kb.bank()
                psY3 = kb.ps[:, bY, 0:128].rearrange("p (h t) -> p h t", t=64)
                _mm(kb, psY3, H.ap, z4(rz)[:, c, :, :], [H, rz], [kb.psb[bY]], start=True, stop=False)
                _mm(kb, psY3, Usb.ap[0:64, :], NB4[0:64, c * 2:c * 2 + 2, 1, :],
                    [Usb, NBt], [kb.psb[bY]], start=False, stop=False)
                _mm(kb, psY3, tok3[3][0:64, c, :], KB4[0:64, c * 2:c * 2 + 2, 1, :],
                    [tok[3], KBt], [kb.psb[bY]], start=False, stop=True)
                _stt(kb, Hn.ap, H.ap, gTt.ap[:, c:c + 1], kb.ps[:, bH, 0:128], ALU.mult, ALU.add,
                     [H, gTt, kb.psb[bH]], [Hn])
                _cp(kb, "act", ysc.ap[0:64, c * 64:(c + 1) * 64], kb.ps[0:64, bY, 0:64], [kb.psb[bY]], [ysc])
                _cp(kb, "act", ysc.ap[64:128, c * 64:(c + 1) * 64], kb.ps[64:128, bY, 64:128], [kb.psb[bY]], [ysc])
                hcur = 1 - hcur
                yield
            dma(kb, "pool", yT[d, fc * 128:(fc + 1) * 128, t0:t0 + W], ysc.ap, reads=ysc.b, writes=[yT_b[d][fc][sc]])

    TP = _NS()
    TP.ld = [kb.f32(WH + 6) for _ in range(3)]
    TP.tp = [kb.f32(W) for _ in range(15)]

    def shift_p(u, out, mu0, mu1, np_=128):
        t1, t2 = TP.tp[13], TP.tp[14]
        mus = [mu_rkv, mu_wa, mu_g]
        _tt(kb, "pool", t1.ap[0:np_, :], u.ap[0:np_, 0:W], u.ap[0:np_, 1:W + 1], ALU.subtract, [u], [t1])
        _stt(kb, out.ap[0:np_, :], t1.ap[0:np_, :], mu0, u.ap[0:np_, 1:W + 1], ALU.mult, ALU.add, [t1, u] + mus, [out])
        _tt(kb, "pool", t2.ap[0:np_, :], u.ap[0:np_, 2:W + 2], u.ap[0:np_, 1:W + 1], ALU.subtract, [u], [t2])
        _stt(kb, out.ap[0:np_, :], t2.ap[0:np_, :], mu1, out.ap[0:np_, :], ALU.mult, ALU.add, [t2, out] + mus, [out])

    def post_gen(fc):
        for ti_ in range(L // W if dbg.get('post', True) else 0):
            t0 = ti_ * W
            (uv, ug0, ug1) = TP.ld
            (y, cen, sq_, rs, yn, vv, g0, g1, bvv, ob_, y1, bo0, bo1) = TP.tp[0:13]
            dma(kb, "sp", uv.ap[:, 0:WH], urwT[1024 + fc * 128:1024 + (fc + 1) * 128, t0:t0 + WH], writes=uv.b)
            dma(kb, "sp", ug0.ap[:, 0:WH], urwT[1792:1920, t0:t0 + WH], writes=ug0.b)
            dma(kb, "sp", ug1.ap[0:32, 0:WH], urwT[1920:1952, t0:t0 + WH], writes=ug1.b)
            fsl = slice(fc * 128, (fc + 1) * 128)
            dma(kb, "sp", y.ap, yT[0, fsl, t0:t0 + W], reads=[yT_b[0][fc][ti_]], writes=y.b)
            dma(kb, "sp", y1.ap, yT[1, fsl, t0:t0 + W], reads=[yT_b[1][fc][ti_]], writes=y1.b)
            dma(kb, "sp", bo0.ap, bT[0, fsl, t0:t0 + W], reads=[bT_b[0][fc][ti_]], writes=bo0.b)
            dma(kb, "sp", bo1.ap, bT[1, fsl, t0:t0 + W], reads=[bT_b[1][fc][ti_]], writes=bo1.b)
            o = (2 * 4 + fc) * 2
            shift_p(uv, vv, mu_rkv.ap[:, o:o + 1], mu_rkv.ap[:, o + 1:o + 2])
            shift_p(ug0, g0, mu_g.ap[:, 0:1], mu_g.ap[:, 1:2])
            yield
            shift_p(ug1, g1, mu_g.ap[0:32, 2:3], mu_g.ap[0:32, 3:4], 32)
            _act(kb, g0.ap, g0.ap, AF.Sigmoid, [g0], [g0])
            _act(kb, g1.ap[0:32, :], g1.ap[0:32, :], AF.Sigmoid, [g1], [g1])
            _tt(kb, "pool", y.ap, y.ap, y1.ap, ALU.add, [y, y1], [y])
            _tt(kb, "pool", bo0.ap, bo0.ap, bo1.ap, ALU.add, [bo0, bo1], [bo0])
            yield
            bM = kb.bank()
            _mm(kb, kb.ps[:, bM, 0:W], bd64.ap, y.ap, [bd64, y], [kb.psb[bM]])
            _stt(kb, cen.ap, kb.ps[:, bM, 0:W], -1.0 / 64, y.ap, ALU.mult, ALU.add, [kb.psb[bM], y], [cen])
            _tt(kb, "pool", sq_.ap, cen.ap, cen.ap, ALU.mult, [cen], [sq_])
            yield
            bV = kb.bank()
            _mm(kb, kb.ps[:, bV, 0:W], bd64.ap, sq_.ap, [bd64, sq_], [kb.psb[bV]])
            _act(kb, rs.ap, kb.ps[:, bV, 0:W], AF.Sqrt, [kb.psb[bV]], [rs], scale=1.0 / 64, bias=GN_EPS)
            kb.op("dve", lambda e, rs=rs: e.reciprocal(out=rs.ap, in_=rs.ap), reads=rs.b, writes=rs.b)
            _tt(kb, "pool", yn.ap, cen.ap, rs.ap, ALU.mult, [cen, rs], [yn])
            _ts(kb, "dve", yn.ap, yn.ap, vec(3, fc), vec(4, fc), ALU.mult, ALU.add, [yn, vecs], [yn])
            yield
            bB = kb.bank()
            _mm(kb, kb.ps[:, bB, 0:W], bd64.ap, bo0.ap, [bd64, bo0], [kb.psb[bB]])
            _tt(kb, "dve", bvv.ap, kb.ps[:, bB, 0:W], vv.ap, ALU.mult, [kb.psb[bB], vv], [bvv])
            _tt(kb, "pool", yn.ap, yn.ap, bvv.ap, ALU.add, [yn, bvv], [yn])
            bG = kb.bank()
            _mm(kb, kb.ps[:, bG, 0:W], g2a.ap[:, fc * 128:(fc + 1) * 128], g0.ap, [g2a, g0], [kb.psb[bG]],
                start=True, stop=False)
            _mm(kb, kb.ps[:, bG, 0:W], g2b.ap[0:32, fc * 128:(fc + 1) * 128], g1.ap[0:32, :], [g2b, g1], [kb.psb[bG]],
                start=False, stop=True)
            ob = ob_.ap.bitcast(BF16)[:, 0:W]
            _tt(kb, "dve", ob, kb.ps[:, bG, 0:W], yn.ap, ALU.mult, [kb.psb[bG], yn], [ob_])
            dma(kb, "pool", yrwT[fc * 128:(fc + 1) * 128, t0:t0 + W], ob, reads=ob_.b)
            yield

    nfc = dbg.get('fcs', 4)
    for fc in range(nfc + 1):
        gens = []
        if fc < nfc:
            gens += [sc_gen(fc, d, TS[d]) for d in range(dbg.get('dirs', 2))]
        if fc > 0:
            gens.append(post_gen(fc - 1))
        for _ in itertools.zip_longest(*gens):
            pass


def merge_phase(kb, C, zhyT, yrwT, gT, x1T, x2T, hyo, rwo, wo):
    wh = kb.bf16(4 * D)
    wr = kb.bf16(4 * D)
    wo_ = kb.bf16(8 * D)
    wh3 = wh.ap.rearrange("p (k d) -> p k d", d=D)
    wr3 = wr.ap.rearrange("p (k d) -> p k d", d=D)
    wo3 = wo_.ap.rearrange("p (k d) -> p k d", d=D)
    dma(kb, "sp", wh3, hyo.rearrange("(k p) d -> p k d", p=128), writes=wh.b)
    dma(kb, "sp", wr3, rwo.rearrange("(k p) d -> p k d", p=128), writes=wr.b)
    dma(kb, "sp", wo3, wo.rearrange("(k p) d -> p k d", p=128), writes=wo_.b)
    zt = [kb.bf16(4 * 512) for _ in range(2)]
    yt = [kb.bf16(4 * 512) for _ in range(2)]
    mrg = [kb.bf16(8 * 512, 8) for _ in range(2)]
    gh = [kb.f32(512) for _ in range(2)]
    gr = [kb.f32(512) for _ in range(2)]
    m1 = [kb.f32(512) for _ in range(2)]
    m2 = [kb.f32(512) for _ in range(2)]
    xr = [kb.f32(512) for _ in range(2)]
    xo = [kb.f32(512) for _ in range(2)]
    zh3 = zhyT.rearrange("(k p) t -> p k t", p=128)
    yr3 = yrwT.rearrange("(k p) t -> p k t", p=128)
    cnt = 0
    for t in range(L // 512):
        ts_ = slice(t * 512, (t + 1) * 512)
        z_, y_, mg = zt[t % 2], yt[t % 2], mrg[t % 2]
        z3 = z_.ap.rearrange("p (k t) -> p k t", t=512)
        y3 = y_.ap.rearrange("p (k t) -> p k t", t=512)
        mg3 = mg.ap.rearrange("p (k t) -> p k t", t=512)
        dma(kb, "sp", z3, zh3[:, :, ts_], writes=z_.b)
        dma(kb, "sp", y3, yr3[:, :, ts_], writes=y_.b)
        for dc in range(8):
            i2 = cnt % 2
            cnt += 1
            dsl = slice(dc * 128, (dc + 1) * 128)
            dma(kb, "sp", gh[i2].ap, gT[dc * 128:(dc + 1) * 128, ts_], writes=gh[i2].b)
            dma(kb, "sp", gr[i2].ap, gT[D + dc * 128:D + (dc + 1) * 128, ts_], writes=gr[i2].b)
            bh, br = kb.bank(), kb.bank()
            for kc in range(4):
                _mm(kb, kb.ps[:, bh, :], wh3[:, kc, dsl], z3[:, kc, :], [wh, z_], [kb.psb[bh]], start=(kc == 0), stop=(kc == 3))
            for kc in range(4):
                _mm(kb, kb.ps[:, br, :], wr3[:, kc, dsl], y3[:, kc, :], [wr, y_], [kb.psb[br]], start=(kc == 0), stop=(kc == 3))
            _tt(kb, "dve", m1[i2].ap, kb.ps[:, bh, :], gh[i2].ap, ALU.mult, [kb.psb[bh], gh[i2]], [m1[i2]])
            _tt(kb, "dve", m2[i2].ap, kb.ps[:, br, :], gr[i2].ap, ALU.mult, [kb.psb[br], gr[i2]], [m2[i2]])
            _tt(kb, "pool", mg3[:, dc, :], m1[i2].ap, m2[i2].ap, ALU.add, [m1[i2], m2[i2]], [mg.b[dc]])
        for dc in range(8):
            i2 = cnt % 2
            cnt += 1
            dsl = slice(dc * 128, (dc + 1) * 128)
            dma(kb, "sp", xr[i2].ap, x1T[dc * 128:(dc + 1) * 128, ts_], writes=xr[i2].b)
            bo = kb.bank()
            for kc in range(8):
                _mm(kb, kb.ps[:, bo, :], wo3[:, kc, dsl], mg3[:, kc, :], [wo_, mg.b[kc]], [kb.psb[bo]],
                    start=(kc == 0), stop=(kc == 7))
            _tt(kb, "dve", xo[i2].ap, kb.ps[:, bo, :], xr[i2].ap, ALU.add, [kb.psb[bo], xr[i2]], [xo[i2]])
            dma(kb, "pool", x2T[dc * 128:(dc + 1) * 128, ts_], xo[i2].ap, reads=xo[i2].b)


def _host_consts():
    c = {}
    c["ones"] = np.ones((128, 128), np.float32)
    c["ident"] = np.eye(128, dtype=np.float32)
    bd = np.zeros((128, 128), np.float32)
    bd[0:64, 0:64] = 1.0
    bd[64:128, 64:128] = 1.0
    c["bd64"] = bd
    s = np.arange(64)[:, None]
    t = np.arange(64)[None, :]
    lt, le, gt, ge = (s < t), (s <= t), (s > t), (s >= t)

    def nb(m0, m1):
        m = np.zeros((64, 4, 2, 64), np.float32)
        m[:, :, 0, :] = m0[:, None, :]
        m[:, :, 1, :] = m1[:, None, :]
        return m.reshape(64, 512)

    c["mNBf"] = nb(lt, le)
    c["mNBb"] = nb(gt, ge)
    c["mAABf"] = np.broadcast_to(gt[:, None, :], (64, 8, 64)).astype(np.float32).reshape(64, 512).copy()
    c["mAABb"] = np.broadcast_to(lt[:, None, :], (64, 8, 64)).astype(np.float32).reshape(64, 512).copy()
    c["I8"] = np.broadcast_to(np.eye(64, dtype=np.float32)[:, None, :], (64, 8, 64)).reshape(64, 512).copy()
    rm = np.ones((128, SC), np.float32)
    rm[:, 0::64] = 0.0
    c["rmask"] = rm
    return c


CONST_SHAPES = {"ones": [128, 128], "ident": [128, 128], "bd64": [128, 128], "mNBf": [64, 512], "mNBb": [64, 512],
                "mAABf": [64, 512], "mAABb": [64, 512], "I8": [64, 512], "rmask": [128, 256]}

PARAM_SHAPES = {
    "norms": [128, 32],
    "rw_muT": [NRW, 2], "rw_w0T": [512, 2], "rw_a0T": [512, 2], "rw_w2t": [64, 1024], "rw_a2t": [64, 1024],
    "rw_g2": [160, 512], "rw_vecs": [128, 20],
}

WEIGHTS = {"ffn1_w_gu": [D, 2 * DFF], "ffn1_w_down": [DFF, D], "ffn2_w_gu": [D, 2 * DFF], "ffn2_w_down": [DFF, D],
           "w_in": [D, NIN], "hy_out": [512, D], "rw_out": [512, D], "w_out": [D, D]}


def _host_params(inputs):
    f = lambda k: np.asarray(inputs[k], np.float32)
    p = {}
    p["norms"] = np.ascontiguousarray(np.concatenate(
        [f(n).reshape(8, 128).T for n in ("ffn1_norm", "mix_norm", "ffn2_norm", "final_norm")], axis=1))
    p["rw_muT"] = np.ascontiguousarray(f("rw_mu").T)
    p["rw_w0T"] = np.ascontiguousarray(f("rw_w0").T)
    p["rw_a0T"] = np.ascontiguousarray(f("rw_a0").T)
    p["rw_w2t"] = np.ascontiguousarray(f("rw_w2").transpose(1, 0, 2).reshape(64, 1024))
    p["rw_a2t"] = np.ascontiguousarray(f("rw_a2").transpose(1, 0, 2).reshape(64, 1024))
    p["rw_g2"] = np.ascontiguousarray(f("rw_g2"))
    p["rw_vecs"] = np.ascontiguousarray(np.concatenate(
        [f(n).reshape(4, 128).T for n in ("rw_k_k", "rw_k_a", "rw_r_k", "rw_ln_w", "rw_ln_b")], axis=1))
    return p


def build_program(stage="full"):
    nc = bass.Bass("TRN2", target_bir_lowering=False)
    I = {}

    def inp(name, shape, dt=F32):
        I[name] = nc.dram_tensor(name, list(shape), dt, kind="ExternalInput").ap()
        return I[name]

    def scr(name, shape, dt=F32, ext=None):
        if ext == "in":
            return inp(name, shape, dt)
        if ext == "out":
            return nc.dram_tensor(name, list(shape), dt, kind="ExternalOutput").ap()
        return nc.dram_tensor(name, list(shape), dt).ap()

    full = stage == "full"
    front = stage in ("full", "front")
    do_rw = stage in ("full", "rwkv")
    do_hy = stage in ("full", "hyena")
    for n, shp in CONST_SHAPES.items():
        inp(n, shp)
    inp("norms", PARAM_SHAPES["norms"])
    if do_rw:
        for n, shp in PARAM_SHAPES.items():
            if n != "norms":
                inp(n, shp)
    if do_hy:
        for n, shp in list(HY_CONST_SHAPES.items()) + list(HY_PARAM_SHAPES.items()):
            inp(n, shp)
    if front:
        inp("xT", [D, L])
        for n, shp in WEIGHTS.items():
            if full or n in ("ffn1_w_gu", "ffn1_w_down", "w_in"):
                inp(n, shp)
    back = stage == "back"
    if back:
        for n in ("hy_out", "rw_out", "w_out", "ffn2_w_gu", "ffn2_w_down"):
            inp(n, WEIGHTS[n])
    uhy = scr("uhy", [L + 2, NHY], ext={"front": "out", "hyena": "in"}.get(stage))
    urwT = scr("urwT", [NRW, L + 2], ext={"front": "out", "rwkv": "in"}.get(stage))
    gT = scr("gT", [NG, L], ext={"front": "out", "back": "in"}.get(stage))
    yrwT = scr("yrwT", [512, L], BF16, ext={"rwkv": "out", "back": "in"}.get(stage))
    zhyT = scr("zhyT", [512, L], BF16, ext={"hyena": "out", "back": "in"}.get(stage))
    outT = None
    if full or stage == "back":
        outT = nc.dram_tensor("outT", [D, L], F32, kind="ExternalOutput").ap()
    wb = {}
    for n, shp in WEIGHTS.items():
        if n in I:
            wb[n] = scr(n + "_b", shp, BF16)
    x1T = scr("x1T", [D, L], ext={"front": "out", "back": "in"}.get(stage))
    x2T = scr("x2T", [D, L])
    x3T = scr("x3T", [D, L])
    HD = {}
    if do_hy:
        HD["ucb"] = scr("hy_ucb", [3, 4, 64, 64 * CG])
        HD["Abuf"] = scr("hy_Abuf", [2, 2, 128, 64, CG])
        HD["Bbuf"] = scr("hy_Bbuf", [2, 2, 64, 128, CG])
        HD["Kf"] = scr("hy_Kf", [2, 4, 2, 64, 128 * CG])
        HD["ucb_b"] = [[Buf() for _ in range(4)] for _ in range(3)]
        HD["Kf_b"] = [[[Buf() for _ in range(32)] for _ in range(4)] for _ in range(2)]
        if stage == "hyena":
            HD["kf_dbg"] = scr("kf_dbg", [2, 4, 128, 64 * CG], ext="out")
            HD["z1_dbg"] = scr("z1_dbg", [4, 64, 64 * CG], ext="out")

    with contextlib.ExitStack() as st:
        kb = KB(nc, st)
        C = {}
        for n in ("ones", "ident", "bd64"):
            C[n] = kb.f32(128)
            dma(kb, "sp", C[n].ap, I[n], writes=C[n].b)
        nrm = kb.f32(32)
        for i, n in enumerate(("ffn1_norm", "mix_norm", "ffn2_norm", "final_norm")):
            t = Tile(nrm.ap[:, i * 8:(i + 1) * 8])
            t.b = nrm.b
            C[n] = t
        dma(kb, "sp", nrm.ap, I["norms"], writes=nrm.b)
        kb.persist()

        if front:
            convert_weights(kb, [(I[n], wb[n], WEIGHTS[n][0], WEIGHTS[n][1]) for n in wb])
            kb.phase_end()
            ffn_phase(kb, C, I["xT"], x1T, "ffn1_norm", wb["ffn1_w_gu"], wb["ffn1_w_down"])
            kb.phase_end()
            inproj_phase(kb, C, x1T, wb["w_in"], uhy, urwT, gT)
            kb.phase_end()
        if do_rw:
            rwkv_phase(kb, C, I, urwT, yrwT, {"yT": scr("rw_yT", [2, 512, L]), "bT": scr("rw_bT", [2, 512, L])})
            kb.phase_end()
        if do_hy:
            hyena_phase(kb, C, I, uhy, zhyT, HD)
            kb.phase_end()
        if back:
            convert_weights(kb, [(I[n], wb[n], WEIGHTS[n][0], WEIGHTS[n][1]) for n in wb])
            kb.phase_end()
        if full or back:
            merge_phase(kb, C, zhyT, yrwT, gT, x1T, x2T, wb["hy_out"], wb["rw_out"], wb["w_out"])
            kb.phase_end()
            ffn_phase(kb, C, x2T, x3T, "ffn2_norm", wb["ffn2_w_gu"], wb["ffn2_w_down"])
            kb.phase_end()
            final_norm_phase(kb, C, x3T, outT, "final_norm")
            kb.phase_end()
        kb.S.emit()
    return nc, list(I.keys())


_NC_CACHE = {}


def _get_program(stage):
    if stage not in _NC_CACHE:
        _NC_CACHE[stage] = build_program(stage)
    return _NC_CACHE[stage]


def host_shared(inputs, names):
    shared = dict(_host_consts())
    shared.update(_host_params(inputs))
    shared.update(_hy_consts())
    shared.update(_hy_host_params(inputs))
    for n in WEIGHTS:
        shared[n] = np.ascontiguousarray(inputs[n], np.float32)
    return {k: v for k, v in shared.items() if k in names}


def kernel(**inputs):
    nc, names = _get_program("full")
    shared = host_shared(inputs, names)
    x = np.asarray(inputs["x"], np.float32)
    in_maps = []
    for b in range(8):
        m = dict(shared)
        m["xT"] = np.ascontiguousarray(x[b].T)
        in_maps.append(m)
    res = run_bass_kernel_spmd(nc, in_maps, core_ids=list(range(8)))
    out = np.stack([np.ascontiguousarray(r["outT"].T) for r in res.results], axis=0)
    return out.astype(np.float32)


NFFT = 8192
CG = 128
HY_DBG = {}


def _hy_consts():
    c = {}
    n1 = np.arange(128, dtype=np.float64)[:, None, None]
    n2 = np.arange(64, dtype=np.float64)[None, :, None]
    k1 = np.arange(128, dtype=np.float64)[None, None, :]
    ang = 2 * np.pi * (n1 * k1 / 128.0 + n2 * k1 / NFFT)
    G = np.stack([np.cos(ang), -np.sin(ang)], axis=2)
    c["hy_G"] = G.reshape(128, 64 * 2 * 128).astype(np.float32)
    a2 = 2 * np.pi * np.arange(64)[:, None] * np.arange(64)[None, :] / 64.0
    c["hy_F2"] = np.concatenate([np.cos(a2), np.sin(a2), -np.sin(a2)], axis=1).astype(np.float32)
    k2 = np.arange(64, dtype=np.float64)[:, None, None]
    k1b = np.arange(128, dtype=np.float64)[None, :, None]
    nl = np.arange(64, dtype=np.float64)[None, None, :]
    angp = 2 * np.pi * (k2 * nl / 64.0 + k1b * nl / NFFT)
    Gp = np.stack([np.cos(angp), np.sin(angp), -np.sin(angp)], axis=2)
    c["hy_Gp"] = Gp.reshape(64, 128 * 3 * 64).astype(np.float32)
    a3 = 2 * np.pi * np.arange(128)[:, None] * np.arange(64)[None, :] / 128.0
    wk = np.full((128, 1), 2.0)
    wk[0] = 1.0
    wk[64] = 1.0
    wk[65:] = 0.0
    c["hy_I2"] = (wk * np.concatenate([np.cos(a3), -np.sin(a3)], axis=1) / NFFT).astype(np.float32)
    n = np.arange(NFFT)
    j = np.where(n <= L, n, NFFT - n).astype(np.float64)
    j[L] = 0
    t = j / (L - 1)
    angf = (2.0 * math.pi / L) * j
    bands = np.linspace(1e-4, 15, 16)
    feats = np.concatenate([t[None, :], np.cos(bands[:, None] * angf[None, :]), -np.sin(bands[:, None] * angf[None, :])], axis=0)
    c["hy_feats"] = feats.astype(np.float32)
    c["hy_negt"] = (-t).reshape(128, 64).astype(np.float32)
    return c


HY_CONST_SHAPES = {"hy_G": [128, 64 * 2 * 128], "hy_F2": [64, 192], "hy_Gp": [64, 128 * 3 * 64], "hy_I2": [128, 128],
                   "hy_feats": [33, NFFT], "hy_negt": [128, 64]}
HY_PARAM_SHAPES = {"hy_conv_w": [3, NHY], "hy_conv_b": [1, NHY], "hy_w1": [33, 64], "hy_w2": [64, 64],
                   "hy_w3a": [65, 2048], "hy_b1f": [64, 3], "hy_decay": [4, 512], "hy_bias_d": [2, 512]}


def _hy_host_params(inputs):
    f = lambda k: np.asarray(inputs[k], np.float32)
    p = {}
    p["hy_conv_w"] = np.ascontiguousarray(f("hy_conv_w"))
    p["hy_conv_b"] = np.ascontiguousarray(f("hy_conv_b").reshape(1, NHY))
    p["hy_w1"] = np.ascontiguousarray(f("hy_ffn_w1"))
    p["hy_w2"] = np.ascontiguousarray(f("hy_ffn_w2"))
    p["hy_w3a"] = np.ascontiguousarray(np.concatenate([f("hy_ffn_w3"), f("hy_ffn_b3").reshape(1, 2048)], axis=0))
    p["hy_b1f"] = np.ascontiguousarray(np.stack([f("hy_ffn_b1"), f("hy_ffn_b2"), f("hy_sin_freq")], axis=1))
    p["hy_decay"] = np.ascontiguousarray(f("hy_decay").reshape(4, 512))
    p["hy_bias_d"] = np.ascontiguousarray(f("hy_bias_d"))
    return p


def _sin_act(kb, out_ap, out_tile, pre, scr, np_):
    TWO_PI = 2.0 * math.pi
    MAGIC = 12582912.0
    _ts(kb, "dve", scr.ap[0:np_, :], pre.ap[0:np_, :], 1.0 / TWO_PI, MAGIC, ALU.mult, ALU.add, [pre], [scr])
    _ts(kb, "dve", scr.ap[0:np_, :], scr.ap[0:np_, :], MAGIC, None, ALU.subtract, None, [scr], [scr])
    _stt(kb, scr.ap[0:np_, :], scr.ap[0:np_, :], -TWO_PI, pre.ap[0:np_, :], ALU.mult, ALU.add, [scr, pre], [scr])
    _ts(kb, "dve", scr.ap[0:np_, :], scr.ap[0:np_, :], 3.141592, -3.141592, ALU.min, ALU.max, [scr], [scr])
    _act(kb, out_ap, scr.ap[0:np_, :], AF.Sin, [scr], [out_tile])


def hyena_phase(kb, C, P, uhy, zhyT, D_):
    dbg = HY_DBG
    ident = C["ident"]
    ucb, Abuf_all, Bbuf_all, Kf = D_["ucb"], D_["Abuf"], D_["Bbuf"], D_["Kf"]
    NG4 = dbg.get("groups", 4)
    cw = [kb.f32(NHY) for _ in range(3)]
    cb = kb.f32(NHY)
    for i in range(3):
        dma(kb, "sp", cw[i].ap[0:64, :], P["hy_conv_w"][i:i + 1, :].partition_broadcast(64), writes=cw[i].b)
        dma(kb, "sp", cw[i].ap[64:128, 0:NHY - CG], P["hy_conv_w"][i:i + 1, CG:NHY].partition_broadcast(64), writes=cw[i].b)
    dma(kb, "sp", cb.ap[0:64, :], P["hy_conv_b"].partition_broadcast(64), writes=cb.b)
    dma(kb, "sp", cb.ap[64:128, 0:NHY - CG], P["hy_conv_b"][:, CG:NHY].partition_broadcast(64), writes=cb.b)
    uin = [kb.f32(66 * CG) for _ in range(2)]
    uo = [kb.f32(64 * CG) for _ in range(2)]
    tmpc = kb.f32(64 * CG)
    uhy_b = uhy[0:L, :].rearrange("(p n) c -> p n c", n=64)
    uhy_h = uhy[2:L + 2, :].rearrange("(p n) c -> p n c", n=64)
    it = 0
    for kind in range(3):
        for gp in range(0, NG4, 2):
            npair = min(2, NG4 - gp)
            NP = 64 * npair
            c0 = kind * 512 + gp * CG
            ui, uo_ = uin[it % 2], uo[it % 2]
            it += 1
            u3 = ui.ap.rearrange("p (n c) -> p n c", c=CG)
            o3 = uo_.ap.rearrange("p (n c) -> p n c", c=CG)
            t3 = tmpc.ap.rearrange("p (n c) -> p n c", c=CG)
            for h in range(npair):
                ch = c0 + h * CG
                dma(kb, "sp", u3[h * 64:(h + 1) * 64, 0:64, :], uhy_b[:, :, ch:ch + CG], writes=ui.b)
                dma(kb, "act", u3[h * 64:(h + 1) * 64, 64:66, :], uhy_h[:, 62:64, ch:ch + CG], writes=ui.b)

            def bc(t):
                return t.ap[0:NP, c0:c0 + CG].unsqueeze(1).to_broadcast([NP, 64, CG])

            _tt(kb, "dve", o3[0:NP], u3[0:NP, 0:64, :], bc(cw[0]), ALU.mult, [ui, cw[0]], [uo_])
            _tt(kb, "pool", t3[0:NP], u3[0:NP, 1:65, :], bc(cw[1]), ALU.mult, [ui, cw[1]], [tmpc])
            _tt(kb, "dve", o3[0:NP], o3[0:NP], t3[0:NP], ALU.add, [uo_, tmpc], [uo_])
            _tt(kb, "pool", t3[0:NP], u3[0:NP, 2:66, :], bc(cw[2]), ALU.mult, [ui, cw[2]], [tmpc])
            _tt(kb, "dve", o3[0:NP], o3[0:NP], t3[0:NP], ALU.add, [uo_, tmpc], [uo_])
            _tt(kb, "dve", o3[0:NP], o3[0:NP], bc(cb), ALU.add, [uo_, cb], [uo_])
            for h in range(npair):
                dma(kb, "pool", ucb[kind, gp + h], uo_.ap[h * 64:(h + 1) * 64, :], reads=uo_.b,
                    writes=[D_["ucb_b"][kind][gp + h]])
    kb.phase_end()

    F2t = kb.f32(192)
    I2t = kb.f32(128)
    dma(kb, "sp", F2t.ap[0:64, :], P["hy_F2"], writes=F2t.b)
    dma(kb, "sp", I2t.ap, P["hy_I2"], writes=I2t.b)
    kb.persist()
    NK1 = 65
    KBATCH = [(k0, min(4, NK1 - k0)) for k0 in range(0, NK1, 4)]
    A_bs = [[Buf() for _ in range(16)] for _ in range(2)]
    B_bs = [[Buf() for _ in range(17)] for _ in range(2)]
    Gc = [kb.f32(1024) for _ in range(2)]
    Ast = [[kb.f32(512) for _ in range(2)] for _ in range(2)]
    Ain = [[kb.f32(512) for _ in range(2)] for _ in range(2)]
    cnt = {"g": 0, "a": 0, "i": 0}

    def fft_s1(src3, srcT, K, aset=0):
        Abuf, A_b = Abuf_all[aset], A_bs[aset]
        for n2b in range(16):
            gt = Gc[cnt["g"] % 2]
            cnt["g"] += 1
            dma(kb, "sp", gt.ap[0:K, :], P["hy_G"][0:K, n2b * 1024:(n2b + 1) * 1024], writes=gt.b)
            g4 = gt.ap.rearrange("p (j r k) -> p j r k", r=2, k=128)
            banks = [kb.bank(), kb.bank()]
            for j in range(4):
                for ri in range(2):
                    _mm(kb, kb.ps[0:NK1, banks[ri], j * 128:(j + 1) * 128], g4[0:K, j, ri, 0:NK1], src3[0:K, n2b * 4 + j, :],
                        [gt, srcT], [kb.psb[banks[ri]]])
            for ri in range(2):
                st_ = Ast[ri][cnt["a"] % 2]
                _cp(kb, "act" if ri == 0 else "dve", st_.ap[0:NK1, :], kb.ps[0:NK1, banks[ri], :], [kb.psb[banks[ri]]], [st_])
                dma(kb, "pool", Abuf[ri, 0:NK1, n2b * 4:(n2b + 1) * 4, :], st_.ap[0:NK1, :].rearrange("p (n c) -> p n c", c=CG),
                    reads=st_.b, writes=[A_b[n2b]])
            cnt["a"] += 1

    def fft_s2(k0, nk, aset=0):
        Abuf, A_b = Abuf_all[aset], A_bs[aset]
        tin = []
        nn = nk * CG
        for ri in range(2):
            t_ = Ain[ri][cnt["i"] % 2]
            dma(kb, "sp", t_.ap[0:64, 0:nn].rearrange("p (k c) -> p k c", c=CG),
                Abuf[ri, k0:k0 + nk, :, :].rearrange("k n c -> n k c"), reads=A_b, writes=t_.b)
            tin.append(t_)
        cnt["i"] += 1
        bre, bim = kb.bank(), kb.bank()
        c2, s2, ns2 = F2t.ap[0:64, 0:64], F2t.ap[0:64, 64:128], F2t.ap[0:64, 128:192]
        _mm(kb, kb.ps[0:64, bre, 0:nn], c2, tin[0].ap[0:64, 0:nn], [F2t, tin[0]], [kb.psb[bre]], start=True, stop=False)
        _mm(kb, kb.ps[0:64, bre, 0:nn], s2, tin[1].ap[0:64, 0:nn], [F2t, tin[1]], [kb.psb[bre]], start=False, stop=True)
        _mm(kb, kb.ps[0:64, bim, 0:nn], c2, tin[1].ap[0:64, 0:nn], [F2t, tin[1]], [kb.psb[bim]], start=True, stop=False)
        _mm(kb, kb.ps[0:64, bim, 0:nn], ns2, tin[0].ap[0:64, 0:nn], [F2t, tin[0]], [kb.psb[bim]], start=False, stop=True)
        return bre, bim

    mark = kb.off
    w1t = kb.f32(64)
    w2t = kb.f32(64)
    b1f = kb.f32(3)
    w3a = kb.f32(2048)
    negt = kb.f32(64)
    h2f = kb.f32(NFFT)
    h2b = kb.f32(NFFT)
    dma(kb, "sp", w1t.ap[0:33, :], P["hy_w1"], writes=w1t.b)
    dma(kb, "sp", w2t.ap[0:64, :], P["hy_w2"], writes=w2t.b)
    dma(kb, "sp", b1f.ap[0:64, :], P["hy_b1f"], writes=b1f.b)
    dma(kb, "sp", w3a.ap[0:65, :], P["hy_w3a"], writes=w3a.b)
    dma(kb, "sp", negt.ap, P["hy_negt"], writes=negt.b)
    kb.op("pool", lambda e: e.memset(h2f.ap, 0.0), writes=h2f.b)
    kb.op("pool", lambda e: e.memset(h2b.ap, 0.0), writes=h2b.b)
    kb.op("pool", lambda e: e.memset(h2f.ap[64:65, 0:L], 1.0), reads=h2f.b, writes=h2f.b)
    kb.op("pool", lambda e: e.memset(h2b.ap[64:65, L + 1:NFFT], 1.0), reads=h2b.b, writes=h2b.b)
    fch = [kb.f32(512) for _ in range(2)]
    pre = kb.f32(512)
    scr = kb.f32(512)
    h1c = kb.f32(512)
    for pc in range(16):
        ft = fch[pc % 2]
        dma(kb, "sp", ft.ap[0:33, :], P["hy_feats"][:, pc * 512:(pc + 1) * 512], writes=ft.b)
        b1 = kb.bank()
        _mm(kb, kb.ps[0:64, b1, :], w1t.ap[0:33, :], ft.ap[0:33, :], [w1t, ft], [kb.psb[b1]])
        _ts(kb, "dve", pre.ap[0:64, :], kb.ps[0:64, b1, :], b1f.ap[0:64, 0:1], b1f.ap[0:64, 2:3], ALU.add, ALU.mult,
            [kb.psb[b1], b1f], [pre])
        _sin_act(kb, h1c.ap[0:64, :], h1c, pre, scr, 64)
        b2 = kb.bank()
        _mm(kb, kb.ps[0:64, b2, :], w2t.ap[0:64, :], h1c.ap[0:64, :], [w2t, h1c], [kb.psb[b2]])
        _ts(kb, "dve", pre.ap[0:64, :], kb.ps[0:64, b2, :], b1f.ap[0:64, 1:2], b1f.ap[0:64, 2:3], ALU.add, ALU.mult,
            [kb.psb[b2], b1f], [pre])
        hdst = h2f if pc < 8 else h2b
        TWO_PI = 2.0 * math.pi
        MAGIC = 12582912.0
        _ts(kb, "dve", scr.ap[0:64, :], pre.ap[0:64, :], 1.0 / TWO_PI, MAGIC, ALU.mult, ALU.add, [pre], [scr])
        _ts(kb, "dve", scr.ap[0:64, :], scr.ap[0:64, :], MAGIC, None, ALU.subtract, None, [scr], [scr])
        _stt(kb, scr.ap[0:64, :], scr.ap[0:64, :], -TWO_PI, pre.ap[0:64, :], ALU.mult, ALU.add, [scr, pre], [scr])
        _ts(kb, "dve", scr.ap[0:64, :], scr.ap[0:64, :], 3.141592, -3.141592, ALU.min, ALU.max, [scr], [scr])
        _act(kb, hdst.ap[0:64, pc * 512:(pc + 1) * 512], scr.ap[0:64, :], AF.Sin, [scr], [hdst])
    kb.op("pool", lambda e: e.memset(h2b.ap[0:64, L:L + 1], 0.0), reads=h2b.b, writes=h2b.b)
    kfs = [kb.f32(64 * CG), kb.f32(64 * CG)]
    absd = kb.f32(CG)
    et = [kb.f32(512) for _ in range(2)]
    sqc = [kb.f32(512) for _ in range(2)]
    rnf = kb.f32(CG)
    kst = [[kb.f32(512) for _ in range(2)] for _ in range(2)]
    h2f3 = h2f.ap.rearrange("p (a b) -> p b a", b=64)
    h2b3 = h2b.ap.rearrange("p (a b) -> p b a", b=64)

    def gen_filter(o, g, kf):
        kf3 = kf.ap.rearrange("p (n c) -> p n c", c=CG)
        for dr in range(2):
            dma(kb, "sp", absd.ap[dr * 64:(dr + 1) * 64, :],
                P["hy_decay"][dr * 2 + o:dr * 2 + o + 1, g * CG:(g + 1) * CG].partition_broadcast(64), writes=absd.b)
        _stt(kb, absd.ap, absd.ap, -1.0, absd.ap, ALU.mult, ALU.max, [absd], [absd])
        bss = kb.bank()
        kb.reserved.add(bss)
        for n2b in range(16):
            bk = kb.bank()
            e_ = et[n2b % 2]
            sq_ = sqc[n2b % 2]
            for j in range(4):
                n2 = n2b * 4 + j
                cf = o * 512 + g * CG
                _mm(kb, kb.ps[:, bk, j * 128:(j + 1) * 128], h2f3[0:65, n2, :], w3a.ap[0:65, cf:cf + CG],
                    [h2f, w3a], [kb.psb[bk]], start=True, stop=False)
                _mm(kb, kb.ps[:, bk, j * 128:(j + 1) * 128], h2b3[0:65, n2, :], w3a.ap[0:65, 1024 + cf:1024 + cf + CG],
                    [h2b, w3a], [kb.psb[bk]], start=False, stop=True)
                _act(kb, e_.ap[:, j * 128:(j + 1) * 128], absd.ap, AF.Exp, [absd, negt], [e_], scale=negt.ap[:, n2:n2 + 1])
            _tt(kb, "dve", kf.ap[:, n2b * 512:(n2b + 1) * 512], kb.ps[:, bk, :], e_.ap, ALU.mult, [kb.psb[bk], e_], [kf])
            _tt(kb, "pool", sq_.ap, kf.ap[:, n2b * 512:(n2b + 1) * 512], kf.ap[:, n2b * 512:(n2b + 1) * 512], ALU.mult,
                [kf], [sq_])
            for j in range(4):
                _mm(kb, kb.ps[:, bss, 0:CG], C["ones"].ap, sq_.ap[:, j * 128:(j + 1) * 128], [C["ones"], sq_],
                    [kb.psb[bss]], start=(n2b == 0 and j == 0), stop=(n2b == 15 and j == 3))
        kb.reserved.discard(bss)
        _act(kb, rnf.ap, kb.ps[:, bss, 0:CG], AF.Sqrt, [kb.psb[bss]], [rnf], bias=1e-6)
        kb.op("dve", lambda e: e.reciprocal(out=rnf.ap, in_=rnf.ap), reads=rnf.b, writes=rnf.b)
        _tt(kb, "dve", kf3, kf3, rnf.ap.unsqueeze(1).to_broadcast([128, 64, CG]), ALU.mult, [kf, rnf], [kf])
        if "kf_dbg" in D_:
            dma(kb, "pool", D_["kf_dbg"][o, g], kf.ap, reads=kf.b)

    def fft_filter(o, g, kf, aset):
        kf3 = kf.ap.rearrange("p (n c) -> p n c", c=CG)
        fft_s1(kf3, kf, 128, aset)
        for k1b, (k0, nk) in enumerate(KBATCH):
            nn = nk * CG
            bre, bim = fft_s2(k0, nk, aset)
            for ri, bk_ in enumerate((bre, bim)):
                st_ = kst[ri][k1b % 2]
                _cp(kb, "act" if ri == 0 else "dve", st_.ap[0:64, 0:nn], kb.ps[0:64, bk_, 0:nn], [kb.psb[bk_]], [st_])
                dma(kb, "pool", Kf[o, g, ri, :, k0 * CG:k0 * CG + nn], st_.ap[0:64, 0:nn], reads=st_.b,
                    writes=[D_["Kf_b"][o][g][k1b]])

    items = [(o, g) for o in range(2) for g in range(NG4)]
    gen_filter(items[0][0], items[0][1], kfs[0])
    for i, (o, g) in enumerate(items):
        if i + 1 < len(items):
            gen_filter(items[i + 1][0], items[i + 1][1], kfs[(i + 1) % 2])
        fft_filter(o, g, kfs[i % 2], i % 2)
    kb.S.barrier()
    kb.off = mark

    import itertools

    def alloc_c():
        T = _NS()
        T.zbuf = kb.f32(64 * CG)
        T.Kt = [kb.f32(512) for _ in range(2)]
        T.Gp = kb.f32(768)
        T.xr, T.xi = kb.f32(512), kb.f32(512)
        T.tt = [kb.f32(512) for _ in range(4)]
        T.Yt = [kb.f32(512) for _ in range(2)]
        T.Bst = [kb.f32(512) for _ in range(2)]
        T.Bin = [kb.f32(512) for _ in range(2)]
        T.xg, T.sk, T.te, T.zo, T.dbc = [kb.f32(512) for _ in range(5)]
        T.zT = kb.bf16(L)
        return T

    def stage_c(g, T, aset):
        Bbuf, B_b = Bbuf_all[aset], B_bs[aset]
        zbuf = T.zbuf
        z3 = zbuf.ap.rearrange("p (n c) -> p n c", c=CG)
        zT3 = T.zT.ap.rearrange("c (p n) -> c p n", n=64)
        dma(kb, "sp", zbuf.ap[0:64, :], ucb[2, g], reads=[D_["ucb_b"][2][g]], writes=zbuf.b)
        for o in range(dbg.get("orders", 2)):
            for rep in range(4):
                dma(kb, "sp", T.dbc.ap[0:64, rep * CG:(rep + 1) * CG],
                    P["hy_bias_d"][o:o + 1, g * CG:(g + 1) * CG].partition_broadcast(64), writes=T.dbc.b)
            fft_s1(z3, zbuf, 64, aset)
            yield
            for k1b, (k0, nk) in enumerate(KBATCH):
                nn = nk * CG
                kt = T.Kt
                for ri in range(2):
                    dma(kb, "sp", kt[ri].ap[0:64, 0:nn], Kf[o, g, ri, :, k0 * CG:k0 * CG + nn],
                        reads=[D_["Kf_b"][o][g][k1b]], writes=kt[ri].b)
                gp = T.Gp
                dma(kb, "sp", gp.ap[0:64, 0:nk * 192], P["hy_Gp"][:, k0 * 192:(k0 + nk) * 192], writes=gp.b)
                gp4 = gp.ap.rearrange("p (j q n) -> p j q n", q=3, n=64)
                bre, bim = fft_s2(k0, nk, aset)
                xr_, xi_ = T.xr, T.xi
                _cp(kb, "act", xr_.ap[0:64, 0:nn], kb.ps[0:64, bre, 0:nn], [kb.psb[bre]], [xr_])
                _cp(kb, "act", xi_.ap[0:64, 0:nn], kb.ps[0:64, bim, 0:nn], [kb.psb[bim]], [xi_])
                yre, yim = T.Yt
                ta, tb_, tc_, td = T.tt
                _tt(kb, "dve", ta.ap[0:64, 0:nn], xr_.ap[0:64, 0:nn], kt[0].ap[0:64, 0:nn], ALU.mult, [xr_, kt[0]], [ta])
                _tt(kb, "dve", tb_.ap[0:64, 0:nn], xi_.ap[0:64, 0:nn], kt[1].ap[0:64, 0:nn], ALU.mult, [xi_, kt[1]], [tb_])
                _tt(kb, "dve", yre.ap[0:64, 0:nn], ta.ap[0:64, 0:nn], tb_.ap[0:64, 0:nn], ALU.subtract, [ta, tb_], [yre])
                _tt(kb, "pool", tc_.ap[0:64, 0:nn], xr_.ap[0:64, 0:nn], kt[1].ap[0:64, 0:nn], ALU.mult, [xr_, kt[1]], [tc_])
                _tt(kb, "pool", td.ap[0:64, 0:nn], xi_.ap[0:64, 0:nn], kt[0].ap[0:64, 0:nn], ALU.mult, [xi_, kt[0]], [td])
                _tt(kb, "pool", yim.ap[0:64, 0:nn], tc_.ap[0:64, 0:nn], td.ap[0:64, 0:nn], ALU.add, [tc_, td], [yim])
                yield
                cre, cim = kb.bank(), kb.bank()
                for j in range(nk):
                    cs_ = slice(j * 128, (j + 1) * 128)
                    _mm(kb, kb.ps[0:64, cre, cs_], gp4[0:64, j, 0, :], yre.ap[0:64, cs_], [gp, yre], [kb.psb[cre]],
                        start=True, stop=False)
                    _mm(kb, kb.ps[0:64, cre, cs_], gp4[0:64, j, 2, :], yim.ap[0:64, cs_], [gp, yim], [kb.psb[cre]],
                        start=False, stop=True)
                    _mm(kb, kb.ps[0:64, cim, cs_], gp4[0:64, j, 1, :], yre.ap[0:64, cs_], [gp, yre], [kb.psb[cim]],
                        start=True, stop=False)
                    _mm(kb, kb.ps[0:64, cim, cs_], gp4[0:64, j, 0, :], yim.ap[0:64, cs_], [gp, yim], [kb.psb[cim]],
                        start=False, stop=True)
                for ri, bk_ in enumerate((cre, cim)):
                    st_ = T.Bst[ri]
                    _cp(kb, "act" if ri == 0 else "dve", st_.ap[0:64, 0:nn], kb.ps[0:64, bk_, 0:nn], [kb.psb[bk_]], [st_])
                    dma(kb, "pool", Bbuf[ri, :, k0:k0 + nk, :], st_.ap[0:64, 0:nn].rearrange("p (k c) -> p k c", c=CG),
                        reads=st_.b, writes=[B_b[k1b]])
                yield
            for nb in range(16):
                bin_ = T.Bin
                for ri in range(2):
                    dma(kb, "sp", bin_[ri].ap[0:NK1, :].rearrange("p (n c) -> p n c", c=CG),
                        Bbuf[ri, nb * 4:(nb + 1) * 4, 0:NK1, :].rearrange("n k c -> k n c"), reads=B_b, writes=bin_[ri].b)
                xg_ = T.xg
                dma(kb, "sp", xg_.ap[0:64, :], ucb[o, g, :, nb * 512:(nb + 1) * 512], reads=[D_["ucb_b"][o][g]], writes=xg_.b)
                by = kb.bank()
                _mm(kb, kb.ps[0:64, by, :], I2t.ap[0:NK1, 0:64], bin_[0].ap[0:NK1, :], [I2t, bin_[0]], [kb.psb[by]], start=True, stop=False)
                _mm(kb, kb.ps[0:64, by, :], I2t.ap[0:NK1, 64:128], bin_[1].ap[0:NK1, :], [I2t, bin_[1]], [kb.psb[by]], start=False, stop=True)
                te = T.te
                if o == 0:
                    sk_ = T.sk
                    dma(kb, "sp", sk_.ap[0:64, :], ucb[2, g, :, nb * 512:(nb + 1) * 512], reads=[D_["ucb_b"][2][g]], writes=sk_.b)
                    _tt(kb, "pool", te.ap[0:64, :], sk_.ap[0:64, :], T.dbc.ap[0:64, :], ALU.mult, [sk_, T.dbc], [te])
                else:
                    _tt(kb, "pool", te.ap[0:64, :], zbuf.ap[0:64, nb * 512:(nb + 1) * 512], T.dbc.ap[0:64, :], ALU.mult,
                        [zbuf, T.dbc], [te])
                _tt(kb, "dve", te.ap[0:64, :], kb.ps[0:64, by, :], te.ap[0:64, :], ALU.add, [kb.psb[by], te], [te])
                if o == 0:
                    _tt(kb, "pool", zbuf.ap[0:64, nb * 512:(nb + 1) * 512], xg_.ap[0:64, :], te.ap[0:64, :], ALU.mult,
                        [xg_, te], [zbuf])
                else:
                    zo_ = T.zo
                    _tt(kb, "pool", zo_.ap[0:64, :], xg_.ap[0:64, :], te.ap[0:64, :], ALU.mult, [xg_, te], [zo_])
                    bt_ = kb.bank()
                    for j in range(4):
                        _tr(kb, kb.ps[:, bt_, j * 64:(j + 1) * 64], zo_.ap[0:64, j * 128:(j + 1) * 128],
                            _IdentView(ident), [zo_], [kb.psb[bt_]])
                    _cp(kb, "act", zT3[:, :, nb * 4:(nb + 1) * 4],
                        kb.ps[:, bt_, 0:256].rearrange("c (j p) -> c p j", p=64), [kb.psb[bt_]], [T.zT])
                yield
            if o == 0 and "z1_dbg" in D_:
                dma(kb, "pool", D_["z1_dbg"][g], zbuf.ap[0:64, :], reads=zbuf.b)
            if o == 1:
                dma(kb, "pool", zhyT[g * CG:(g + 1) * CG, :], T.zT.ap, reads=T.zT.b)

    TC = [alloc_c(), alloc_c()]
    for g0 in range(0, NG4, 2):
        gens = [stage_c(g0 + i, TC[i], i) for i in range(min(2, NG4 - g0))]
        for _ in itertools.zip_longest(*gens):
            pass


class _IdentView:
    def __init__(self, ident):
        self.ap = ident.ap[0:64, 0:64]
        self.b = ident.b
```

```python
import contextlib
import math
import numpy as np
import concourse.bass as bass
import concourse.mybir as mybir
from concourse.bass_utils import run_bass_kernel_spmd

F32 = mybir.dt.float32
BF16 = mybir.dt.bfloat16
AF = mybir.ActivationFunctionType
ALU = mybir.AluOpType
AX = mybir.AxisListType

D = 1024
L = 4096
DFF = 2816
NHY = 1536
NRW = 1952
NG = 2048
NIN = NHY + NRW + NG
RMS_EPS = 1e-6
GN_EPS = 64e-5

SEM_EPOCH = 30000
N_DMA_SLOTS = 12


class Buf:
    __slots__ = ("lw", "rd")

    def __init__(self):
        self.lw = None
        self.rd = []


class Op:
    __slots__ = ("eng", "fn", "deps", "dma", "needed", "tok", "slot", "noinst")


class Sched:
    ENGS = ("pe", "act", "dve", "pool", "sp")

    def __init__(self, nc):
        self.nc = nc
        self.streams = {e: [] for e in self.ENGS}
        self.dma_count = {e: 0 for e in self.ENGS}
        self.dma_slot_last = {}

    def op(self, eng, fn, reads=(), writes=(), dma=False, extra=(), noinst=False):
        o = Op()
        o.noinst = noinst
        o.eng = eng
        o.fn = fn
        o.dma = dma
        o.needed = False
        deps = {}
        for b in reads:
            if b.lw is not None:
                deps[id(b.lw)] = b.lw
        for b in writes:
            if b.lw is not None:
                deps[id(b.lw)] = b.lw
            for r in b.rd:
                deps[id(r)] = r
        for d in extra:
            deps[id(d)] = d
        if dma:
            k = self.dma_count[eng]
            self.dma_count[eng] += 1
            o.slot = (eng, k % N_DMA_SLOTS, k // N_DMA_SLOTS)
            prev = self.dma_slot_last.get((eng, k % N_DMA_SLOTS))
            if prev is not None:
                deps[id(prev)] = prev
            self.dma_slot_last[(eng, k % N_DMA_SLOTS)] = o
        dl = []
        for d in deps.values():
            if d is o:
                continue
            if (not d.dma) and d.eng == eng and eng == "pe" and not o.dma:
                continue
            assert not d.noinst
            d.needed = True
            dl.append(d)
        o.deps = dl
        for b in reads:
            b.rd.append(o)
        for b in writes:
            b.lw = o
            b.rd = []
        self.streams[eng].append(o)
        return o

    def barrier(self):
        lasts = list(self.dma_slot_last.values())
        for s in self.streams.values():
            for o in reversed(s):
                if not o.noinst:
                    lasts.append(o)
                    break
        for e in self.ENGS:
            self.op(e, lambda eh: None, extra=lasts, noinst=True)

    def emit(self):
        nc = self.nc
        with contextlib.ExitStack() as st:
            esem = {}
            for e in self.ENGS:
                n_sig = sum(1 for o in self.streams[e] if o.needed and not o.dma)
                n_ep = max(1, (n_sig + SEM_EPOCH - 1) // SEM_EPOCH)
                esem[e] = [st.enter_context(nc.semaphore(f"s_{e}_{i}")) for i in range(n_ep)]
            dsem = {}
            for e in self.ENGS:
                if self.dma_count[e] > 0:
                    dsem[e] = [st.enter_context(nc.semaphore(f"d_{e}_{i}")) for i in range(N_DMA_SLOTS)]
            for e in self.ENGS:
                c = 0
                for o in self.streams[e]:
                    if o.dma:
                        _, s, r = o.slot
                        o.tok = (dsem[e][s], 16 * (r + 1), ("d", e, s))
                    elif o.needed:
                        ep = c // SEM_EPOCH
                        o.tok = (esem[e][ep], (c % SEM_EPOCH) + 1, ("e", e, ep))
                        c += 1
                    else:
                        o.tok = None
            block = st.enter_context(nc.Block())
            hmap = {"pe": block.tensor, "act": block.scalar, "dve": block.vector,
                    "pool": block.gpsimd, "sp": block.sync}
            for e in self.ENGS:
                stream = self.streams[e]
                if not stream:
                    continue

                def section(eh, stream=stream):
                    waited = {}
                    for o in stream:
                        for d in o.deps:
                            sem, val, key = d.tok
                            if waited.get(key, 0) >= val:
                                continue
                            eh.wait_ge(sem, val)
                            waited[key] = val
                        ins = o.fn(eh)
                        if ins is None:
                            continue
                        if o.dma:
                            ins.then_inc(o.tok[0], 16)
                        elif o.needed:
                            ins.then_inc(o.tok[0], 1)

                hmap[e](section)


class Tile:
    def __init__(self, ap, n=1):
        self.ap = ap
        self.b = [Buf() for _ in range(n)]


class KB:
    def __init__(self, nc, st):
        self.nc = nc
        self.S = Sched(nc)
        self.arena = st.enter_context(nc.sbuf_tensor("arena", [128, 49152], F32))
        self.ps = st.enter_context(nc.psum_tensor("psum", [128, 8, 512], F32))
        self.psb = [Buf() for _ in range(8)]
        self.ps_i = 0
        self.reserved = set()
        self.base = 0
        self.off = 0
        self.rr = 0

    def f32(self, n, nb=1):
        assert self.off + n <= 49152, ("sbuf overflow", self.off, n)
        ap = self.arena[:, self.off:self.off + n]
        self.off += n
        return Tile(ap, nb)

    def bf16(self, n, nb=1):
        w = (n + 1) // 2
        assert self.off + w <= 49152, ("sbuf overflow", self.off, w)
        ap = self.arena[:, self.off:self.off + w].bitcast(BF16)[:, 0:n]
        self.off += w
        return Tile(ap, nb)

    def persist(self):
        self.base = self.off

    def phase_end(self):
        self.S.barrier()
        self.off = self.base

    def bank(self):
        while self.ps_i in self.reserved:
            self.ps_i = (self.ps_i + 1) % 8
        i = self.ps_i
        self.ps_i = (self.ps_i + 1) % 8
        return i

    def op(self, eng, fn, reads=(), writes=(), dma=False):
        return self.S.op(eng, fn, reads=reads, writes=writes, dma=dma)

    def ew_eng(self):
        self.rr += 1
        return "dve" if (self.rr % 3) else "pool"


def dma(kb, eng, out, in_, reads=(), writes=(), slow=False):
    if slow:
        return kb.op(eng, lambda e: e.dma_start(out=out, in_=in_, allow_slow_non_contiguous=True),
                     reads=reads, writes=writes, dma=True)
    return kb.op(eng, lambda e: e.dma_start(out=out, in_=in_), reads=reads, writes=writes, dma=True)


def convert_weights(kb, pairs):
    CW = 2816
    stg = [kb.f32(CW) for _ in range(3)]
    stb = [kb.bf16(CW) for _ in range(3)]
    i = 0
    engs = ("dve", "act", "dve", "act", "pool")
    for src, dst, R, C in pairs:
        for r0 in range(0, R, 128):
            rr = min(128, R - r0)
            for c0 in range(0, C, CW):
                cc = min(CW, C - c0)
                a, b = stg[i % 3], stb[i % 3]
                dma(kb, "sp", a.ap[0:rr, 0:cc], src[r0:r0 + rr, c0:c0 + cc], writes=a.b)
                eng = engs[i % 5]
                if eng == "act":
                    kb.op("act", lambda e, a=a, b=b, rr=rr, cc=cc: e.copy(out=b.ap[0:rr, 0:cc], in_=a.ap[0:rr, 0:cc]),
                          reads=a.b, writes=b.b)
                else:
                    kb.op(eng, lambda e, a=a, b=b, rr=rr, cc=cc: e.tensor_copy(out=b.ap[0:rr, 0:cc], in_=a.ap[0:rr, 0:cc]),
                          reads=a.b, writes=b.b)
                dma(kb, "pool" if i % 2 else "act", dst[r0:r0 + rr, c0:c0 + cc], b.ap[0:rr, 0:cc], reads=b.b)
                i += 1


def rmsnorm_T(kb, C, xt, g, out_ap_fn, sq, rstd):
    x3 = xt.ap
    sq3 = sq.ap.rearrange("p (a b) -> p a b", b=512)
    bk = kb.bank()
    for kc in range(8):
        kb.op("act", lambda e, kc=kc: e.activation(out=sq3[:, kc, :], in_=x3[:, kc, :], func=AF.Square),
              reads=xt.b, writes=[sq.b[kc]])
    for kc in range(8):
        kb.op("pe", lambda e, kc=kc: e.matmul(kb.ps[:, bk, :], lhsT=C["ones"].ap, rhs=sq3[:, kc, :],
                                              start=(kc == 0), stop=(kc == 7)),
              reads=[sq.b[kc]] + C["ones"].b, writes=[kb.psb[bk]])
    kb.op("act", lambda e: e.activation(out=rstd.ap, in_=kb.ps[:, bk, :], func=AF.Sqrt, scale=1.0 / D, bias=RMS_EPS),
          reads=[kb.psb[bk]], writes=rstd.b)
    kb.op("dve", lambda e: e.reciprocal(out=rstd.ap, in_=rstd.ap), reads=rstd.b, writes=rstd.b)
    outs = []
    for kc in range(8):
        o = out_ap_fn(kc)
        outs.append(o)
    return outs


def ffn_phase(kb, C, xin, xout, gname, wgu, wdn):
    TQ = 1024
    xn = kb.bf16(8 * TQ, 8)
    G = kb.bf16(22 * TQ, 22)
    xn3 = xn.ap.rearrange("p (a b) -> p a b", b=TQ)
    G3 = G.ap.rearrange("p (a b) -> p a b", b=TQ)
    xts = [kb.f32(8 * 512) for _ in range(2)]
    sq = kb.f32(8 * 512, 8)
    rstd = kb.f32(512)
    wg = [kb.bf16(8 * 256) for _ in range(2)]
    wu = [kb.bf16(8 * 256) for _ in range(2)]
    wd = [kb.bf16(22 * 512) for _ in range(2)]
    sg = [kb.f32(512) for _ in range(2)]
    xr = [kb.f32(512) for _ in range(2)]
    xo = [kb.f32(512) for _ in range(2)]
    g = C[gname]
    xin3 = xin.rearrange("(kc p) t -> p kc t", p=128)
    wgu3 = wgu.rearrange("(kc p) f -> p kc f", p=128)
    wdn3 = wdn.rearrange("(fc p) d -> p fc d", p=128)
    cnt = 0
    for q in range(L // TQ):
        for t in range(2):
            tok = q * TQ + t * 512
            xt = xts[t]
            x3 = xt.ap.rearrange("p (a b) -> p a b", b=512)
            xt3 = Tile(x3)
            xt3.b = xt.b
            dma(kb, "sp", x3, xin3[:, :, tok:tok + 512], writes=xt.b)
            rmsnorm_T(kb, C, xt3, g, lambda kc: None, sq, rstd)
            for kc in range(8):
                eng = "dve"
                kb.op(eng, lambda e, kc=kc, x3=x3, t=t: e.scalar_tensor_tensor(
                    out=xn3[:, kc, t * 512:(t + 1) * 512], in0=x3[:, kc, :], scalar=g.ap[:, kc:kc + 1],
                    in1=rstd.ap, op0=ALU.mult, op1=ALU.mult),
                    reads=xt.b + rstd.b + g.b, writes=[xn.b[kc]])
        for s in range(11):
            a, b = wg[s % 2], wu[s % 2]
            a3 = a.ap.rearrange("p (a b) -> p a b", b=256)
            b3 = b.ap.rearrange("p (a b) -> p a b", b=256)
            dma(kb, "sp", a3, wgu3[:, :, s * 256:(s + 1) * 256], writes=a.b)
            dma(kb, "sp", b3, wgu3[:, :, DFF + s * 256:DFF + (s + 1) * 256], writes=b.b)
            for fcl in range(2):
                fc = s * 2 + fcl
                for t in range(2):
                    bg = kb.bank()
                    bu = kb.bank()
                    for kc in range(8):
                        kb.op("pe", lambda e, kc=kc, a3=a3, fcl=fcl, t=t, bg=bg: e.matmul(
                            kb.ps[:, bg, :], lhsT=a3[:, kc, fcl * 128:(fcl + 1) * 128],
                            rhs=xn3[:, kc, t * 512:(t + 1) * 512], start=(kc == 0), stop=(kc == 7)),
                            reads=a.b + [xn.b[kc]], writes=[kb.psb[bg]])
                    for kc in range(8):
                        kb.op("pe", lambda e, kc=kc, b3=b3, fcl=fcl, t=t, bu=bu: e.matmul(
                            kb.ps[:, bu, :], lhsT=b3[:, kc, fcl * 128:(fcl + 1) * 128],
                            rhs=xn3[:, kc, t * 512:(t + 1) * 512], start=(kc == 0), stop=(kc == 7)),
                            reads=b.b + [xn.b[kc]], writes=[kb.psb[bu]])
                    sgt = sg[cnt % 2]
                    cnt += 1
                    kb.op("act", lambda e, sgt=sgt, bg=bg: e.activation(out=sgt.ap, in_=kb.ps[:, bg, :], func=AF.Silu),
                          reads=[kb.psb[bg]], writes=sgt.b)
                    kb.op("dve", lambda e, sgt=sgt, bu=bu, fc=fc, t=t: e.tensor_tensor(
                        out=G3[:, fc, t * 512:(t + 1) * 512], in0=kb.ps[:, bu, :], in1=sgt.ap, op=ALU.mult),
                        reads=[kb.psb[bu]] + sgt.b, writes=[G.b[fc]])
        for ds in range(2):
            w = wd[ds % 2]
            w3 = w.ap.rearrange("p (a b) -> p a b", b=512)
            dma(kb, "sp", w3, wdn3[:, :, ds * 512:(ds + 1) * 512], writes=w.b)
            for dcl in range(4):
                dc = ds * 4 + dcl
                for t in range(2):
                    tok = q * TQ + t * 512
                    bo = kb.bank()
                    xrt, xot = xr[cnt % 2], xo[cnt % 2]
                    cnt += 1
                    dma(kb, "sp", xrt.ap, xin[dc * 128:(dc + 1) * 128, tok:tok + 512], writes=xrt.b)
                    for fc in range(22):
                        kb.op("pe", lambda e, fc=fc, w3=w3, dcl=dcl, t=t, bo=bo: e.matmul(
                            kb.ps[:, bo, :], lhsT=w3[:, fc, dcl * 128:(dcl + 1) * 128],
                            rhs=G3[:, fc, t * 512:(t + 1) * 512], start=(fc == 0), stop=(fc == 21)),
                            reads=w.b + [G.b[fc]], writes=[kb.psb[bo]])
                    kb.op("dve", lambda e, xrt=xrt, xot=xot, bo=bo: e.scalar_tensor_tensor(
                        out=xot.ap, in0=kb.ps[:, bo, :], scalar=0.5, in1=xrt.ap, op0=ALU.mult, op1=ALU.add),
                        reads=[kb.psb[bo]] + xrt.b, writes=xot.b)
                    dma(kb, "pool", xout[dc * 128:(dc + 1) * 128, tok:tok + 512], xot.ap, reads=xot.b)


def final_norm_phase(kb, C, xin, out, gname):
    g = C[gname]
    xts = [kb.f32(8 * 512) for _ in range(2)]
    ots = [kb.f32(8 * 512) for _ in range(2)]
    sq = kb.f32(8 * 512, 8)
    rstd = kb.f32(512)
    xin3 = xin.rearrange("(kc p) t -> p kc t", p=128)
    out3 = out.rearrange("(kc p) t -> p kc t", p=128)
    for t in range(L // 512):
        xt, ot = xts[t % 2], ots[t % 2]
        x3 = xt.ap.rearrange("p (a b) -> p a b", b=512)
        o3 = ot.ap.rearrange("p (a b) -> p a b", b=512)
        xt3 = Tile(x3)
        xt3.b = xt.b
        dma(kb, "sp", x3, xin3[:, :, t * 512:(t + 1) * 512], writes=xt.b)
        rmsnorm_T(kb, C, xt3, g, lambda kc: None, sq, rstd)
        for kc in range(8):
            eng = "dve"
            kb.op(eng, lambda e, kc=kc, x3=x3, o3=o3: e.scalar_tensor_tensor(
                out=o3[:, kc, :], in0=x3[:, kc, :], scalar=g.ap[:, kc:kc + 1], in1=rstd.ap,
                op0=ALU.mult, op1=ALU.mult), reads=xt.b + rstd.b + g.b, writes=ot.b)
        dma(kb, "pool", out3[:, :, t * 512:(t + 1) * 512], o3, reads=ot.b)


def inproj_phase(kb, C, x1T, win, uhy, urwT, gT):
    TQ = 1024
    g = C["mix_norm"]
    xn = kb.bf16(8 * TQ, 8)
    xn3 = xn.ap.rearrange("p (a b) -> p a b", b=TQ)
    xts = [kb.f32(8 * 512) for _ in range(2)]
    sq = kb.f32(8 * 512, 8)
    rstd = kb.f32(512)
    why = kb.bf16(8 * NHY)
    why3 = why.ap.rearrange("p (a b) -> p a b", b=NHY)
    slabs = [kb.bf16(8 * 512) for _ in range(2)]
    stg = [kb.f32(512) for _ in range(4)]
    zero = kb.f32(NHY)
    x1T3 = x1T.rearrange("(kc p) t -> p kc t", p=128)
    win3 = win.rearrange("(kc p) f -> p kc f", p=128)
    kb.op("pool", lambda e: e.memset(zero.ap, 0.0), writes=zero.b)
    dma(kb, "sp", uhy[0:1, :], zero.ap[0:1, :], reads=zero.b)
    dma(kb, "sp", uhy[L + 1:L + 2, :], zero.ap[0:1, :], reads=zero.b)
    for r0 in range(0, NRW, 128):
        rr = min(128, NRW - r0)
        dma(kb, "sp", urwT[r0:r0 + rr, 0:1], zero.ap[0:rr, 0:1], reads=zero.b, slow=True)
        dma(kb, "sp", urwT[r0:r0 + rr, L + 1:L + 2], zero.ap[0:rr, 0:1], reads=zero.b, slow=True)
    dma(kb, "sp", why3, win3[:, :, 0:NHY], writes=why.b)
    cnt = 0
    for q in range(L // TQ):
        for t in range(2):
            tok = q * TQ + t * 512
            xt = xts[t]
            x3 = xt.ap.rearrange("p (a b) -> p a b", b=512)
            xt3 = Tile(x3)
            xt3.b = xt.b
            dma(kb, "sp", x3, x1T3[:, :, tok:tok + 512], writes=xt.b)
            rmsnorm_T(kb, C, xt3, g, lambda kc: None, sq, rstd)
            for kc in range(8):
                kb.op("dve", lambda e, kc=kc, x3=x3, t=t: e.scalar_tensor_tensor(
                    out=xn3[:, kc, t * 512:(t + 1) * 512], in0=x3[:, kc, :], scalar=g.ap[:, kc:kc + 1],
                    in1=rstd.ap, op0=ALU.mult, op1=ALU.mult),
                    reads=xt.b + rstd.b + g.b, writes=[xn.b[kc]])
        for (dst, col0, ncols, gate, coff) in ((urwT, NHY, NRW, False, 1), (gT, NHY + NRW, NG, True, 0)):
            for s0 in range(0, ncols, 512):
                cw = min(512, ncols - s0)
                sl = slabs[cnt % 2]
                sl3 = sl.ap.rearrange("p (a b) -> p a b", b=512)
                dma(kb, "sp", sl3[:, :, 0:cw], win3[:, :, col0 + s0:col0 + s0 + cw], writes=sl.b)
                for c0 in range(0, cw, 128):
                    m = min(128, cw - c0)
                    for t in range(2):
                        tok = q * TQ + t * 512
                        bk = kb.bank()
                        for kc in range(8):
                            kb.op("pe", lambda e, kc=kc, sl3=sl3, c0=c0, m=m, t=t, bk=bk: e.matmul(
                                kb.ps[0:m, bk, :], lhsT=sl3[:, kc, c0:c0 + m],
                                rhs=xn3[:, kc, t * 512:(t + 1) * 512], start=(kc == 0), stop=(kc == 7)),
                                reads=sl.b + [xn.b[kc]], writes=[kb.psb[bk]])
                        sg_ = stg[cnt % 4]
                        cnt += 1
                        if gate:
                            kb.op("act", lambda e, sg_=sg_, bk=bk, m=m: e.activation(
                                out=sg_.ap[0:m, :], in_=kb.ps[0:m, bk, :], func=AF.Sigmoid),
                                reads=[kb.psb[bk]], writes=sg_.b)
                        else:
                            kb.op("dve", lambda e, sg_=sg_, bk=bk, m=m: e.tensor_copy(
                                out=sg_.ap[0:m, :], in_=kb.ps[0:m, bk, :]),
                                reads=[kb.psb[bk]], writes=sg_.b)
                        r0 = s0 + c0
                        dma(kb, "pool", dst[r0:r0 + m, coff + tok:coff + tok + 512], sg_.ap[0:m, :], reads=sg_.b)
        for tb in range(TQ // 128):
            tok = q * TQ + tb * 128
            for cs in range(3):
                bk = kb.bank()
                for kc in range(8):
                    kb.op("pe", lambda e, kc=kc, tb=tb, cs=cs, bk=bk: e.matmul(
                        kb.ps[:, bk, :], lhsT=xn3[:, kc, tb * 128:(tb + 1) * 128],
                        rhs=why3[:, kc, cs * 512:(cs + 1) * 512], start=(kc == 0), stop=(kc == 7)),
                        reads=why.b + [xn.b[kc]], writes=[kb.psb[bk]])
                sg_ = stg[cnt % 4]
                cnt += 1
                kb.op("act", lambda e, sg_=sg_, bk=bk: e.copy(out=sg_.ap, in_=kb.ps[:, bk, :]),
                      reads=[kb.psb[bk]], writes=sg_.b)
                dma(kb, "pool", uhy[1 + tok:1 + tok + 128, cs * 512:(cs + 1) * 512], sg_.ap, reads=sg_.b)


def _bl(*tiles):
    out = []
    for t in tiles:
        out.extend(t.b if isinstance(t, Tile) else [t])
    return out


def _tt(kb, eng, out, a, b, op, R, W):
    return kb.op(eng, lambda e: e.tensor_tensor(out=out, in0=a, in1=b, op=op), reads=_bl(*R), writes=_bl(*W))


def _ts(kb, eng, out, a, s1, s2, op0, op1, R, W):
    if op1 is None:
        return kb.op(eng, lambda e: e.tensor_scalar(out=out, in0=a, scalar1=s1, scalar2=None, op0=op0),
                     reads=_bl(*R), writes=_bl(*W))
    return kb.op(eng, lambda e: e.tensor_scalar(out=out, in0=a, scalar1=s1, scalar2=s2, op0=op0, op1=op1),
                 reads=_bl(*R), writes=_bl(*W))


def _stt(kb, out, a, s, b, op0, op1, R, W):
    return kb.op("dve", lambda e: e.scalar_tensor_tensor(out=out, in0=a, scalar=s, in1=b, op0=op0, op1=op1),
                 reads=_bl(*R), writes=_bl(*W))


def _act(kb, out, a, func, R, W, scale=1.0, bias=0.0):
    return kb.op("act", lambda e: e.activation(out=out, in_=a, func=func, scale=scale, bias=bias),
                 reads=_bl(*R), writes=_bl(*W))


def _cp(kb, eng, out, a, R, W):
    if eng == "act":
        return kb.op("act", lambda e: e.copy(out=out, in_=a), reads=_bl(*R), writes=_bl(*W))
    return kb.op(eng, lambda e: e.tensor_copy(out=out, in_=a), reads=_bl(*R), writes=_bl(*W))


def _mm(kb, out, lhsT, rhs, R, W, start=True, stop=True):
    return kb.op("pe", lambda e: e.matmul(out, lhsT=lhsT, rhs=rhs, start=start, stop=stop),
                 reads=_bl(*R), writes=_bl(*W))


def _tr(kb, out, in_, ident, R, W):
    return kb.op("pe", lambda e: e.transpose(out, in_, ident.ap), reads=_bl(*R) + ident.b, writes=_bl(*W))


KAPPA = math.exp(-0.5)
SC = 256
NCH = SC // 64


RW_DBG = {}


class _NS:
    pass


def rwkv_phase(kb, C, P, urwT, yrwT, RD):
    import itertools
    dbg = RW_DBG
    ident, bd64 = C["ident"], C["bd64"]
    W = SC
    WH = W + 2
    NI = NCH * 2
    yT, bT = RD["yT"], RD["bT"]
    yT_b = [[[Buf() for _ in range(L // W)] for _ in range(4)] for _ in range(2)]
    bT_b = [[[Buf() for _ in range(L // W)] for _ in range(4)] for _ in range(2)]
    w2t = kb.f32(1024)
    a2t = kb.f32(1024)
    g2a = kb.f32(512)
    g2b = kb.f32(512)
    mNB = [kb.f32(512), kb.f32(512)]
    mAAB = [kb.f32(512), kb.f32(512)]
    I8 = kb.f32(512)
    vecs = kb.f32(20)
    w0t = kb.f32(8)
    a0t = kb.f32(8)
    mu_rkv = kb.f32(24)
    mu_wa = kb.f32(8)
    mu_g = kb.f32(4)
    dma(kb, "sp", w2t.ap[0:64, :], P["rw_w2t"], writes=w2t.b)
    dma(kb, "sp", a2t.ap[0:64, :], P["rw_a2t"], writes=a2t.b)
    dma(kb, "sp", g2a.ap, P["rw_g2"][0:128, :], writes=g2a.b)
    dma(kb, "sp", g2b.ap[0:32, :], P["rw_g2"][128:160, :], writes=g2b.b)
    dma(kb, "sp", mNB[0].ap[0:64, :], P["mNBf"], writes=mNB[0].b)
    dma(kb, "sp", mNB[1].ap[0:64, :], P["mNBb"], writes=mNB[1].b)
    dma(kb, "sp", mAAB[0].ap[0:64, :], P["mAABf"], writes=mAAB[0].b)
    dma(kb, "sp", mAAB[1].ap[0:64, :], P["mAABb"], writes=mAAB[1].b)
    dma(kb, "sp", I8.ap[0:64, :], P["I8"], writes=I8.b)
    rmask = kb.f32(SC)
    dma(kb, "sp", rmask.ap, P["rmask"], writes=rmask.b)
    dma(kb, "sp", vecs.ap, P["rw_vecs"], writes=vecs.b)
    for fc in range(4):
        dma(kb, "sp", w0t.ap[:, fc * 2:fc * 2 + 2], P["rw_w0T"][fc * 128:(fc + 1) * 128, :], writes=w0t.b)
        dma(kb, "sp", a0t.ap[:, fc * 2:fc * 2 + 2], P["rw_a0T"][fc * 128:(fc + 1) * 128, :], writes=a0t.b)
        for kind in range(3):
            o = (kind * 4 + fc) * 2
            r0 = kind * 512 + fc * 128
            dma(kb, "sp", mu_rkv.ap[:, o:o + 2], P["rw_muT"][r0:r0 + 128, :], writes=mu_rkv.b)
    for i in range(4):
        r0 = 1536 + i * 64
        dma(kb, "sp", mu_wa.ap[0:64, i * 2:i * 2 + 2], P["rw_muT"][r0:r0 + 64, :], writes=mu_wa.b)
    dma(kb, "sp", mu_g.ap[:, 0:2], P["rw_muT"][1792:1920, :], writes=mu_g.b)
    dma(kb, "sp", mu_g.ap[0:32, 2:4], P["rw_muT"][1920:1952, :], writes=mu_g.b)

    def vec(i, fc):
        return vecs.ap[:, i * 4 + fc:i * 4 + fc + 1]

    def v3(t, b=64):
        return t.ap.rearrange("p (a b) -> p a b", b=b)

    def z4(t):
        return t.ap.rearrange("p (c h t) -> p c h t", h=2, t=64)

    N3 = lambda t: t.ap.rearrange("p (i t) -> p i t", t=64)

    def alloc_stream():
        T = _NS()
        T.ld = [kb.f32(WH + 6) for _ in range(5)]
        T.tp = [kb.f32(W) for _ in range(23)]
        T.ar = kb.f32(2 * W)
        T.tok = [kb.f32(NCH * 128) for _ in range(4)]
        T.NBt = kb.f32(NCH * 2 * 128)
        T.KBt = kb.f32(NCH * 2 * 128)
        T.AAB = kb.bf16(NI * 64)
        T.N0 = kb.bf16(NI * 64)
        T.Nk = [kb.bf16(NI * 64) for _ in range(2)]
        T.Ak = [kb.bf16(NI * 64) for _ in range(2)]
        T.Pk = [kb.bf16(NI * 64) for _ in range(2)]
        T.Pf = kb.f32(NI * 64)
        T.AKV = kb.f32(NI * 64)
        T.W2 = kb.f32(NI * 64)
        T.Hs = [kb.f32(128), kb.f32(128)]
        T.Usb = kb.f32(128)
        T.gTt = kb.f32(NCH)
        T.ysc = kb.f32(W)
        T.bsc = kb.f32(W)
        T.pad = [kb.f32(NI * 64) for _ in range(5)]
        for zt in T.pad:
            kb.op("pool", lambda e, zt=zt: e.memset(zt.ap, 0.0), writes=zt.b)
        return T

    TS = [alloc_stream(), alloc_stream()]

    def shift(T, u, out, mu0, mu1, np_=128, seng="pool"):
        t1, t2 = T.tp[5], T.tp[6]
        mus = [mu_rkv, mu_wa, mu_g]
        _tt(kb, seng, t1.ap[0:np_, :], u.ap[0:np_, 0:W], u.ap[0:np_, 1:W + 1], ALU.subtract, [u], [t1])
        _stt(kb, out.ap[0:np_, :], t1.ap[0:np_, :], mu0, u.ap[0:np_, 1:W + 1], ALU.mult, ALU.add, [t1, u] + mus, [out])
        _tt(kb, seng, t2.ap[0:np_, :], u.ap[0:np_, 2:W + 2], u.ap[0:np_, 1:W + 1], ALU.subtract, [u], [t2])
        _stt(kb, out.ap[0:np_, :], t2.ap[0:np_, :], mu1, out.ap[0:np_, :], ALU.mult, ALU.add, [t2, out] + mus, [out])

    def sc_gen(fc, d, T):
        hcur = 0
        kb.op("pool", lambda e: e.memset(T.Hs[0].ap, 0.0), writes=T.Hs[0].b)
        sc_list = list(range(L // W)) if d == 0 else list(range(L // W - 1, -1, -1))
        sc_list = sc_list[:dbg.get('nsc', len(sc_list))]
        (r, k, v, wdx, adx, t1, t2, sg, lr, kraw, rn, kk, kd, bv, cA, cB, ginc, gexc, ginv, gts,
         bt, bh, kh) = T.tp
        tw, tmp = t1, t2
        ar, tok, NBt, KBt, AAB, Nk, Ak, Pk, AKV, W2 = T.ar, T.tok, T.NBt, T.KBt, T.AAB, T.Nk, T.Ak, T.Pk, T.AKV, T.W2
        N0, Pf = T.N0, T.Pf
        btz, ktz, atz, rz, W1Tz = T.pad
        Hs, Usb, gTt, ysc, bsc = T.Hs, T.Usb, T.gTt, T.ysc, T.bsc
        ar4 = ar.ap.rearrange("p (c q t) -> p c q t", q=2, t=64)
        NB4 = NBt.ap.rearrange("p (i q t) -> p i q t", q=2, t=64)
        KB4 = KBt.ap.rearrange("p (i q t) -> p i q t", q=2, t=64)
        tok3 = [t.ap.rearrange("p (c f) -> p c f", f=128) for t in tok]
        for sci, sc in enumerate(sc_list):
            t0 = sc * W
            (ur, uk, uv, uw, ua) = T.ld
            dma(kb, "sp", uw.ap[0:64, 0:WH], urwT[1536 + d * 64:1536 + (d + 1) * 64, t0:t0 + WH], writes=uw.b)
            dma(kb, "sp", ua.ap[0:64, 0:WH], urwT[1664 + d * 64:1664 + (d + 1) * 64, t0:t0 + WH], writes=ua.b)
            dma(kb, "sp", uk.ap[:, 0:WH], urwT[512 + fc * 128:512 + (fc + 1) * 128, t0:t0 + WH], writes=uk.b)
            dma(kb, "sp", ur.ap[:, 0:WH], urwT[fc * 128:(fc + 1) * 128, t0:t0 + WH], writes=ur.b)
            dma(kb, "sp", uv.ap[:, 0:WH], urwT[1024 + fc * 128:1024 + (fc + 1) * 128, t0:t0 + WH], writes=uv.b)

            def mu3(kind):
                o = (kind * 4 + fc) * 2
                return mu_rkv.ap[:, o:o + 1], mu_rkv.ap[:, o + 1:o + 2]

            shift(T, uw, wdx, mu_wa.ap[0:64, d * 2:d * 2 + 1], mu_wa.ap[0:64, d * 2 + 1:d * 2 + 2], 64, "dve")
            yield
            shift(T, ua, adx, mu_wa.ap[0:64, 4 + d * 2:5 + d * 2], mu_wa.ap[0:64, 5 + d * 2:6 + d * 2], 64, "dve")
            yield
            shift(T, uk, k, *mu3(1))
            yield
            shift(T, ur, r, *mu3(0))
            yield
            shift(T, uv, v, *mu3(2))
            yield
            _act(kb, tw.ap[0:64, :], wdx.ap[0:64, :], AF.Tanh, [wdx], [tw])
            b1 = kb.bank()
            _mm(kb, kb.ps[:, b1, 0:W], w2t.ap[0:64, d * 512 + fc * 128:d * 512 + (fc + 1) * 128], tw.ap[0:64, :],
                [w2t, tw], [kb.psb[b1]])
            _act(kb, sg.ap, kb.ps[:, b1, 0:W], AF.Sigmoid, [kb.psb[b1], w0t], [sg],
                 bias=w0t.ap[:, fc * 2 + d:fc * 2 + d + 1])
            b2 = kb.bank()
            _mm(kb, kb.ps[:, b2, 0:W], a2t.ap[0:64, d * 512 + fc * 128:d * 512 + (fc + 1) * 128], adx.ap[0:64, :],
                [a2t, adx], [kb.psb[b2]])
            _act(kb, lr.ap, kb.ps[:, b2, 0:W], AF.Sigmoid, [kb.psb[b2], a0t], [lr],
                 bias=a0t.ap[:, fc * 2 + d:fc * 2 + d + 1])
            yield
            _act(kb, kraw.ap, k.ap, AF.Square, [k, vecs], [kraw], scale=vec(0, fc))
            b3 = kb.bank()
            _mm(kb, kb.ps[:, b3, 0:W], bd64.ap, kraw.ap, [bd64, kraw], [kb.psb[b3]])
            _act(kb, rn.ap, kb.ps[:, b3, 0:W], AF.Sqrt, [kb.psb[b3]], [rn])
            _ts(kb, "dve", rn.ap, rn.ap, 1e-12, None, ALU.max, None, [rn], [rn])
            kb.op("dve", lambda e: e.reciprocal(out=rn.ap, in_=rn.ap), reads=rn.b, writes=rn.b)
            _stt(kb, kk.ap, k.ap, vec(0, fc), rn.ap, ALU.mult, ALU.mult, [k, vecs, rn], [kk])
            yield
            _ts(kb, "dve", tmp.ap, lr.ap, -1.0, vec(1, fc), ALU.add, ALU.mult, [lr, vecs], [tmp])
            _stt(kb, kd.ap, tmp.ap, 1.0, k.ap, ALU.add, ALU.mult, [tmp, k], [kd])
            _tt(kb, "pool", bv.ap, kk.ap, lr.ap, ALU.mult, [kk, lr], [bv])
            _stt(kb, bsc.ap, r.ap, vec(2, fc), kd.ap, ALU.mult, ALU.mult, [r, kd, vecs], [bsc])
            dma(kb, "pool", bT[d, fc * 128:(fc + 1) * 128, t0:t0 + W], bsc.ap, reads=bsc.b, writes=[bT_b[d][fc][sc]])
            if d == 0:
                kb.op("dve", lambda e: e.tensor_tensor_scan(out=cB.ap, data0=rmask.ap, data1=sg.ap, initial=0.0,
                                                           op0=ALU.mult, op1=ALU.add),
                      reads=_bl(rmask, sg), writes=cB.b)
            else:
                kb.op("dve", lambda e: e.tensor_tensor_scan(out=cA.ap, data0=rmask.ap, data1=sg.ap, initial=0.0,
                                                           op0=ALU.mult, op1=ALU.add),
                      reads=_bl(rmask, sg), writes=cA.b)
                pre3 = v3(cA)
                _tt(kb, "dve", v3(cB), pre3, pre3[:, :, 63:64].to_broadcast([128, NCH, 64]), ALU.subtract, [cA], [cB])
                _tt(kb, "dve", cB.ap, sg.ap, cB.ap, ALU.subtract, [sg, cB], [cB])
            yield
            cs = cB
            cs3 = v3(cs)
            ti = 63 if d == 0 else 0
            totb = cs3[:, :, ti:ti + 1].to_broadcast([128, NCH, 64])
            _act(kb, ginc.ap, cs.ap, AF.Exp, [cs], [ginc], scale=-KAPPA)
            _act(kb, ginv.ap, cs.ap, AF.Exp, [cs], [ginv], scale=KAPPA)
            _tt(kb, "dve", tmp.ap, cs.ap, sg.ap, ALU.subtract, [cs, sg], [tmp])
            _act(kb, gexc.ap, tmp.ap, AF.Exp, [tmp], [gexc], scale=-KAPPA)
            _tt(kb, "dve", v3(cA), cs3, totb, ALU.subtract, [cs], [cA])
            _act(kb, gts.ap, cA.ap, AF.Exp, [cA], [gts], scale=KAPPA)
            _act(kb, gTt.ap, cs3[:, :, ti], AF.Exp, [cs], [gTt], scale=-KAPPA)
            yield
            _stt(kb, ar4[:, :, 0, :], v3(kk), -1.0, v3(gexc), ALU.mult, ALU.mult, [kk, gexc], [ar])
            _tt(kb, "pool", ar4[:, :, 1, :], v3(r), v3(ginc), ALU.mult, [r, ginc], [ar])
            _tt(kb, "dve", bt.ap, bv.ap, ginv.ap, ALU.mult, [bv, ginv], [bt])
            yield
            for h2 in range(2):
                ps_ = slice(h2 * 64, (h2 + 1) * 64)
                _stt(kb, z4(atz)[ps_, :, h2, :], v3(kk)[ps_], -1.0, v3(gexc)[ps_], ALU.mult, ALU.mult, [kk, gexc], [atz])
                _tt(kb, "pool", z4(rz)[ps_, :, h2, :], v3(r)[ps_], v3(ginc)[ps_], ALU.mult, [r, ginc], [rz])
                _tt(kb, "dve", z4(btz)[ps_, :, h2, :], v3(bv)[ps_], v3(ginv)[ps_], ALU.mult, [bv, ginv], [btz])
                _tt(kb, "pool", z4(ktz)[ps_, :, h2, :], v3(kd)[ps_], v3(ginv)[ps_], ALU.mult, [kd, ginv], [ktz])
            yield
            _tt(kb, "pool", bh.ap, bv.ap, gts.ap, ALU.mult, [bv, gts], [bh])
            _tt(kb, "dve", kh.ap, kd.ap, gts.ap, ALU.mult, [kd, gts], [kh])
            yield
            for qi, (srct, fn) in enumerate(((ar, lambda c: ar4[:, c, 0, :]), (bh, lambda c: bh.ap[:, c * 64:(c + 1) * 64]),
                                            (kh, lambda c: kh.ap[:, c * 64:(c + 1) * 64]),
                                            (v, lambda c: v.ap[:, c * 64:(c + 1) * 64]))):
                bk = kb.bank()
                for c in range(NCH):
                    _tr(kb, kb.ps[0:64, bk, c * 128:(c + 1) * 128], fn(c), ident, [srct], [kb.psb[bk]])
                _cp(kb, "act" if qi % 2 else "dve", tok[qi].ap[0:64, :], kb.ps[0:64, bk, :], [kb.psb[bk]], [tok[qi]])
                if qi % 2 == 1:
                    yield
            yield
            for (lt, dstt) in ((btz, NBt), (ktz, KBt)):
                for half in range(2):
                    bk = kb.bank()
                    for ii in range(4):
                        i = half * 4 + ii
                        c, h2 = i // 2, i % 2
                        _mm(kb, kb.ps[0:64, bk, ii * 128:(ii + 1) * 128], z4(lt)[:, c, h2, :],
                            ar.ap[:, c * 128:(c + 1) * 128], [lt, ar], [kb.psb[bk]])
                    _tt(kb, "dve", dstt.ap[0:64, half * 512:(half + 1) * 512], kb.ps[0:64, bk, :],
                        mNB[d].ap[0:64, :], ALU.mult, [kb.psb[bk], mNB[d]], [dstt])
                yield
            bk = kb.bank()
            for i in range(NI):
                c, h2 = i // 2, i % 2
                _mm(kb, kb.ps[0:64, bk, i * 64:(i + 1) * 64], z4(atz)[:, c, h2, :],
                    bt.ap[:, c * 64:(c + 1) * 64], [atz, bt], [kb.psb[bk]])
            _tt(kb, "dve", AAB.ap[0:64, :], kb.ps[0:64, bk, :], mAAB[d].ap[0:64, :], ALU.mult,
                [kb.psb[bk], mAAB[d]], [AAB])
            _tt(kb, "dve", N3(Pk[0])[0:64], NB4[0:64, :, 0, :], N3(I8)[0:64], ALU.add, [NBt, I8], [Pk[0]])
            _cp(kb, "act", N3(N0)[0:64], NB4[0:64, :, 0, :], [NBt], [N0])
            yield
            curN = lambda i: N3(N0)[0:64, i, :]
            curNt = N0
            curA = AAB
            pc = 0
            for lev in range(5):
                Nn, An = Nk[lev % 2], Ak[lev % 2]
                bA = kb.bank()
                for i in range(NI):
                    _mm(kb, kb.ps[0:64, bA, i * 64:(i + 1) * 64], curN(i), N3(curA)[0:64, i, :], [curNt, curA], [kb.psb[bA]])
                if lev < 4:
                    bN = kb.bank()
                    for i in range(NI):
                        _mm(kb, kb.ps[0:64, bN, i * 64:(i + 1) * 64], N3(curA)[0:64, i, :], curN(i), [curNt, curA], [kb.psb[bN]])
                _cp(kb, "act", An.ap[0:64, :], kb.ps[0:64, bA, :], [kb.psb[bA]], [An])
                if lev < 4:
                    _cp(kb, "dve", Nn.ap[0:64, :], kb.ps[0:64, bN, :], [kb.psb[bN]], [Nn])
                yield
                bP = kb.bank()
                for i in range(NI):
                    _mm(kb, kb.ps[0:64, bP, i * 64:(i + 1) * 64], N3(An)[0:64, i, :], N3(Pk[pc])[0:64, i, :],
                        [An, Pk[pc]], [kb.psb[bP]])
                _tt(kb, "dve", Pk[1 - pc].ap[0:64, :], kb.ps[0:64, bP, :], Pk[pc].ap[0:64, :], ALU.add,
                    [kb.psb[bP], Pk[pc]], [Pk[1 - pc]])
                pc = 1 - pc
                curA = An
                curNt = Nn
                curN = (lambda Nn: (lambda i: N3(Nn)[0:64, i, :]))(Nn)
                yield
            _cp(kb, "dve", Pf.ap[0:64, :], Pk[pc].ap[0:64, :], [Pk[pc]], [Pf])
            Pm = Pf
            bk = kb.bank()
            for i in range(NI):
                c, h2 = i // 2, i % 2
                _mm(kb, kb.ps[0:64, bk, i * 64:(i + 1) * 64], KB4[0:64, i, 0, :],
                    tok3[3][0:64, c, h2 * 64:(h2 + 1) * 64], [KBt, tok[3]], [kb.psb[bk]])
            _cp(kb, "act", AKV.ap[0:64, :], kb.ps[0:64, bk, :], [kb.psb[bk]], [AKV])
            bk2 = kb.bank()
            for i in range(NI):
                c, h2 = i // 2, i % 2
                _mm(kb, kb.ps[:, bk2, i * 64:(i + 1) * 64], tok3[0][0:64, c, :], N3(Pm)[0:64, i, :],
                    [tok[0], Pm], [kb.psb[bk2]])
            ps4 = kb.ps[:, bk2, :].rearrange("p (c h t) -> p c h t", h=2, t=64)
            _cp(kb, "dve", z4(W1Tz)[0:64, :, 0, :], ps4[0:64, :, 0, :], [kb.psb[bk2]], [W1Tz])
            _cp(kb, "dve", z4(W1Tz)[64:128, :, 1, :], ps4[64:128, :, 1, :], [kb.psb[bk2]], [W1Tz])
            yield
            bk = kb.bank()
            for i in range(NI):
                _mm(kb, kb.ps[0:64, bk, i * 64:(i + 1) * 64], N3(Pm)[0:64, i, :], N3(AKV)[0:64, i, :],
                    [Pm, AKV], [kb.psb[bk]])
            _cp(kb, "act", W2.ap[0:64, :], kb.ps[0:64, bk, :], [kb.psb[bk]], [W2])
            W1T3 = N3(W1Tz)
            yield
            corder = list(range(NCH)) if d == 0 else list(range(NCH - 1, -1, -1))
            for c in corder:
                H, Hn = Hs[hcur], Hs[1 - hcur]
                bU = kb.bank()
                for h2 in range(2):
                    _mm(kb, kb.ps[0:64, bU, h2 * 64:(h2 + 1) * 64], W1T3[:, c * 2 + h2, :],
                        H.ap[:, h2 * 64:(h2 + 1) * 64], [W1Tz, H], [kb.psb[bU]])
                _tt(kb, "dve", Usb.ap[0:64, :], kb.ps[0:64, bU, 0:128], W2.ap[0:64, c * 128:(c + 1) * 128], ALU.add,
                    [kb.psb[bU], W2], [Usb])
                yield
                bH = kb.bank()
                _mm(kb, kb.ps[:, bH, 0:128], tok3[2][0:64, c, :], tok3[3][0:64, c, :], [tok[2], tok[3]],
                    [kb.psb[bH]], start=True, stop=False)
                _mm(kb, kb.ps[:, bH, 0:128], tok3[1][0:64, c, :], Usb.ap[0:64, :], [tok[1], Usb],
                    [kb.psb[bH]], start=False, stop=True)
                bY = kb.bank()
                psY3 = kb.ps[:, bY, 0:128].rearrange("p (h t) -> p h t", t=64)
                _mm(kb, psY3, H.ap, z4(rz)[:, c, :, :], [H, rz], [kb.psb[bY]], start=True, stop=False)
                _mm(kb, psY3, Usb.ap[0:64, :], NB4[0:64, c * 2:c * 2 + 2, 1, :],
                    [Usb, NBt], [kb.psb[bY]], start=False, stop=False)
                _mm(kb, psY3, tok3[3][0:64, c, :], KB4[0:64, c * 2:c * 2 + 2, 1, :],
                    [tok[3], KBt], [kb.psb[bY]], start=False, stop=True)
                _stt(kb, Hn.ap, H.ap, gTt.ap[:, c:c + 1], kb.ps[:, bH, 0:128], ALU.mult, ALU.add,
                     [H, gTt, kb.psb[bH]], [Hn])
                _cp(kb, "act", ysc.ap[0:64, c * 64:(c + 1) * 64], kb.ps[0:64, bY, 0:64], [kb.psb[bY]], [ysc])
                _cp(kb, "act", ysc.ap[64:128, c * 64:(c + 1) * 64], kb.ps[64:128, bY, 64:128], [kb.psb[bY]], [ysc])
                hcur = 1 - hcur
                yield
            dma(kb, "pool", yT[d, fc * 128:(fc + 1) * 128, t0:t0 + W], ysc.ap, reads=ysc.b, writes=[yT_b[d][fc][sc]])

    TP = _NS()
    TP.ld = [kb.f32(WH + 6) for _ in range(3)]
    TP.tp = [kb.f32(W) for _ in range(15)]

    def shift_p(u, out, mu0, mu1, np_=128):
        t1, t2 = TP.tp[13], TP.tp[14]
        mus = [mu_rkv, mu_wa, mu_g]
        _tt(kb, "pool", t1.ap[0:np_, :], u.ap[0:np_, 0:W], u.ap[0:np_, 1:W + 1], ALU.subtract, [u], [t1])
        _stt(kb, out.ap[0:np_, :], t1.ap[0:np_, :], mu0, u.ap[0:np_, 1:W + 1], ALU.mult, ALU.add, [t1, u] + mus, [out])
        _tt(kb, "pool", t2.ap[0:np_, :], u.ap[0:np_, 2:W + 2], u.ap[0:np_, 1:W + 1], ALU.subtract, [u], [t2])
        _stt(kb, out.ap[0:np_, :], t2.ap[0:np_, :], mu1, out.ap[0:np_, :], ALU.mult, ALU.add, [t2, out] + mus, [out])

    def post_gen(fc):
        for ti_ in range(L // W if dbg.get('post', True) else 0):
            t0 = ti_ * W
            (uv, ug0, ug1) = TP.ld
            (y, cen, sq_, rs, yn, vv, g0, g1, bvv, ob_, y1, bo0, bo1) = TP.tp[0:13]
            dma(kb, "sp", uv.ap[:, 0:WH], urwT[1024 + fc * 128:1024 + (fc + 1) * 128, t0:t0 + WH], writes=uv.b)
            dma(kb, "sp", ug0.ap[:, 0:WH], urwT[1792:1920, t0:t0 + WH], writes=ug0.b)
            dma(kb, "sp", ug1.ap[0:32, 0:WH], urwT[1920:1952, t0:t0 + WH], writes=ug1.b)
            fsl = slice(fc * 128, (fc + 1) * 128)
            dma(kb, "sp", y.ap, yT[0, fsl, t0:t0 + W], reads=[yT_b[0][fc][ti_]], writes=y.b)
            dma(kb, "sp", y1.ap, yT[1, fsl, t0:t0 + W], reads=[yT_b[1][fc][ti_]], writes=y1.b)
            dma(kb, "sp", bo0.ap, bT[0, fsl, t0:t0 + W], reads=[bT_b[0][fc][ti_]], writes=bo0.b)
            dma(kb, "sp", bo1.ap, bT[1, fsl, t0:t0 + W], reads=[bT_b[1][fc][ti_]], writes=bo1.b)
            o = (2 * 4 + fc) * 2
            shift_p(uv, vv, mu_rkv.ap[:, o:o + 1], mu_rkv.ap[:, o + 1:o + 2])
            shift_p(ug0, g0, mu_g.ap[:, 0:1], mu_g.ap[:, 1:2])
            yield
            shift_p(ug1, g1, mu_g.ap[0:32, 2:3], mu_g.ap[0:32, 3:4], 32)
            _act(kb, g0.ap, g0.ap, AF.Sigmoid, [g0], [g0])
            _act(kb, g1.ap[0:32, :], g1.ap[0:32, :], AF.Sigmoid, [g1], [g1])
            _tt(kb, "pool", y.ap, y.ap, y1.ap, ALU.add, [y, y1], [y])
            _tt(kb, "pool", bo0.ap, bo0.ap, bo1.ap, ALU.add, [bo0, bo1], [bo0])
            yield
            bM = kb.bank()
            _mm(kb, kb.ps[:, bM, 0:W], bd64.ap, y.ap, [bd64, y], [kb.psb[bM]])
            _stt(kb, cen.ap, kb.ps[:, bM, 0:W], -1.0 / 64, y.ap, ALU.mult, ALU.add, [kb.psb[bM], y], [cen])
            _tt(kb, "pool", sq_.ap, cen.ap, cen.ap, ALU.mult, [cen], [sq_])
            yield
            bV = kb.bank()
            _mm(kb, kb.ps[:, bV, 0:W], bd64.ap, sq_.ap, [bd64, sq_], [kb.psb[bV]])
            _act(kb, rs.ap, kb.ps[:, bV, 0:W], AF.Sqrt, [kb.psb[bV]], [rs], scale=1.0 / 64, bias=GN_EPS)
            kb.op("dve", lambda e, rs=rs: e.reciprocal(out=rs.ap, in_=rs.ap), reads=rs.b, writes=rs.b)
            _tt(kb, "pool", yn.ap, cen.ap, rs.ap, ALU.mult, [cen, rs], [yn])
            _ts(kb, "dve", yn.ap, yn.ap, vec(3, fc), vec(4, fc), ALU.mult, ALU.add, [yn, vecs], [yn])
            yield
            bB = kb.bank()
            _mm(kb, kb.ps[:, bB, 0:W], bd64.ap, bo0.ap, [bd64, bo0], [kb.psb[bB]])
            _tt(kb, "dve", bvv.ap, kb.ps[:, bB, 0:W], vv.ap, ALU.mult, [kb.psb[bB], vv], [bvv])
            _tt(kb, "pool", yn.ap, yn.ap, bvv.ap, ALU.add, [yn, bvv], [yn])
            bG = kb.bank()
            _mm(kb, kb.ps[:, bG, 0:W], g2a.ap[:, fc * 128:(fc + 1) * 128], g0.ap, [g2a, g0], [kb.psb[bG]],
                start=True, stop=False)
            _mm(kb, kb.ps[:, bG, 0:W], g2b.ap[0:32, fc * 128:(fc + 1) * 128], g1.ap[0:32, :], [g2b, g1], [kb.psb[bG]],
                start=False, stop=True)
            ob = ob_.ap.bitcast(BF16)[:, 0:W]
            _tt(kb, "dve", ob, kb.ps[:, bG, 0:W], yn.ap, ALU.mult, [kb.psb[bG], yn], [ob_])
            dma(kb, "pool", yrwT[fc * 128:(fc + 1) * 128, t0:t0 + W], ob, reads=ob_.b)
            yield

    nfc = dbg.get('fcs', 4)
    for fc in range(nfc + 1):
        gens = []
        if fc < nfc:
            gens += [sc_gen(fc, d, TS[d]) for d in range(dbg.get('dirs', 2))]
        if fc > 0:
            gens.append(post_gen(fc - 1))
        for _ in itertools.zip_longest(*gens):
            pass


def merge_phase(kb, C, zhyT, yrwT, gT, x1T, x2T, hyo, rwo, wo):
    wh = kb.bf16(4 * D)
    wr = kb.bf16(4 * D)
    wo_ = kb.bf16(8 * D)
    wh3 = wh.ap.rearrange("p (k d) -> p k d", d=D)
    wr3 = wr.ap.rearrange("p (k d) -> p k d", d=D)
    wo3 = wo_.ap.rearrange("p (k d) -> p k d", d=D)
    dma(kb, "sp", wh3, hyo.rearrange("(k p) d -> p k d", p=128), writes=wh.b)
    dma(kb, "sp", wr3, rwo.rearrange("(k p) d -> p k d", p=128), writes=wr.b)
    dma(kb, "sp", wo3, wo.rearrange("(k p) d -> p k d", p=128), writes=wo_.b)
    zt = [kb.bf16(4 * 512) for _ in range(2)]
    yt = [kb.bf16(4 * 512) for _ in range(2)]
    mrg = [kb.bf16(8 * 512, 8) for _ in range(2)]
    gh = [kb.f32(512) for _ in range(2)]
    gr = [kb.f32(512) for _ in range(2)]
    m1 = [kb.f32(512) for _ in range(2)]
    m2 = [kb.f32(512) for _ in range(2)]
    xr = [kb.f32(512) for _ in range(2)]
    xo = [kb.f32(512) for _ in range(2)]
    zh3 = zhyT.rearrange("(k p) t -> p k t", p=128)
    yr3 = yrwT.rearrange("(k p) t -> p k t", p=128)
    cnt = 0
    for t in range(L // 512):
        ts_ = slice(t * 512, (t + 1) * 512)
        z_, y_, mg = zt[t % 2], yt[t % 2], mrg[t % 2]
        z3 = z_.ap.rearrange("p (k t) -> p k t", t=512)
        y3 = y_.ap.rearrange("p (k t) -> p k t", t=512)
        mg3 = mg.ap.rearrange("p (k t) -> p k t", t=512)
        dma(kb, "sp", z3, zh3[:, :, ts_], writes=z_.b)
        dma(kb, "sp", y3, yr3[:, :, ts_], writes=y_.b)
        for dc in range(8):
            i2 = cnt % 2
            cnt += 1
            dsl = slice(dc * 128, (dc + 1) * 128)
            dma(kb, "sp", gh[i2].ap, gT[dc * 128:(dc + 1) * 128, ts_], writes=gh[i2].b)
            dma(kb, "sp", gr[i2].ap, gT[D + dc * 128:D + (dc + 1) * 128, ts_], writes=gr[i2].b)
            bh, br = kb.bank(), kb.bank()
            for kc in range(4):
                _mm(kb, kb.ps[:, bh, :], wh3[:, kc, dsl], z3[:, kc, :], [wh, z_], [kb.psb[bh]], start=(kc == 0), stop=(kc == 3))
            for kc in range(4):
                _mm(kb, kb.ps[:, br, :], wr3[:, kc, dsl], y3[:, kc, :], [wr, y_], [kb.psb[br]], start=(kc == 0), stop=(kc == 3))
            _tt(kb, "dve", m1[i2].ap, kb.ps[:, bh, :], gh[i2].ap, ALU.mult, [kb.psb[bh], gh[i2]], [m1[i2]])
            _tt(kb, "dve", m2[i2].ap, kb.ps[:, br, :], gr[i2].ap, ALU.mult, [kb.psb[br], gr[i2]], [m2[i2]])
            _tt(kb, "pool", mg3[:, dc, :], m1[i2].ap, m2[i2].ap, ALU.add, [m1[i2], m2[i2]], [mg.b[dc]])
        for dc in range(8):
            i2 = cnt % 2
            cnt += 1
            dsl = slice(dc * 128, (dc + 1) * 128)
            dma(kb, "sp", xr[i2].ap, x1T[dc * 128:(dc + 1) * 128, ts_], writes=xr[i2].b)
            bo = kb.bank()
            for kc in range(8):
                _mm(kb, kb.ps[:, bo, :], wo3[:, kc, dsl], mg3[:, kc, :], [wo_, mg.b[kc]], [kb.psb[bo]],
                    start=(kc == 0), stop=(kc == 7))
            _tt(kb, "dve", xo[i2].ap, kb.ps[:, bo, :], xr[i2].ap, ALU.add, [kb.psb[bo], xr[i2]], [xo[i2]])
            dma(kb, "pool", x2T[dc * 128:(dc + 1) * 128, ts_], xo[i2].ap, reads=xo[i2].b)


def _host_consts():
    c = {}
    c["ones"] = np.ones((128, 128), np.float32)
    c["ident"] = np.eye(128, dtype=np.float32)
    bd = np.zeros((128, 128), np.float32)
    bd[0:64, 0:64] = 1.0
    bd[64:128, 64:128] = 1.0
    c["bd64"] = bd
    s = np.arange(64)[:, None]
    t = np.arange(64)[None, :]
    lt, le, gt, ge = (s < t), (s <= t), (s > t), (s >= t)

    def nb(m0, m1):
        m = np.zeros((64, 4, 2, 64), np.float32)
        m[:, :, 0, :] = m0[:, None, :]
        m[:, :, 1, :] = m1[:, None, :]
        return m.reshape(64, 512)

    c["mNBf"] = nb(lt, le)
    c["mNBb"] = nb(gt, ge)
    c["mAABf"] = np.broadcast_to(gt[:, None, :], (64, 8, 64)).astype(np.float32).reshape(64, 512).copy()
    c["mAABb"] = np.broadcast_to(lt[:, None, :], (64, 8, 64)).astype(np.float32).reshape(64, 512).copy()
    c["I8"] = np.broadcast_to(np.eye(64, dtype=np.float32)[:, None, :], (64, 8, 64)).reshape(64, 512).copy()
    rm = np.ones((128, SC), np.float32)
    rm[:, 0::64] = 0.0
    c["rmask"] = rm
    return c


CONST_SHAPES = {"ones": [128, 128], "ident": [128, 128], "bd64": [128, 128], "mNBf": [64, 512], "mNBb": [64, 512],
                "mAABf": [64, 512], "mAABb": [64, 512], "I8": [64, 512], "rmask": [128, 256]}

PARAM_SHAPES = {
    "norms": [128, 32],
    "rw_muT": [NRW, 2], "rw_w0T": [512, 2], "rw_a0T": [512, 2], "rw_w2t": [64, 1024], "rw_a2t": [64, 1024],
    "rw_g2": [160, 512], "rw_vecs": [128, 20],
}

WEIGHTS = {"ffn1_w_gu": [D, 2 * DFF], "ffn1_w_down": [DFF, D], "ffn2_w_gu": [D, 2 * DFF], "ffn2_w_down": [DFF, D],
           "w_in": [D, NIN], "hy_out": [512, D], "rw_out": [512, D], "w_out": [D, D]}


def _host_params(inputs):
    f = lambda k: np.asarray(inputs[k], np.float32)
    p = {}
    p["norms"] = np.ascontiguousarray(np.concatenate(
        [f(n).reshape(8, 128).T for n in ("ffn1_norm", "mix_norm", "ffn2_norm", "final_norm")], axis=1))
    p["rw_muT"] = np.ascontiguousarray(f("rw_mu").T)
    p["rw_w0T"] = np.ascontiguousarray(f("rw_w0").T)
    p["rw_a0T"] = np.ascontiguousarray(f("rw_a0").T)
    p["rw_w2t"] = np.ascontiguousarray(f("rw_w2").transpose(1, 0, 2).reshape(64, 1024))
    p["rw_a2t"] = np.ascontiguousarray(f("rw_a2").transpose(1, 0, 2).reshape(64, 1024))
    p["rw_g2"] = np.ascontiguousarray(f("rw_g2"))
    p["rw_vecs"] = np.ascontiguousarray(np.concatenate(
        [f(n).reshape(4, 128).T for n in ("rw_k_k", "rw_k_a", "rw_r_k", "rw_ln_w", "rw_ln_b")], axis=1))
    return p


def build_program(stage="full"):
    nc = bass.Bass("TRN2", target_bir_lowering=False)
    I = {}

    def inp(name, shape, dt=F32):
        I[name] = nc.dram_tensor(name, list(shape), dt, kind="ExternalInput").ap()
        return I[name]

    def scr(name, shape, dt=F32, ext=None):
        if ext == "in":
            return inp(name, shape, dt)
        if ext == "out":
            return nc.dram_tensor(name, list(shape), dt, kind="ExternalOutput").ap()
        return nc.dram_tensor(name, list(shape), dt).ap()

    full = stage == "full"
    front = stage in ("full", "front")
    do_rw = stage in ("full", "rwkv")
    do_hy = stage in ("full", "hyena")
    for n, shp in CONST_SHAPES.items():
        inp(n, shp)
    inp("norms", PARAM_SHAPES["norms"])
    if do_rw:
        for n, shp in PARAM_SHAPES.items():
            if n != "norms":
                inp(n, shp)
    if do_hy:
        for n, shp in list(HY_CONST_SHAPES.items()) + list(HY_PARAM_SHAPES.items()):
            inp(n, shp)
    if front:
        inp("xT", [D, L])
        for n, shp in WEIGHTS.items():
            if full or n in ("ffn1_w_gu", "ffn1_w_down", "w_in"):
                inp(n, shp)
    back = stage == "back"
    if back:
        for n in ("hy_out", "rw_out", "w_out", "ffn2_w_gu", "ffn2_w_down"):
            inp(n, WEIGHTS[n])
    uhy = scr("uhy", [L + 2, NHY], ext={"front": "out", "hyena": "in"}.get(stage))
    urwT = scr("urwT", [NRW, L + 2], ext={"front": "out", "rwkv": "in"}.get(stage))
    gT = scr("gT", [NG, L], ext={"front": "out", "back": "in"}.get(stage))
    yrwT = scr("yrwT", [512, L], BF16, ext={"rwkv": "out", "back": "in"}.get(stage))
    zhyT = scr("zhyT", [512, L], BF16, ext={"hyena": "out", "back": "in"}.get(stage))
    outT = None
    if full or stage == "back":
        outT = nc.dram_tensor("outT", [D, L], F32, kind="ExternalOutput").ap()
    wb = {}
    for n, shp in WEIGHTS.items():
        if n in I:
            wb[n] = scr(n + "_b", shp, BF16)
    x1T = scr("x1T", [D, L], ext={"front": "out", "back": "in"}.get(stage))
    x2T = scr("x2T", [D, L])
    x3T = scr("x3T", [D, L])
    HD = {}
    if do_hy:
        HD["ucb"] = scr("hy_ucb", [3, 4, 64, 64 * CG])
        HD["Abuf"] = scr("hy_Abuf", [2, 2, 128, 64, CG])
        HD["Bbuf"] = scr("hy_Bbuf", [2, 2, 64, 128, CG])
        HD["Kf"] = scr("hy_Kf", [2, 4, 2, 64, 128 * CG])
        HD["ucb_b"] = [[Buf() for _ in range(4)] for _ in range(3)]
        HD["Kf_b"] = [[[Buf() for _ in range(32)] for _ in range(4)] for _ in range(2)]
        if stage == "hyena":
            HD["kf_dbg"] = scr("kf_dbg", [2, 4, 128, 64 * CG], ext="out")
            HD["z1_dbg"] = scr("z1_dbg", [4, 64, 64 * CG], ext="out")

    with contextlib.ExitStack() as st:
        kb = KB(nc, st)
        C = {}
        for n in ("ones", "ident", "bd64"):
            C[n] = kb.f32(128)
            dma(kb, "sp", C[n].ap, I[n], writes=C[n].b)
        nrm = kb.f32(32)
        for i, n in enumerate(("ffn1_norm", "mix_norm", "ffn2_norm", "final_norm")):
            t = Tile(nrm.ap[:, i * 8:(i + 1) * 8])
            t.b = nrm.b
            C[n] = t
        dma(kb, "sp", nrm.ap, I["norms"], writes=nrm.b)
        kb.persist()

        if front:
            convert_weights(kb, [(I[n], wb[n], WEIGHTS[n][0], WEIGHTS[n][1]) for n in wb])
            kb.phase_end()
            ffn_phase(kb, C, I["xT"], x1T, "ffn1_norm", wb["ffn1_w_gu"], wb["ffn1_w_down"])
            kb.phase_end()
            inproj_phase(kb, C, x1T, wb["w_in"], uhy, urwT, gT)
            kb.phase_end()
        if do_rw:
            rwkv_phase(kb, C, I, urwT, yrwT, {"yT": scr("rw_yT", [2, 512, L]), "bT": scr("rw_bT", [2, 512, L])})
            kb.phase_end()
        if do_hy:
            hyena_phase(kb, C, I, uhy, zhyT, HD)
            kb.phase_end()
        if back:
            convert_weights(kb, [(I[n], wb[n], WEIGHTS[n][0], WEIGHTS[n][1]) for n in wb])
            kb.phase_end()
        if full or back:
            merge_phase(kb, C, zhyT, yrwT, gT, x1T, x2T, wb["hy_out"], wb["rw_out"], wb["w_out"])
            kb.phase_end()
            ffn_phase(kb, C, x2T, x3T, "ffn2_norm", wb["ffn2_w_gu"], wb["ffn2_w_down"])
            kb.phase_end()
            final_norm_phase(kb, C, x3T, outT, "final_norm")
            kb.phase_end()
        kb.S.emit()
    return nc, list(I.keys())


_NC_CACHE = {}


def _get_program(stage):
    if stage not in _NC_CACHE:
        _NC_CACHE[stage] = build_program(stage)
    return _NC_CACHE[stage]


def host_shared(inputs, names):
    shared = dict(_host_consts())
    shared.update(_host_params(inputs))
    shared.update(_hy_consts())
    shared.update(_hy_host_params(inputs))
    for n in WEIGHTS:
        shared[n] = np.ascontiguousarray(inputs[n], np.float32)
    return {k: v for k, v in shared.items() if k in names}


def kernel(**inputs):
    nc, names = _get_program("full")
    shared = host_shared(inputs, names)
    x = np.asarray(inputs["x"], np.float32)
    in_maps = []
    for b in range(8):
        m = dict(shared)
        m["xT"] = np.ascontiguousarray(x[b].T)
        in_maps.append(m)
    res = run_bass_kernel_spmd(nc, in_maps, core_ids=list(range(8)))
    out = np.stack([np.ascontiguousarray(r["outT"].T) for r in res.results], axis=0)
    return out.astype(np.float32)


NFFT = 8192
CG = 128
HY_DBG = {}


def _hy_consts():
    c = {}
    n1 = np.arange(128, dtype=np.float64)[:, None, None]
    n2 = np.arange(64, dtype=np.float64)[None, :, None]
    k1 = np.arange(128, dtype=np.float64)[None, None, :]
    ang = 2 * np.pi * (n1 * k1 / 128.0 + n2 * k1 / NFFT)
    G = np.stack([np.cos(ang), -np.sin(ang)], axis=2)
    c["hy_G"] = G.reshape(128, 64 * 2 * 128).astype(np.float32)
    a2 = 2 * np.pi * np.arange(64)[:, None] * np.arange(64)[None, :] / 64.0
    c["hy_F2"] = np.concatenate([np.cos(a2), np.sin(a2), -np.sin(a2)], axis=1).astype(np.float32)
    k2 = np.arange(64, dtype=np.float64)[:, None, None]
    k1b = np.arange(128, dtype=np.float64)[None, :, None]
    nl = np.arange(64, dtype=np.float64)[None, None, :]
    angp = 2 * np.pi * (k2 * nl / 64.0 + k1b * nl / NFFT)
    Gp = np.stack([np.cos(angp), np.sin(angp), -np.sin(angp)], axis=2)
    c["hy_Gp"] = Gp.reshape(64, 128 * 3 * 64).astype(np.float32)
    a3 = 2 * np.pi * np.arange(128)[:, None] * np.arange(64)[None, :] / 128.0
    wk = np.full((128, 1), 2.0)
    wk[0] = 1.0
    wk[64] = 1.0
    wk[65:] = 0.0
    c["hy_I2"] = (wk * np.concatenate([np.cos(a3), -np.sin(a3)], axis=1) / NFFT).astype(np.float32)
    n = np.arange(NFFT)
    j = np.where(n <= L, n, NFFT - n).astype(np.float64)
    j[L] = 0
    t = j / (L - 1)
    angf = (2.0 * math.pi / L) * j
    bands = np.linspace(1e-4, 15, 16)
    feats = np.concatenate([t[None, :], np.cos(bands[:, None] * angf[None, :]), -np.sin(bands[:, None] * angf[None, :])], axis=0)
    c["hy_feats"] = feats.astype(np.float32)
    c["hy_negt"] = (-t).reshape(128, 64).astype(np.float32)
    return c


HY_CONST_SHAPES = {"hy_G": [128, 64 * 2 * 128], "hy_F2": [64, 192], "hy_Gp": [64, 128 * 3 * 64], "hy_I2": [128, 128],
                   "hy_feats": [33, NFFT], "hy_negt": [128, 64]}
HY_PARAM_SHAPES = {"hy_conv_w": [3, NHY], "hy_conv_b": [1, NHY], "hy_w1": [33, 64], "hy_w2": [64, 64],
                   "hy_w3a": [65, 2048], "hy_b1f": [64, 3], "hy_decay": [4, 512], "hy_bias_d": [2, 512]}


def _hy_host_params(inputs):
    f = lambda k: np.asarray(inputs[k], np.float32)
    p = {}
    p["hy_conv_w"] = np.ascontiguousarray(f("hy_conv_w"))
    p["hy_conv_b"] = np.ascontiguousarray(f("hy_conv_b").reshape(1, NHY))
    p["hy_w1"] = np.ascontiguousarray(f("hy_ffn_w1"))
    p["hy_w2"] = np.ascontiguousarray(f("hy_ffn_w2"))
    p["hy_w3a"] = np.ascontiguousarray(np.concatenate([f("hy_ffn_w3"), f("hy_ffn_b3").reshape(1, 2048)], axis=0))
    p["hy_b1f"] = np.ascontiguousarray(np.stack([f("hy_ffn_b1"), f("hy_ffn_b2"), f("hy_sin_freq")], axis=1))
    p["hy_decay"] = np.ascontiguousarray(f("hy_decay").reshape(4, 512))
    p["hy_bias_d"] = np.ascontiguousarray(f("hy_bias_d"))
    return p


def _sin_act(kb, out_ap, out_tile, pre, scr, np_):
    TWO_PI = 2.0 * math.pi
    MAGIC = 12582912.0
    _ts(kb, "dve", scr.ap[0:np_, :], pre.ap[0:np_, :], 1.0 / TWO_PI, MAGIC, ALU.mult, ALU.add, [pre], [scr])
    _ts(kb, "dve", scr.ap[0:np_, :], scr.ap[0:np_, :], MAGIC, None, ALU.subtract, None, [scr], [scr])
    _stt(kb, scr.ap[0:np_, :], scr.ap[0:np_, :], -TWO_PI, pre.ap[0:np_, :], ALU.mult, ALU.add, [scr, pre], [scr])
    _ts(kb, "dve", scr.ap[0:np_, :], scr.ap[0:np_, :], 3.141592, -3.141592, ALU.min, ALU.max, [scr], [scr])
    _act(kb, out_ap, scr.ap[0:np_, :], AF.Sin, [scr], [out_tile])


def hyena_phase(kb, C, P, uhy, zhyT, D_):
    dbg = HY_DBG
    ident = C["ident"]
    ucb, Abuf_all, Bbuf_all, Kf = D_["ucb"], D_["Abuf"], D_["Bbuf"], D_["Kf"]
    NG4 = dbg.get("groups", 4)
    cw = [kb.f32(NHY) for _ in range(3)]
    cb = kb.f32(NHY)
    for i in range(3):
        dma(kb, "sp", cw[i].ap[0:64, :], P["hy_conv_w"][i:i + 1, :].partition_broadcast(64), writes=cw[i].b)
        dma(kb, "sp", cw[i].ap[64:128, 0:NHY - CG], P["hy_conv_w"][i:i + 1, CG:NHY].partition_broadcast(64), writes=cw[i].b)
    dma(kb, "sp", cb.ap[0:64, :], P["hy_conv_b"].partition_broadcast(64), writes=cb.b)
    dma(kb, "sp", cb.ap[64:128, 0:NHY - CG], P["hy_conv_b"][:, CG:NHY].partition_broadcast(64), writes=cb.b)
    uin = [kb.f32(66 * CG) for _ in range(2)]
    uo = [kb.f32(64 * CG) for _ in range(2)]
    tmpc = kb.f32(64 * CG)
    uhy_b = uhy[0:L, :].rearrange("(p n) c -> p n c", n=64)
    uhy_h = uhy[2:L + 2, :].rearrange("(p n) c -> p n c", n=64)
    it = 0
    for kind in range(3):
        for gp in range(0, NG4, 2):
            npair = min(2, NG4 - gp)
            NP = 64 * npair
            c0 = kind * 512 + gp * CG
            ui, uo_ = uin[it % 2], uo[it % 2]
            it += 1
            u3 = ui.ap.rearrange("p (n c) -> p n c", c=CG)
            o3 = uo_.ap.rearrange("p (n c) -> p n c", c=CG)
            t3 = tmpc.ap.rearrange("p (n c) -> p n c", c=CG)
            for h in range(npair):
                ch = c0 + h * CG
                dma(kb, "sp", u3[h * 64:(h + 1) * 64, 0:64, :], uhy_b[:, :, ch:ch + CG], writes=ui.b)
                dma(kb, "act", u3[h * 64:(h + 1) * 64, 64:66, :], uhy_h[:, 62:64, ch:ch + CG], writes=ui.b)

            def bc(t):
                return t.ap[0:NP, c0:c0 + CG].unsqueeze(1).to_broadcast([NP, 64, CG])

            _tt(kb, "dve", o3[0:NP], u3[0:NP, 0:64, :], bc(cw[0]), ALU.mult, [ui, cw[0]], [uo_])
            _tt(kb, "pool", t3[0:NP], u3[0:NP, 1:65, :], bc(cw[1]), ALU.mult, [ui, cw[1]], [tmpc])
            _tt(kb, "dve", o3[0:NP], o3[0:NP], t3[0:NP], ALU.add, [uo_, tmpc], [uo_])
            _tt(kb, "pool", t3[0:NP], u3[0:NP, 2:66, :], bc(cw[2]), ALU.mult, [ui, cw[2]], [tmpc])
            _tt(kb, "dve", o3[0:NP], o3[0:NP], t3[0:NP], ALU.add, [uo_, tmpc], [uo_])
            _tt(kb, "dve", o3[0:NP], o3[0:NP], bc(cb), ALU.add, [uo_, cb], [uo_])
            for h in range(npair):
                dma(kb, "pool", ucb[kind, gp + h], uo_.ap[h * 64:(h + 1) * 64, :], reads=uo_.b,
                    writes=[D_["ucb_b"][kind][gp + h]])
    kb.phase_end()

    F2t = kb.f32(192)
    I2t = kb.f32(128)
    dma(kb, "sp", F2t.ap[0:64, :], P["hy_F2"], writes=F2t.b)
    dma(kb, "sp", I2t.ap, P["hy_I2"], writes=I2t.b)
    kb.persist()
    NK1 = 65
    KBATCH = [(k0, min(4, NK1 - k0)) for k0 in range(0, NK1, 4)]
    A_bs = [[Buf() for _ in range(16)] for _ in range(2)]
    B_bs = [[Buf() for _ in range(17)] for _ in range(2)]
    Gc = [kb.f32(1024) for _ in range(2)]
    Ast = [[kb.f32(512) for _ in range(2)] for _ in range(2)]
    Ain = [[kb.f32(512) for _ in range(2)] for _ in range(2)]
    cnt = {"g": 0, "a": 0, "i": 0}

    def fft_s1(src3, srcT, K, aset=0):
        Abuf, A_b = Abuf_all[aset], A_bs[aset]
        for n2b in range(16):
            gt = Gc[cnt["g"] % 2]
            cnt["g"] += 1
            dma(kb, "sp", gt.ap[0:K, :], P["hy_G"][0:K, n2b * 1024:(n2b + 1) * 1024], writes=gt.b)
            g4 = gt.ap.rearrange("p (j r k) -> p j r k", r=2, k=128)
            banks = [kb.bank(), kb.bank()]
            for j in range(4):
                for ri in range(2):
                    _mm(kb, kb.ps[0:NK1, banks[ri], j * 128:(j + 1) * 128], g4[0:K, j, ri, 0:NK1], src3[0:K, n2b * 4 + j, :],
                        [gt, srcT], [kb.psb[banks[ri]]])
            for ri in range(2):
                st_ = Ast[ri][cnt["a"] % 2]
                _cp(kb, "act" if ri == 0 else "dve", st_.ap[0:NK1, :], kb.ps[0:NK1, banks[ri], :], [kb.psb[banks[ri]]], [st_])
                dma(kb, "pool", Abuf[ri, 0:NK1, n2b * 4:(n2b + 1) * 4, :], st_.ap[0:NK1, :].rearrange("p (n c) -> p n c", c=CG),
                    reads=st_.b, writes=[A_b[n2b]])
            cnt["a"] += 1

    def fft_s2(k0, nk, aset=0):
        Abuf, A_b = Abuf_all[aset], A_bs[aset]
        tin = []
        nn = nk * CG
        for ri in range(2):
            t_ = Ain[ri][cnt["i"] % 2]
            dma(kb, "sp", t_.ap[0:64, 0:nn].rearrange("p (k c) -> p k c", c=CG),
                Abuf[ri, k0:k0 + nk, :, :].rearrange("k n c -> n k c"), reads=A_b, writes=t_.b)
            tin.append(t_)
        cnt["i"] += 1
        bre, bim = kb.bank(), kb.bank()
        c2, s2, ns2 = F2t.ap[0:64, 0:64], F2t.ap[0:64, 64:128], F2t.ap[0:64, 128:192]
        _mm(kb, kb.ps[0:64, bre, 0:nn], c2, tin[0].ap[0:64, 0:nn], [F2t, tin[0]], [kb.psb[bre]], start=True, stop=False)
        _mm(kb, kb.ps[0:64, bre, 0:nn], s2, tin[1].ap[0:64, 0:nn], [F2t, tin[1]], [kb.psb[bre]], start=False, stop=True)
        _mm(kb, kb.ps[0:64, bim, 0:nn], c2, tin[1].ap[0:64, 0:nn], [F2t, tin[1]], [kb.psb[bim]], start=True, stop=False)
        _mm(kb, kb.ps[0:64, bim, 0:nn], ns2, tin[0].ap[0:64, 0:nn], [F2t, tin[0]], [kb.psb[bim]], start=False, stop=True)
        return bre, bim

    mark = kb.off
    w1t = kb.f32(64)
    w2t = kb.f32(64)
    b1f = kb.f32(3)
    w3a = kb.f32(2048)
    negt = kb.f32(64)
    h2f = kb.f32(NFFT)
    h2b = kb.f32(NFFT)
    dma(kb, "sp", w1t.ap[0:33, :], P["hy_w1"], writes=w1t.b)
    dma(kb, "sp", w2t.ap[0:64, :], P["hy_w2"], writes=w2t.b)
    dma(kb, "sp", b1f.ap[0:64, :], P["hy_b1f"], writes=b1f.b)
    dma(kb, "sp", w3a.ap[0:65, :], P["hy_w3a"], writes=w3a.b)
    dma(kb, "sp", negt.ap, P["hy_negt"], writes=negt.b)
    kb.op("pool", lambda e: e.memset(h2f.ap, 0.0), writes=h2f.b)
    kb.op("pool", lambda e: e.memset(h2b.ap, 0.0), writes=h2b.b)
    kb.op("pool", lambda e: e.memset(h2f.ap[64:65, 0:L], 1.0), reads=h2f.b, writes=h2f.b)
    kb.op("pool", lambda e: e.memset(h2b.ap[64:65, L + 1:NFFT], 1.0), reads=h2b.b, writes=h2b.b)
    fch = [kb.f32(512) for _ in range(2)]
    pre = kb.f32(512)
    scr = kb.f32(512)
    h1c = kb.f32(512)
    for pc in range(16):
        ft = fch[pc % 2]
        dma(kb, "sp", ft.ap[0:33, :], P["hy_feats"][:, pc * 512:(pc + 1) * 512], writes=ft.b)
        b1 = kb.bank()
        _mm(kb, kb.ps[0:64, b1, :], w1t.ap[0:33, :], ft.ap[0:33, :], [w1t, ft], [kb.psb[b1]])
        _ts(kb, "dve", pre.ap[0:64, :], kb.ps[0:64, b1, :], b1f.ap[0:64, 0:1], b1f.ap[0:64, 2:3], ALU.add, ALU.mult,
            [kb.psb[b1], b1f], [pre])
        _sin_act(kb, h1c.ap[0:64, :], h1c, pre, scr, 64)
        b2 = kb.bank()
        _mm(kb, kb.ps[0:64, b2, :], w2t.ap[0:64, :], h1c.ap[0:64, :], [w2t, h1c], [kb.psb[b2]])
        _ts(kb, "dve", pre.ap[0:64, :], kb.ps[0:64, b2, :], b1f.ap[0:64, 1:2], b1f.ap[0:64, 2:3], ALU.add, ALU.mult,
            [kb.psb[b2], b1f], [pre])
        hdst = h2f if pc < 8 else h2b
        TWO_PI = 2.0 * math.pi
        MAGIC = 12582912.0
        _ts(kb, "dve", scr.ap[0:64, :], pre.ap[0:64, :], 1.0 / TWO_PI, MAGIC, ALU.mult, ALU.add, [pre], [scr])
        _ts(kb, "dve", scr.ap[0:64, :], scr.ap[0:64, :], MAGIC, None, ALU.subtract, None, [scr], [scr])
        _stt(kb, scr.ap[0:64, :], scr.ap[0:64, :], -TWO_PI, pre.ap[0:64, :], ALU.mult, ALU.add, [scr, pre], [scr])
        _ts(kb, "dve", scr.ap[0:64, :], scr.ap[0:64, :], 3.141592, -3.141592, ALU.min, ALU.max, [scr], [scr])
        _act(kb, hdst.ap[0:64, pc * 512:(pc + 1) * 512], scr.ap[0:64, :], AF.Sin, [scr], [hdst])
    kb.op("pool", lambda e: e.memset(h2b.ap[0:64, L:L + 1], 0.0), reads=h2b.b, writes=h2b.b)
    kfs = [kb.f32(64 * CG), kb.f32(64 * CG)]
    absd = kb.f32(CG)
    et = [kb.f32(512) for _ in range(2)]
    sqc = [kb.f32(512) for _ in range(2)]
    rnf = kb.f32(CG)
    kst = [[kb.f32(512) for _ in range(2)] for _ in range(2)]
    h2f3 = h2f.ap.rearrange("p (a b) -> p b a", b=64)
    h2b3 = h2b.ap.rearrange("p (a b) -> p b a", b=64)

    def gen_filter(o, g, kf):
        kf3 = kf.ap.rearrange("p (n c) -> p n c", c=CG)
        for dr in range(2):
            dma(kb, "sp", absd.ap[dr * 64:(dr + 1) * 64, :],
                P["hy_decay"][dr * 2 + o:dr * 2 + o + 1, g * CG:(g + 1) * CG].partition_broadcast(64), writes=absd.b)
        _stt(kb, absd.ap, absd.ap, -1.0, absd.ap, ALU.mult, ALU.max, [absd], [absd])
        bss = kb.bank()
        kb.reserved.add(bss)
        for n2b in range(16):
            bk = kb.bank()
            e_ = et[n2b % 2]
            sq_ = sqc[n2b % 2]
            for j in range(4):
                n2 = n2b * 4 + j
                cf = o * 512 + g * CG
                _mm(kb, kb.ps[:, bk, j * 128:(j + 1) * 128], h2f3[0:65, n2, :], w3a.ap[0:65, cf:cf + CG],
                    [h2f, w3a], [kb.psb[bk]], start=True, stop=False)
                _mm(kb, kb.ps[:, bk, j * 128:(j + 1) * 128], h2b3[0:65, n2, :], w3a.ap[0:65, 1024 + cf:1024 + cf + CG],
                    [h2b, w3a], [kb.psb[bk]], start=False, stop=True)
                _act(kb, e_.ap[:, j * 128:(j + 1) * 128], absd.ap, AF.Exp, [absd, negt], [e_], scale=negt.ap[:, n2:n2 + 1])
            _tt(kb, "dve", kf.ap[:, n2b * 512:(n2b + 1) * 512], kb.ps[:, bk, :], e_.ap, ALU.mult, [kb.psb[bk], e_], [kf])
            _tt(kb, "pool", sq_.ap, kf.ap[:, n2b * 512:(n2b + 1) * 512], kf.ap[:, n2b * 512:(n2b + 1) * 512], ALU.mult,
                [kf], [sq_])
            for j in range(4):
                _mm(kb, kb.ps[:, bss, 0:CG], C["ones"].ap, sq_.ap[:, j * 128:(j + 1) * 128], [C["ones"], sq_],
                    [kb.psb[bss]], start=(n2b == 0 and j == 0), stop=(n2b == 15 and j == 3))
        kb.reserved.discard(bss)
        _act(kb, rnf.ap, kb.ps[:, bss, 0:CG], AF.Sqrt, [kb.psb[bss]], [rnf], bias=1e-6)
        kb.op("dve", lambda e: e.reciprocal(out=rnf.ap, in_=rnf.ap), reads=rnf.b, writes=rnf.b)
        _tt(kb, "dve", kf3, kf3, rnf.ap.unsqueeze(1).to_broadcast([128, 64, CG]), ALU.mult, [kf, rnf], [kf])
        if "kf_dbg" in D_:
            dma(kb, "pool", D_["kf_dbg"][o, g], kf.ap, reads=kf.b)

    def fft_filter(o, g, kf, aset):
        kf3 = kf.ap.rearrange("p (n c) -> p n c", c=CG)
        fft_s1(kf3, kf, 128, aset)
        for k1b, (k0, nk) in enumerate(KBATCH):
            nn = nk * CG
            bre, bim = fft_s2(k0, nk, aset)
            for ri, bk_ in enumerate((bre, bim)):
                st_ = kst[ri][k1b % 2]
                _cp(kb, "act" if ri == 0 else "dve", st_.ap[0:64, 0:nn], kb.ps[0:64, bk_, 0:nn], [kb.psb[bk_]], [st_])
                dma(kb, "pool", Kf[o, g, ri, :, k0 * CG:k0 * CG + nn], st_.ap[0:64, 0:nn], reads=st_.b,
                    writes=[D_["Kf_b"][o][g][k1b]])

    items = [(o, g) for o in range(2) for g in range(NG4)]
    gen_filter(items[0][0], items[0][1], kfs[0])
    for i, (o, g) in enumerate(items):
        if i + 1 < len(items):
            gen_filter(items[i + 1][0], items[i + 1][1], kfs[(i + 1) % 2])
        fft_filter(o, g, kfs[i % 2], i % 2)
    kb.S.barrier()
    kb.off = mark

    import itertools

    def alloc_c():
        T = _NS()
        T.zbuf = kb.f32(64 * CG)
        T.Kt = [kb.f32(512) for _ in range(2)]
        T.Gp = kb.f32(768)
        T.xr, T.xi = kb.f32(512), kb.f32(512)
        T.tt = [kb.f32(512) for _ in range(4)]
        T.Yt = [kb.f32(512) for _ in range(2)]
        T.Bst = [kb.f32(512) for _ in range(2)]
        T.Bin = [kb.f32(512) for _ in range(2)]
        T.xg, T.sk, T.te, T.zo, T.dbc = [kb.f32(512) for _ in range(5)]
        T.zT = kb.bf16(L)
        return T

    def stage_c(g, T, aset):
        Bbuf, B_b = Bbuf_all[aset], B_bs[aset]
        zbuf = T.zbuf
        z3 = zbuf.ap.rearrange("p (n c) -> p n c", c=CG)
        zT3 = T.zT.ap.rearrange("c (p n) -> c p n", n=64)
        dma(kb, "sp", zbuf.ap[0:64, :], ucb[2, g], reads=[D_["ucb_b"][2][g]], writes=zbuf.b)
        for o in range(dbg.get("orders", 2)):
            for rep in range(4):
                dma(kb, "sp", T.dbc.ap[0:64, rep * CG:(rep + 1) * CG],
                    P["hy_bias_d"][o:o + 1, g * CG:(g + 1) * CG].partition_broadcast(64), writes=T.dbc.b)
            fft_s1(z3, zbuf, 64, aset)
            yield
            for k1b, (k0, nk) in enumerate(KBATCH):
                nn = nk * CG
                kt = T.Kt
                for ri in range(2):
                    dma(kb, "sp", kt[ri].ap[0:64, 0:nn], Kf[o, g, ri, :, k0 * CG:k0 * CG + nn],
                        reads=[D_["Kf_b"][o][g][k1b]], writes=kt[ri].b)
                gp = T.Gp
                dma(kb, "sp", gp.ap[0:64, 0:nk * 192], P["hy_Gp"][:, k0 * 192:(k0 + nk) * 192], writes=gp.b)
                gp4 = gp.ap.rearrange("p (j q n) -> p j q n", q=3, n=64)
                bre, bim = fft_s2(k0, nk, aset)
                xr_, xi_ = T.xr, T.xi
                _cp(kb, "act", xr_.ap[0:64, 0:nn], kb.ps[0:64, bre, 0:nn], [kb.psb[bre]], [xr_])
                _cp(kb, "act", xi_.ap[0:64, 0:nn], kb.ps[0:64, bim, 0:nn], [kb.psb[bim]], [xi_])
                yre, yim = T.Yt
                ta, tb_, tc_, td = T.tt
                _tt(kb, "dve", ta.ap[0:64, 0:nn], xr_.ap[0:64, 0:nn], kt[0].ap[0:64, 0:nn], ALU.mult, [xr_, kt[0]], [ta])
                _tt(kb, "dve", tb_.ap[0:64, 0:nn], xi_.ap[0:64, 0:nn], kt[1].ap[0:64, 0:nn], ALU.mult, [xi_, kt[1]], [tb_])
                _tt(kb, "dve", yre.ap[0:64, 0:nn], ta.ap[0:64, 0:nn], tb_.ap[0:64, 0:nn], ALU.subtract, [ta, tb_], [yre])
                _tt(kb, "pool", tc_.ap[0:64, 0:nn], xr_.ap[0:64, 0:nn], kt[1].ap[0:64, 0:nn], ALU.mult, [xr_, kt[1]], [tc_])
                _tt(kb, "pool", td.ap[0:64, 0:nn], xi_.ap[0:64, 0:nn], kt[0].ap[0:64, 0:nn], ALU.mult, [xi_, kt[0]], [td])
                _tt(kb, "pool", yim.ap[0:64, 0:nn], tc_.ap[0:64, 0:nn], td.ap[0:64, 0:nn], ALU.add, [tc_, td], [yim])
                yield
                cre, cim = kb.bank(), kb.bank()
                for j in range(nk):
                    cs_ = slice(j * 128, (j + 1) * 128)
                    _mm(kb, kb.ps[0:64, cre, cs_], gp4[0:64, j, 0, :], yre.ap[0:64, cs_], [gp, yre], [kb.psb[cre]],
                        start=True, stop=False)
                    _mm(kb, kb.ps[0:64, cre, cs_], gp4[0:64, j, 2, :], yim.ap[0:64, cs_], [gp, yim], [kb.psb[cre]],
                        start=False, stop=True)
                    _mm(kb, kb.ps[0:64, cim, cs_], gp4[0:64, j, 1, :], yre.ap[0:64, cs_], [gp, yre], [kb.psb[cim]],
                        start=True, stop=False)
                    _mm(kb, kb.ps[0:64, cim, cs_], gp4[0:64, j, 0, :], yim.ap[0:64, cs_], [gp, yim], [kb.psb[cim]],
                        start=False, stop=True)
                for ri, bk_ in enumerate((cre, cim)):
                    st_ = T.Bst[ri]
                    _cp(kb, "act" if ri == 0 else "dve", st_.ap[0:64, 0:nn], kb.ps[0:64, bk_, 0:nn], [kb.psb[bk_]], [st_])
                    dma(kb, "pool", Bbuf[ri, :, k0:k0 + nk, :], st_.ap[0:64, 0:nn].rearrange("p (k c) -> p k c", c=CG),
                        reads=st_.b, writes=[B_b[k1b]])
                yield
            for nb in range(16):
                bin_ = T.Bin
                for ri in range(2):
                    dma(kb, "sp", bin_[ri].ap[0:NK1, :].rearrange("p (n c) -> p n c", c=CG),
                        Bbuf[ri, nb * 4:(nb + 1) * 4, 0:NK1, :].rearrange("n k c -> k n c"), reads=B_b, writes=bin_[ri].b)
                xg_ = T.xg
                dma(kb, "sp", xg_.ap[0:64, :], ucb[o, g, :, nb * 512:(nb + 1) * 512], reads=[D_["ucb_b"][o][g]], writes=xg_.b)
                by = kb.bank()
                _mm(kb, kb.ps[0:64, by, :], I2t.ap[0:NK1, 0:64], bin_[0].ap[0:NK1, :], [I2t, bin_[0]], [kb.psb[by]], start=True, stop=False)
                _mm(kb, kb.ps[0:64, by, :], I2t.ap[0:NK1, 64:128], bin_[1].ap[0:NK1, :], [I2t, bin_[1]], [kb.psb[by]], start=False, stop=True)
                te = T.te
                if o == 0:
                    sk_ = T.sk
                    dma(kb, "sp", sk_.ap[0:64, :], ucb[2, g, :, nb * 512:(nb + 1) * 512], reads=[D_["ucb_b"][2][g]], writes=sk_.b)
                    _tt(kb, "pool", te.ap[0:64, :], sk_.ap[0:64, :], T.dbc.ap[0:64, :], ALU.mult, [sk_, T.dbc], [te])
                else:
                    _tt(kb, "pool", te.ap[0:64, :], zbuf.ap[0:64, nb * 512:(nb + 1) * 512], T.dbc.ap[0:64, :], ALU.mult,
                        [zbuf, T.dbc], [te])
                _tt(kb, "dve", te.ap[0:64, :], kb.ps[0:64, by, :], te.ap[0:64, :], ALU.add, [kb.psb[by], te], [te])
                if o == 0:
                    _tt(kb, "pool", zbuf.ap[0:64, nb * 512:(nb + 1) * 512], xg_.ap[0:64, :], te.ap[0:64, :], ALU.mult,
                        [xg_, te], [zbuf])
                else:
                    zo_ = T.zo
                    _tt(kb, "pool", zo_.ap[0:64, :], xg_.ap[0:64, :], te.ap[0:64, :], ALU.mult, [xg_, te], [zo_])
                    bt_ = kb.bank()
                    for j in range(4):
                        _tr(kb, kb.ps[:, bt_, j * 64:(j + 1) * 64], zo_.ap[0:64, j * 128:(j + 1) * 128],
                            _IdentView(ident), [zo_], [kb.psb[bt_]])
                    _cp(kb, "act", zT3[:, :, nb * 4:(nb + 1) * 4],
                        kb.ps[:, bt_, 0:256].rearrange("c (j p) -> c p j", p=64), [kb.psb[bt_]], [T.zT])
                yield
            if o == 0 and "z1_dbg" in D_:
                dma(kb, "pool", D_["z1_dbg"][g], zbuf.ap[0:64, :], reads=zbuf.b)
            if o == 1:
                dma(kb, "pool", zhyT[g * CG:(g + 1) * CG, :], T.zT.ap, reads=T.zT.b)

    TC = [alloc_c(), alloc_c()]
    for g0 in range(0, NG4, 2):
        gens = [stage_c(g0 + i, TC[i], i) for i in range(min(2, NG4 - g0))]
        for _ in itertools.zip_longest(*gens):
            pass


class _IdentView:
    def __init__(self, ident):
        self.ap = ident.ap[0:64, 0:64]
        self.b = ident.b
```

```python
import contextlib
import math
import numpy as np
import concourse.bass as bass
import concourse.mybir as mybir
from concourse.bass_utils import run_bass_kernel_spmd

F32 = mybir.dt.float32
BF16 = mybir.dt.bfloat16
AF = mybir.ActivationFunctionType
ALU = mybir.AluOpType
AX = mybir.AxisListType

D = 1024
L = 4096
DFF = 2816
NHY = 1536
NRW = 1952
NG = 2048
NIN = NHY + NRW + NG
RMS_EPS = 1e-6
GN_EPS = 64e-5

SEM_EPOCH = 30000
N_DMA_SLOTS = 12


class Buf:
    __slots__ = ("lw", "rd")

    def __init__(self):
        self.lw = None
        self.rd = []


class Op:
    __slots__ = ("eng", "fn", "deps", "dma", "needed", "tok", "slot", "noinst")


class Sched:
    ENGS = ("pe", "act", "dve", "pool", "sp")

    def __init__(self, nc):
        self.nc = nc
        self.streams = {e: [] for e in self.ENGS}
        self.dma_count = {e: 0 for e in self.ENGS}
        self.dma_slot_last = {}

    def op(self, eng, fn, reads=(), writes=(), dma=False, extra=(), noinst=False):
        o = Op()
        o.noinst = noinst
        o.eng = eng
        o.fn = fn
        o.dma = dma
        o.needed = False
        deps = {}
        for b in reads:
            if b.lw is not None:
                deps[id(b.lw)] = b.lw
        for b in writes:
            if b.lw is not None:
                deps[id(b.lw)] = b.lw
            for r in b.rd:
                deps[id(r)] = r
        for d in extra:
            deps[id(d)] = d
        if dma:
            k = self.dma_count[eng]
            self.dma_count[eng] += 1
            o.slot = (eng, k % N_DMA_SLOTS, k // N_DMA_SLOTS)
            prev = self.dma_slot_last.get((eng, k % N_DMA_SLOTS))
            if prev is not None:
                deps[id(prev)] = prev
            self.dma_slot_last[(eng, k % N_DMA_SLOTS)] = o
        dl = []
        for d in deps.values():
            if d is o:
                continue
            if (not d.dma) and d.eng == eng and eng == "pe" and not o.dma:
                continue
            assert not d.noinst
            d.needed = True
            dl.append(d)
        o.deps = dl
        for b in reads:
            b.rd.append(o)
        for b in writes:
            b.lw = o
            b.rd = []
        self.streams[eng].append(o)
        return o

    def barrier(self):
        lasts = list(self.dma_slot_last.values())
        for s in self.streams.values():
            for o in reversed(s):
                if not o.noinst:
                    lasts.append(o)
                    break
        for e in self.ENGS:
            self.op(e, lambda eh: None, extra=lasts, noinst=True)

    def emit(self):
        nc = self.nc
        with contextlib.ExitStack() as st:
            esem = {}
            for e in self.ENGS:
                n_sig = sum(1 for o in self.streams[e] if o.needed and not o.dma)
                n_ep = max(1, (n_sig + SEM_EPOCH - 1) // SEM_EPOCH)
                esem[e] = [st.enter_context(nc.semaphore(f"s_{e}_{i}")) for i in range(n_ep)]
            dsem = {}
            for e in self.ENGS:
                if self.dma_count[e] > 0:
                    dsem[e] = [st.enter_context(nc.semaphore(f"d_{e}_{i}")) for i in range(N_DMA_SLOTS)]
            for e in self.ENGS:
                c = 0
                for o in self.streams[e]:
                    if o.dma:
                        _, s, r = o.slot
                        o.tok = (dsem[e][s], 16 * (r + 1), ("d", e, s))
                    elif o.needed:
                        ep = c // SEM_EPOCH
                        o.tok = (esem[e][ep], (c % SEM_EPOCH) + 1, ("e", e, ep))
                        c += 1
                    else:
                        o.tok = None
            block = st.enter_context(nc.Block())
            hmap = {"pe": block.tensor, "act": block.scalar, "dve": block.vector,
                    "pool": block.gpsimd, "sp": block.sync}
            for e in self.ENGS:
                stream = self.streams[e]
                if not stream:
                    continue

                def section(eh, stream=stream):
                    waited = {}
                    for o in stream:
                        for d in o.deps:
                            sem, val, key = d.tok
                            if waited.get(key, 0) >= val:
                                continue
                            eh.wait_ge(sem, val)
                            waited[key] = val
                        ins = o.fn(eh)
                        if ins is None:
                            continue
                        if o.dma:
                            ins.then_inc(o.tok[0], 16)
                        elif o.needed:
                            ins.then_inc(o.tok[0], 1)

                hmap[e](section)


class Tile:
    def __init__(self, ap, n=1):
        self.ap = ap
        self.b = [Buf() for _ in range(n)]


class KB:
    def __init__(self, nc, st):
        self.nc = nc
        self.S = Sched(nc)
        self.arena = st.enter_context(nc.sbuf_tensor("arena", [128, 49152], F32))
        self.ps = st.enter_context(nc.psum_tensor("psum", [128, 8, 512], F32))
        self.psb = [Buf() for _ in range(8)]
        self.ps_i = 0
        self.reserved = set()
        self.base = 0
        self.off = 0
        self.rr = 0

    def f32(self, n, nb=1):
        assert self.off + n <= 49152, ("sbuf overflow", self.off, n)
        ap = self.arena[:, self.off:self.off + n]
        self.off += n
        return Tile(ap, nb)

    def bf16(self, n, nb=1):
        w = (n + 1) // 2
        assert self.off + w <= 49152, ("sbuf overflow", self.off, w)
        ap = self.arena[:, self.off:self.off + w].bitcast(BF16)[:, 0:n]
        self.off += w
        return Tile(ap, nb)

    def persist(self):
        self.base = self.off

    def phase_end(self):
        self.S.barrier()
        self.off = self.base

    def bank(self):
        while self.ps_i in self.reserved:
            self.ps_i = (self.ps_i + 1) % 8
        i = self.ps_i
        self.ps_i = (self.ps_i + 1) % 8
        return i

    def op(self, eng, fn, reads=(), writes=(), dma=False):
        return self.S.op(eng, fn, reads=reads, writes=writes, dma=dma)

    def ew_eng(self):
        self.rr += 1
        return "dve" if (self.rr % 3) else "pool"


def dma(kb, eng, out, in_, reads=(), writes=(), slow=False):
    if slow:
        return kb.op(eng, lambda e: e.dma_start(out=out, in_=in_, allow_slow_non_contiguous=True),
                     reads=reads, writes=writes, dma=True)
    return kb.op(eng, lambda e: e.dma_start(out=out, in_=in_), reads=reads, writes=writes, dma=True)


def convert_weights(kb, pairs):
    CW = 2816
    stg = [kb.f32(CW) for _ in range(3)]
    stb = [kb.bf16(CW) for _ in range(3)]
    i = 0
    engs = ("dve", "act", "dve", "act", "pool")
    for src, dst, R, C in pairs:
        for r0 in range(0, R, 128):
            rr = min(128, R - r0)
            for c0 in range(0, C, CW):
                cc = min(CW, C - c0)
                a, b = stg[i % 3], stb[i % 3]
                dma(kb, "sp", a.ap[0:rr, 0:cc], src[r0:r0 + rr, c0:c0 + cc], writes=a.b)
                eng = engs[i % 5]
                if eng == "act":
                    kb.op("act", lambda e, a=a, b=b, rr=rr, cc=cc: e.copy(out=b.ap[0:rr, 0:cc], in_=a.ap[0:rr, 0:cc]),
                          reads=a.b, writes=b.b)
                else:
                    kb.op(eng, lambda e, a=a, b=b, rr=rr, cc=cc: e.tensor_copy(out=b.ap[0:rr, 0:cc], in_=a.ap[0:rr, 0:cc]),
                          reads=a.b, writes=b.b)
                dma(kb, "pool" if i % 2 else "act", dst[r0:r0 + rr, c0:c0 + cc], b.ap[0:rr, 0:cc], reads=b.b)
                i += 1


def rmsnorm_T(kb, C, xt, g, out_ap_fn, sq, rstd):
    x3 = xt.ap
    sq3 = sq.ap.rearrange("p (a b) -> p a b", b=512)
    bk = kb.bank()
    for kc in range(8):
        kb.op("act", lambda e, kc=kc: e.activation(out=sq3[:, kc, :], in_=x3[:, kc, :], func=AF.Square),
              reads=xt.b, writes=[sq.b[kc]])
    for kc in range(8):
        kb.op("pe", lambda e, kc=kc: e.matmul(kb.ps[:, bk, :], lhsT=C["ones"].ap, rhs=sq3[:, kc, :],
                                              start=(kc == 0), stop=(kc == 7)),
              reads=[sq.b[kc]] + C["ones"].b, writes=[kb.psb[bk]])
    kb.op("act", lambda e: e.activation(out=rstd.ap, in_=kb.ps[:, bk, :], func=AF.Sqrt, scale=1.0 / D, bias=RMS_EPS),
          reads=[kb.psb[bk]], writes=rstd.b)
    kb.op("dve", lambda e: e.reciprocal(out=rstd.ap, in_=rstd.ap), reads=rstd.b, writes=rstd.b)
    outs = []
    for kc in range(8):
        o = out_ap_fn(kc)
        outs.append(o)
    return outs


def ffn_phase(kb, C, xin, xout, gname, wgu, wdn):
    TQ = 1024
    xn = kb.bf16(8 * TQ, 8)
    G = kb.bf16(22 * TQ, 22)
    xn3 = xn.ap.rearrange("p (a b) -> p a b", b=TQ)
    G3 = G.ap.rearrange("p (a b) -> p a b", b=TQ)
    xts = [kb.f32(8 * 512) for _ in range(2)]
    sq = kb.f32(8 * 512, 8)
    rstd = kb.f32(512)
    wg = [kb.bf16(8 * 256) for _ in range(2)]
    wu = [kb.bf16(8 * 256) for _ in range(2)]
    wd = [kb.bf16(22 * 512) for _ in range(2)]
    sg = [kb.f32(512) for _ in range(2)]
    xr = [kb.f32(512) for _ in range(2)]
    xo = [kb.f32(512) for _ in range(2)]
    g = C[gname]
    xin3 = xin.rearrange("(kc p) t -> p kc t", p=128)
    wgu3 = wgu.rearrange("(kc p) f -> p kc f", p=128)
    wdn3 = wdn.rearrange("(fc p) d -> p fc d", p=128)
    cnt = 0
    for q in range(L // TQ):
        for t in range(2):
            tok = q * TQ + t * 512
            xt = xts[t]
            x3 = xt.ap.rearrange("p (a b) -> p a b", b=512)
            xt3 = Tile(x3)
            xt3.b = xt.b
            dma(kb, "sp", x3, xin3[:, :, tok:tok + 512], writes=xt.b)
            rmsnorm_T(kb, C, xt3, g, lambda kc: None, sq, rstd)
            for kc in range(8):
                eng = "dve"
                kb.op(eng, lambda e, kc=kc, x3=x3, t=t: e.scalar_tensor_tensor(
                    out=xn3[:, kc, t * 512:(t + 1) * 512], in0=x3[:, kc, :], scalar=g.ap[:, kc:kc + 1],
                    in1=rstd.ap, op0=ALU.mult, op1=ALU.mult),
                    reads=xt.b + rstd.b + g.b, writes=[xn.b[kc]])
        for s in range(11):
            a, b = wg[s % 2], wu[s % 2]
            a3 = a.ap.rearrange("p (a b) -> p a b", b=256)
            b3 = b.ap.rearrange("p (a b) -> p a b", b=256)
            dma(kb, "sp", a3, wgu3[:, :, s * 256:(s + 1) * 256], writes=a.b)
            dma(kb, "sp", b3, wgu3[:, :, DFF + s * 256:DFF + (s + 1) * 256], writes=b.b)
            for fcl in range(2):
                fc = s * 2 + fcl
                for t in range(2):
                    bg = kb.bank()
                    bu = kb.bank()
                    for kc in range(8):
                        kb.op("pe", lambda e, kc=kc, a3=a3, fcl=fcl, t=t, bg=bg: e.matmul(
                            kb.ps[:, bg, :], lhsT=a3[:, kc, fcl * 128:(fcl + 1) * 128],
                            rhs=xn3[:, kc, t * 512:(t + 1) * 512], start=(kc == 0), stop=(kc == 7)),
                            reads=a.b + [xn.b[kc]], writes=[kb.psb[bg]])
                    for kc in range(8):
                        kb.op("pe", lambda e, kc=kc, b3=b3, fcl=fcl, t=t, bu=bu: e.matmul(
                            kb.ps[:, bu, :], lhsT=b3[:, kc, fcl * 128:(fcl + 1) * 128],
                            rhs=xn3[:, kc, t * 512:(t + 1) * 512], start=(kc == 0), stop=(kc == 7)),
                            reads=b.b + [xn.b[kc]], writes=[kb.psb[bu]])
                    sgt = sg[cnt % 2]
                    cnt += 1
                    kb.op("act", lambda e, sgt=sgt, bg=bg: e.activation(out=sgt.ap, in_=kb.ps[:, bg, :], func=AF.Silu),
                          reads=[kb.psb[bg]], writes=sgt.b)
                    kb.op("dve", lambda e, sgt=sgt, bu=bu, fc=fc, t=t: e.tensor_tensor(
                        out=G3[:, fc, t * 512:(t + 1) * 512], in0=kb.ps[:, bu, :], in1=sgt.ap, op=ALU.mult),
                        reads=[kb.psb[bu]] + sgt.b, writes=[G.b[fc]])
        for ds in range(2):
            w = wd[ds % 2]
            w3 = w.ap.rearrange("p (a b) -> p a b", b=512)
            dma(kb, "sp", w3, wdn3[:, :, ds * 512:(ds + 1) * 512], writes=w.b)
            for dcl in range(4):
                dc = ds * 4 + dcl
                for t in range(2):
                    tok = q * TQ + t * 512
                    bo = kb.bank()
                    xrt, xot = xr[cnt % 2], xo[cnt % 2]
                    cnt += 1
                    dma(kb, "sp", xrt.ap, xin[dc * 128:(dc + 1) * 128, tok:tok + 512], writes=xrt.b)
                    for fc in range(22):
                        kb.op("pe", lambda e, fc=fc, w3=w3, dcl=dcl, t=t, bo=bo: e.matmul(
                            kb.ps[:, bo, :], lhsT=w3[:, fc, dcl * 128:(dcl + 1) * 128],
                            rhs=G3[:, fc, t * 512:(t + 1) * 512], start=(fc == 0), stop=(fc == 21)),
                            reads=w.b + [G.b[fc]], writes=[kb.psb[bo]])
                    kb.op("dve", lambda e, xrt=xrt, xot=xot, bo=bo: e.scalar_tensor_tensor(
                        out=xot.ap, in0=kb.ps[:, bo, :], scalar=0.5, in1=xrt.ap, op0=ALU.mult, op1=ALU.add),
                        reads=[kb.psb[bo]] + xrt.b, writes=xot.b)
                    dma(kb, "pool", xout[dc * 128:(dc + 1) * 128, tok:tok + 512], xot.ap, reads=xot.b)


def final_norm_phase(kb, C, xin, out, gname):
    g = C[gname]
    xts = [kb.f32(8 * 512) for _ in range(2)]
    ots = [kb.f32(8 * 512) for _ in range(2)]
    sq = kb.f32(8 * 512, 8)
    rstd = kb.f32(512)
    xin3 = xin.rearrange("(kc p) t -> p kc t", p=128)
    out3 = out.rearrange("(kc p) t -> p kc t", p=128)
    for t in range(L // 512):
        xt, ot = xts[t % 2], ots[t % 2]
        x3 = xt.ap.rearrange("p (a b) -> p a b", b=512)
        o3 = ot.ap.rearrange("p (a b) -> p a b", b=512)
        xt3 = Tile(x3)
        xt3.b = xt.b
        dma(kb, "sp", x3, xin3[:, :, t * 512:(t + 1) * 512], writes=xt.b)
        rmsnorm_T(kb, C, xt3, g, lambda kc: None, sq, rstd)
        for kc in range(8):
            eng = "dve"
            kb.op(eng, lambda e, kc=kc, x3=x3, o3=o3: e.scalar_tensor_tensor(
                out=o3[:, kc, :], in0=x3[:, kc, :], scalar=g.ap[:, kc:kc + 1], in1=rstd.ap,
                op0=ALU.mult, op1=ALU.mult), reads=xt.b + rstd.b + g.b, writes=ot.b)
        dma(kb, "pool", out3[:, :, t * 512:(t + 1) * 512], o3, reads=ot.b)


def inproj_phase(kb, C, x1T, win, uhy, urwT, gT):
    TQ = 1024
    g = C["mix_norm"]
    xn = kb.bf16(8 * TQ, 8)
    xn3 = xn.ap.rearrange("p (a b) -> p a b", b=TQ)
    xts = [kb.f32(8 * 512) for _ in range(2)]
    sq = kb.f32(8 * 512, 8)
    rstd = kb.f32(512)
    why = kb.bf16(8 * NHY)
    why3 = why.ap.rearrange("p (a b) -> p a b", b=NHY)
    slabs = [kb.bf16(8 * 512) for _ in range(2)]
    stg = [kb.f32(512) for _ in range(4)]
    zero = kb.f32(NHY)
    x1T3 = x1T.rearrange("(kc p) t -> p kc t", p=128)
    win3 = win.rearrange("(kc p) f -> p kc f", p=128)
    kb.op("pool", lambda e: e.memset(zero.ap, 0.0), writes=zero.b)
    dma(kb, "sp", uhy[0:1, :], zero.ap[0:1, :], reads=zero.b)
    dma(kb, "sp", uhy[L + 1:L + 2, :], zero.ap[0:1, :], reads=zero.b)
    for r0 in range(0, NRW, 128):
        rr = min(128, NRW - r0)
        dma(kb, "sp", urwT[r0:r0 + rr, 0:1], zero.ap[0:rr, 0:1], reads=zero.b, slow=True)
        dma(kb, "sp", urwT[r0:r0 + rr, L + 1:L + 2], zero.ap[0:rr, 0:1], reads=zero.b, slow=True)
    dma(kb, "sp", why3, win3[:, :, 0:NHY], writes=why.b)
    cnt = 0
    for q in range(L // TQ):
        for t in range(2):
            tok = q * TQ + t * 512
            xt = xts[t]
            x3 = xt.ap.rearrange("p (a b) -> p a b", b=512)
            xt3 = Tile(x3)
            xt3.b = xt.b
            dma(kb, "sp", x3, x1T3[:, :, tok:tok + 512], writes=xt.b)
            rmsnorm_T(kb, C, xt3, g, lambda kc: None, sq, rstd)
            for kc in range(8):
                kb.op("dve", lambda e, kc=kc, x3=x3, t=t: e.scalar_tensor_tensor(
                    out=xn3[:, kc, t * 512:(t + 1) * 512], in0=x3[:, kc, :], scalar=g.ap[:, kc:kc + 1],
                    in1=rstd.ap, op0=ALU.mult, op1=ALU.mult),
                    reads=xt.b + rstd.b + g.b, writes=[xn.b[kc]])
        for (dst, col0, ncols, gate, coff) in ((urwT, NHY, NRW, False, 1), (gT, NHY + NRW, NG, True, 0)):
            for s0 in range(0, ncols, 512):
                cw = min(512, ncols - s0)
                sl = slabs[cnt % 2]
                sl3 = sl.ap.rearrange("p (a b) -> p a b", b=512)
                dma(kb, "sp", sl3[:, :, 0:cw], win3[:, :, col0 + s0:col0 + s0 + cw], writes=sl.b)
                for c0 in range(0, cw, 128):
                    m = min(128, cw - c0)
                    for t in range(2):
                        tok = q * TQ + t * 512
                        bk = kb.bank()
                        for kc in range(8):
                            kb.op("pe", lambda e, kc=kc, sl3=sl3, c0=c0, m=m, t=t, bk=bk: e.matmul(
                                kb.ps[0:m, bk, :], lhsT=sl3[:, kc, c0:c0 + m],
                                rhs=xn3[:, kc, t * 512:(t + 1) * 512], start=(kc == 0), stop=(kc == 7)),
                                reads=sl.b + [xn.b[kc]], writes=[kb.psb[bk]])
                        sg_ = stg[cnt % 4]
                        cnt += 1
                        if gate:
                            kb.op("act", lambda e, sg_=sg_, bk=bk, m=m: e.activation(
                                out=sg_.ap[0:m, :], in_=kb.ps[0:m, bk, :], func=AF.Sigmoid),
                                reads=[kb.psb[bk]], writes=sg_.b)
                        else:
                            kb.op("dve", lambda e, sg_=sg_, bk=bk, m=m: e.tensor_copy(
                                out=sg_.ap[0:m, :], in_=kb.ps[0:m, bk, :]),
                                reads=[kb.psb[bk]], writes=sg_.b)
                        r0 = s0 + c0
                        dma(kb, "pool", dst[r0:r0 + m, coff + tok:coff + tok + 512], sg_.ap[0:m, :], reads=sg_.b)
        for tb in range(TQ // 128):
            tok = q * TQ + tb * 128
            for cs in range(3):
                bk = kb.bank()
                for kc in range(8):
                    kb.op("pe", lambda e, kc=kc, tb=tb, cs=cs, bk=bk: e.matmul(
                        kb.ps[:, bk, :], lhsT=xn3[:, kc, tb * 128:(tb + 1) * 128],
                        rhs=why3[:, kc, cs * 512:(cs + 1) * 512], start=(kc == 0), stop=(kc == 7)),
                        reads=why.b + [xn.b[kc]], writes=[kb.psb[bk]])
                sg_ = stg[cnt % 4]
                cnt += 1
                kb.op("act", lambda e, sg_=sg_, bk=bk: e.copy(out=sg_.ap, in_=kb.ps[:, bk, :]),
                      reads=[kb.psb[bk]], writes=sg_.b)
                dma(kb, "pool", uhy[1 + tok:1 + tok + 128, cs * 512:(cs + 1) * 512], sg_.ap, reads=sg_.b)


def _bl(*tiles):
    out = []
    for t in tiles:
        out.extend(t.b if isinstance(t, Tile) else [t])
    return out


def _tt(kb, eng, out, a, b, op, R, W):
    return kb.op(eng, lambda e: e.tensor_tensor(out=out, in0=a, in1=b, op=op), reads=_bl(*R), writes=_bl(*W))


def _ts(kb, eng, out, a, s1, s2, op0, op1, R, W):
    if op1 is None:
        return kb.op(eng, lambda e: e.tensor_scalar(out=out, in0=a, scalar1=s1, scalar2=None, op0=op0),
                     reads=_bl(*R), writes=_bl(*W))
    return kb.op(eng, lambda e: e.tensor_scalar(out=out, in0=a, scalar1=s1, scalar2=s2, op0=op0, op1=op1),
                 reads=_bl(*R), writes=_bl(*W))


def _stt(kb, out, a, s, b, op0, op1, R, W):
    return kb.op("dve", lambda e: e.scalar_tensor_tensor(out=out, in0=a, scalar=s, in1=b, op0=op0, op1=op1),
                 reads=_bl(*R), writes=_bl(*W))


def _act(kb, out, a, func, R, W, scale=1.0, bias=0.0):
    return kb.op("act", lambda e: e.activation(out=out, in_=a, func=func, scale=scale, bias=bias),
                 reads=_bl(*R), writes=_bl(*W))


def _cp(kb, eng, out, a, R, W):
    if eng == "act":
        return kb.op("act", lambda e: e.copy(out=out, in_=a), reads=_bl(*R), writes=_bl(*W))
    return kb.op(eng, lambda e: e.tensor_copy(out=out, in_=a), reads=_bl(*R), writes=_bl(*W))


def _mm(kb, out, lhsT, rhs, R, W, start=True, stop=True):
    return kb.op("pe", lambda e: e.matmul(out, lhsT=lhsT, rhs=rhs, start=start, stop=stop),
                 reads=_bl(*R), writes=_bl(*W))


def _tr(kb, out, in_, ident, R, W):
    return kb.op("pe", lambda e: e.transpose(out, in_, ident.ap), reads=_bl(*R) + ident.b, writes=_bl(*W))


KAPPA = math.exp(-0.5)
SC = 256
NCH = SC // 64


RW_DBG = {}


class _NS:
    pass


def rwkv_phase(kb, C, P, urwT, yrwT, RD):
    import itertools
    dbg = RW_DBG
    ident, bd64 = C["ident"], C["bd64"]
    W = SC
    WH = W + 2
    NI = NCH * 2
    yT, bT = RD["yT"], RD["bT"]
    yT_b = [[[Buf() for _ in range(L // W)] for _ in range(4)] for _ in range(2)]
    bT_b = [[[Buf() for _ in range(L // W)] for _ in range(4)] for _ in range(2)]
    w2t = kb.f32(1024)
    a2t = kb.f32(1024)
    g2a = kb.f32(512)
    g2b = kb.f32(512)
    mNB = [kb.f32(512), kb.f32(512)]
    mAAB = [kb.f32(512), kb.f32(512)]
    I8 = kb.f32(512)
    vecs = kb.f32(20)
    w0t = kb.f32(8)
    a0t = kb.f32(8)
    mu_rkv = kb.f32(24)
    mu_wa = kb.f32(8)
    mu_g = kb.f32(4)
    dma(kb, "sp", w2t.ap[0:64, :], P["rw_w2t"], writes=w2t.b)
    dma(kb, "sp", a2t.ap[0:64, :], P["rw_a2t"], writes=a2t.b)
    dma(kb, "sp", g2a.ap, P["rw_g2"][0:128, :], writes=g2a.b)
    dma(kb, "sp", g2b.ap[0:32, :], P["rw_g2"][128:160, :], writes=g2b.b)
    dma(kb, "sp", mNB[0].ap[0:64, :], P["mNBf"], writes=mNB[0].b)
    dma(kb, "sp", mNB[1].ap[0:64, :], P["mNBb"], writes=mNB[1].b)
    dma(kb, "sp", mAAB[0].ap[0:64, :], P["mAABf"], writes=mAAB[0].b)
    dma(kb, "sp", mAAB[1].ap[0:64, :], P["mAABb"], writes=mAAB[1].b)
    dma(kb, "sp", I8.ap[0:64, :], P["I8"], writes=I8.b)
    rmask = kb.f32(SC)
    dma(kb, "sp", rmask.ap, P["rmask"], writes=rmask.b)
    dma(kb, "sp", vecs.ap, P["rw_vecs"], writes=vecs.b)
    for fc in range(4):
        dma(kb, "sp", w0t.ap[:, fc * 2:fc * 2 + 2], P["rw_w0T"][fc * 128:(fc + 1) * 128, :], writes=w0t.b)
        dma(kb, "sp", a0t.ap[:, fc * 2:fc * 2 + 2], P["rw_a0T"][fc * 128:(fc + 1) * 128, :], writes=a0t.b)
        for kind in range(3):
            o = (kind * 4 + fc) * 2
            r0 = kind * 512 + fc * 128
            dma(kb, "sp", mu_rkv.ap[:, o:o + 2], P["rw_muT"][r0:r0 + 128, :], writes=mu_rkv.b)
    for i in range(4):
        r0 = 1536 + i * 64
        dma(kb, "sp", mu_wa.ap[0:64, i * 2:i * 2 + 2], P["rw_muT"][r0:r0 + 64, :], writes=mu_wa.b)
    dma(kb, "sp", mu_g.ap[:, 0:2], P["rw_muT"][1792:1920, :], writes=mu_g.b)
    dma(kb, "sp", mu_g.ap[0:32, 2:4], P["rw_muT"][1920:1952, :], writes=mu_g.b)

    def vec(i, fc):
        return vecs.ap[:, i * 4 + fc:i * 4 + fc + 1]

    def v3(t, b=64):
        return t.ap.rearrange("p (a b) -> p a b", b=b)

    def z4(t):
        return t.ap.rearrange("p (c h t) -> p c h t", h=2, t=64)

    N3 = lambda t: t.ap.rearrange("p (i t) -> p i t", t=64)

    def alloc_stream():
        T = _NS()
        T.ld = [kb.f32(WH + 6) for _ in range(5)]
        T.tp = [kb.f32(W) for _ in range(23)]
        T.ar = kb.f32(2 * W)
        T.tok = [kb.f32(NCH * 128) for _ in range(4)]
        T.NBt = kb.f32(NCH * 2 * 128)
        T.KBt = kb.f32(NCH * 2 * 128)
        T.AAB = kb.bf16(NI * 64)
        T.N0 = kb.bf16(NI * 64)
        T.Nk = [kb.bf16(NI * 64) for _ in range(2)]
        T.Ak = [kb.bf16(NI * 64) for _ in range(2)]
        T.Pk = [kb.bf16(NI * 64) for _ in range(2)]
        T.Pf = kb.f32(NI * 64)
        T.AKV = kb.f32(NI * 64)
        T.W2 = kb.f32(NI * 64)
        T.Hs = [kb.f32(128), kb.f32(128)]
        T.Usb = kb.f32(128)
        T.gTt = kb.f32(NCH)
        T.ysc = kb.f32(W)
        T.bsc = kb.f32(W)
        T.pad = [kb.f32(NI * 64) for _ in range(5)]
        for zt in T.pad:
            kb.op("pool", lambda e, zt=zt: e.memset(zt.ap, 0.0), writes=zt.b)
        return T

    TS = [alloc_stream(), alloc_stream()]

    def shift(T, u, out, mu0, mu1, np_=128, seng="pool"):
        t1, t2 = T.tp[5], T.tp[6]
        mus = [mu_rkv, mu_wa, mu_g]
        _tt(kb, seng, t1.ap[0:np_, :], u.ap[0:np_, 0:W], u.ap[0:np_, 1:W + 1], ALU.subtract, [u], [t1])
        _stt(kb, out.ap[0:np_, :], t1.ap[0:np_, :], mu0, u.ap[0:np_, 1:W + 1], ALU.mult, ALU.add, [t1, u] + mus, [out])
        _tt(kb, seng, t2.ap[0:np_, :], u.ap[0:np_, 2:W + 2], u.ap[0:np_, 1:W + 1], ALU.subtract, [u], [t2])
        _stt(kb, out.ap[0:np_, :], t2.ap[0:np_, :], mu1, out.ap[0:np_, :], ALU.mult, ALU.add, [t2, out] + mus, [out])

    def sc_gen(fc, d, T):
        hcur = 0
        kb.op("pool", lambda e: e.memset(T.Hs[0].ap, 0.0), writes=T.Hs[0].b)
        sc_list = list(range(L // W)) if d == 0 else list(range(L // W - 1, -1, -1))
        sc_list = sc_list[:dbg.get('nsc', len(sc_list))]
        (r, k, v, wdx, adx, t1, t2, sg, lr, kraw, rn, kk, kd, bv, cA, cB, ginc, gexc, ginv, gts,
         bt, bh, kh) = T.tp
        tw, tmp = t1, t2
        ar, tok, NBt, KBt, AAB, Nk, Ak, Pk, AKV, W2 = T.ar, T.tok, T.NBt, T.KBt, T.AAB, T.Nk, T.Ak, T.Pk, T.AKV, T.W2
        N0, Pf = T.N0, T.Pf
        btz, ktz, atz, rz, W1Tz = T.pad
        Hs, Usb, gTt, ysc, bsc = T.Hs, T.Usb, T.gTt, T.ysc, T.bsc
        ar4 = ar.ap.rearrange("p (c q t) -> p c q t", q=2, t=64)
        NB4 = NBt.ap.rearrange("p (i q t) -> p i q t", q=2, t=64)
        KB4 = KBt.ap.rearrange("p (i q t) -> p i q t", q=2, t=64)
        tok3 = [t.ap.rearrange("p (c f) -> p c f", f=128) for t in tok]
        for sci, sc in enumerate(sc_list):
            t0 = sc * W
            (ur, uk, uv, uw, ua) = T.ld
            dma(kb, "sp", uw.ap[0:64, 0:WH], urwT[1536 + d * 64:1536 + (d + 1) * 64, t0:t0 + WH], writes=uw.b)
            dma(kb, "sp", ua.ap[0:64, 0:WH], urwT[1664 + d * 64:1664 + (d + 1) * 64, t0:t0 + WH], writes=ua.b)
            dma(kb, "sp", uk.ap[:, 0:WH], urwT[512 + fc * 128:512 + (fc + 1) * 128, t0:t0 + WH], writes=uk.b)
            dma(kb, "sp", ur.ap[:, 0:WH], urwT[fc * 128:(fc + 1) * 128, t0:t0 + WH], writes=ur.b)
            dma(kb, "sp", uv.ap[:, 0:WH], urwT[1024 + fc * 128:1024 + (fc + 1) * 128, t0:t0 + WH], writes=uv.b)

            def mu3(kind):
                o = (kind * 4 + fc) * 2
                return mu_rkv.ap[:, o:o + 1], mu_rkv.ap[:, o + 1:o + 2]

            shift(T, uw, wdx, mu_wa.ap[0:64, d * 2:d * 2 + 1], mu_wa.ap[0:64, d * 2 + 1:d * 2 + 2], 64, "dve")
            yield
            shift(T, ua, adx, mu_wa.ap[0:64, 4 + d * 2:5 + d * 2], mu_wa.ap[0:64, 5 + d * 2:6 + d * 2], 64, "dve")
            yield
            shift(T, uk, k, *mu3(1))
            yield
            shift(T, ur, r, *mu3(0))
            yield
            shift(T, uv, v, *mu3(2))
            yield
            _act(kb, tw.ap[0:64, :], wdx.ap[0:64, :], AF.Tanh, [wdx], [tw])
            b1 = kb.bank()
            _mm(kb, kb.ps[:, b1, 0:W], w2t.ap[0:64, d * 512 + fc * 128:d * 512 + (fc + 1) * 128], tw.ap[0:64, :],
                [w2t, tw], [kb.psb[b1]])
            _act(kb, sg.ap, kb.ps[:, b1, 0:W], AF.Sigmoid, [kb.psb[b1], w0t], [sg],
                 bias=w0t.ap[:, fc * 2 + d:fc * 2 + d + 1])
            b2 = kb.bank()
            _mm(kb, kb.ps[:, b2, 0:W], a2t.ap[0:64, d * 512 + fc * 128:d * 512 + (fc + 1) * 128], adx.ap[0:64, :],
                [a2t, adx], [kb.psb[b2]])
            _act(kb, lr.ap, kb.ps[:, b2, 0:W], AF.Sigmoid, [kb.psb[b2], a0t], [lr],
                 bias=a0t.ap[:, fc * 2 + d:fc * 2 + d + 1])
            yield
            _act(kb, kraw.ap, k.ap, AF.Square, [k, vecs], [kraw], scale=vec(0, fc))
            b3 = kb.bank()
            _mm(kb, kb.ps[:, b3, 0:W], bd64.ap, kraw.ap, [bd64, kraw], [kb.psb[b3]])
            _act(kb, rn.ap, kb.ps[:, b3, 0:W], AF.Sqrt, [kb.psb[b3]], [rn])
            _ts(kb, "dve", rn.ap, rn.ap, 1e-12, None, ALU.max, None, [rn], [rn])
            kb.op("dve", lambda e: e.reciprocal(out=rn.ap, in_=rn.ap), reads=rn.b, writes=rn.b)
            _stt(kb, kk.ap, k.ap, vec(0, fc), rn.ap, ALU.mult, ALU.mult, [k, vecs, rn], [kk])
            yield
            _ts(kb, "dve", tmp.ap, lr.ap, -1.0, vec(1, fc), ALU.add, ALU.mult, [lr, vecs], [tmp])
            _stt(kb, kd.ap, tmp.ap, 1.0, k.ap, ALU.add, ALU.mult, [tmp, k], [kd])
            _tt(kb, "pool", bv.ap, kk.ap, lr.ap, ALU.mult, [kk, lr], [bv])
            _stt(kb, bsc.ap, r.ap, vec(2, fc), kd.ap, ALU.mult, ALU.mult, [r, kd, vecs], [bsc])
            dma(kb, "pool", bT[d, fc * 128:(fc + 1) * 128, t0:t0 + W], bsc.ap, reads=bsc.b, writes=[bT_b[d][fc][sc]])
            if d == 0:
                kb.op("dve", lambda e: e.tensor_tensor_scan(out=cB.ap, data0=rmask.ap, data1=sg.ap, initial=0.0,
                                                           op0=ALU.mult, op1=ALU.add),
                      reads=_bl(rmask, sg), writes=cB.b)
            else:
                kb.op("dve", lambda e: e.tensor_tensor_scan(out=cA.ap, data0=rmask.ap, data1=sg.ap, initial=0.0,
                                                           op0=ALU.mult, op1=ALU.add),
                      reads=_bl(rmask, sg), writes=cA.b)
                pre3 = v3(cA)
                _tt(kb, "dve", v3(cB), pre3, pre3[:, :, 63:64].to_broadcast([128, NCH, 64]), ALU.subtract, [cA], [cB])
                _tt(kb, "dve", cB.ap, sg.ap, cB.ap, ALU.subtract, [sg, cB], [cB])
            yield
            cs = cB
            cs3 = v3(cs)
            ti = 63 if d == 0 else 0
            totb = cs3[:, :, ti:ti + 1].to_broadcast([128, NCH, 64])
            _act(kb, ginc.ap, cs.ap, AF.Exp, [cs], [ginc], scale=-KAPPA)
            _act(kb, ginv.ap, cs.ap, AF.Exp, [cs], [ginv], scale=KAPPA)
            _tt(kb, "dve", tmp.ap, cs.ap, sg.ap, ALU.subtract, [cs, sg], [tmp])
            _act(kb, gexc.ap, tmp.ap, AF.Exp, [tmp], [gexc], scale=-KAPPA)
            _tt(kb, "dve", v3(cA), cs3, totb, ALU.subtract, [cs], [cA])
            _act(kb, gts.ap, cA.ap, AF.Exp, [cA], [gts], scale=KAPPA)
            _act(kb, gTt.ap, cs3[:, :, ti], AF.Exp, [cs], [gTt], scale=-KAPPA)
            yield
            _stt(kb, ar4[:, :, 0, :], v3(kk), -1.0, v3(gexc), ALU.mult, ALU.mult, [kk, gexc], [ar])
            _tt(kb, "pool", ar4[:, :, 1, :], v3(r), v3(ginc), ALU.mult, [r, ginc], [ar])
            _tt(kb, "dve", bt.ap, bv.ap, ginv.ap, ALU.mult, [bv, ginv], [bt])
            yield
            for h2 in range(2):
                ps_ = slice(h2 * 64, (h2 + 1) * 64)
                _stt(kb, z4(atz)[ps_, :, h2, :], v3(kk)[ps_], -1.0, v3(gexc)[ps_], ALU.mult, ALU.mult, [kk, gexc], [atz])
                _tt(kb, "pool", z4(rz)[ps_, :, h2, :], v3(r)[ps_], v3(ginc)[ps_], ALU.mult, [r, ginc], [rz])
                _tt(kb, "dve", z4(btz)[ps_, :, h2, :], v3(bv)[ps_], v3(ginv)[ps_], ALU.mult, [bv, ginv], [btz])
                _tt(kb, "pool", z4(ktz)[ps_, :, h2, :], v3(kd)[ps_], v3(ginv)[ps_], ALU.mult, [kd, ginv], [ktz])
            yield
            _tt(kb, "pool", bh.ap, bv.ap, gts.ap, ALU.mult, [bv, gts], [bh])
            _tt(kb, "dve", kh.ap, kd.ap, gts.ap, ALU.mult, [kd, gts], [kh])
            yield
            for qi, (srct, fn) in enumerate(((ar, lambda c: ar4[:, c, 0, :]), (bh, lambda c: bh.ap[:, c * 64:(c + 1) * 64]),
                                            (kh, lambda c: kh.ap[:, c * 64:(c + 1) * 64]),
                                            (v, lambda c: v.ap[:, c * 64:(c + 1) * 64]))):
                bk = kb.bank()
                for c in range(NCH):
                    _tr(kb, kb.ps[0:64, bk, c * 128:(c + 1) * 128], fn(c), ident, [srct], [kb.psb[bk]])
                _cp(kb, "act" if qi % 2 else "dve", tok[qi].ap[0:64, :], kb.ps[0:64, bk, :], [kb.psb[bk]], [tok[qi]])
                if qi % 2 == 1:
                    yield
            yield
            for (lt, dstt) in ((btz, NBt), (ktz, KBt)):
                for half in range(2):
                    bk = kb.bank()
                    for ii in range(4):
                        i = half * 4 + ii
                        c, h2 = i // 2, i % 2
                        _mm(kb, kb.ps[0:64, bk, ii * 128:(ii + 1) * 128], z4(lt)[:, c, h2, :],
                            ar.ap[:, c * 128:(c + 1) * 128], [lt, ar], [kb.psb[bk]])
                    _tt(kb, "dve", dstt.ap[0:64, half * 512:(half + 1) * 512], kb.ps[0:64, bk, :],
                        mNB[d].ap[0:64, :], ALU.mult, [kb.psb[bk], mNB[d]], [dstt])
                yield
            bk = kb.bank()
            for i in range(NI):
                c, h2 = i // 2, i % 2
                _mm(kb, kb.ps[0:64, bk, i * 64:(i + 1) * 64], z4(atz)[:, c, h2, :],
                    bt.ap[:, c * 64:(c + 1) * 64], [atz, bt], [kb.psb[bk]])
            _tt(kb, "dve", AAB.ap[0:64, :], kb.ps[0:64, bk, :], mAAB[d].ap[0:64, :], ALU.mult,
                [kb.psb[bk], mAAB[d]], [AAB])
            _tt(kb, "dve", N3(Pk[0])[0:64], NB4[0:64, :, 0, :], N3(I8)[0:64], ALU.add, [NBt, I8], [Pk[0]])
            _cp(kb, "act", N3(N0)[0:64], NB4[0:64, :, 0, :], [NBt], [N0])
            yield
            curN = lambda i: N3(N0)[0:64, i, :]
            curNt = N0
            curA = AAB
            pc = 0
            for lev in range(5):
                Nn, An = Nk[lev % 2], Ak[lev % 2]
                bA = kb.bank()
                for i in range(NI):
                    _mm(kb, kb.ps[0:64, bA, i * 64:(i + 1) * 64], curN(i), N3(curA)[0:64, i, :], [curNt, curA], [kb.psb[bA]])
                if lev < 4:
                    bN = kb.bank()
                    for i in range(NI):
                        _mm(kb, kb.ps[0:64, bN, i * 64:(i + 1) * 64], N3(curA)[0:64, i, :], curN(i), [curNt, curA], [kb.psb[bN]])
                _cp(kb, "act", An.ap[0:64, :], kb.ps[0:64, bA, :], [kb.psb[bA]], [An])
                if lev < 4:
                    _cp(kb, "dve", Nn.ap[0:64, :], kb.ps[0:64, bN, :], [kb.psb[bN]], [Nn])
                yield
                bP = kb.bank()
                for i in range(NI):
                    _mm(kb, kb.ps[0:64, bP, i * 64:(i + 1) * 64], N3(An)[0:64, i, :], N3(Pk[pc])[0:64, i, :],
                        [An, Pk[pc]], [kb.psb[bP]])
                _tt(kb, "dve", Pk[1 - pc].ap[0:64, :], kb.ps[0:64, bP, :], Pk[pc].ap[0:64, :], ALU.add,
                    [kb.psb[bP], Pk[pc]], [Pk[1 - pc]])
                pc = 1 - pc
                curA = An
                curNt = Nn
                curN = (lambda Nn: (lambda i: N3(Nn)[0:64, i, :]))(Nn)
                yield
            _cp(kb, "dve", Pf.ap[0:64, :], Pk[pc].ap[0:64, :], [Pk[pc]], [Pf])
            Pm = Pf
            bk = kb.bank()
            for i in range(NI):
                c, h2 = i // 2, i % 2
                _mm(kb, kb.ps[0:64, bk, i * 64:(i + 1) * 64], KB4[0:64, i, 0, :],
                    tok3[3][0:64, c, h2 * 64:(h2 + 1) * 64], [KBt, tok[3]], [kb.psb[bk]])
            _cp(kb, "act", AKV.ap[0:64, :], kb.ps[0:64, bk, :], [kb.psb[bk]], [AKV])
            bk2 = kb.bank()
            for i in range(NI):
                c, h2 = i // 2, i % 2
                _mm(kb, kb.ps[:, bk2, i * 64:(i + 1) * 64], tok3[0][0:64, c, :], N3(Pm)[0:64, i, :],
                    [tok[0], Pm], [kb.psb[bk2]])
            ps4 = kb.ps[:, bk2, :].rearrange("p (c h t) -> p c h t", h=2, t=64)
            _cp(kb, "dve", z4(W1Tz)[0:64, :, 0, :], ps4[0:64, :, 0, :], [kb.psb[bk2]], [W1Tz])
            _cp(kb, "dve", z4(W1Tz)[64:128, :, 1, :], ps4[64:128, :, 1, :], [kb.psb[bk2]], [W1Tz])
            yield
            bk = kb.bank()
            for i in range(NI):
                _mm(kb, kb.ps[0:64, bk, i * 64:(i + 1) * 64], N3(Pm)[0:64, i, :], N3(AKV)[0:64, i, :],
                    [Pm, AKV], [kb.psb[bk]])
            _cp(kb, "act", W2.ap[0:64, :], kb.ps[0:64, bk, :], [kb.psb[bk]], [W2])
            W1T3 = N3(W1Tz)
            yield
            corder = list(range(NCH)) if d == 0 else list(range(NCH - 1, -1, -1))
            for c in corder:
                H, Hn = Hs[hcur], Hs[1 - hcur]
                bU = kb.bank()
                for h2 in range(2):
                    _mm(kb, kb.ps[0:64, bU, h2 * 64:(h2 + 1) * 64], W1T3[:, c * 2 + h2, :],
                        H.ap[:, h2 * 64:(h2 + 1) * 64], [W1Tz, H], [kb.psb[bU]])
                _tt(kb, "dve", Usb.ap[0:64, :], kb.ps[0:64, bU, 0:128], W2.ap[0:64, c * 128:(c + 1) * 128], ALU.add,
                    [kb.psb[bU], W2], [Usb])
                yield
                bH = kb.bank()
                _mm(kb, kb.ps[:, bH, 0:128], tok3[2][0:64, c, :], tok3[3][0:64, c, :], [tok[2], tok[3]],
                    [kb.psb[bH]], start=True, stop=False)
                _mm(kb, kb.ps[:, bH, 0:128], tok3[1][0:64, c, :], Usb.ap[0:64, :], [tok[1], Usb],
                    [kb.psb[bH]], start=False, stop=True)
                bY = kb.bank()
                psY3 = kb.ps[:, bY, 0:128].rearrange("p (h t) -> p h t", t=64)
                _mm(kb, psY3, H.ap, z4(rz)[:, c, :, :], [H, rz], [kb.psb[bY]], start=True, stop=False)
                _mm(kb, psY3, Usb.ap[0:64, :], NB4[0:64, c * 2:c * 2 + 2, 1, :],
                    [Usb, NBt], [kb.psb[bY]], start=False, stop=False)
                _mm(kb, psY3, tok3[3][0:64, c, :], KB4[0:64, c * 2:c * 2 + 2, 1, :],
                    [tok[3], KBt], [kb.psb[bY]], start=False, stop=True)
                _stt(kb, Hn.ap, H.ap, gTt.ap[:, c:c + 1], kb.ps[:, bH, 0:128], ALU.mult, ALU.add,
                     [H, gTt, kb.psb[bH]], [Hn])
                _cp(kb, "act", ysc.ap[0:64, c * 64:(c + 1) * 64], kb.ps[0:64, bY, 0:64], [kb.psb[bY]], [ysc])
                _cp(kb, "act", ysc.ap[64:128, c * 64:(c + 1) * 64], kb.ps[64:128, bY, 64:128], [kb.psb[bY]], [ysc])
                hcur = 1 - hcur
                yield
            dma(kb, "pool", yT[d, fc * 128:(fc + 1) * 128, t0:t0 + W], ysc.ap, reads=ysc.b, writes=[yT_b[d][fc][sc]])

    TP = _NS()
    TP.ld = [kb.f32(WH + 6) for _ in range(3)]
    TP.tp = [kb.f32(W) for _ in range(15)]

    def shift_p(u, out, mu0, mu1, np_=128):
        t1, t2 = TP.tp[13], TP.tp[14]
        mus = [mu_rkv, mu_wa, mu_g]
        _tt(kb, "pool", t1.ap[0:np_, :], u.ap[0:np_, 0:W], u.ap[0:np_, 1:W + 1], ALU.subtract, [u], [t1])
        _stt(kb, out.ap[0:np_, :], t1.ap[0:np_, :], mu0, u.ap[0:np_, 1:W + 1], ALU.mult, ALU.add, [t1, u] + mus, [out])
        _tt(kb, "pool", t2.ap[0:np_, :], u.ap[0:np_, 2:W + 2], u.ap[0:np_, 1:W + 1], ALU.subtract, [u], [t2])
        _stt(kb, out.ap[0:np_, :], t2.ap[0:np_, :], mu1, out.ap[0:np_, :], ALU.mult, ALU.add, [t2, out] + mus, [out])

    def post_gen(fc):
        for ti_ in range(L // W if dbg.get('post', True) else 0):
            t0 = ti_ * W
            (uv, ug0, ug1) = TP.ld
            (y, cen, sq_, rs, yn, vv, g0, g1, bvv, ob_, y1, bo0, bo1) = TP.tp[0:13]
            dma(kb, "sp", uv.ap[:, 0:WH], urwT[1024 + fc * 128:1024 + (fc + 1) * 128, t0:t0 + WH], writes=uv.b)
            dma(kb, "sp", ug0.ap[:, 0:WH], urwT[1792:1920, t0:t0 + WH], writes=ug0.b)
            dma(kb, "sp", ug1.ap[0:32, 0:WH], urwT[1920:1952, t0:t0 + WH], writes=ug1.b)
            fsl = slice(fc * 128, (fc + 1) * 128)
            dma(kb, "sp", y.ap, yT[0, fsl, t0:t0 + W], reads=[yT_b[0][fc][ti_]], writes=y.b)
            dma(kb, "sp", y1.ap, yT[1, fsl, t0:t0 + W], reads=[yT_b[1][fc][ti_]], writes=y1.b)
            dma(kb, "sp", bo0.ap, bT[0, fsl, t0:t0 + W], reads=[bT_b[0][fc][ti_]], writes=bo0.b)
            dma(kb, "sp", bo1.ap, bT[1, fsl, t0:t0 + W], reads=[bT_b[1][fc][ti_]], writes=bo1.b)
            o = (2 * 4 + fc) * 2
            shift_p(uv, vv, mu_rkv.ap[:, o:o + 1], mu_rkv.ap[:, o + 1:o + 2])
            shift_p(ug0, g0, mu_g.ap[:, 0:1], mu_g.ap[:, 1:2])
            yield
            shift_p(ug1, g1, mu_g.ap[0:32, 2:3], mu_g.ap[0:32, 3:4], 32)
            _act(kb, g0.ap, g0.ap, AF.Sigmoid, [g0], [g0])
            _act(kb, g1.ap[0:32, :], g1.ap[0:32, :], AF.Sigmoid, [g1], [g1])
            _tt(kb, "pool", y.ap, y.ap, y1.ap, ALU.add, [y, y1], [y])
            _tt(kb, "pool", bo0.ap, bo0.ap, bo1.ap, ALU.add, [bo0, bo1], [bo0])
            yield
            bM = kb.bank()
            _mm(kb, kb.ps[:, bM, 0:W], bd64.ap, y.ap, [bd64, y], [kb.psb[bM]])
            _stt(kb, cen.ap, kb.ps[:, bM, 0:W], -1.0 / 64, y.ap, ALU.mult, ALU.add, [kb.psb[bM], y], [cen])
            _tt(kb, "pool", sq_.ap, cen.ap, cen.ap, ALU.mult, [cen], [sq_])
            yield
            bV = kb.bank()
            _mm(kb, kb.ps[:, bV, 0:W], bd64.ap, sq_.ap, [bd64, sq_], [kb.psb[bV]])
            _act(kb, rs.ap, kb.ps[:, bV, 0:W], AF.Sqrt, [kb.psb[bV]], [rs], scale=1.0 / 64, bias=GN_EPS)
            kb.op("dve", lambda e, rs=rs: e.reciprocal(out=rs.ap, in_=rs.ap), reads=rs.b, writes=rs.b)
            _tt(kb, "pool", yn.ap, cen.ap, rs.ap, ALU.mult, [cen, rs], [yn])
            _ts(kb, "dve", yn.ap, yn.ap, vec(3, fc), vec(4, fc), ALU.mult, ALU.add, [yn, vecs], [yn])
            yield
            bB = kb.bank()
            _mm(kb, kb.ps[:, bB, 0:W], bd64.ap, bo0.ap, [bd64, bo0], [kb.psb[bB]])
            _tt(kb, "dve", bvv.ap, kb.ps[:, bB, 0:W], vv.ap, ALU.mult, [kb.psb[bB], vv], [bvv])
            _tt(kb, "pool", yn.ap, yn.ap, bvv.ap, ALU.add, [yn, bvv], [yn])
            bG = kb.bank()
            _mm(kb, kb.ps[:, bG, 0:W], g2a.ap[:, fc * 128:(fc + 1) * 128], g0.ap, [g2a, g0], [kb.psb[bG]],
                start=True, stop=False)
            _mm(kb, kb.ps[:, bG, 0:W], g2b.ap[0:32, fc * 128:(fc + 1) * 128], g1.ap[0:32, :], [g2b, g1], [kb.psb[bG]],
                start=False, stop=True)
            ob = ob_.ap.bitcast(BF16)[:, 0:W]
            _tt(kb, "dve", ob, kb.ps[:, bG, 0:W], yn.ap, ALU.mult, [kb.psb[bG], yn], [ob_])
            dma(kb, "pool", yrwT[fc * 128:(fc + 1) * 128, t0:t0 + W], ob, reads=ob_.b)
            yield

    nfc = dbg.get('fcs', 4)
    for fc in range(nfc + 1):
        gens = []
        if fc < nfc:
            gens += [sc_gen(fc, d, TS[d]) for d in range(dbg.get('dirs', 2))]
        if fc > 0:
            gens.append(post_gen(fc - 1))
        for _ in itertools.zip_longest(*gens):
            pass


def merge_phase(kb, C, zhyT, yrwT, gT, x1T, x2T, hyo, rwo, wo):
    wh = kb.bf16(4 * D)
    wr = kb.bf16(4 * D)
    wo_ = kb.bf16(8 * D)
    wh3 = wh.ap.rearrange("p (k d) -> p k d", d=D)
    wr3 = wr.ap.rearrange("p (k d) -> p k d", d=D)
    wo3 = wo_.ap.rearrange("p (k d) -> p k d", d=D)
    dma(kb, "sp", wh3, hyo.rearrange("(k p) d -> p k d", p=128), writes=wh.b)
    dma(kb, "sp", wr3, rwo.rearrange("(k p) d -> p k d", p=128), writes=wr.b)
    dma(kb, "sp", wo3, wo.rearrange("(k p) d -> p k d", p=128), writes=wo_.b)
    zt = [kb.bf16(4 * 512) for _ in range(2)]
    yt = [kb.bf16(4 * 512) for _ in range(2)]
    mrg = [kb.bf16(8 * 512, 8) for _ in range(2)]
    gh = [kb.f32(512) for _ in range(2)]
    gr = [kb.f32(512) for _ in range(2)]
    m1 = [kb.f32(512) for _ in range(2)]
    m2 = [kb.f32(512) for _ in range(2)]
    xr = [kb.f32(512) for _ in range(2)]
    xo = [kb.f32(512) for _ in range(2)]
    zh3 = zhyT.rearrange("(k p) t -> p k t", p=128)
    yr3 = yrwT.rearrange("(k p) t -> p k t", p=128)
    cnt = 0
    for t in range(L // 512):
        ts_ = slice(t * 512, (t + 1) * 512)
        z_, y_, mg = zt[t % 2], yt[t % 2], mrg[t % 2]
        z3 = z_.ap.rearrange("p (k t) -> p k t", t=512)
        y3 = y_.ap.rearrange("p (k t) -> p k t", t=512)
        mg3 = mg.ap.rearrange("p (k t) -> p k t", t=512)
        dma(kb, "sp", z3, zh3[:, :, ts_], writes=z_.b)
        dma(kb, "sp", y3, yr3[:, :, ts_], writes=y_.b)
        for dc in range(8):
            i2 = cnt % 2
            cnt += 1
            dsl = slice(dc * 128, (dc + 1) * 128)
            dma(kb, "sp", gh[i2].ap, gT[dc * 128:(dc + 1) * 128, ts_], writes=gh[i2].b)
            dma(kb, "sp", gr[i2].ap, gT[D + dc * 128:D + (dc + 1) * 128, ts_], writes=gr[i2].b)
            bh, br = kb.bank(), kb.bank()
            for kc in range(4):
                _mm(kb, kb.ps[:, bh, :], wh3[:, kc, dsl], z3[:, kc, :], [wh, z_], [kb.psb[bh]], start=(kc == 0), stop=(kc == 3))
            for kc in range(4):
                _mm(kb, kb.ps[:, br, :], wr3[:, kc, dsl], y3[:, kc, :], [wr, y_], [kb.psb[br]], start=(kc == 0), stop=(kc == 3))
            _tt(kb, "dve", m1[i2].ap, kb.ps[:, bh, :], gh[i2].ap, ALU.mult, [kb.psb[bh], gh[i2]], [m1[i2]])
            _tt(kb, "dve", m2[i2].ap, kb.ps[:, br, :], gr[i2].ap, ALU.mult, [kb.psb[br], gr[i2]], [m2[i2]])
            _tt(kb, "pool", mg3[:, dc, :], m1[i2].ap, m2[i2].ap, ALU.add, [m1[i2], m2[i2]], [mg.b[dc]])
        for dc in range(8):
            i2 = cnt % 2
            cnt += 1
            dsl = slice(dc * 128, (dc + 1) * 128)
            dma(kb, "sp", xr[i2].ap, x1T[dc * 128:(dc + 1) * 128, ts_], writes=xr[i2].b)
            bo = kb.bank()
            for kc in range(8):
                _mm(kb, kb.ps[:, bo, :], wo3[:, kc, dsl], mg3[:, kc, :], [wo_, mg.b[kc]], [kb.psb[bo]],
                    start=(kc == 0), stop=(kc == 7))
            _tt(kb, "dve", xo[i2].ap, kb.ps[:, bo, :], xr[i2].ap, ALU.add, [kb.psb[bo], xr[i2]], [xo[i2]])
            dma(kb, "pool", x2T[dc * 128:(dc + 1) * 128, ts_], xo[i2].ap, reads=xo[i2].b)


def _host_consts():
    c = {}
    c["ones"] = np.ones((128, 128), np.float32)
    c["ident"] = np.eye(128, dtype=np.float32)
    bd = np.zeros((128, 128), np.float32)
    bd[0:64, 0:64] = 1.0
    bd[64:128, 64:128] = 1.0
    c["bd64"] = bd
    s = np.arange(64)[:, None]
    t = np.arange(64)[None, :]
    lt, le, gt, ge = (s < t), (s <= t), (s > t), (s >= t)

    def nb(m0, m1):
        m = np.zeros((64, 4, 2, 64), np.float32)
        m[:, :, 0, :] = m0[:, None, :]
        m[:, :, 1, :] = m1[:, None, :]
        return m.reshape(64, 512)

    c["mNBf"] = nb(lt, le)
    c["mNBb"] = nb(gt, ge)
    c["mAABf"] = np.broadcast_to(gt[:, None, :], (64, 8, 64)).astype(np.float32).reshape(64, 512).copy()
    c["mAABb"] = np.broadcast_to(lt[:, None, :], (64, 8, 64)).astype(np.float32).reshape(64, 512).copy()
    c["I8"] = np.broadcast_to(np.eye(64, dtype=np.float32)[:, None, :], (64, 8, 64)).reshape(64, 512).copy()
    rm = np.ones((128, SC), np.float32)
    rm[:, 0::64] = 0.0
    c["rmask"] = rm
    return c


CONST_SHAPES = {"ones": [128, 128], "ident": [128, 128], "bd64": [128, 128], "mNBf": [64, 512], "mNBb": [64, 512],
                "mAABf": [64, 512], "mAABb": [64, 512], "I8": [64, 512], "rmask": [128, 256]}

PARAM_SHAPES = {
    "norms": [128, 32],
    "rw_muT": [NRW, 2], "rw_w0T": [512, 2], "rw_a0T": [512, 2], "rw_w2t": [64, 1024], "rw_a2t": [64, 1024],
    "rw_g2": [160, 512], "rw_vecs": [128, 20],
}

WEIGHTS = {"ffn1_w_gu": [D, 2 * DFF], "ffn1_w_down": [DFF, D], "ffn2_w_gu": [D, 2 * DFF], "ffn2_w_down": [DFF, D],
           "w_in": [D, NIN], "hy_out": [512, D], "rw_out": [512, D], "w_out": [D, D]}


def _host_params(inputs):
    f = lambda k: np.asarray(inputs[k], np.float32)
    p = {}
    p["norms"] = np.ascontiguousarray(np.concatenate(
        [f(n).reshape(8, 128).T for n in ("ffn1_norm", "mix_norm", "ffn2_norm", "final_norm")], axis=1))
    p["rw_muT"] = np.ascontiguousarray(f("rw_mu").T)
    p["rw_w0T"] = np.ascontiguousarray(f("rw_w0").T)
    p["rw_a0T"] = np.ascontiguousarray(f("rw_a0").T)
    p["rw_w2t"] = np.ascontiguousarray(f("rw_w2").transpose(1, 0, 2).reshape(64, 1024))
    p["rw_a2t"] = np.ascontiguousarray(f("rw_a2").transpose(1, 0, 2).reshape(64, 1024))
    p["rw_g2"] = np.ascontiguousarray(f("rw_g2"))
    p["rw_vecs"] = np.ascontiguousarray(np.concatenate(
        [f(n).reshape(4, 128).T for n in ("rw_k_k", "rw_k_a", "rw_r_k", "rw_ln_w", "rw_ln_b")], axis=1))
    return p


def build_program(stage="full"):
    nc = bass.Bass("TRN2", target_bir_lowering=False)
    I = {}

    def inp(name, shape, dt=F32):
        I[name] = nc.dram_tensor(name, list(shape), dt, kind="ExternalInput").ap()
        return I[name]

    def scr(name, shape, dt=F32, ext=None):
        if ext == "in":
            return inp(name, shape, dt)
        if ext == "out":
            return nc.dram_tensor(name, list(shape), dt, kind="ExternalOutput").ap()
        return nc.dram_tensor(name, list(shape), dt).ap()

    full = stage == "full"
    front = stage in ("full", "front")
    do_rw = stage in ("full", "rwkv")
    do_hy = stage in ("full", "hyena")
    for n, shp in CONST_SHAPES.items():
        inp(n, shp)
    inp("norms", PARAM_SHAPES["norms"])
    if do_rw:
        for n, shp in PARAM_SHAPES.items():
            if n != "norms":
                inp(n, shp)
    if do_hy:
        for n, shp in list(HY_CONST_SHAPES.items()) + list(HY_PARAM_SHAPES.items()):
            inp(n, shp)
    if front:
        inp("xT", [D, L])
        for n, shp in WEIGHTS.items():
            if full or n in ("ffn1_w_gu", "ffn1_w_down", "w_in"):
                inp(n, shp)
    back = stage == "back"
    if back:
        for n in ("hy_out", "rw_out", "w_out", "ffn2_w_gu", "ffn2_w_down"):
            inp(n, WEIGHTS[n])
    uhy = scr("uhy", [L + 2, NHY], ext={"front": "out", "hyena": "in"}.get(stage))
    urwT = scr("urwT", [NRW, L + 2], ext={"front": "out", "rwkv": "in"}.get(stage))
    gT = scr("gT", [NG, L], ext={"front": "out", "back": "in"}.get(stage))
    yrwT = scr("yrwT", [512, L], BF16, ext={"rwkv": "out", "back": "in"}.get(stage))
    zhyT = scr("zhyT", [512, L], BF16, ext={"hyena": "out", "back": "in"}.get(stage))
    outT = None
    if full or stage == "back":
        outT = nc.dram_tensor("outT", [D, L], F32, kind="ExternalOutput").ap()
    wb = {}
    for n, shp in WEIGHTS.items():
        if n in I:
            wb[n] = scr(n + "_b", shp, BF16)
    x1T = scr("x1T", [D, L], ext={"front": "out", "back": "in"}.get(stage))
    x2T = scr("x2T", [D, L])
    x3T = scr("x3T", [D, L])
    HD = {}
    if do_hy:
        HD["ucb"] = scr("hy_ucb", [3, 4, 64, 64 * CG])
        HD["Abuf"] = scr("hy_Abuf", [2, 2, 128, 64, CG])
        HD["Bbuf"] = scr("hy_Bbuf", [2, 2, 64, 128, CG])
        HD["Kf"] = scr("hy_Kf", [2, 4, 2, 64, 128 * CG])
        HD["ucb_b"] = [[Buf() for _ in range(4)] for _ in range(3)]
        HD["Kf_b"] = [[[Buf() for _ in range(32)] for _ in range(4)] for _ in range(2)]
        if stage == "hyena":
            HD["kf_dbg"] = scr("kf_dbg", [2, 4, 128, 64 * CG], ext="out")
            HD["z1_dbg"] = scr("z1_dbg", [4, 64, 64 * CG], ext="out")

    with contextlib.ExitStack() as st:
        kb = KB(nc, st)
        C = {}
        for n in ("ones", "ident", "bd64"):
            C[n] = kb.f32(128)
            dma(kb, "sp", C[n].ap, I[n], writes=C[n].b)
        nrm = kb.f32(32)
        for i, n in enumerate(("ffn1_norm", "mix_norm", "ffn2_norm", "final_norm")):
            t = Tile(nrm.ap[:, i * 8:(i + 1) * 8])
            t.b = nrm.b
            C[n] = t
        dma(kb, "sp", nrm.ap, I["norms"], writes=nrm.b)
        kb.persist()

        if front:
            convert_weights(kb, [(I[n], wb[n], WEIGHTS[n][0], WEIGHTS[n][1]) for n in wb])
            kb.phase_end()
            ffn_phase(kb, C, I["xT"], x1T, "ffn1_norm", wb["ffn1_w_gu"], wb["ffn1_w_down"])
            kb.phase_end()
            inproj_phase(kb, C, x1T, wb["w_in"], uhy, urwT, gT)
            kb.phase_end()
        if do_rw:
            rwkv_phase(kb, C, I, urwT, yrwT, {"yT": scr("rw_yT", [2, 512, L]), "bT": scr("rw_bT", [2, 512, L])})
            kb.phase_end()
        if do_hy:
            hyena_phase(kb, C, I, uhy, zhyT, HD)
            kb.phase_end()
        if back:
            convert_weights(kb, [(I[n], wb[n], WEIGHTS[n][0], WEIGHTS[n][1]) for n in wb])
            kb.phase_end()
        if full or back:
            merge_phase(kb, C, zhyT, yrwT, gT, x1T, x2T, wb["hy_out"], wb["rw_out"], wb["w_out"])
            kb.phase_end()
            ffn_phase(kb, C, x2T, x3T, "ffn2_norm", wb["ffn2_w_gu"], wb["ffn2_w_down"])
            kb.phase_end()
            final_norm_phase(kb, C, x3T, outT, "final_norm")
            kb.phase_end()
        kb.S.emit()
    return nc, list(I.keys())


_NC_CACHE = {}


def _get_program(stage):
    if stage not in _NC_CACHE:
        _NC_CACHE[stage] = build_program(stage)
    return _NC_CACHE[stage]


def host_shared(inputs, names):
    shared = dict(_host_consts())
    shared.update(_host_params(inputs))
    shared.update(_hy_consts())
    shared.update(_hy_host_params(inputs))
    for n in WEIGHTS:
        shared[n] = np.ascontiguousarray(inputs[n], np.float32)
    return {k: v for k, v in shared.items() if k in names}


def kernel(**inputs):
    nc, names = _get_program("full")
    shared = host_shared(inputs, names)
    x = np.asarray(inputs["x"], np.float32)
    in_maps = []
    for b in range(8):
        m = dict(shared)
        m["xT"] = np.ascontiguousarray(x[b].T)
        in_maps.append(m)
    res = run_bass_kernel_spmd(nc, in_maps, core_ids=list(range(8)))
    out = np.stack([np.ascontiguousarray(r["outT"].T) for r in res.results], axis=0)
    return out.astype(np.float32)


NFFT = 8192
CG = 128
HY_DBG = {}


def _hy_consts():
    c = {}
    n1 = np.arange(128, dtype=np.float64)[:, None, None]
    n2 = np.arange(64, dtype=np.float64)[None, :, None]
    k1 = np.arange(128, dtype=np.float64)[None, None, :]
    ang = 2 * np.pi * (n1 * k1 / 128.0 + n2 * k1 / NFFT)
    G = np.stack([np.cos(ang), -np.sin(ang)], axis=2)
    c["hy_G"] = G.reshape(128, 64 * 2 * 128).astype(np.float32)
    a2 = 2 * np.pi * np.arange(64)[:, None] * np.arange(64)[None, :] / 64.0
    c["hy_F2"] = np.concatenate([np.cos(a2), np.sin(a2), -np.sin(a2)], axis=1).astype(np.float32)
    k2 = np.arange(64, dtype=np.float64)[:, None, None]
    k1b = np.arange(128, dtype=np.float64)[None, :, None]
    nl = np.arange(64, dtype=np.float64)[None, None, :]
    angp = 2 * np.pi * (k2 * nl / 64.0 + k1b * nl / NFFT)
    Gp = np.stack([np.cos(angp), np.sin(angp), -np.sin(angp)], axis=2)
    c["hy_Gp"] = Gp.reshape(64, 128 * 3 * 64).astype(np.float32)
    a3 = 2 * np.pi * np.arange(128)[:, None] * np.arange(64)[None, :] / 128.0
    wk = np.full((128, 1), 2.0)
    wk[0] = 1.0
    wk[64] = 1.0
    wk[65:] = 0.0
    c["hy_I2"] = (wk * np.concatenate([np.cos(a3), -np.sin(a3)], axis=1) / NFFT).astype(np.float32)
    n = np.arange(NFFT)
    j = np.where(n <= L, n, NFFT - n).astype(np.float64)
    j[L] = 0
    t = j / (L - 1)
    angf = (2.0 * math.pi / L) * j
    bands = np.linspace(1e-4, 15, 16)
    feats = np.concatenate([t[None, :], np.cos(bands[:, None] * angf[None, :]), -np.sin(bands[:, None] * angf[None, :])], axis=0)
    c["hy_feats"] = feats.astype(np.float32)
    c["hy_negt"] = (-t).reshape(128, 64).astype(np.float32)
    return c


HY_CONST_SHAPES = {"hy_G": [128, 64 * 2 * 128], "hy_F2": [64, 192], "hy_Gp": [64, 128 * 3 * 64], "hy_I2": [128, 128],
                   "hy_feats": [33, NFFT], "hy_negt": [128, 64]}
HY_PARAM_SHAPES = {"hy_conv_w": [3, NHY], "hy_conv_b": [1, NHY], "hy_w1": [33, 64], "hy_w2": [64, 64],
                   "hy_w3a": [65, 2048], "hy_b1f": [64, 3], "hy_decay": [4, 512], "hy_bias_d": [2, 512]}


def _hy_host_params(inputs):
    f = lambda k: np.asarray(inputs[k], np.float32)
    p = {}
    p["hy_conv_w"] = np.ascontiguousarray(f("hy_conv_w"))
    p["hy_conv_b"] = np.ascontiguousarray(f("hy_conv_b").reshape(1, NHY))
    p["hy_w1"] = np.ascontiguousarray(f("hy_ffn_w1"))
    p["hy_w2"] = np.ascontiguousarray(f("hy_ffn_w2"))
    p["hy_w3a"] = np.ascontiguousarray(np.concatenate([f("hy_ffn_w3"), f("hy_ffn_b3").reshape(1, 2048)], axis=0))
    p["hy_b1f"] = np.ascontiguousarray(np.stack([f("hy_ffn_b1"), f("hy_ffn_b2"), f("hy_sin_freq")], axis=1))
    p["hy_decay"] = np.ascontiguousarray(f("hy_decay").reshape(4, 512))
    p["hy_bias_d"] = np.ascontiguousarray(f("hy_bias_d"))
    return p


def _sin_act(kb, out_ap, out_tile, pre, scr, np_):
    TWO_PI = 2.0 * math.pi
    MAGIC = 12582912.0
    _ts(kb, "dve", scr.ap[0:np_, :], pre.ap[0:np_, :], 1.0 / TWO_PI, MAGIC, ALU.mult, ALU.add, [pre], [scr])
    _ts(kb, "dve", scr.ap[0:np_, :], scr.ap[0:np_, :], MAGIC, None, ALU.subtract, None, [scr], [scr])
    _stt(kb, scr.ap[0:np_, :], scr.ap[0:np_, :], -TWO_PI, pre.ap[0:np_, :], ALU.mult, ALU.add, [scr, pre], [scr])
    _ts(kb, "dve", scr.ap[0:np_, :], scr.ap[0:np_, :], 3.141592, -3.141592, ALU.min, ALU.max, [scr], [scr])
    _act(kb, out_ap, scr.ap[0:np_, :], AF.Sin, [scr], [out_tile])


def hyena_phase(kb, C, P, uhy, zhyT, D_):
    dbg = HY_DBG
    ident = C["ident"]
    ucb, Abuf_all, Bbuf_all, Kf = D_["ucb"], D_["Abuf"], D_["Bbuf"], D_["Kf"]
    NG4 = dbg.get("groups", 4)
    cw = [kb.f32(NHY) for _ in range(3)]
    cb = kb.f32(NHY)
    for i in range(3):
        dma(kb, "sp", cw[i].ap[0:64, :], P["hy_conv_w"][i:i + 1, :].partition_broadcast(64), writes=cw[i].b)
        dma(kb, "sp", cw[i].ap[64:128, 0:NHY - CG], P["hy_conv_w"][i:i + 1, CG:NHY].partition_broadcast(64), writes=cw[i].b)
    dma(kb, "sp", cb.ap[0:64, :], P["hy_conv_b"].partition_broadcast(64), writes=cb.b)
    dma(kb, "sp", cb.ap[64:128, 0:NHY - CG], P["hy_conv_b"][:, CG:NHY].partition_broadcast(64), writes=cb.b)
    uin = [kb.f32(66 * CG) for _ in range(2)]
    uo = [kb.f32(64 * CG) for _ in range(2)]
    tmpc = kb.f32(64 * CG)
    uhy_b = uhy[0:L, :].rearrange("(p n) c -> p n c", n=64)
    uhy_h = uhy[2:L + 2, :].rearrange("(p n) c -> p n c", n=64)
    it = 0
    for kind in range(3):
        for gp in range(0, NG4, 2):
            npair = min(2, NG4 - gp)
            NP = 64 * npair
            c0 = kind * 512 + gp * CG
            ui, uo_ = uin[it % 2], uo[it % 2]
            it += 1
            u3 = ui.ap.rearrange("p (n c) -> p n c", c=CG)
            o3 = uo_.ap.rearrange("p (n c) -> p n c", c=CG)
            t3 = tmpc.ap.rearrange("p (n c) -> p n c", c=CG)
            for h in range(npair):
                ch = c0 + h * CG
                dma(kb, "sp", u3[h * 64:(h + 1) * 64, 0:64, :], uhy_b[:, :, ch:ch + CG], writes=ui.b)
                dma(kb, "act", u3[h * 64:(h + 1) * 64, 64:66, :], uhy_h[:, 62:64, ch:ch + CG], writes=ui.b)

            def bc(t):
                return t.ap[0:NP, c0:c0 + CG].unsqueeze(1).to_broadcast([NP, 64, CG])

            _tt(kb, "dve", o3[0:NP], u3[0:NP, 0:64, :], bc(cw[0]), ALU.mult, [ui, cw[0]], [uo_])
            _tt(kb, "pool", t3[0:NP], u3[0:NP, 1:65, :], bc(cw[1]), ALU.mult, [ui, cw[1]], [tmpc])
            _tt(kb, "dve", o3[0:NP], o3[0:NP], t3[0:NP], ALU.add, [uo_, tmpc], [uo_])
            _tt(kb, "pool", t3[0:NP], u3[0:NP, 2:66, :], bc(cw[2]), ALU.mult, [ui, cw[2]], [tmpc])
            _tt(kb, "dve", o3[0:NP], o3[0:NP], t3[0:NP], ALU.add, [uo_, tmpc], [uo_])
            _tt(kb, "dve", o3[0:NP], o3[0:NP], bc(cb), ALU.add, [uo_, cb], [uo_])
            for h in range(npair):
                dma(kb, "pool", ucb[kind, gp + h], uo_.ap[h * 64:(h + 1) * 64, :], reads=uo_.b,
                    writes=[D_["ucb_b"][kind][gp + h]])
    kb.phase_end()

    F2t = kb.f32(192)
    I2t = kb.f32(128)
    dma(kb, "sp", F2t.ap[0:64, :], P["hy_F2"], writes=F2t.b)
    dma(kb, "sp", I2t.ap, P["hy_I2"], writes=I2t.b)
    kb.persist()
    NK1 = 65
    KBATCH = [(k0, min(4, NK1 - k0)) for k0 in range(0, NK1, 4)]
    A_bs = [[Buf() for _ in range(16)] for _ in range(2)]
    B_bs = [[Buf() for _ in range(17)] for _ in range(2)]
    Gc = [kb.f32(1024) for _ in range(2)]
    Ast = [[kb.f32(512) for _ in range(2)] for _ in range(2)]
    Ain = [[kb.f32(512) for _ in range(2)] for _ in range(2)]
    cnt = {"g": 0, "a": 0, "i": 0}

    def fft_s1(src3, srcT, K, aset=0):
        Abuf, A_b = Abuf_all[aset], A_bs[aset]
        for n2b in range(16):
            gt = Gc[cnt["g"] % 2]
            cnt["g"] += 1
            dma(kb, "sp", gt.ap[0:K, :], P["hy_G"][0:K, n2b * 1024:(n2b + 1) * 1024], writes=gt.b)
            g4 = gt.ap.rearrange("p (j r k) -> p j r k", r=2, k=128)
            banks = [kb.bank(), kb.bank()]
            for j in range(4):
                for ri in range(2):
                    _mm(kb, kb.ps[0:NK1, banks[ri], j * 128:(j + 1) * 128], g4[0:K, j, ri, 0:NK1], src3[0:K, n2b * 4 + j, :],
                        [gt, srcT], [kb.psb[banks[ri]]])
            for ri in range(2):
                st_ = Ast[ri][cnt["a"] % 2]
                _cp(kb, "act" if ri == 0 else "dve", st_.ap[0:NK1, :], kb.ps[0:NK1, banks[ri], :], [kb.psb[banks[ri]]], [st_])
                dma(kb, "pool", Abuf[ri, 0:NK1, n2b * 4:(n2b + 1) * 4, :], st_.ap[0:NK1, :].rearrange("p (n c) -> p n c", c=CG),
                    reads=st_.b, writes=[A_b[n2b]])
            cnt["a"] += 1

    def fft_s2(k0, nk, aset=0):
        Abuf, A_b = Abuf_all[aset], A_bs[aset]
        tin = []
        nn = nk * CG
        for ri in range(2):
            t_ = Ain[ri][cnt["i"] % 2]
            dma(kb, "sp", t_.ap[0:64, 0:nn].rearrange("p (k c) -> p k c", c=CG),
                Abuf[ri, k0:k0 + nk, :, :].rearrange("k n c -> n k c"), reads=A_b, writes=t_.b)
            tin.append(t_)
        cnt["i"] += 1
        bre, bim = kb.bank(), kb.bank()
        c2, s2, ns2 = F2t.ap[0:64, 0:64], F2t.ap[0:64, 64:128], F2t.ap[0:64, 128:192]
        _mm(kb, kb.ps[0:64, bre, 0:nn], c2, tin[0].ap[0:64, 0:nn], [F2t, tin[0]], [kb.psb[bre]], start=True, stop=False)
        _mm(kb, kb.ps[0:64, bre, 0:nn], s2, tin[1].ap[0:64, 0:nn], [F2t, tin[1]], [kb.psb[bre]], start=False, stop=True)
        _mm(kb, kb.ps[0:64, bim, 0:nn], c2, tin[1].ap[0:64, 0:nn], [F2t, tin[1]], [kb.psb[bim]], start=True, stop=False)
        _mm(kb, kb.ps[0:64, bim, 0:nn], ns2, tin[0].ap[0:64, 0:nn], [F2t, tin[0]], [kb.psb[bim]], start=False, stop=True)
        return bre, bim

    mark = kb.off
    w1t = kb.f32(64)
    w2t = kb.f32(64)
    b1f = kb.f32(3)
    w3a = kb.f32(2048)
    negt = kb.f32(64)
    h2f = kb.f32(NFFT)
    h2b = kb.f32(NFFT)
    dma(kb, "sp", w1t.ap[0:33, :], P["hy_w1"], writes=w1t.b)
    dma(kb, "sp", w2t.ap[0:64, :], P["hy_w2"], writes=w2t.b)
    dma(kb, "sp", b1f.ap[0:64, :], P["hy_b1f"], writes=b1f.b)
    dma(kb, "sp", w3a.ap[0:65, :], P["hy_w3a"], writes=w3a.b)
    dma(kb, "sp", negt.ap, P["hy_negt"], writes=negt.b)
    kb.op("pool", lambda e: e.memset(h2f.ap, 0.0), writes=h2f.b)
    kb.op("pool", lambda e: e.memset(h2b.ap, 0.0), writes=h2b.b)
    kb.op("pool", lambda e: e.memset(h2f.ap[64:65, 0:L], 1.0), reads=h2f.b, writes=h2f.b)
    kb.op("pool", lambda e: e.memset(h2b.ap[64:65, L + 1:NFFT], 1.0), reads=h2b.b, writes=h2b.b)
    fch = [kb.f32(512) for _ in range(2)]
    pre = kb.f32(512)
    scr = kb.f32(512)
    h1c = kb.f32(512)
    for pc in range(16):
        ft = fch[pc % 2]
        dma(kb, "sp", ft.ap[0:33, :], P["hy_feats"][:, pc * 512:(pc + 1) * 512], writes=ft.b)
        b1 = kb.bank()
        _mm(kb, kb.ps[0:64, b1, :], w1t.ap[0:33, :], ft.ap[0:33, :], [w1t, ft], [kb.psb[b1]])
        _ts(kb, "dve", pre.ap[0:64, :], kb.ps[0:64, b1, :], b1f.ap[0:64, 0:1], b1f.ap[0:64, 2:3], ALU.add, ALU.mult,
            [kb.psb[b1], b1f], [pre])
        _sin_act(kb, h1c.ap[0:64, :], h1c, pre, scr, 64)
        b2 = kb.bank()
        _mm(kb, kb.ps[0:64, b2, :], w2t.ap[0:64, :], h1c.ap[0:64, :], [w2t, h1c], [kb.psb[b2]])
        _ts(kb, "dve", pre.ap[0:64, :], kb.ps[0:64, b2, :], b1f.ap[0:64, 1:2], b1f.ap[0:64, 2:3], ALU.add, ALU.mult,
            [kb.psb[b2], b1f], [pre])
        hdst = h2f if pc < 8 else h2b
        TWO_PI = 2.0 * math.pi
        MAGIC = 12582912.0
        _ts(kb, "dve", scr.ap[0:64, :], pre.ap[0:64, :], 1.0 / TWO_PI, MAGIC, ALU.mult, ALU.add, [pre], [scr])
        _ts(kb, "dve", scr.ap[0:64, :], scr.ap[0:64, :], MAGIC, None, ALU.subtract, None, [scr], [scr])
        _stt(kb, scr.ap[0:64, :], scr.ap[0:64, :], -TWO_PI, pre.ap[0:64, :], ALU.mult, ALU.add, [scr, pre], [scr])
        _ts(kb, "dve", scr.ap[0:64, :], scr.ap[0:64, :], 3.141592, -3.141592, ALU.min, ALU.max, [scr], [scr])
        _act(kb, hdst.ap[0:64, pc * 512:(pc + 1) * 512], scr.ap[0:64, :], AF.Sin, [scr], [hdst])
    kb.op("pool", lambda e: e.memset(h2b.ap[0:64, L:L + 1], 0.0), reads=h2b.b, writes=h2b.b)
    kfs = [kb.f32(64 * CG), kb.f32(64 * CG)]
    absd = kb.f32(CG)
    et = [kb.f32(512) for _ in range(2)]
    sqc = [kb.bf16(512) for _ in range(2)]
    ones_bf = kb.bf16(128)
    kb.op("dve", lambda e: e.tensor_copy(out=ones_bf.ap, in_=C["ones"].ap), reads=C["ones"].b, writes=ones_bf.b)
    rnf = kb.f32(CG)
    kst = [[kb.f32(512) for _ in range(2)] for _ in range(2)]
    h2f3 = h2f.ap.rearrange("p (a b) -> p b a", b=64)
    h2b3 = h2b.ap.rearrange("p (a b) -> p b a", b=64)

    def gen_filter(o, g, kf):
        kf3 = kf.ap.rearrange("p (n c) -> p n c", c=CG)
        for dr in range(2):
            dma(kb, "sp", absd.ap[dr * 64:(dr + 1) * 64, :],
                P["hy_decay"][dr * 2 + o:dr * 2 + o + 1, g * CG:(g + 1) * CG].partition_broadcast(64), writes=absd.b)
        _stt(kb, absd.ap, absd.ap, -1.0, absd.ap, ALU.mult, ALU.max, [absd], [absd])
        bss = kb.bank()
        kb.reserved.add(bss)
        for n2b in range(16):
            bk = kb.bank()
            e_ = et[n2b % 2]
            sq_ = sqc[n2b % 2]
            for j in range(4):
                n2 = n2b * 4 + j
                cf = o * 512 + g * CG
                _mm(kb, kb.ps[:, bk, j * 128:(j + 1) * 128], h2f3[0:65, n2, :], w3a.ap[0:65, cf:cf + CG],
                    [h2f, w3a], [kb.psb[bk]], start=True, stop=False)
                _mm(kb, kb.ps[:, bk, j * 128:(j + 1) * 128], h2b3[0:65, n2, :], w3a.ap[0:65, 1024 + cf:1024 + cf + CG],
                    [h2b, w3a], [kb.psb[bk]], start=False, stop=True)
                _act(kb, e_.ap[:, j * 128:(j + 1) * 128], absd.ap, AF.Exp, [absd, negt], [e_], scale=negt.ap[:, n2:n2 + 1])
            _tt(kb, "dve", kf.ap[:, n2b * 512:(n2b + 1) * 512], kb.ps[:, bk, :], e_.ap, ALU.mult, [kb.psb[bk], e_], [kf])
            _tt(kb, "pool", sq_.ap, kf.ap[:, n2b * 512:(n2b + 1) * 512], kf.ap[:, n2b * 512:(n2b + 1) * 512], ALU.mult,
                [kf], [sq_])
            for j in range(4):
                _mm(kb, kb.ps[:, bss, 0:CG], ones_bf.ap, sq_.ap[:, j * 128:(j + 1) * 128], [ones_bf, sq_],
                    [kb.psb[bss]], start=(n2b == 0 and j == 0), stop=(n2b == 15 and j == 3))
        kb.reserved.discard(bss)
        _act(kb, rnf.ap, kb.ps[:, bss, 0:CG], AF.Sqrt, [kb.psb[bss]], [rnf], bias=1e-6)
        kb.op("dve", lambda e: e.reciprocal(out=rnf.ap, in_=rnf.ap), reads=rnf.b, writes=rnf.b)
        _tt(kb, "dve", kf3, kf3, rnf.ap.unsqueeze(1).to_broadcast([128, 64, CG]), ALU.mult, [kf, rnf], [kf])
        if "kf_dbg" in D_:
            dma(kb, "pool", D_["kf_dbg"][o, g], kf.ap, reads=kf.b)

    def fft_filter(o, g, kf, aset):
        kf3 = kf.ap.rearrange("p (n c) -> p n c", c=CG)
        fft_s1(kf3, kf, 128, aset)
        for k1b, (k0, nk) in enumerate(KBATCH):
            nn = nk * CG
            bre, bim = fft_s2(k0, nk, aset)
            for ri, bk_ in enumerate((bre, bim)):
                st_ = kst[ri][k1b % 2]
                _cp(kb, "act" if ri == 0 else "dve", st_.ap[0:64, 0:nn], kb.ps[0:64, bk_, 0:nn], [kb.psb[bk_]], [st_])
                dma(kb, "pool", Kf[o, g, ri, :, k0 * CG:k0 * CG + nn], st_.ap[0:64, 0:nn], reads=st_.b,
                    writes=[D_["Kf_b"][o][g][k1b]])

    items = [(o, g) for o in range(2) for g in range(NG4)]
    gen_filter(items[0][0], items[0][1], kfs[0])
    for i, (o, g) in enumerate(items):
        if i + 1 < len(items):
            gen_filter(items[i + 1][0], items[i + 1][1], kfs[(i + 1) % 2])
        fft_filter(o, g, kfs[i % 2], i % 2)
    kb.S.barrier()
    kb.off = mark

    import itertools

    def alloc_c():
        T = _NS()
        T.zbuf = kb.f32(64 * CG)
        T.Kt = [kb.f32(512) for _ in range(2)]
        T.Gp = kb.f32(768)
        T.xr, T.xi = kb.f32(512), kb.f32(512)
        T.tt = [kb.f32(512) for _ in range(4)]
        T.Yt = [kb.f32(512) for _ in range(2)]
        T.Bst = [kb.f32(512) for _ in range(2)]
        T.Bin = [kb.f32(512) for _ in range(2)]
        T.xg, T.sk, T.te, T.zo, T.dbc = [kb.f32(512) for _ in range(5)]
        T.zT = kb.bf16(L)
        return T

    def stage_c(g, T, aset):
        Bbuf, B_b = Bbuf_all[aset], B_bs[aset]
        zbuf = T.zbuf
        z3 = zbuf.ap.rearrange("p (n c) -> p n c", c=CG)
        zT3 = T.zT.ap.rearrange("c (p n) -> c p n", n=64)
        dma(kb, "sp", zbuf.ap[0:64, :], ucb[2, g], reads=[D_["ucb_b"][2][g]], writes=zbuf.b)
        for o in range(dbg.get("orders", 2)):
            for rep in range(4):
                dma(kb, "sp", T.dbc.ap[0:64, rep * CG:(rep + 1) * CG],
                    P["hy_bias_d"][o:o + 1, g * CG:(g + 1) * CG].partition_broadcast(64), writes=T.dbc.b)
            fft_s1(z3, zbuf, 64, aset)
            yield
            for k1b, (k0, nk) in enumerate(KBATCH):
                nn = nk * CG
                kt = T.Kt
                for ri in range(2):
                    dma(kb, "sp", kt[ri].ap[0:64, 0:nn], Kf[o, g, ri, :, k0 * CG:k0 * CG + nn],
                        reads=[D_["Kf_b"][o][g][k1b]], writes=kt[ri].b)
                gp = T.Gp
                dma(kb, "sp", gp.ap[0:64, 0:nk * 192], P["hy_Gp"][:, k0 * 192:(k0 + nk) * 192], writes=gp.b)
                gp4 = gp.ap.rearrange("p (j q n) -> p j q n", q=3, n=64)
                bre, bim = fft_s2(k0, nk, aset)
                xr_, xi_ = T.xr, T.xi
                _cp(kb, "act", xr_.ap[0:64, 0:nn], kb.ps[0:64, bre, 0:nn], [kb.psb[bre]], [xr_])
                _cp(kb, "act", xi_.ap[0:64, 0:nn], kb.ps[0:64, bim, 0:nn], [kb.psb[bim]], [xi_])
                yre, yim = T.Yt
                ta, tb_, tc_, td = T.tt
                _tt(kb, "dve", ta.ap[0:64, 0:nn], xr_.ap[0:64, 0:nn], kt[0].ap[0:64, 0:nn], ALU.mult, [xr_, kt[0]], [ta])
                _tt(kb, "dve", tb_.ap[0:64, 0:nn], xi_.ap[0:64, 0:nn], kt[1].ap[0:64, 0:nn], ALU.mult, [xi_, kt[1]], [tb_])
                _tt(kb, "dve", yre.ap[0:64, 0:nn], ta.ap[0:64, 0:nn], tb_.ap[0:64, 0:nn], ALU.subtract, [ta, tb_], [yre])
                _tt(kb, "pool", tc_.ap[0:64, 0:nn], xr_.ap[0:64, 0:nn], kt[1].ap[0:64, 0:nn], ALU.mult, [xr_, kt[1]], [tc_])
                _tt(kb, "pool", td.ap[0:64, 0:nn], xi_.ap[0:64, 0:nn], kt[0].ap[0:64, 0:nn], ALU.mult, [xi_, kt[0]], [td])
                _tt(kb, "pool", yim.ap[0:64, 0:nn], tc_.ap[0:64, 0:nn], td.ap[0:64, 0:nn], ALU.add, [tc_, td], [yim])
                yield
                cre, cim = kb.bank(), kb.bank()
                for j in range(nk):
                    cs_ = slice(j * 128, (j + 1) * 128)
                    _mm(kb, kb.ps[0:64, cre, cs_], gp4[0:64, j, 0, :], yre.ap[0:64, cs_], [gp, yre], [kb.psb[cre]],
                        start=True, stop=False)
                    _mm(kb, kb.ps[0:64, cre, cs_], gp4[0:64, j, 2, :], yim.ap[0:64, cs_], [gp, yim], [kb.psb[cre]],
                        start=False, stop=True)
                    _mm(kb, kb.ps[0:64, cim, cs_], gp4[0:64, j, 1, :], yre.ap[0:64, cs_], [gp, yre], [kb.psb[cim]],
                        start=True, stop=False)
                    _mm(kb, kb.ps[0:64, cim, cs_], gp4[0:64, j, 0, :], yim.ap[0:64, cs_], [gp, yim], [kb.psb[cim]],
                        start=False, stop=True)
                for ri, bk_ in enumerate((cre, cim)):
                    st_ = T.Bst[ri]
                    _cp(kb, "act" if ri == 0 else "dve", st_.ap[0:64, 0:nn], kb.ps[0:64, bk_, 0:nn], [kb.psb[bk_]], [st_])
                    dma(kb, "pool", Bbuf[ri, :, k0:k0 + nk, :], st_.ap[0:64, 0:nn].rearrange("p (k c) -> p k c", c=CG),
                        reads=st_.b, writes=[B_b[k1b]])
                yield
            for nb in range(16):
                bin_ = T.Bin
                for ri in range(2):
                    dma(kb, "sp", bin_[ri].ap[0:NK1, :].rearrange("p (n c) -> p n c", c=CG),
                        Bbuf[ri, nb * 4:(nb + 1) * 4, 0:NK1, :].rearrange("n k c -> k n c"), reads=B_b, writes=bin_[ri].b)
                xg_ = T.xg
                dma(kb, "sp", xg_.ap[0:64, :], ucb[o, g, :, nb * 512:(nb + 1) * 512], reads=[D_["ucb_b"][o][g]], writes=xg_.b)
                by = kb.bank()
                _mm(kb, kb.ps[0:64, by, :], I2t.ap[0:NK1, 0:64], bin_[0].ap[0:NK1, :], [I2t, bin_[0]], [kb.psb[by]], start=True, stop=False)
                _mm(kb, kb.ps[0:64, by, :], I2t.ap[0:NK1, 64:128], bin_[1].ap[0:NK1, :], [I2t, bin_[1]], [kb.psb[by]], start=False, stop=True)
                te = T.te
                if o == 0:
                    sk_ = T.sk
                    dma(kb, "sp", sk_.ap[0:64, :], ucb[2, g, :, nb * 512:(nb + 1) * 512], reads=[D_["ucb_b"][2][g]], writes=sk_.b)
                    _tt(kb, "pool", te.ap[0:64, :], sk_.ap[0:64, :], T.dbc.ap[0:64, :], ALU.mult, [sk_, T.dbc], [te])
                else:
                    _tt(kb, "pool", te.ap[0:64, :], zbuf.ap[0:64, nb * 512:(nb + 1) * 512], T.dbc.ap[0:64, :], ALU.mult,
                        [zbuf, T.dbc], [te])
                _tt(kb, "dve", te.ap[0:64, :], kb.ps[0:64, by, :], te.ap[0:64, :], ALU.add, [kb.psb[by], te], [te])
                if o == 0:
                    _tt(kb, "pool", zbuf.ap[0:64, nb * 512:(nb + 1) * 512], xg_.ap[0:64, :], te.ap[0:64, :], ALU.mult,
                        [xg_, te], [zbuf])
                else:
                    zo_ = T.zo
                    _tt(kb, "pool", zo_.ap[0:64, :], xg_.ap[0:64, :], te.ap[0:64, :], ALU.mult, [xg_, te], [zo_])
                    bt_ = kb.bank()
                    for j in range(4):
                        _tr(kb, kb.ps[:, bt_, j * 64:(j + 1) * 64], zo_.ap[0:64, j * 128:(j + 1) * 128],
                            _IdentView(ident), [zo_], [kb.psb[bt_]])
                    _cp(kb, "act", zT3[:, :, nb * 4:(nb + 1) * 4],
                        kb.ps[:, bt_, 0:256].rearrange("c (j p) -> c p j", p=64), [kb.psb[bt_]], [T.zT])
                yield
            if o == 0 and "z1_dbg" in D_:
                dma(kb, "pool", D_["z1_dbg"][g], zbuf.ap[0:64, :], reads=zbuf.b)
            if o == 1:
                dma(kb, "pool", zhyT[g * CG:(g + 1) * CG, :], T.zT.ap, reads=T.zT.b)

    TC = [alloc_c(), alloc_c()]
    for g0 in range(0, NG4, 2):
        gens = [stage_c(g0 + i, TC[i], i) for i in range(min(2, NG4 - g0))]
        for _ in itertools.zip_longest(*gens):
            pass


class _IdentView:
    def __init__(self, ident):
        self.ap = ident.ap[0:64, 0:64]
        self.b = ident.b
```

```python
import contextlib
import math
import numpy as np
import concourse.bass as bass
import concourse.mybir as mybir
from concourse.bass_utils import run_bass_kernel_spmd

F32 = mybir.dt.float32
BF16 = mybir.dt.bfloat16
AF = mybir.ActivationFunctionType
ALU = mybir.AluOpType
AX = mybir.AxisListType

D = 1024
L = 4096
DFF = 2816
NHY = 1536
NRW = 1952
NG = 2048
NIN = NHY + NRW + NG
RMS_EPS = 1e-6
GN_EPS = 64e-5

SEM_EPOCH = 30000
N_DMA_SLOTS = 12


class Buf:
    __slots__ = ("lw", "rd")

    def __init__(self):
        self.lw = None
        self.rd = []


class Op:
    __slots__ = ("eng", "fn", "deps", "dma", "needed", "tok", "slot", "noinst")


class Sched:
    ENGS = ("pe", "act", "dve", "pool", "sp")

    def __init__(self, nc):
        self.nc = nc
        self.streams = {e: [] for e in self.ENGS}
        self.dma_count = {e: 0 for e in self.ENGS}
        self.dma_slot_last = {}

    def op(self, eng, fn, reads=(), writes=(), dma=False, extra=(), noinst=False):
        o = Op()
        o.noinst = noinst
        o.eng = eng
        o.fn = fn
        o.dma = dma
        o.needed = False
        deps = {}
        for b in reads:
            if b.lw is not None:
                deps[id(b.lw)] = b.lw
        for b in writes:
            if b.lw is not None:
                deps[id(b.lw)] = b.lw
            for r in b.rd:
                deps[id(r)] = r
        for d in extra:
            deps[id(d)] = d
        if dma:
            k = self.dma_count[eng]
            self.dma_count[eng] += 1
            o.slot = (eng, k % N_DMA_SLOTS, k // N_DMA_SLOTS)
            prev = self.dma_slot_last.get((eng, k % N_DMA_SLOTS))
            if prev is not None:
                deps[id(prev)] = prev
            self.dma_slot_last[(eng, k % N_DMA_SLOTS)] = o
        dl = []
        for d in deps.values():
            if d is o:
                continue
            if (not d.dma) and d.eng == eng and eng == "pe" and not o.dma:
                continue
            assert not d.noinst
            d.needed = True
            dl.append(d)
        o.deps = dl
        for b in reads:
            b.rd.append(o)
        for b in writes:
            b.lw = o
            b.rd = []
        self.streams[eng].append(o)
        return o

    def barrier(self):
        lasts = list(self.dma_slot_last.values())
        for s in self.streams.values():
            for o in reversed(s):
                if not o.noinst:
                    lasts.append(o)
                    break
        for e in self.ENGS:
            self.op(e, lambda eh: None, extra=lasts, noinst=True)

    def emit(self):
        nc = self.nc
        with contextlib.ExitStack() as st:
            esem = {}
            for e in self.ENGS:
                n_sig = sum(1 for o in self.streams[e] if o.needed and not o.dma)
                n_ep = max(1, (n_sig + SEM_EPOCH - 1) // SEM_EPOCH)
                esem[e] = [st.enter_context(nc.semaphore(f"s_{e}_{i}")) for i in range(n_ep)]
            dsem = {}
            for e in self.ENGS:
                if self.dma_count[e] > 0:
                    dsem[e] = [st.enter_context(nc.semaphore(f"d_{e}_{i}")) for i in range(N_DMA_SLOTS)]
            for e in self.ENGS:
                c = 0
                for o in self.streams[e]:
                    if o.dma:
                        _, s, r = o.slot
                        o.tok = (dsem[e][s], 16 * (r + 1), ("d", e, s))
                    elif o.needed:
                        ep = c // SEM_EPOCH
                        o.tok = (esem[e][ep], (c % SEM_EPOCH) + 1, ("e", e, ep))
                        c += 1
                    else:
                        o.tok = None
            block = st.enter_context(nc.Block())
            hmap = {"pe": block.tensor, "act": block.scalar, "dve": block.vector,
                    "pool": block.gpsimd, "sp": block.sync}
            for e in self.ENGS:
                stream = self.streams[e]
                if not stream:
                    continue

                def section(eh, stream=stream):
                    waited = {}
                    for o in stream:
                        for d in o.deps:
                            sem, val, key = d.tok
                            if waited.get(key, 0) >= val:
                                continue
                            eh.wait_ge(sem, val)
                            waited[key] = val
                        ins = o.fn(eh)
                        if ins is None:
                            continue
                        if o.dma:
                            ins.then_inc(o.tok[0], 16)
                        elif o.needed:
                            ins.then_inc(o.tok[0], 1)

                hmap[e](section)


class Tile:
    def __init__(self, ap, n=1):
        self.ap = ap
        self.b = [Buf() for _ in range(n)]


class KB:
    def __init__(self, nc, st):
        self.nc = nc
        self.S = Sched(nc)
        self.arena = st.enter_context(nc.sbuf_tensor("arena", [128, 49152], F32))
        self.ps = st.enter_context(nc.psum_tensor("psum", [128, 8, 512], F32))
        self.psb = [Buf() for _ in range(8)]
        self.ps_i = 0
        self.reserved = set()
        self.base = 0
        self.off = 0
        self.rr = 0

    def f32(self, n, nb=1):
        assert self.off + n <= 49152, ("sbuf overflow", self.off, n)
        ap = self.arena[:, self.off:self.off + n]
        self.off += n
        return Tile(ap, nb)

    def bf16(self, n, nb=1):
        w = (n + 1) // 2
        assert self.off + w <= 49152, ("sbuf overflow", self.off, w)
        ap = self.arena[:, self.off:self.off + w].bitcast(BF16)[:, 0:n]
        self.off += w
        return Tile(ap, nb)

    def persist(self):
        self.base = self.off

    def phase_end(self):
        self.S.barrier()
        self.off = self.base

    def bank(self):
        while self.ps_i in self.reserved:
            self.ps_i = (self.ps_i + 1) % 8
        i = self.ps_i
        self.ps_i = (self.ps_i + 1) % 8
        return i

    def op(self, eng, fn, reads=(), writes=(), dma=False):
        return self.S.op(eng, fn, reads=reads, writes=writes, dma=dma)

    def ew_eng(self):
        self.rr += 1
        return "dve" if (self.rr % 3) else "pool"


def dma(kb, eng, out, in_, reads=(), writes=(), slow=False):
    if slow:
        return kb.op(eng, lambda e: e.dma_start(out=out, in_=in_, allow_slow_non_contiguous=True),
                     reads=reads, writes=writes, dma=True)
    return kb.op(eng, lambda e: e.dma_start(out=out, in_=in_), reads=reads, writes=writes, dma=True)


def convert_weights(kb, pairs):
    CW = 2816
    stg = [kb.f32(CW) for _ in range(3)]
    stb = [kb.bf16(CW) for _ in range(3)]
    i = 0
    engs = ("dve", "act", "dve", "act", "pool")
    for src, dst, R, C in pairs:
        for r0 in range(0, R, 128):
            rr = min(128, R - r0)
            for c0 in range(0, C, CW):
                cc = min(CW, C - c0)
                a, b = stg[i % 3], stb[i % 3]
                dma(kb, "sp", a.ap[0:rr, 0:cc], src[r0:r0 + rr, c0:c0 + cc], writes=a.b)
                eng = engs[i % 5]
                if eng == "act":
                    kb.op("act", lambda e, a=a, b=b, rr=rr, cc=cc: e.copy(out=b.ap[0:rr, 0:cc], in_=a.ap[0:rr, 0:cc]),
                          reads=a.b, writes=b.b)
                else:
                    kb.op(eng, lambda e, a=a, b=b, rr=rr, cc=cc: e.tensor_copy(out=b.ap[0:rr, 0:cc], in_=a.ap[0:rr, 0:cc]),
                          reads=a.b, writes=b.b)
                dma(kb, "pool" if i % 2 else "act", dst[r0:r0 + rr, c0:c0 + cc], b.ap[0:rr, 0:cc], reads=b.b)
                i += 1


def rmsnorm_T(kb, C, xt, g, out_ap_fn, sq, rstd):
    x3 = xt.ap
    sq3 = sq.ap.rearrange("p (a b) -> p a b", b=512)
    bk = kb.bank()
    for kc in range(8):
        kb.op("act", lambda e, kc=kc: e.activation(out=sq3[:, kc, :], in_=x3[:, kc, :], func=AF.Square),
              reads=xt.b, writes=[sq.b[kc]])
    for kc in range(8):
        kb.op("pe", lambda e, kc=kc: e.matmul(kb.ps[:, bk, :], lhsT=C["ones_bf"].ap, rhs=sq3[:, kc, :],
                                              start=(kc == 0), stop=(kc == 7)),
              reads=[sq.b[kc]] + C["ones_bf"].b, writes=[kb.psb[bk]])
    kb.op("act", lambda e: e.activation(out=rstd.ap, in_=kb.ps[:, bk, :], func=AF.Sqrt, scale=1.0 / D, bias=RMS_EPS),
          reads=[kb.psb[bk]], writes=rstd.b)
    kb.op("dve", lambda e: e.reciprocal(out=rstd.ap, in_=rstd.ap), reads=rstd.b, writes=rstd.b)
    outs = []
    for kc in range(8):
        o = out_ap_fn(kc)
        outs.append(o)
    return outs


def ffn_phase(kb, C, xin, xout, gname, wgu, wdn):
    TQ = 1024
    xn = kb.bf16(8 * TQ, 8)
    G = kb.bf16(22 * TQ, 22)
    xn3 = xn.ap.rearrange("p (a b) -> p a b", b=TQ)
    G3 = G.ap.rearrange("p (a b) -> p a b", b=TQ)
    xts = [kb.f32(8 * 512) for _ in range(2)]
    sq = kb.bf16(8 * 512, 8)
    rstd = kb.f32(512)
    wg = [kb.bf16(8 * 256) for _ in range(2)]
    wu = [kb.bf16(8 * 256) for _ in range(2)]
    wd = [kb.bf16(22 * 512) for _ in range(2)]
    sg = [kb.f32(512) for _ in range(2)]
    xr = [kb.f32(512) for _ in range(2)]
    xo = [kb.f32(512) for _ in range(2)]
    g = C[gname]
    xin3 = xin.rearrange("(kc p) t -> p kc t", p=128)
    wgu3 = wgu.rearrange("(kc p) f -> p kc f", p=128)
    wdn3 = wdn.rearrange("(fc p) d -> p fc d", p=128)
    cnt = 0
    for q in range(L // TQ):
        for t in range(2):
            tok = q * TQ + t * 512
            xt = xts[t]
            x3 = xt.ap.rearrange("p (a b) -> p a b", b=512)
            xt3 = Tile(x3)
            xt3.b = xt.b
            dma(kb, "sp", x3, xin3[:, :, tok:tok + 512], writes=xt.b)
            rmsnorm_T(kb, C, xt3, g, lambda kc: None, sq, rstd)
            for kc in range(8):
                eng = "dve"
                kb.op(eng, lambda e, kc=kc, x3=x3, t=t: e.scalar_tensor_tensor(
                    out=xn3[:, kc, t * 512:(t + 1) * 512], in0=x3[:, kc, :], scalar=g.ap[:, kc:kc + 1],
                    in1=rstd.ap, op0=ALU.mult, op1=ALU.mult),
                    reads=xt.b + rstd.b + g.b, writes=[xn.b[kc]])
        for s in range(11):
            a, b = wg[s % 2], wu[s % 2]
            a3 = a.ap.rearrange("p (a b) -> p a b", b=256)
            b3 = b.ap.rearrange("p (a b) -> p a b", b=256)
            dma(kb, "sp", a3, wgu3[:, :, s * 256:(s + 1) * 256], writes=a.b)
            dma(kb, "sp", b3, wgu3[:, :, DFF + s * 256:DFF + (s + 1) * 256], writes=b.b)
            for fcl in range(2):
                fc = s * 2 + fcl
                for t in range(2):
                    bg = kb.bank()
                    bu = kb.bank()
                    for kc in range(8):
                        kb.op("pe", lambda e, kc=kc, a3=a3, fcl=fcl, t=t, bg=bg: e.matmul(
                            kb.ps[:, bg, :], lhsT=a3[:, kc, fcl * 128:(fcl + 1) * 128],
                            rhs=xn3[:, kc, t * 512:(t + 1) * 512], start=(kc == 0), stop=(kc == 7)),
                            reads=a.b + [xn.b[kc]], writes=[kb.psb[bg]])
                    for kc in range(8):
                        kb.op("pe", lambda e, kc=kc, b3=b3, fcl=fcl, t=t, bu=bu: e.matmul(
                            kb.ps[:, bu, :], lhsT=b3[:, kc, fcl * 128:(fcl + 1) * 128],
                            rhs=xn3[:, kc, t * 512:(t + 1) * 512], start=(kc == 0), stop=(kc == 7)),
                            reads=b.b + [xn.b[kc]], writes=[kb.psb[bu]])
                    sgt = sg[cnt % 2]
                    cnt += 1
                    kb.op("act", lambda e, sgt=sgt, bg=bg: e.activation(out=sgt.ap, in_=kb.ps[:, bg, :], func=AF.Silu),
                          reads=[kb.psb[bg]], writes=sgt.b)
                    kb.op("dve", lambda e, sgt=sgt, bu=bu, fc=fc, t=t: e.tensor_tensor(
                        out=G3[:, fc, t * 512:(t + 1) * 512], in0=kb.ps[:, bu, :], in1=sgt.ap, op=ALU.mult),
                        reads=[kb.psb[bu]] + sgt.b, writes=[G.b[fc]])
        for ds in range(2):
            w = wd[ds % 2]
            w3 = w.ap.rearrange("p (a b) -> p a b", b=512)
            dma(kb, "sp", w3, wdn3[:, :, ds * 512:(ds + 1) * 512], writes=w.b)
            for dcl in range(4):
                dc = ds * 4 + dcl
                for t in range(2):
                    tok = q * TQ + t * 512
                    bo = kb.bank()
                    xrt, xot = xr[cnt % 2], xo[cnt % 2]
                    cnt += 1
                    dma(kb, "sp", xrt.ap, xin[dc * 128:(dc + 1) * 128, tok:tok + 512], writes=xrt.b)
                    for fc in range(22):
                        kb.op("pe", lambda e, fc=fc, w3=w3, dcl=dcl, t=t, bo=bo: e.matmul(
                            kb.ps[:, bo, :], lhsT=w3[:, fc, dcl * 128:(dcl + 1) * 128],
                            rhs=G3[:, fc, t * 512:(t + 1) * 512], start=(fc == 0), stop=(fc == 21)),
                            reads=w.b + [G.b[fc]], writes=[kb.psb[bo]])
                    kb.op("dve", lambda e, xrt=xrt, xot=xot, bo=bo: e.scalar_tensor_tensor(
                        out=xot.ap, in0=kb.ps[:, bo, :], scalar=0.5, in1=xrt.ap, op0=ALU.mult, op1=ALU.add),
                        reads=[kb.psb[bo]] + xrt.b, writes=xot.b)
                    dma(kb, "pool", xout[dc * 128:(dc + 1) * 128, tok:tok + 512], xot.ap, reads=xot.b)


def final_norm_phase(kb, C, xin, out, gname):
    g = C[gname]
    xts = [kb.f32(8 * 512) for _ in range(2)]
    ots = [kb.f32(8 * 512) for _ in range(2)]
    sq = kb.bf16(8 * 512, 8)
    rstd = kb.f32(512)
    xin3 = xin.rearrange("(kc p) t -> p kc t", p=128)
    out3 = out.rearrange("(kc p) t -> p kc t", p=128)
    for t in range(L // 512):
        xt, ot = xts[t % 2], ots[t % 2]
        x3 = xt.ap.rearrange("p (a b) -> p a b", b=512)
        o3 = ot.ap.rearrange("p (a b) -> p a b", b=512)
        xt3 = Tile(x3)
        xt3.b = xt.b
        dma(kb, "sp", x3, xin3[:, :, t * 512:(t + 1) * 512], writes=xt.b)
        rmsnorm_T(kb, C, xt3, g, lambda kc: None, sq, rstd)
        for kc in range(8):
            eng = "dve"
            kb.op(eng, lambda e, kc=kc, x3=x3, o3=o3: e.scalar_tensor_tensor(
                out=o3[:, kc, :], in0=x3[:, kc, :], scalar=g.ap[:, kc:kc + 1], in1=rstd.ap,
                op0=ALU.mult, op1=ALU.mult), reads=xt.b + rstd.b + g.b, writes=ot.b)
        dma(kb, "pool", out3[:, :, t * 512:(t + 1) * 512], o3, reads=ot.b)


def inproj_phase(kb, C, x1T, win, uhy, urwT, gT):
    TQ = 1024
    g = C["mix_norm"]
    xn = kb.bf16(8 * TQ, 8)
    xn3 = xn.ap.rearrange("p (a b) -> p a b", b=TQ)
    xts = [kb.f32(8 * 512) for _ in range(2)]
    sq = kb.bf16(8 * 512, 8)
    rstd = kb.f32(512)
    why = kb.bf16(8 * NHY)
    why3 = why.ap.rearrange("p (a b) -> p a b", b=NHY)
    slabs = [kb.bf16(8 * 512) for _ in range(2)]
    stg = [kb.f32(512) for _ in range(4)]
    zero = kb.f32(NHY)
    x1T3 = x1T.rearrange("(kc p) t -> p kc t", p=128)
    win3 = win.rearrange("(kc p) f -> p kc f", p=128)
    kb.op("pool", lambda e: e.memset(zero.ap, 0.0), writes=zero.b)
    dma(kb, "sp", uhy[0:1, :], zero.ap[0:1, :], reads=zero.b)
    dma(kb, "sp", uhy[L + 1:L + 2, :], zero.ap[0:1, :], reads=zero.b)
    for r0 in range(0, NRW, 128):
        rr = min(128, NRW - r0)
        dma(kb, "sp", urwT[r0:r0 + rr, 0:1], zero.ap[0:rr, 0:1], reads=zero.b, slow=True)
        dma(kb, "sp", urwT[r0:r0 + rr, L + 1:L + 2], zero.ap[0:rr, 0:1], reads=zero.b, slow=True)
    dma(kb, "sp", why3, win3[:, :, 0:NHY], writes=why.b)
    cnt = 0
    for q in range(L // TQ):
        for t in range(2):
            tok = q * TQ + t * 512
            xt = xts[t]
            x3 = xt.ap.rearrange("p (a b) -> p a b", b=512)
            xt3 = Tile(x3)
            xt3.b = xt.b
            dma(kb, "sp", x3, x1T3[:, :, tok:tok + 512], writes=xt.b)
            rmsnorm_T(kb, C, xt3, g, lambda kc: None, sq, rstd)
            for kc in range(8):
                kb.op("dve", lambda e, kc=kc, x3=x3, t=t: e.scalar_tensor_tensor(
                    out=xn3[:, kc, t * 512:(t + 1) * 512], in0=x3[:, kc, :], scalar=g.ap[:, kc:kc + 1],
                    in1=rstd.ap, op0=ALU.mult, op1=ALU.mult),
                    reads=xt.b + rstd.b + g.b, writes=[xn.b[kc]])
        for (dst, col0, ncols, gate, coff) in ((urwT, NHY, NRW, False, 1), (gT, NHY + NRW, NG, True, 0)):
            for s0 in range(0, ncols, 512):
                cw = min(512, ncols - s0)
                sl = slabs[cnt % 2]
                sl3 = sl.ap.rearrange("p (a b) -> p a b", b=512)
                dma(kb, "sp", sl3[:, :, 0:cw], win3[:, :, col0 + s0:col0 + s0 + cw], writes=sl.b)
                for c0 in range(0, cw, 128):
                    m = min(128, cw - c0)
                    for t in range(2):
                        tok = q * TQ + t * 512
                        bk = kb.bank()
                        for kc in range(8):
                            kb.op("pe", lambda e, kc=kc, sl3=sl3, c0=c0, m=m, t=t, bk=bk: e.matmul(
                                kb.ps[0:m, bk, :], lhsT=sl3[:, kc, c0:c0 + m],
                                rhs=xn3[:, kc, t * 512:(t + 1) * 512], start=(kc == 0), stop=(kc == 7)),
                                reads=sl.b + [xn.b[kc]], writes=[kb.psb[bk]])
                        sg_ = stg[cnt % 4]
                        cnt += 1
                        if gate:
                            kb.op("act", lambda e, sg_=sg_, bk=bk, m=m: e.activation(
                                out=sg_.ap[0:m, :], in_=kb.ps[0:m, bk, :], func=AF.Sigmoid),
                                reads=[kb.psb[bk]], writes=sg_.b)
                        else:
                            kb.op("dve", lambda e, sg_=sg_, bk=bk, m=m: e.tensor_copy(
                                out=sg_.ap[0:m, :], in_=kb.ps[0:m, bk, :]),
                                reads=[kb.psb[bk]], writes=sg_.b)
                        r0 = s0 + c0
                        dma(kb, "pool", dst[r0:r0 + m, coff + tok:coff + tok + 512], sg_.ap[0:m, :], reads=sg_.b)
        for tb in range(TQ // 128):
            tok = q * TQ + tb * 128
            for cs in range(3):
                bk = kb.bank()
                for kc in range(8):
                    kb.op("pe", lambda e, kc=kc, tb=tb, cs=cs, bk=bk: e.matmul(
                        kb.ps[:, bk, :], lhsT=xn3[:, kc, tb * 128:(tb + 1) * 128],
                        rhs=why3[:, kc, cs * 512:(cs + 1) * 512], start=(kc == 0), stop=(kc == 7)),
                        reads=why.b + [xn.b[kc]], writes=[kb.psb[bk]])
                sg_ = stg[cnt % 4]
                cnt += 1
                kb.op("act", lambda e, sg_=sg_, bk=bk: e.copy(out=sg_.ap, in_=kb.ps[:, bk, :]),
                      reads=[kb.psb[bk]], writes=sg_.b)
                dma(kb, "pool", uhy[1 + tok:1 + tok + 128, cs * 512:(cs + 1) * 512], sg_.ap, reads=sg_.b)


def _bl(*tiles):
    out = []
    for t in tiles:
        out.extend(t.b if isinstance(t, Tile) else [t])
    return out


def _tt(kb, eng, out, a, b, op, R, W):
    return kb.op(eng, lambda e: e.tensor_tensor(out=out, in0=a, in1=b, op=op), reads=_bl(*R), writes=_bl(*W))


def _ts(kb, eng, out, a, s1, s2, op0, op1, R, W):
    if op1 is None:
        return kb.op(eng, lambda e: e.tensor_scalar(out=out, in0=a, scalar1=s1, scalar2=None, op0=op0),
                     reads=_bl(*R), writes=_bl(*W))
    return kb.op(eng, lambda e: e.tensor_scalar(out=out, in0=a, scalar1=s1, scalar2=s2, op0=op0, op1=op1),
                 reads=_bl(*R), writes=_bl(*W))


def _stt(kb, out, a, s, b, op0, op1, R, W):
    return kb.op("dve", lambda e: e.scalar_tensor_tensor(out=out, in0=a, scalar=s, in1=b, op0=op0, op1=op1),
                 reads=_bl(*R), writes=_bl(*W))


def _act(kb, out, a, func, R, W, scale=1.0, bias=0.0):
    return kb.op("act", lambda e: e.activation(out=out, in_=a, func=func, scale=scale, bias=bias),
                 reads=_bl(*R), writes=_bl(*W))


def _cp(kb, eng, out, a, R, W):
    if eng == "act":
        return kb.op("act", lambda e: e.copy(out=out, in_=a), reads=_bl(*R), writes=_bl(*W))
    return kb.op(eng, lambda e: e.tensor_copy(out=out, in_=a), reads=_bl(*R), writes=_bl(*W))


def _mm(kb, out, lhsT, rhs, R, W, start=True, stop=True):
    return kb.op("pe", lambda e: e.matmul(out, lhsT=lhsT, rhs=rhs, start=start, stop=stop),
                 reads=_bl(*R), writes=_bl(*W))


def _tr(kb, out, in_, ident, R, W):
    return kb.op("pe", lambda e: e.transpose(out, in_, ident.ap), reads=_bl(*R) + ident.b, writes=_bl(*W))


KAPPA = math.exp(-0.5)
SC = 256
NCH = SC // 64


RW_DBG = {}


class _NS:
    pass


def rwkv_phase(kb, C, P, urwT, yrwT, RD):
    import itertools
    dbg = RW_DBG
    ident, bd64 = C["ident"], C["bd64"]
    W = SC
    WH = W + 2
    NI = NCH * 2
    yT, bT = RD["yT"], RD["bT"]
    yT_b = [[[Buf() for _ in range(L // W)] for _ in range(4)] for _ in range(2)]
    bT_b = [[[Buf() for _ in range(L // W)] for _ in range(4)] for _ in range(2)]
    w2t = kb.f32(1024)
    a2t = kb.f32(1024)
    g2a = kb.f32(512)
    g2b = kb.f32(512)
    mNB = [kb.f32(512), kb.f32(512)]
    mAAB = [kb.f32(512), kb.f32(512)]
    I8 = kb.f32(512)
    vecs = kb.f32(20)
    w0t = kb.f32(8)
    a0t = kb.f32(8)
    mu_rkv = kb.f32(24)
    mu_wa = kb.f32(8)
    mu_g = kb.f32(4)
    dma(kb, "sp", w2t.ap[0:64, :], P["rw_w2t"], writes=w2t.b)
    dma(kb, "sp", a2t.ap[0:64, :], P["rw_a2t"], writes=a2t.b)
    dma(kb, "sp", g2a.ap, P["rw_g2"][0:128, :], writes=g2a.b)
    dma(kb, "sp", g2b.ap[0:32, :], P["rw_g2"][128:160, :], writes=g2b.b)
    dma(kb, "sp", mNB[0].ap[0:64, :], P["mNBf"], writes=mNB[0].b)
    dma(kb, "sp", mNB[1].ap[0:64, :], P["mNBb"], writes=mNB[1].b)
    dma(kb, "sp", mAAB[0].ap[0:64, :], P["mAABf"], writes=mAAB[0].b)
    dma(kb, "sp", mAAB[1].ap[0:64, :], P["mAABb"], writes=mAAB[1].b)
    dma(kb, "sp", I8.ap[0:64, :], P["I8"], writes=I8.b)
    rmask = kb.f32(SC)
    dma(kb, "sp", rmask.ap, P["rmask"], writes=rmask.b)
    dma(kb, "sp", vecs.ap, P["rw_vecs"], writes=vecs.b)
    for fc in range(4):
        dma(kb, "sp", w0t.ap[:, fc * 2:fc * 2 + 2], P["rw_w0T"][fc * 128:(fc + 1) * 128, :], writes=w0t.b)
        dma(kb, "sp", a0t.ap[:, fc * 2:fc * 2 + 2], P["rw_a0T"][fc * 128:(fc + 1) * 128, :], writes=a0t.b)
        for kind in range(3):
            o = (kind * 4 + fc) * 2
            r0 = kind * 512 + fc * 128
            dma(kb, "sp", mu_rkv.ap[:, o:o + 2], P["rw_muT"][r0:r0 + 128, :], writes=mu_rkv.b)
    for i in range(4):
        r0 = 1536 + i * 64
        dma(kb, "sp", mu_wa.ap[0:64, i * 2:i * 2 + 2], P["rw_muT"][r0:r0 + 64, :], writes=mu_wa.b)
    dma(kb, "sp", mu_g.ap[:, 0:2], P["rw_muT"][1792:1920, :], writes=mu_g.b)
    dma(kb, "sp", mu_g.ap[0:32, 2:4], P["rw_muT"][1920:1952, :], writes=mu_g.b)

    def vec(i, fc):
        return vecs.ap[:, i * 4 + fc:i * 4 + fc + 1]

    def v3(t, b=64):
        return t.ap.rearrange("p (a b) -> p a b", b=b)

    def z4(t):
        return t.ap.rearrange("p (c h t) -> p c h t", h=2, t=64)

    N3 = lambda t: t.ap.rearrange("p (i t) -> p i t", t=64)

    def alloc_stream():
        T = _NS()
        T.ld = [kb.f32(WH + 6) for _ in range(5)]
        T.tp = [kb.f32(W) for _ in range(23)]
        T.ar = kb.f32(2 * W)
        T.tok = [kb.f32(NCH * 128) for _ in range(4)]
        T.NBt = kb.f32(NCH * 2 * 128)
        T.KBt = kb.f32(NCH * 2 * 128)
        T.AAB = kb.bf16(NI * 64)
        T.N0 = kb.bf16(NI * 64)
        T.Nk = [kb.bf16(NI * 64) for _ in range(2)]
        T.Ak = [kb.bf16(NI * 64) for _ in range(2)]
        T.Pk = [kb.bf16(NI * 64) for _ in range(2)]
        T.Pf = kb.f32(NI * 64)
        T.AKV = kb.f32(NI * 64)
        T.W2 = kb.f32(NI * 64)
        T.Hs = [kb.f32(128), kb.f32(128)]
        T.Usb = kb.f32(128)
        T.gTt = kb.f32(NCH)
        T.ysc = kb.f32(W)
        T.bsc = kb.f32(W)
        T.pad = [kb.f32(NI * 64) for _ in range(5)]
        for zt in T.pad:
            kb.op("pool", lambda e, zt=zt: e.memset(zt.ap, 0.0), writes=zt.b)
        return T

    TS = [alloc_stream(), alloc_stream()]

    def shift(T, u, out, mu0, mu1, np_=128, seng="pool"):
        t1, t2 = T.tp[5], T.tp[6]
        mus = [mu_rkv, mu_wa, mu_g]
        _tt(kb, seng, t1.ap[0:np_, :], u.ap[0:np_, 0:W], u.ap[0:np_, 1:W + 1], ALU.subtract, [u], [t1])
        _stt(kb, out.ap[0:np_, :], t1.ap[0:np_, :], mu0, u.ap[0:np_, 1:W + 1], ALU.mult, ALU.add, [t1, u] + mus, [out])
        _tt(kb, seng, t2.ap[0:np_, :], u.ap[0:np_, 2:W + 2], u.ap[0:np_, 1:W + 1], ALU.subtract, [u], [t2])
        _stt(kb, out.ap[0:np_, :], t2.ap[0:np_, :], mu1, out.ap[0:np_, :], ALU.mult, ALU.add, [t2, out] + mus, [out])

    def sc_gen(fc, d, T):
        hcur = 0
        kb.op("pool", lambda e: e.memset(T.Hs[0].ap, 0.0), writes=T.Hs[0].b)
        sc_list = list(range(L // W)) if d == 0 else list(range(L // W - 1, -1, -1))
        sc_list = sc_list[:dbg.get('nsc', len(sc_list))]
        (r, k, v, wdx, adx, t1, t2, sg, lr, kraw, rn, kk, kd, bv, cA, cB, ginc, gexc, ginv, gts,
         bt, bh, kh) = T.tp
        tw, tmp = t1, t2
        ar, tok, NBt, KBt, AAB, Nk, Ak, Pk, AKV, W2 = T.ar, T.tok, T.NBt, T.KBt, T.AAB, T.Nk, T.Ak, T.Pk, T.AKV, T.W2
        N0, Pf = T.N0, T.Pf
        btz, ktz, atz, rz, W1Tz = T.pad
        Hs, Usb, gTt, ysc, bsc = T.Hs, T.Usb, T.gTt, T.ysc, T.bsc
        ar4 = ar.ap.rearrange("p (c q t) -> p c q t", q=2, t=64)
        NB4 = NBt.ap.rearrange("p (i q t) -> p i q t", q=2, t=64)
        KB4 = KBt.ap.rearrange("p (i q t) -> p i q t", q=2, t=64)
        tok3 = [t.ap.rearrange("p (c f) -> p c f", f=128) for t in tok]
        for sci, sc in enumerate(sc_list):
            t0 = sc * W
            (ur, uk, uv, uw, ua) = T.ld
            dma(kb, "sp", uw.ap[0:64, 0:WH], urwT[1536 + d * 64:1536 + (d + 1) * 64, t0:t0 + WH], writes=uw.b)
            dma(kb, "sp", ua.ap[0:64, 0:WH], urwT[1664 + d * 64:1664 + (d + 1) * 64, t0:t0 + WH], writes=ua.b)
            dma(kb, "sp", uk.ap[:, 0:WH], urwT[512 + fc * 128:512 + (fc + 1) * 128, t0:t0 + WH], writes=uk.b)
            dma(kb, "sp", ur.ap[:, 0:WH], urwT[fc * 128:(fc + 1) * 128, t0:t0 + WH], writes=ur.b)
            dma(kb, "sp", uv.ap[:, 0:WH], urwT[1024 + fc * 128:1024 + (fc + 1) * 128, t0:t0 + WH], writes=uv.b)

            def mu3(kind):
                o = (kind * 4 + fc) * 2
                return mu_rkv.ap[:, o:o + 1], mu_rkv.ap[:, o + 1:o + 2]

            shift(T, uw, wdx, mu_wa.ap[0:64, d * 2:d * 2 + 1], mu_wa.ap[0:64, d * 2 + 1:d * 2 + 2], 64, "dve")
            yield
            shift(T, ua, adx, mu_wa.ap[0:64, 4 + d * 2:5 + d * 2], mu_wa.ap[0:64, 5 + d * 2:6 + d * 2], 64, "dve")
            yield
            shift(T, uk, k, *mu3(1))
            yield
            shift(T, ur, r, *mu3(0))
            yield
            shift(T, uv, v, *mu3(2))
            yield
            _act(kb, tw.ap[0:64, :], wdx.ap[0:64, :], AF.Tanh, [wdx], [tw])
            b1 = kb.bank()
            _mm(kb, kb.ps[:, b1, 0:W], w2t.ap[0:64, d * 512 + fc * 128:d * 512 + (fc + 1) * 128], tw.ap[0:64, :],
                [w2t, tw], [kb.psb[b1]])
            _act(kb, sg.ap, kb.ps[:, b1, 0:W], AF.Sigmoid, [kb.psb[b1], w0t], [sg],
                 bias=w0t.ap[:, fc * 2 + d:fc * 2 + d + 1])
            b2 = kb.bank()
            _mm(kb, kb.ps[:, b2, 0:W], a2t.ap[0:64, d * 512 + fc * 128:d * 512 + (fc + 1) * 128], adx.ap[0:64, :],
                [a2t, adx], [kb.psb[b2]])
            _act(kb, lr.ap, kb.ps[:, b2, 0:W], AF.Sigmoid, [kb.psb[b2], a0t], [lr],
                 bias=a0t.ap[:, fc * 2 + d:fc * 2 + d + 1])
            yield
            _act(kb, kraw.ap, k.ap, AF.Square, [k, vecs], [kraw], scale=vec(0, fc))
            b3 = kb.bank()
            _mm(kb, kb.ps[:, b3, 0:W], bd64.ap, kraw.ap, [bd64, kraw], [kb.psb[b3]])
            _act(kb, rn.ap, kb.ps[:, b3, 0:W], AF.Sqrt, [kb.psb[b3]], [rn])
            _ts(kb, "dve", rn.ap, rn.ap, 1e-12, None, ALU.max, None, [rn], [rn])
            kb.op("dve", lambda e: e.reciprocal(out=rn.ap, in_=rn.ap), reads=rn.b, writes=rn.b)
            _stt(kb, kk.ap, k.ap, vec(0, fc), rn.ap, ALU.mult, ALU.mult, [k, vecs, rn], [kk])
            yield
            _ts(kb, "dve", tmp.ap, lr.ap, -1.0, vec(1, fc), ALU.add, ALU.mult, [lr, vecs], [tmp])
            _stt(kb, kd.ap, tmp.ap, 1.0, k.ap, ALU.add, ALU.mult, [tmp, k], [kd])
            _tt(kb, "pool", bv.ap, kk.ap, lr.ap, ALU.mult, [kk, lr], [bv])
            _stt(kb, bsc.ap, r.ap, vec(2, fc), kd.ap, ALU.mult, ALU.mult, [r, kd, vecs], [bsc])
            dma(kb, "pool", bT[d, fc * 128:(fc + 1) * 128, t0:t0 + W], bsc.ap, reads=bsc.b, writes=[bT_b[d][fc][sc]])
            if d == 0:
                kb.op("dve", lambda e: e.tensor_tensor_scan(out=cB.ap, data0=rmask.ap, data1=sg.ap, initial=0.0,
                                                           op0=ALU.mult, op1=ALU.add),
                      reads=_bl(rmask, sg), writes=cB.b)
            else:
                kb.op("dve", lambda e: e.tensor_tensor_scan(out=cA.ap, data0=rmask.ap, data1=sg.ap, initial=0.0,
                                                           op0=ALU.mult, op1=ALU.add),
                      reads=_bl(rmask, sg), writes=cA.b)
                pre3 = v3(cA)
                _tt(kb, "dve", v3(cB), pre3, pre3[:, :, 63:64].to_broadcast([128, NCH, 64]), ALU.subtract, [cA], [cB])
                _tt(kb, "dve", cB.ap, sg.ap, cB.ap, ALU.subtract, [sg, cB], [cB])
            yield
            cs = cB
            cs3 = v3(cs)
            ti = 63 if d == 0 else 0
            totb = cs3[:, :, ti:ti + 1].to_broadcast([128, NCH, 64])
            _act(kb, ginc.ap, cs.ap, AF.Exp, [cs], [ginc], scale=-KAPPA)
            _act(kb, ginv.ap, cs.ap, AF.Exp, [cs], [ginv], scale=KAPPA)
            _tt(kb, "dve", tmp.ap, cs.ap, sg.ap, ALU.subtract, [cs, sg], [tmp])
            _act(kb, gexc.ap, tmp.ap, AF.Exp, [tmp], [gexc], scale=-KAPPA)
            _tt(kb, "dve", v3(cA), cs3, totb, ALU.subtract, [cs], [cA])
            _act(kb, gts.ap, cA.ap, AF.Exp, [cA], [gts], scale=KAPPA)
            _act(kb, gTt.ap, cs3[:, :, ti], AF.Exp, [cs], [gTt], scale=-KAPPA)
            yield
            _stt(kb, ar4[:, :, 0, :], v3(kk), -1.0, v3(gexc), ALU.mult, ALU.mult, [kk, gexc], [ar])
            _tt(kb, "pool", ar4[:, :, 1, :], v3(r), v3(ginc), ALU.mult, [r, ginc], [ar])
            _tt(kb, "dve", bt.ap, bv.ap, ginv.ap, ALU.mult, [bv, ginv], [bt])
            yield
            for h2 in range(2):
                ps_ = slice(h2 * 64, (h2 + 1) * 64)
                _stt(kb, z4(atz)[ps_, :, h2, :], v3(kk)[ps_], -1.0, v3(gexc)[ps_], ALU.mult, ALU.mult, [kk, gexc], [atz])
                _tt(kb, "pool", z4(rz)[ps_, :, h2, :], v3(r)[ps_], v3(ginc)[ps_], ALU.mult, [r, ginc], [rz])
                _tt(kb, "dve", z4(btz)[ps_, :, h2, :], v3(bv)[ps_], v3(ginv)[ps_], ALU.mult, [bv, ginv], [btz])
                _tt(kb, "pool", z4(ktz)[ps_, :, h2, :], v3(kd)[ps_], v3(ginv)[ps_], ALU.mult, [kd, ginv], [ktz])
            yield
            _tt(kb, "pool", bh.ap, bv.ap, gts.ap, ALU.mult, [bv, gts], [bh])
            _tt(kb, "dve", kh.ap, kd.ap, gts.ap, ALU.mult, [kd, gts], [kh])
            yield
            for qi, (srct, fn) in enumerate(((ar, lambda c: ar4[:, c, 0, :]), (bh, lambda c: bh.ap[:, c * 64:(c + 1) * 64]),
                                            (kh, lambda c: kh.ap[:, c * 64:(c + 1) * 64]),
                                            (v, lambda c: v.ap[:, c * 64:(c + 1) * 64]))):
                bk = kb.bank()
                for c in range(NCH):
                    _tr(kb, kb.ps[0:64, bk, c * 128:(c + 1) * 128], fn(c), ident, [srct], [kb.psb[bk]])
                _cp(kb, "act" if qi % 2 else "dve", tok[qi].ap[0:64, :], kb.ps[0:64, bk, :], [kb.psb[bk]], [tok[qi]])
                if qi % 2 == 1:
                    yield
            yield
            for (lt, dstt) in ((btz, NBt), (ktz, KBt)):
                for half in range(2):
                    bk = kb.bank()
                    for ii in range(4):
                        i = half * 4 + ii
                        c, h2 = i // 2, i % 2
                        _mm(kb, kb.ps[0:64, bk, ii * 128:(ii + 1) * 128], z4(lt)[:, c, h2, :],
                            ar.ap[:, c * 128:(c + 1) * 128], [lt, ar], [kb.psb[bk]])
                    _tt(kb, "dve", dstt.ap[0:64, half * 512:(half + 1) * 512], kb.ps[0:64, bk, :],
                        mNB[d].ap[0:64, :], ALU.mult, [kb.psb[bk], mNB[d]], [dstt])
                yield
            bk = kb.bank()
            for i in range(NI):
                c, h2 = i // 2, i % 2
                _mm(kb, kb.ps[0:64, bk, i * 64:(i + 1) * 64], z4(atz)[:, c, h2, :],
                    bt.ap[:, c * 64:(c + 1) * 64], [atz, bt], [kb.psb[bk]])
            _tt(kb, "dve", AAB.ap[0:64, :], kb.ps[0:64, bk, :], mAAB[d].ap[0:64, :], ALU.mult,
                [kb.psb[bk], mAAB[d]], [AAB])
            _tt(kb, "dve", N3(Pk[0])[0:64], NB4[0:64, :, 0, :], N3(I8)[0:64], ALU.add, [NBt, I8], [Pk[0]])
            _cp(kb, "act", N3(N0)[0:64], NB4[0:64, :, 0, :], [NBt], [N0])
            yield
            curN = lambda i: N3(N0)[0:64, i, :]
            curNt = N0
            curA = AAB
            pc = 0
            for lev in range(5):
                Nn, An = Nk[lev % 2], Ak[lev % 2]
                bA = kb.bank()
                for i in range(NI):
                    _mm(kb, kb.ps[0:64, bA, i * 64:(i + 1) * 64], curN(i), N3(curA)[0:64, i, :], [curNt, curA], [kb.psb[bA]])
                if lev < 4:
                    bN = kb.bank()
                    for i in range(NI):
                        _mm(kb, kb.ps[0:64, bN, i * 64:(i + 1) * 64], N3(curA)[0:64, i, :], curN(i), [curNt, curA], [kb.psb[bN]])
                _cp(kb, "act", An.ap[0:64, :], kb.ps[0:64, bA, :], [kb.psb[bA]], [An])
                if lev < 4:
                    _cp(kb, "dve", Nn.ap[0:64, :], kb.ps[0:64, bN, :], [kb.psb[bN]], [Nn])
                yield
                bP = kb.bank()
                for i in range(NI):
                    _mm(kb, kb.ps[0:64, bP, i * 64:(i + 1) * 64], N3(An)[0:64, i, :], N3(Pk[pc])[0:64, i, :],
                        [An, Pk[pc]], [kb.psb[bP]])
                _tt(kb, "dve", Pk[1 - pc].ap[0:64, :], kb.ps[0:64, bP, :], Pk[pc].ap[0:64, :], ALU.add,
                    [kb.psb[bP], Pk[pc]], [Pk[1 - pc]])
                pc = 1 - pc
                curA = An
                curNt = Nn
                curN = (lambda Nn: (lambda i: N3(Nn)[0:64, i, :]))(Nn)
                yield
            _cp(kb, "dve", Pf.ap[0:64, :], Pk[pc].ap[0:64, :], [Pk[pc]], [Pf])
            Pm = Pf
            bk = kb.bank()
            for i in range(NI):
                c, h2 = i // 2, i % 2
                _mm(kb, kb.ps[0:64, bk, i * 64:(i + 1) * 64], KB4[0:64, i, 0, :],
                    tok3[3][0:64, c, h2 * 64:(h2 + 1) * 64], [KBt, tok[3]], [kb.psb[bk]])
            _cp(kb, "act", AKV.ap[0:64, :], kb.ps[0:64, bk, :], [kb.psb[bk]], [AKV])
            bk2 = kb.bank()
            for i in range(NI):
                c, h2 = i // 2, i % 2
                _mm(kb, kb.ps[:, bk2, i * 64:(i + 1) * 64], tok3[0][0:64, c, :], N3(Pm)[0:64, i, :],
                    [tok[0], Pm], [kb.psb[bk2]])
            ps4 = kb.ps[:, bk2, :].rearrange("p (c h t) -> p c h t", h=2, t=64)
            _cp(kb, "dve", z4(W1Tz)[0:64, :, 0, :], ps4[0:64, :, 0, :], [kb.psb[bk2]], [W1Tz])
            _cp(kb, "dve", z4(W1Tz)[64:128, :, 1, :], ps4[64:128, :, 1, :], [kb.psb[bk2]], [W1Tz])
            yield
            bk = kb.bank()
            for i in range(NI):
                _mm(kb, kb.ps[0:64, bk, i * 64:(i + 1) * 64], N3(Pm)[0:64, i, :], N3(AKV)[0:64, i, :],
                    [Pm, AKV], [kb.psb[bk]])
            _cp(kb, "act", W2.ap[0:64, :], kb.ps[0:64, bk, :], [kb.psb[bk]], [W2])
            W1T3 = N3(W1Tz)
            yield
            corder = list(range(NCH)) if d == 0 else list(range(NCH - 1, -1, -1))
            for c in corder:
                H, Hn = Hs[hcur], Hs[1 - hcur]
                bU = kb.bank()
                for h2 in range(2):
                    _mm(kb, kb.ps[0:64, bU, h2 * 64:(h2 + 1) * 64], W1T3[:, c * 2 + h2, :],
                        H.ap[:, h2 * 64:(h2 + 1) * 64], [W1Tz, H], [kb.psb[bU]])
                _tt(kb, "dve", Usb.ap[0:64, :], kb.ps[0:64, bU, 0:128], W2.ap[0:64, c * 128:(c + 1) * 128], ALU.add,
                    [kb.psb[bU], W2], [Usb])
                yield
                bH = kb.bank()
                _mm(kb, kb.ps[:, bH, 0:128], tok3[2][0:64, c, :], tok3[3][0:64, c, :], [tok[2], tok[3]],
                    [kb.psb[bH]], start=True, stop=False)
                _mm(kb, kb.ps[:, bH, 0:128], tok3[1][0:64, c, :], Usb.ap[0:64, :], [tok[1], Usb],
                    [kb.psb[bH]], start=False, stop=True)
                bY = kb.bank()
                psY3 = kb.ps[:, bY, 0:128].rearrange("p (h t) -> p h t", t=64)
                _mm(kb, psY3, H.ap, z4(rz)[:, c, :, :], [H, rz], [kb.psb[bY]], start=True, stop=False)
                _mm(kb, psY3, Usb.ap[0:64, :], NB4[0:64, c * 2:c * 2 + 2, 1, :],
                    [Usb, NBt], [kb.psb[bY]], start=False, stop=False)
                _mm(kb, psY3, tok3[3][0:64, c, :], KB4[0:64, c * 2:c * 2 + 2, 1, :],
                    [tok[3], KBt], [kb.psb[bY]], start=False, stop=True)
                _stt(kb, Hn.ap, H.ap, gTt.ap[:, c:c + 1], kb.ps[:, bH, 0:128], ALU.mult, ALU.add,
                     [H, gTt, kb.psb[bH]], [Hn])
                _cp(kb, "act", ysc.ap[0:64, c * 64:(c + 1) * 64], kb.ps[0:64, bY, 0:64], [kb.psb[bY]], [ysc])
                _cp(kb, "act", ysc.ap[64:128, c * 64:(c + 1) * 64], kb.ps[64:128, bY, 64:128], [kb.psb[bY]], [ysc])
                hcur = 1 - hcur
                yield
            dma(kb, "pool", yT[d, fc * 128:(fc + 1) * 128, t0:t0 + W], ysc.ap, reads=ysc.b, writes=[yT_b[d][fc][sc]])

    TP = _NS()
    TP.ld = [kb.f32(WH + 6) for _ in range(3)]
    TP.tp = [kb.f32(W) for _ in range(15)]

    def shift_p(u, out, mu0, mu1, np_=128):
        t1, t2 = TP.tp[13], TP.tp[14]
        mus = [mu_rkv, mu_wa, mu_g]
        _tt(kb, "pool", t1.ap[0:np_, :], u.ap[0:np_, 0:W], u.ap[0:np_, 1:W + 1], ALU.subtract, [u], [t1])
        _stt(kb, out.ap[0:np_, :], t1.ap[0:np_, :], mu0, u.ap[0:np_, 1:W + 1], ALU.mult, ALU.add, [t1, u] + mus, [out])
        _tt(kb, "pool", t2.ap[0:np_, :], u.ap[0:np_, 2:W + 2], u.ap[0:np_, 1:W + 1], ALU.subtract, [u], [t2])
        _stt(kb, out.ap[0:np_, :], t2.ap[0:np_, :], mu1, out.ap[0:np_, :], ALU.mult, ALU.add, [t2, out] + mus, [out])

    def post_gen(fc):
        for ti_ in range(L // W if dbg.get('post', True) else 0):
            t0 = ti_ * W
            (uv, ug0, ug1) = TP.ld
            (y, cen, sq_, rs, yn, vv, g0, g1, bvv, ob_, y1, bo0, bo1) = TP.tp[0:13]
            dma(kb, "sp", uv.ap[:, 0:WH], urwT[1024 + fc * 128:1024 + (fc + 1) * 128, t0:t0 + WH], writes=uv.b)
            dma(kb, "sp", ug0.ap[:, 0:WH], urwT[1792:1920, t0:t0 + WH], writes=ug0.b)
            dma(kb, "sp", ug1.ap[0:32, 0:WH], urwT[1920:1952, t0:t0 + WH], writes=ug1.b)
            fsl = slice(fc * 128, (fc + 1) * 128)
            dma(kb, "sp", y.ap, yT[0, fsl, t0:t0 + W], reads=[yT_b[0][fc][ti_]], writes=y.b)
            dma(kb, "sp", y1.ap, yT[1, fsl, t0:t0 + W], reads=[yT_b[1][fc][ti_]], writes=y1.b)
            dma(kb, "sp", bo0.ap, bT[0, fsl, t0:t0 + W], reads=[bT_b[0][fc][ti_]], writes=bo0.b)
            dma(kb, "sp", bo1.ap, bT[1, fsl, t0:t0 + W], reads=[bT_b[1][fc][ti_]], writes=bo1.b)
            o = (2 * 4 + fc) * 2
            shift_p(uv, vv, mu_rkv.ap[:, o:o + 1], mu_rkv.ap[:, o + 1:o + 2])
            shift_p(ug0, g0, mu_g.ap[:, 0:1], mu_g.ap[:, 1:2])
            yield
            shift_p(ug1, g1, mu_g.ap[0:32, 2:3], mu_g.ap[0:32, 3:4], 32)
            _act(kb, g0.ap, g0.ap, AF.Sigmoid, [g0], [g0])
            _act(kb, g1.ap[0:32, :], g1.ap[0:32, :], AF.Sigmoid, [g1], [g1])
            _tt(kb, "pool", y.ap, y.ap, y1.ap, ALU.add, [y, y1], [y])
            _tt(kb, "pool", bo0.ap, bo0.ap, bo1.ap, ALU.add, [bo0, bo1], [bo0])
            yield
            bM = kb.bank()
            _mm(kb, kb.ps[:, bM, 0:W], bd64.ap, y.ap, [bd64, y], [kb.psb[bM]])
            _stt(kb, cen.ap, kb.ps[:, bM, 0:W], -1.0 / 64, y.ap, ALU.mult, ALU.add, [kb.psb[bM], y], [cen])
            _tt(kb, "pool", sq_.ap, cen.ap, cen.ap, ALU.mult, [cen], [sq_])
            yield
            bV = kb.bank()
            _mm(kb, kb.ps[:, bV, 0:W], bd64.ap, sq_.ap, [bd64, sq_], [kb.psb[bV]])
            _act(kb, rs.ap, kb.ps[:, bV, 0:W], AF.Sqrt, [kb.psb[bV]], [rs], scale=1.0 / 64, bias=GN_EPS)
            kb.op("dve", lambda e, rs=rs: e.reciprocal(out=rs.ap, in_=rs.ap), reads=rs.b, writes=rs.b)
            _tt(kb, "pool", yn.ap, cen.ap, rs.ap, ALU.mult, [cen, rs], [yn])
            _ts(kb, "dve", yn.ap, yn.ap, vec(3, fc), vec(4, fc), ALU.mult, ALU.add, [yn, vecs], [yn])
            yield
            bB = kb.bank()
            _mm(kb, kb.ps[:, bB, 0:W], bd64.ap, bo0.ap, [bd64, bo0], [kb.psb[bB]])
            _tt(kb, "dve", bvv.ap, kb.ps[:, bB, 0:W], vv.ap, ALU.mult, [kb.psb[bB], vv], [bvv])
            _tt(kb, "pool", yn.ap, yn.ap, bvv.ap, ALU.add, [yn, bvv], [yn])
            bG = kb.bank()
            _mm(kb, kb.ps[:, bG, 0:W], g2a.ap[:, fc * 128:(fc + 1) * 128], g0.ap, [g2a, g0], [kb.psb[bG]],
                start=True, stop=False)
            _mm(kb, kb.ps[:, bG, 0:W], g2b.ap[0:32, fc * 128:(fc + 1) * 128], g1.ap[0:32, :], [g2b, g1], [kb.psb[bG]],
                start=False, stop=True)
            ob = ob_.ap.bitcast(BF16)[:, 0:W]
            _tt(kb, "dve", ob, kb.ps[:, bG, 0:W], yn.ap, ALU.mult, [kb.psb[bG], yn], [ob_])
            dma(kb, "pool", yrwT[fc * 128:(fc + 1) * 128, t0:t0 + W], ob, reads=ob_.b)
            yield

    nfc = dbg.get('fcs', 4)
    for fc in range(nfc + 1):
        gens = []
        if fc < nfc:
            gens += [sc_gen(fc, d, TS[d]) for d in range(dbg.get('dirs', 2))]
        if fc > 0:
            gens.append(post_gen(fc - 1))
        for _ in itertools.zip_longest(*gens):
            pass


def merge_phase(kb, C, zhyT, yrwT, gT, x1T, x2T, hyo, rwo, wo):
    wh = kb.bf16(4 * D)
    wr = kb.bf16(4 * D)
    wo_ = kb.bf16(8 * D)
    wh3 = wh.ap.rearrange("p (k d) -> p k d", d=D)
    wr3 = wr.ap.rearrange("p (k d) -> p k d", d=D)
    wo3 = wo_.ap.rearrange("p (k d) -> p k d", d=D)
    dma(kb, "sp", wh3, hyo.rearrange("(k p) d -> p k d", p=128), writes=wh.b)
    dma(kb, "sp", wr3, rwo.rearrange("(k p) d -> p k d", p=128), writes=wr.b)
    dma(kb, "sp", wo3, wo.rearrange("(k p) d -> p k d", p=128), writes=wo_.b)
    zt = [kb.bf16(4 * 512) for _ in range(2)]
    yt = [kb.bf16(4 * 512) for _ in range(2)]
    mrg = [kb.bf16(8 * 512, 8) for _ in range(2)]
    gh = [kb.f32(512) for _ in range(2)]
    gr = [kb.f32(512) for _ in range(2)]
    m1 = [kb.f32(512) for _ in range(2)]
    m2 = [kb.f32(512) for _ in range(2)]
    xr = [kb.f32(512) for _ in range(2)]
    xo = [kb.f32(512) for _ in range(2)]
    zh3 = zhyT.rearrange("(k p) t -> p k t", p=128)
    yr3 = yrwT.rearrange("(k p) t -> p k t", p=128)
    cnt = 0
    for t in range(L // 512):
        ts_ = slice(t * 512, (t + 1) * 512)
        z_, y_, mg = zt[t % 2], yt[t % 2], mrg[t % 2]
        z3 = z_.ap.rearrange("p (k t) -> p k t", t=512)
        y3 = y_.ap.rearrange("p (k t) -> p k t", t=512)
        mg3 = mg.ap.rearrange("p (k t) -> p k t", t=512)
        dma(kb, "sp", z3, zh3[:, :, ts_], writes=z_.b)
        dma(kb, "sp", y3, yr3[:, :, ts_], writes=y_.b)
        for dc in range(8):
            i2 = cnt % 2
            cnt += 1
            dsl = slice(dc * 128, (dc + 1) * 128)
            dma(kb, "sp", gh[i2].ap, gT[dc * 128:(dc + 1) * 128, ts_], writes=gh[i2].b)
            dma(kb, "sp", gr[i2].ap, gT[D + dc * 128:D + (dc + 1) * 128, ts_], writes=gr[i2].b)
            bh, br = kb.bank(), kb.bank()
            for kc in range(4):
                _mm(kb, kb.ps[:, bh, :], wh3[:, kc, dsl], z3[:, kc, :], [wh, z_], [kb.psb[bh]], start=(kc == 0), stop=(kc == 3))
            for kc in range(4):
                _mm(kb, kb.ps[:, br, :], wr3[:, kc, dsl], y3[:, kc, :], [wr, y_], [kb.psb[br]], start=(kc == 0), stop=(kc == 3))
            _tt(kb, "dve", m1[i2].ap, kb.ps[:, bh, :], gh[i2].ap, ALU.mult, [kb.psb[bh], gh[i2]], [m1[i2]])
            _tt(kb, "dve", m2[i2].ap, kb.ps[:, br, :], gr[i2].ap, ALU.mult, [kb.psb[br], gr[i2]], [m2[i2]])
            _tt(kb, "pool", mg3[:, dc, :], m1[i2].ap, m2[i2].ap, ALU.add, [m1[i2], m2[i2]], [mg.b[dc]])
        for dc in range(8):
            i2 = cnt % 2
            cnt += 1
            dsl = slice(dc * 128, (dc + 1) * 128)
            dma(kb, "sp", xr[i2].ap, x1T[dc * 128:(dc + 1) * 128, ts_], writes=xr[i2].b)
            bo = kb.bank()
            for kc in range(8):
                _mm(kb, kb.ps[:, bo, :], wo3[:, kc, dsl], mg3[:, kc, :], [wo_, mg.b[kc]], [kb.psb[bo]],
                    start=(kc == 0), stop=(kc == 7))
            _tt(kb, "dve", xo[i2].ap, kb.ps[:, bo, :], xr[i2].ap, ALU.add, [kb.psb[bo], xr[i2]], [xo[i2]])
            dma(kb, "pool", x2T[dc * 128:(dc + 1) * 128, ts_], xo[i2].ap, reads=xo[i2].b)


def _host_consts():
    c = {}
    c["ones"] = np.ones((128, 128), np.float32)
    c["ident"] = np.eye(128, dtype=np.float32)
    bd = np.zeros((128, 128), np.float32)
    bd[0:64, 0:64] = 1.0
    bd[64:128, 64:128] = 1.0
    c["bd64"] = bd
    s = np.arange(64)[:, None]
    t = np.arange(64)[None, :]
    lt, le, gt, ge = (s < t), (s <= t), (s > t), (s >= t)

    def nb(m0, m1):
        m = np.zeros((64, 4, 2, 64), np.float32)
        m[:, :, 0, :] = m0[:, None, :]
        m[:, :, 1, :] = m1[:, None, :]
        return m.reshape(64, 512)

    c["mNBf"] = nb(lt, le)
    c["mNBb"] = nb(gt, ge)
    c["mAABf"] = np.broadcast_to(gt[:, None, :], (64, 8, 64)).astype(np.float32).reshape(64, 512).copy()
    c["mAABb"] = np.broadcast_to(lt[:, None, :], (64, 8, 64)).astype(np.float32).reshape(64, 512).copy()
    c["I8"] = np.broadcast_to(np.eye(64, dtype=np.float32)[:, None, :], (64, 8, 64)).reshape(64, 512).copy()
    rm = np.ones((128, SC), np.float32)
    rm[:, 0::64] = 0.0
    c["rmask"] = rm
    return c


CONST_SHAPES = {"ones": [128, 128], "ident": [128, 128], "bd64": [128, 128], "mNBf": [64, 512], "mNBb": [64, 512],
                "mAABf": [64, 512], "mAABb": [64, 512], "I8": [64, 512], "rmask": [128, 256]}

PARAM_SHAPES = {
    "norms": [128, 32],
    "rw_muT": [NRW, 2], "rw_w0T": [512, 2], "rw_a0T": [512, 2], "rw_w2t": [64, 1024], "rw_a2t": [64, 1024],
    "rw_g2": [160, 512], "rw_vecs": [128, 20],
}

WEIGHTS = {"ffn1_w_gu": [D, 2 * DFF], "ffn1_w_down": [DFF, D], "ffn2_w_gu": [D, 2 * DFF], "ffn2_w_down": [DFF, D],
           "w_in": [D, NIN], "hy_out": [512, D], "rw_out": [512, D], "w_out": [D, D]}


def _host_params(inputs):
    f = lambda k: np.asarray(inputs[k], np.float32)
    p = {}
    p["norms"] = np.ascontiguousarray(np.concatenate(
        [f(n).reshape(8, 128).T for n in ("ffn1_norm", "mix_norm", "ffn2_norm", "final_norm")], axis=1))
    p["rw_muT"] = np.ascontiguousarray(f("rw_mu").T)
    p["rw_w0T"] = np.ascontiguousarray(f("rw_w0").T)
    p["rw_a0T"] = np.ascontiguousarray(f("rw_a0").T)
    p["rw_w2t"] = np.ascontiguousarray(f("rw_w2").transpose(1, 0, 2).reshape(64, 1024))
    p["rw_a2t"] = np.ascontiguousarray(f("rw_a2").transpose(1, 0, 2).reshape(64, 1024))
    p["rw_g2"] = np.ascontiguousarray(f("rw_g2"))
    p["rw_vecs"] = np.ascontiguousarray(np.concatenate(
        [f(n).reshape(4, 128).T for n in ("rw_k_k", "rw_k_a", "rw_r_k", "rw_ln_w", "rw_ln_b")], axis=1))
    return p


def build_program(stage="full"):
    nc = bass.Bass("TRN2", target_bir_lowering=False)
    I = {}

    def inp(name, shape, dt=F32):
        I[name] = nc.dram_tensor(name, list(shape), dt, kind="ExternalInput").ap()
        return I[name]

    def scr(name, shape, dt=F32, ext=None):
        if ext == "in":
            return inp(name, shape, dt)
        if ext == "out":
            return nc.dram_tensor(name, list(shape), dt, kind="ExternalOutput").ap()
        return nc.dram_tensor(name, list(shape), dt).ap()

    full = stage == "full"
    front = stage in ("full", "front")
    do_rw = stage in ("full", "rwkv")
    do_hy = stage in ("full", "hyena")
    for n, shp in CONST_SHAPES.items():
        inp(n, shp)
    inp("norms", PARAM_SHAPES["norms"])
    if do_rw:
        for n, shp in PARAM_SHAPES.items():
            if n != "norms":
                inp(n, shp)
    if do_hy:
        for n, shp in list(HY_CONST_SHAPES.items()) + list(HY_PARAM_SHAPES.items()):
            inp(n, shp)
    if front:
        inp("xT", [D, L])
        for n, shp in WEIGHTS.items():
            if full or n in ("ffn1_w_gu", "ffn1_w_down", "w_in"):
                inp(n, shp)
    back = stage == "back"
    if back:
        for n in ("hy_out", "rw_out", "w_out", "ffn2_w_gu", "ffn2_w_down"):
            inp(n, WEIGHTS[n])
    uhy = scr("uhy", [L + 2, NHY], ext={"front": "out", "hyena": "in"}.get(stage))
    urwT = scr("urwT", [NRW, L + 2], ext={"front": "out", "rwkv": "in"}.get(stage))
    gT = scr("gT", [NG, L], ext={"front": "out", "back": "in"}.get(stage))
    yrwT = scr("yrwT", [512, L], BF16, ext={"rwkv": "out", "back": "in"}.get(stage))
    zhyT = scr("zhyT", [512, L], BF16, ext={"hyena": "out", "back": "in"}.get(stage))
    outT = None
    if full or stage == "back":
        outT = nc.dram_tensor("outT", [D, L], F32, kind="ExternalOutput").ap()
    wb = {}
    for n, shp in WEIGHTS.items():
        if n in I:
            wb[n] = scr(n + "_b", shp, BF16)
    x1T = scr("x1T", [D, L], ext={"front": "out", "back": "in"}.get(stage))
    x2T = scr("x2T", [D, L])
    x3T = scr("x3T", [D, L])
    HD = {}
    if do_hy:
        HD["ucb"] = scr("hy_ucb", [3, 4, 64, 64 * CG])
        HD["Abuf"] = scr("hy_Abuf", [2, 2, 128, 64, CG])
        HD["Bbuf"] = scr("hy_Bbuf", [2, 2, 64, 128, CG])
        HD["Kf"] = scr("hy_Kf", [2, 4, 2, 64, 128 * CG])
        HD["ucb_b"] = [[Buf() for _ in range(4)] for _ in range(3)]
        HD["Kf_b"] = [[[Buf() for _ in range(32)] for _ in range(4)] for _ in range(2)]
        if stage == "hyena":
            HD["kf_dbg"] = scr("kf_dbg", [2, 4, 128, 64 * CG], ext="out")
            HD["z1_dbg"] = scr("z1_dbg", [4, 64, 64 * CG], ext="out")

    with contextlib.ExitStack() as st:
        kb = KB(nc, st)
        C = {}
        for n in ("ones", "ident", "bd64"):
            C[n] = kb.f32(128)
            dma(kb, "sp", C[n].ap, I[n], writes=C[n].b)
        nrm = kb.f32(32)
        for i, n in enumerate(("ffn1_norm", "mix_norm", "ffn2_norm", "final_norm")):
            t = Tile(nrm.ap[:, i * 8:(i + 1) * 8])
            t.b = nrm.b
            C[n] = t
        dma(kb, "sp", nrm.ap, I["norms"], writes=nrm.b)
        C["ones_bf"] = kb.bf16(128)
        kb.op("dve", lambda e: e.tensor_copy(out=C["ones_bf"].ap, in_=C["ones"].ap), reads=C["ones"].b,
              writes=C["ones_bf"].b)
        kb.persist()

        if front:
            convert_weights(kb, [(I[n], wb[n], WEIGHTS[n][0], WEIGHTS[n][1]) for n in wb])
            kb.phase_end()
            ffn_phase(kb, C, I["xT"], x1T, "ffn1_norm", wb["ffn1_w_gu"], wb["ffn1_w_down"])
            kb.phase_end()
            inproj_phase(kb, C, x1T, wb["w_in"], uhy, urwT, gT)
            kb.phase_end()
        if do_rw:
            rwkv_phase(kb, C, I, urwT, yrwT, {"yT": scr("rw_yT", [2, 512, L]), "bT": scr("rw_bT", [2, 512, L])})
            kb.phase_end()
        if do_hy:
            hyena_phase(kb, C, I, uhy, zhyT, HD)
            kb.phase_end()
        if back:
            convert_weights(kb, [(I[n], wb[n], WEIGHTS[n][0], WEIGHTS[n][1]) for n in wb])
            kb.phase_end()
        if full or back:
            merge_phase(kb, C, zhyT, yrwT, gT, x1T, x2T, wb["hy_out"], wb["rw_out"], wb["w_out"])
            kb.phase_end()
            ffn_phase(kb, C, x2T, x3T, "ffn2_norm", wb["ffn2_w_gu"], wb["ffn2_w_down"])
            kb.phase_end()
            final_norm_phase(kb, C, x3T, outT, "final_norm")
            kb.phase_end()
        kb.S.emit()
    return nc, list(I.keys())


_NC_CACHE = {}


def _get_program(stage):
    if stage not in _NC_CACHE:
        _NC_CACHE[stage] = build_program(stage)
    return _NC_CACHE[stage]


def host_shared(inputs, names):
    shared = dict(_host_consts())
    shared.update(_host_params(inputs))
    shared.update(_hy_consts())
    shared.update(_hy_host_params(inputs))
    for n in WEIGHTS:
        shared[n] = np.ascontiguousarray(inputs[n], np.float32)
    return {k: v for k, v in shared.items() if k in names}


def kernel(**inputs):
    nc, names = _get_program("full")
    shared = host_shared(inputs, names)
    x = np.asarray(inputs["x"], np.float32)
    in_maps = []
    for b in range(8):
        m = dict(shared)
        m["xT"] = np.ascontiguousarray(x[b].T)
        in_maps.append(m)
    res = run_bass_kernel_spmd(nc, in_maps, core_ids=list(range(8)))
    out = np.stack([np.ascontiguousarray(r["outT"].T) for r in res.results], axis=0)
    return out.astype(np.float32)


NFFT = 8192
CG = 128
HY_DBG = {}


def _hy_consts():
    c = {}
    n1 = np.arange(128, dtype=np.float64)[:, None, None]
    n2 = np.arange(64, dtype=np.float64)[None, :, None]
    k1 = np.arange(128, dtype=np.float64)[None, None, :]
    ang = 2 * np.pi * (n1 * k1 / 128.0 + n2 * k1 / NFFT)
    G = np.stack([np.cos(ang), -np.sin(ang)], axis=2)
    c["hy_G"] = G.reshape(128, 64 * 2 * 128).astype(np.float32)
    a2 = 2 * np.pi * np.arange(64)[:, None] * np.arange(64)[None, :] / 64.0
    c["hy_F2"] = np.concatenate([np.cos(a2), np.sin(a2), -np.sin(a2)], axis=1).astype(np.float32)
    k2 = np.arange(64, dtype=np.float64)[:, None, None]
    k1b = np.arange(128, dtype=np.float64)[None, :, None]
    nl = np.arange(64, dtype=np.float64)[None, None, :]
    angp = 2 * np.pi * (k2 * nl / 64.0 + k1b * nl / NFFT)
    Gp = np.stack([np.cos(angp), np.sin(angp), -np.sin(angp)], axis=2)
    c["hy_Gp"] = Gp.reshape(64, 128 * 3 * 64).astype(np.float32)
    a3 = 2 * np.pi * np.arange(128)[:, None] * np.arange(64)[None, :] / 128.0
    wk = np.full((128, 1), 2.0)
    wk[0] = 1.0
    wk[64] = 1.0
    wk[65:] = 0.0
    c["hy_I2"] = (wk * np.concatenate([np.cos(a3), -np.sin(a3)], axis=1) / NFFT).astype(np.float32)
    n = np.arange(NFFT)
    j = np.where(n <= L, n, NFFT - n).astype(np.float64)
    j[L] = 0
    t = j / (L - 1)
    angf = (2.0 * math.pi / L) * j
    bands = np.linspace(1e-4, 15, 16)
    feats = np.concatenate([t[None, :], np.cos(bands[:, None] * angf[None, :]), -np.sin(bands[:, None] * angf[None, :])], axis=0)
    c["hy_feats"] = feats.astype(np.float32)
    c["hy_negt"] = (-t).reshape(128, 64).astype(np.float32)
    return c


HY_CONST_SHAPES = {"hy_G": [128, 64 * 2 * 128], "hy_F2": [64, 192], "hy_Gp": [64, 128 * 3 * 64], "hy_I2": [128, 128],
                   "hy_feats": [33, NFFT], "hy_negt": [128, 64]}
HY_PARAM_SHAPES = {"hy_conv_w": [3, NHY], "hy_conv_b": [1, NHY], "hy_w1": [33, 64], "hy_w2": [64, 64],
                   "hy_w3a": [65, 2048], "hy_b1f": [64, 3], "hy_decay": [4, 512], "hy_bias_d": [2, 512]}


def _hy_host_params(inputs):
    f = lambda k: np.asarray(inputs[k], np.float32)
    p = {}
    p["hy_conv_w"] = np.ascontiguousarray(f("hy_conv_w"))
    p["hy_conv_b"] = np.ascontiguousarray(f("hy_conv_b").reshape(1, NHY))
    p["hy_w1"] = np.ascontiguousarray(f("hy_ffn_w1"))
    p["hy_w2"] = np.ascontiguousarray(f("hy_ffn_w2"))
    p["hy_w3a"] = np.ascontiguousarray(np.concatenate([f("hy_ffn_w3"), f("hy_ffn_b3").reshape(1, 2048)], axis=0))
    p["hy_b1f"] = np.ascontiguousarray(np.stack([f("hy_ffn_b1"), f("hy_ffn_b2"), f("hy_sin_freq")], axis=1))
    p["hy_decay"] = np.ascontiguousarray(f("hy_decay").reshape(4, 512))
    p["hy_bias_d"] = np.ascontiguousarray(f("hy_bias_d"))
    return p


def _sin_act(kb, out_ap, out_tile, pre, scr, np_):
    TWO_PI = 2.0 * math.pi
    MAGIC = 12582912.0
    _ts(kb, "dve", scr.ap[0:np_, :], pre.ap[0:np_, :], 1.0 / TWO_PI, MAGIC, ALU.mult, ALU.add, [pre], [scr])
    _ts(kb, "dve", scr.ap[0:np_, :], scr.ap[0:np_, :], MAGIC, None, ALU.subtract, None, [scr], [scr])
    _stt(kb, scr.ap[0:np_, :], scr.ap[0:np_, :], -TWO_PI, pre.ap[0:np_, :], ALU.mult, ALU.add, [scr, pre], [scr])
    _ts(kb, "dve", scr.ap[0:np_, :], scr.ap[0:np_, :], 3.141592, -3.141592, ALU.min, ALU.max, [scr], [scr])
    _act(kb, out_ap, scr.ap[0:np_, :], AF.Sin, [scr], [out_tile])


def hyena_phase(kb, C, P, uhy, zhyT, D_):
    dbg = HY_DBG
    ident = C["ident"]
    ucb, Abuf_all, Bbuf_all, Kf = D_["ucb"], D_["Abuf"], D_["Bbuf"], D_["Kf"]
    NG4 = dbg.get("groups", 4)
    cw = [kb.f32(NHY) for _ in range(3)]
    cb = kb.f32(NHY)
    for i in range(3):
        dma(kb, "sp", cw[i].ap[0:64, :], P["hy_conv_w"][i:i + 1, :].partition_broadcast(64), writes=cw[i].b)
        dma(kb, "sp", cw[i].ap[64:128, 0:NHY - CG], P["hy_conv_w"][i:i + 1, CG:NHY].partition_broadcast(64), writes=cw[i].b)
    dma(kb, "sp", cb.ap[0:64, :], P["hy_conv_b"].partition_broadcast(64), writes=cb.b)
    dma(kb, "sp", cb.ap[64:128, 0:NHY - CG], P["hy_conv_b"][:, CG:NHY].partition_broadcast(64), writes=cb.b)
    uin = [kb.f32(66 * CG) for _ in range(2)]
    uo = [kb.f32(64 * CG) for _ in range(2)]
    tmpc = kb.f32(64 * CG)
    uhy_b = uhy[0:L, :].rearrange("(p n) c -> p n c", n=64)
    uhy_h = uhy[2:L + 2, :].rearrange("(p n) c -> p n c", n=64)
    it = 0
    for kind in range(3):
        for gp in range(0, NG4, 2):
            npair = min(2, NG4 - gp)
            NP = 64 * npair
            c0 = kind * 512 + gp * CG
            ui, uo_ = uin[it % 2], uo[it % 2]
            it += 1
            u3 = ui.ap.rearrange("p (n c) -> p n c", c=CG)
            o3 = uo_.ap.rearrange("p (n c) -> p n c", c=CG)
            t3 = tmpc.ap.rearrange("p (n c) -> p n c", c=CG)
            for h in range(npair):
                ch = c0 + h * CG
                dma(kb, "sp", u3[h * 64:(h + 1) * 64, 0:64, :], uhy_b[:, :, ch:ch + CG], writes=ui.b)
                dma(kb, "act", u3[h * 64:(h + 1) * 64, 64:66, :], uhy_h[:, 62:64, ch:ch + CG], writes=ui.b)

            def bc(t):
                return t.ap[0:NP, c0:c0 + CG].unsqueeze(1).to_broadcast([NP, 64, CG])

            _tt(kb, "dve", o3[0:NP], u3[0:NP, 0:64, :], bc(cw[0]), ALU.mult, [ui, cw[0]], [uo_])
            _tt(kb, "pool", t3[0:NP], u3[0:NP, 1:65, :], bc(cw[1]), ALU.mult, [ui, cw[1]], [tmpc])
            _tt(kb, "dve", o3[0:NP], o3[0:NP], t3[0:NP], ALU.add, [uo_, tmpc], [uo_])
            _tt(kb, "pool", t3[0:NP], u3[0:NP, 2:66, :], bc(cw[2]), ALU.mult, [ui, cw[2]], [tmpc])
            _tt(kb, "dve", o3[0:NP], o3[0:NP], t3[0:NP], ALU.add, [uo_, tmpc], [uo_])
            _tt(kb, "dve", o3[0:NP], o3[0:NP], bc(cb), ALU.add, [uo_, cb], [uo_])
            for h in range(npair):
                dma(kb, "pool", ucb[kind, gp + h], uo_.ap[h * 64:(h + 1) * 64, :], reads=uo_.b,
                    writes=[D_["ucb_b"][kind][gp + h]])
    kb.phase_end()

    F2t = kb.f32(192)
    I2t = kb.f32(128)
    dma(kb, "sp", F2t.ap[0:64, :], P["hy_F2"], writes=F2t.b)
    dma(kb, "sp", I2t.ap, P["hy_I2"], writes=I2t.b)
    kb.persist()
    NK1 = 65
    KBATCH = [(k0, min(4, NK1 - k0)) for k0 in range(0, NK1, 4)]
    A_bs = [[Buf() for _ in range(16)] for _ in range(2)]
    B_bs = [[Buf() for _ in range(17)] for _ in range(2)]
    Gc = [kb.f32(1024) for _ in range(2)]
    Ast = [[kb.f32(512) for _ in range(2)] for _ in range(2)]
    Ain = [[kb.f32(512) for _ in range(2)] for _ in range(2)]
    cnt = {"g": 0, "a": 0, "i": 0}

    def fft_s1(src3, srcT, K, aset=0):
        Abuf, A_b = Abuf_all[aset], A_bs[aset]
        for n2b in range(16):
            gt = Gc[cnt["g"] % 2]
            cnt["g"] += 1
            dma(kb, "sp", gt.ap[0:K, :], P["hy_G"][0:K, n2b * 1024:(n2b + 1) * 1024], writes=gt.b)
            g4 = gt.ap.rearrange("p (j r k) -> p j r k", r=2, k=128)
            banks = [kb.bank(), kb.bank()]
            for j in range(4):
                for ri in range(2):
                    _mm(kb, kb.ps[0:NK1, banks[ri], j * 128:(j + 1) * 128], g4[0:K, j, ri, 0:NK1], src3[0:K, n2b * 4 + j, :],
                        [gt, srcT], [kb.psb[banks[ri]]])
            for ri in range(2):
                st_ = Ast[ri][cnt["a"] % 2]
                _cp(kb, "act" if ri == 0 else "dve", st_.ap[0:NK1, :], kb.ps[0:NK1, banks[ri], :], [kb.psb[banks[ri]]], [st_])
                dma(kb, "pool", Abuf[ri, 0:NK1, n2b * 4:(n2b + 1) * 4, :], st_.ap[0:NK1, :].rearrange("p (n c) -> p n c", c=CG),
                    reads=st_.b, writes=[A_b[n2b]])
            cnt["a"] += 1

    def fft_s2(k0, nk, aset=0):
        Abuf, A_b = Abuf_all[aset], A_bs[aset]
        tin = []
        nn = nk * CG
        for ri in range(2):
            t_ = Ain[ri][cnt["i"] % 2]
            dma(kb, "sp", t_.ap[0:64, 0:nn].rearrange("p (k c) -> p k c", c=CG),
                Abuf[ri, k0:k0 + nk, :, :].rearrange("k n c -> n k c"), reads=A_b, writes=t_.b)
            tin.append(t_)
        cnt["i"] += 1
        bre, bim = kb.bank(), kb.bank()
        c2, s2, ns2 = F2t.ap[0:64, 0:64], F2t.ap[0:64, 64:128], F2t.ap[0:64, 128:192]
        _mm(kb, kb.ps[0:64, bre, 0:nn], c2, tin[0].ap[0:64, 0:nn], [F2t, tin[0]], [kb.psb[bre]], start=True, stop=False)
        _mm(kb, kb.ps[0:64, bre, 0:nn], s2, tin[1].ap[0:64, 0:nn], [F2t, tin[1]], [kb.psb[bre]], start=False, stop=True)
        _mm(kb, kb.ps[0:64, bim, 0:nn], c2, tin[1].ap[0:64, 0:nn], [F2t, tin[1]], [kb.psb[bim]], start=True, stop=False)
        _mm(kb, kb.ps[0:64, bim, 0:nn], ns2, tin[0].ap[0:64, 0:nn], [F2t, tin[0]], [kb.psb[bim]], start=False, stop=True)
        return bre, bim

    mark = kb.off
    w1t = kb.f32(64)
    w2t = kb.f32(64)
    b1f = kb.f32(3)
    w3a = kb.f32(2048)
    negt = kb.f32(64)
    h2f = kb.f32(NFFT)
    h2b = kb.f32(NFFT)
    dma(kb, "sp", w1t.ap[0:33, :], P["hy_w1"], writes=w1t.b)
    dma(kb, "sp", w2t.ap[0:64, :], P["hy_w2"], writes=w2t.b)
    dma(kb, "sp", b1f.ap[0:64, :], P["hy_b1f"], writes=b1f.b)
    dma(kb, "sp", w3a.ap[0:65, :], P["hy_w3a"], writes=w3a.b)
    dma(kb, "sp", negt.ap, P["hy_negt"], writes=negt.b)
    kb.op("pool", lambda e: e.memset(h2f.ap, 0.0), writes=h2f.b)
    kb.op("pool", lambda e: e.memset(h2b.ap, 0.0), writes=h2b.b)
    kb.op("pool", lambda e: e.memset(h2f.ap[64:65, 0:L], 1.0), reads=h2f.b, writes=h2f.b)
    kb.op("pool", lambda e: e.memset(h2b.ap[64:65, L + 1:NFFT], 1.0), reads=h2b.b, writes=h2b.b)
    fch = [kb.f32(512) for _ in range(2)]
    pre = kb.f32(512)
    scr = kb.f32(512)
    h1c = kb.f32(512)
    for pc in range(16):
        ft = fch[pc % 2]
        dma(kb, "sp", ft.ap[0:33, :], P["hy_feats"][:, pc * 512:(pc + 1) * 512], writes=ft.b)
        b1 = kb.bank()
        _mm(kb, kb.ps[0:64, b1, :], w1t.ap[0:33, :], ft.ap[0:33, :], [w1t, ft], [kb.psb[b1]])
        _ts(kb, "dve", pre.ap[0:64, :], kb.ps[0:64, b1, :], b1f.ap[0:64, 0:1], b1f.ap[0:64, 2:3], ALU.add, ALU.mult,
            [kb.psb[b1], b1f], [pre])
        _sin_act(kb, h1c.ap[0:64, :], h1c, pre, scr, 64)
        b2 = kb.bank()
        _mm(kb, kb.ps[0:64, b2, :], w2t.ap[0:64, :], h1c.ap[0:64, :], [w2t, h1c], [kb.psb[b2]])
        _ts(kb, "dve", pre.ap[0:64, :], kb.ps[0:64, b2, :], b1f.ap[0:64, 1:2], b1f.ap[0:64, 2:3], ALU.add, ALU.mult,
            [kb.psb[b2], b1f], [pre])
        hdst = h2f if pc < 8 else h2b
        TWO_PI = 2.0 * math.pi
        MAGIC = 12582912.0
        _ts(kb, "dve", scr.ap[0:64, :], pre.ap[0:64, :], 1.0 / TWO_PI, MAGIC, ALU.mult, ALU.add, [pre], [scr])
        _ts(kb, "dve", scr.ap[0:64, :], scr.ap[0:64, :], MAGIC, None, ALU.subtract, None, [scr], [scr])
        _stt(kb, scr.ap[0:64, :], scr.ap[0:64, :], -TWO_PI, pre.ap[0:64, :], ALU.mult, ALU.add, [scr, pre], [scr])
        _ts(kb, "dve", scr.ap[0:64, :], scr.ap[0:64, :], 3.141592, -3.141592, ALU.min, ALU.max, [scr], [scr])
        _act(kb, hdst.ap[0:64, pc * 512:(pc + 1) * 512], scr.ap[0:64, :], AF.Sin, [scr], [hdst])
    kb.op("pool", lambda e: e.memset(h2b.ap[0:64, L:L + 1], 0.0), reads=h2b.b, writes=h2b.b)
    kfs = [kb.f32(64 * CG), kb.f32(64 * CG)]
    absd = kb.f32(CG)
    et = [kb.f32(512) for _ in range(2)]
    sqc = [kb.bf16(512) for _ in range(2)]
    ones_bf = kb.bf16(128)
    kb.op("dve", lambda e: e.tensor_copy(out=ones_bf.ap, in_=C["ones"].ap), reads=C["ones"].b, writes=ones_bf.b)
    rnf = kb.f32(CG)
    kst = [[kb.f32(512) for _ in range(2)] for _ in range(2)]
    h2f3 = h2f.ap.rearrange("p (a b) -> p b a", b=64)
    h2b3 = h2b.ap.rearrange("p (a b) -> p b a", b=64)

    def gen_filter(o, g, kf):
        kf3 = kf.ap.rearrange("p (n c) -> p n c", c=CG)
        for dr in range(2):
            dma(kb, "sp", absd.ap[dr * 64:(dr + 1) * 64, :],
                P["hy_decay"][dr * 2 + o:dr * 2 + o + 1, g * CG:(g + 1) * CG].partition_broadcast(64), writes=absd.b)
        _stt(kb, absd.ap, absd.ap, -1.0, absd.ap, ALU.mult, ALU.max, [absd], [absd])
        bss = kb.bank()
        kb.reserved.add(bss)
        for n2b in range(16):
            bk = kb.bank()
            e_ = et[n2b % 2]
            sq_ = sqc[n2b % 2]
            for j in range(4):
                n2 = n2b * 4 + j
                cf = o * 512 + g * CG
                _mm(kb, kb.ps[:, bk, j * 128:(j + 1) * 128], h2f3[0:65, n2, :], w3a.ap[0:65, cf:cf + CG],
                    [h2f, w3a], [kb.psb[bk]], start=True, stop=False)
                _mm(kb, kb.ps[:, bk, j * 128:(j + 1) * 128], h2b3[0:65, n2, :], w3a.ap[0:65, 1024 + cf:1024 + cf + CG],
                    [h2b, w3a], [kb.psb[bk]], start=False, stop=True)
                _act(kb, e_.ap[:, j * 128:(j + 1) * 128], absd.ap, AF.Exp, [absd, negt], [e_], scale=negt.ap[:, n2:n2 + 1])
            _tt(kb, "dve", kf.ap[:, n2b * 512:(n2b + 1) * 512], kb.ps[:, bk, :], e_.ap, ALU.mult, [kb.psb[bk], e_], [kf])
            _tt(kb, "pool", sq_.ap, kf.ap[:, n2b * 512:(n2b + 1) * 512], kf.ap[:, n2b * 512:(n2b + 1) * 512], ALU.mult,
                [kf], [sq_])
            for j in range(4):
                _mm(kb, kb.ps[:, bss, 0:CG], ones_bf.ap, sq_.ap[:, j * 128:(j + 1) * 128], [ones_bf, sq_],
                    [kb.psb[bss]], start=(n2b == 0 and j == 0), stop=(n2b == 15 and j == 3))
        kb.reserved.discard(bss)
        _act(kb, rnf.ap, kb.ps[:, bss, 0:CG], AF.Sqrt, [kb.psb[bss]], [rnf], bias=1e-6)
        kb.op("dve", lambda e: e.reciprocal(out=rnf.ap, in_=rnf.ap), reads=rnf.b, writes=rnf.b)
        _tt(kb, "dve", kf3, kf3, rnf.ap.unsqueeze(1).to_broadcast([128, 64, CG]), ALU.mult, [kf, rnf], [kf])
        if "kf_dbg" in D_:
            dma(kb, "pool", D_["kf_dbg"][o, g], kf.ap, reads=kf.b)

    def fft_filter(o, g, kf, aset):
        kf3 = kf.ap.rearrange("p (n c) -> p n c", c=CG)
        fft_s1(kf3, kf, 128, aset)
        for k1b, (k0, nk) in enumerate(KBATCH):
            nn = nk * CG
            bre, bim = fft_s2(k0, nk, aset)
            for ri, bk_ in enumerate((bre, bim)):
                st_ = kst[ri][k1b % 2]
                _cp(kb, "act" if ri == 0 else "dve", st_.ap[0:64, 0:nn], kb.ps[0:64, bk_, 0:nn], [kb.psb[bk_]], [st_])
                dma(kb, "pool", Kf[o, g, ri, :, k0 * CG:k0 * CG + nn], st_.ap[0:64, 0:nn], reads=st_.b,
                    writes=[D_["Kf_b"][o][g][k1b]])

    items = [(o, g) for o in range(2) for g in range(NG4)]
    gen_filter(items[0][0], items[0][1], kfs[0])
    for i, (o, g) in enumerate(items):
        if i + 1 < len(items):
            gen_filter(items[i + 1][0], items[i + 1][1], kfs[(i + 1) % 2])
        fft_filter(o, g, kfs[i % 2], i % 2)
    kb.S.barrier()
    kb.off = mark

    import itertools

    def alloc_c():
        T = _NS()
        T.zbuf = kb.f32(64 * CG)
        T.Kt = [kb.f32(512) for _ in range(2)]
        T.Gp = kb.f32(768)
        T.xr, T.xi = kb.f32(512), kb.f32(512)
        T.tt = [kb.f32(512) for _ in range(4)]
        T.Yt = [kb.f32(512) for _ in range(2)]
        T.Bst = [kb.f32(512) for _ in range(2)]
        T.Bin = [kb.f32(512) for _ in range(2)]
        T.xg, T.sk, T.te, T.zo, T.dbc = [kb.f32(512) for _ in range(5)]
        T.zT = kb.bf16(L)
        return T

    def stage_c(g, T, aset):
        Bbuf, B_b = Bbuf_all[aset], B_bs[aset]
        zbuf = T.zbuf
        z3 = zbuf.ap.rearrange("p (n c) -> p n c", c=CG)
        zT3 = T.zT.ap.rearrange("c (p n) -> c p n", n=64)
        dma(kb, "sp", zbuf.ap[0:64, :], ucb[2, g], reads=[D_["ucb_b"][2][g]], writes=zbuf.b)
        for o in range(dbg.get("orders", 2)):
            for rep in range(4):
                dma(kb, "sp", T.dbc.ap[0:64, rep * CG:(rep + 1) * CG],
                    P["hy_bias_d"][o:o + 1, g * CG:(g + 1) * CG].partition_broadcast(64), writes=T.dbc.b)
            fft_s1(z3, zbuf, 64, aset)
            yield
            for k1b, (k0, nk) in enumerate(KBATCH):
                nn = nk * CG
                kt = T.Kt
                for ri in range(2):
                    dma(kb, "sp", kt[ri].ap[0:64, 0:nn], Kf[o, g, ri, :, k0 * CG:k0 * CG + nn],
                        reads=[D_["Kf_b"][o][g][k1b]], writes=kt[ri].b)
                gp = T.Gp
                dma(kb, "sp", gp.ap[0:64, 0:nk * 192], P["hy_Gp"][:, k0 * 192:(k0 + nk) * 192], writes=gp.b)
                gp4 = gp.ap.rearrange("p (j q n) -> p j q n", q=3, n=64)
                bre, bim = fft_s2(k0, nk, aset)
                xr_, xi_ = T.xr, T.xi
                _cp(kb, "act", xr_.ap[0:64, 0:nn], kb.ps[0:64, bre, 0:nn], [kb.psb[bre]], [xr_])
                _cp(kb, "act", xi_.ap[0:64, 0:nn], kb.ps[0:64, bim, 0:nn], [kb.psb[bim]], [xi_])
                yre, yim = T.Yt
                ta, tb_, tc_, td = T.tt
                _tt(kb, "dve", ta.ap[0:64, 0:nn], xr_.ap[0:64, 0:nn], kt[0].ap[0:64, 0:nn], ALU.mult, [xr_, kt[0]], [ta])
                _tt(kb, "dve", tb_.ap[0:64, 0:nn], xi_.ap[0:64, 0:nn], kt[1].ap[0:64, 0:nn], ALU.mult, [xi_, kt[1]], [tb_])
                _tt(kb, "dve", yre.ap[0:64, 0:nn], ta.ap[0:64, 0:nn], tb_.ap[0:64, 0:nn], ALU.subtract, [ta, tb_], [yre])
                _tt(kb, "pool", tc_.ap[0:64, 0:nn], xr_.ap[0:64, 0:nn], kt[1].ap[0:64, 0:nn], ALU.mult, [xr_, kt[1]], [tc_])
                _tt(kb, "pool", td.ap[0:64, 0:nn], xi_.ap[0:64, 0:nn], kt[0].ap[0:64, 0:nn], ALU.mult, [xi_, kt[0]], [td])
                _tt(kb, "pool", yim.ap[0:64, 0:nn], tc_.ap[0:64, 0:nn], td.ap[0:64, 0:nn], ALU.add, [tc_, td], [yim])
                yield
                cre, cim = kb.bank(), kb.bank()
                for j in range(nk):
                    cs_ = slice(j * 128, (j + 1) * 128)
                    _mm(kb, kb.ps[0:64, cre, cs_], gp4[0:64, j, 0, :], yre.ap[0:64, cs_], [gp, yre], [kb.psb[cre]],
                        start=True, stop=False)
                    _mm(kb, kb.ps[0:64, cre, cs_], gp4[0:64, j, 2, :], yim.ap[0:64, cs_], [gp, yim], [kb.psb[cre]],
                        start=False, stop=True)
                    _mm(kb, kb.ps[0:64, cim, cs_], gp4[0:64, j, 1, :], yre.ap[0:64, cs_], [gp, yre], [kb.psb[cim]],
                        start=True, stop=False)
                    _mm(kb, kb.ps[0:64, cim, cs_], gp4[0:64, j, 0, :], yim.ap[0:64, cs_], [gp, yim], [kb.psb[cim]],
                        start=False, stop=True)
                for ri, bk_ in enumerate((cre, cim)):
                    st_ = T.Bst[ri]
                    _cp(kb, "act" if ri == 0 else "dve", st_.ap[0:64, 0:nn], kb.ps[0:64, bk_, 0:nn], [kb.psb[bk_]], [st_])
                    dma(kb, "pool", Bbuf[ri, :, k0:k0 + nk, :], st_.ap[0:64, 0:nn].rearrange("p (k c) -> p k c", c=CG),
                        reads=st_.b, writes=[B_b[k1b]])
                yield
            for nb in range(16):
                bin_ = T.Bin
                for ri in range(2):
                    dma(kb, "sp", bin_[ri].ap[0:NK1, :].rearrange("p (n c) -> p n c", c=CG),
                        Bbuf[ri, nb * 4:(nb + 1) * 4, 0:NK1, :].rearrange("n k c -> k n c"), reads=B_b, writes=bin_[ri].b)
                xg_ = T.xg
                dma(kb, "sp", xg_.ap[0:64, :], ucb[o, g, :, nb * 512:(nb + 1) * 512], reads=[D_["ucb_b"][o][g]], writes=xg_.b)
                by = kb.bank()
                _mm(kb, kb.ps[0:64, by, :], I2t.ap[0:NK1, 0:64], bin_[0].ap[0:NK1, :], [I2t, bin_[0]], [kb.psb[by]], start=True, stop=False)
                _mm(kb, kb.ps[0:64, by, :], I2t.ap[0:NK1, 64:128], bin_[1].ap[0:NK1, :], [I2t, bin_[1]], [kb.psb[by]], start=False, stop=True)
                te = T.te
                if o == 0:
                    sk_ = T.sk
                    dma(kb, "sp", sk_.ap[0:64, :], ucb[2, g, :, nb * 512:(nb + 1) * 512], reads=[D_["ucb_b"][2][g]], writes=sk_.b)
                    _tt(kb, "pool", te.ap[0:64, :], sk_.ap[0:64, :], T.dbc.ap[0:64, :], ALU.mult, [sk_, T.dbc], [te])
                else:
                    _tt(kb, "pool", te.ap[0:64, :], zbuf.ap[0:64, nb * 512:(nb + 1) * 512], T.dbc.ap[0:64, :], ALU.mult,
                        [zbuf, T.dbc], [te])
                _tt(kb, "dve", te.ap[0:64, :], kb.ps[0:64, by, :], te.ap[0:64, :], ALU.add, [kb.psb[by], te], [te])
                if o == 0:
                    _tt(kb, "pool", zbuf.ap[0:64, nb * 512:(nb + 1) * 512], xg_.ap[0:64, :], te.ap[0:64, :], ALU.mult,
                        [xg_, te], [zbuf])
                else:
                    zo_ = T.zo
                    _tt(kb, "pool", zo_.ap[0:64, :], xg_.ap[0:64, :], te.ap[0:64, :], ALU.mult, [xg_, te], [zo_])
                    bt_ = kb.bank()
                    for j in range(4):
                        _tr(kb, kb.ps[:, bt_, j * 64:(j + 1) * 64], zo_.ap[0:64, j * 128:(j + 1) * 128],
                            _IdentView(ident), [zo_], [kb.psb[bt_]])
                    _cp(kb, "act", zT3[:, :, nb * 4:(nb + 1) * 4],
                        kb.ps[:, bt_, 0:256].rearrange("c (j p) -> c p j", p=64), [kb.psb[bt_]], [T.zT])
                yield
            if o == 0 and "z1_dbg" in D_:
                dma(kb, "pool", D_["z1_dbg"][g], zbuf.ap[0:64, :], reads=zbuf.b)
            if o == 1:
                dma(kb, "pool", zhyT[g * CG:(g + 1) * CG, :], T.zT.ap, reads=T.zT.b)

    TC = [alloc_c(), alloc_c()]
    for g0 in range(0, NG4, 2):
        gens = [stage_c(g0 + i, TC[i], i) for i in range(min(2, NG4 - g0))]
        for _ in itertools.zip_longest(*gens):
            pass


class _IdentView:
    def __init__(self, ident):
        self.ap = ident.ap[0:64, 0:64]
        self.b = ident.b
```
